# Optimizing a Trainium2 kernel written in Bass

```python
import jax, jax.numpy as jnp
from jax import lax
import numpy as np

D_MODEL = 1024
BATCH = 4
SEQ = 4096
DEPTH = 4
DEC_BATCH = 128
DEC_SEQ = 4
PAST_LEN = 2048
PAGE_SIZE = 128

N_A_LAYERS = DEPTH // 2
N_B_LAYERS = DEPTH - N_A_LAYERS
D_RNN = D_MODEL
N_LRU_BLOCKS = 4
LRU_BLOCK = D_RNN // N_LRU_BLOCKS
LRU_C = 8.0
CONV_A_WIDTH = 4
D_FF = 3 * D_MODEL
FFN_CONV_WIDTH = 3
HEAD_DIM = 64
N_KV_HEADS = 8
WINDOWS = (128, 512, 2048)
DILATIONS = (1, 4, 16)
N_GROUPS = len(WINDOWS)
MAX_WINDOW = max(WINDOWS)
BLOCK = 128
ROT_DIM = HEAD_DIM // 4
ROPE_THETA = 500000.0
EPS = 1e-6

kernel_name = 'dual_path_rglru_dilated_swa_step'


def rmsnorm(x, g):
    xf = x.astype(jnp.float32)
    y = xf * lax.rsqrt(jnp.mean(xf * xf, axis=-1, keepdims=True) + EPS)
    return (y * g.astype(jnp.float32)).astype(x.dtype)


def rope(x, pos):
    half = ROT_DIM // 2
    inv = ROPE_THETA ** (-jnp.arange(half, dtype=jnp.float32) * (2.0 / ROT_DIM))
    ang = pos.astype(jnp.float32)[:, None] * inv[None, :]
    shape = (1, pos.shape[0]) + (1,) * (x.ndim - 3) + (half,)
    cos = jnp.cos(ang).reshape(shape)
    sin = jnp.sin(ang).reshape(shape)
    xf = x.astype(jnp.float32)
    x1, x2, rest = xf[..., :half], xf[..., half:ROT_DIM], xf[..., ROT_DIM:]
    out = jnp.concatenate([x1 * cos - x2 * sin, x2 * cos + x1 * sin, rest], axis=-1)
    return out.astype(x.dtype)


def causal_dwconv(prev, x, w, b):
    K = w.shape[0]
    T = x.shape[1]
    xc = jnp.concatenate([prev.astype(x.dtype), x], axis=1)
    y = b + sum(xc[:, j:j + T] * w[j] for j in range(K))
    return y, xc[:, T:]


def _lin_combine(e1, e2):
    a1, b1 = e1
    a2, b2 = e2
    return a1 * a2, a2 * b1 + b2


def recurrent_block(h, conv_prev, h0, w_in, conv_w, conv_b, r_w, r_b, i_w, i_b, lam, w_out):
    B, T, _ = h.shape
    gate, xb = jnp.split(h @ w_in, 2, axis=-1)
    xc, conv_state = causal_dwconv(conv_prev, xb, conv_w, conv_b)
    xf = xc.astype(jnp.float32)
    xblk = xf.reshape(B, T, N_LRU_BLOCKS, LRU_BLOCK)
    r = jax.nn.sigmoid(jnp.einsum('btnc,ncd->btnd', xblk, r_w.astype(jnp.float32)) + r_b.astype(jnp.float32)).reshape(B, T, D_RNN)
    i = jax.nn.sigmoid(jnp.einsum('btnc,ncd->btnd', xblk, i_w.astype(jnp.float32)) + i_b.astype(jnp.float32)).reshape(B, T, D_RNN)
    log_a = -LRU_C * r * jax.nn.softplus(-lam.astype(jnp.float32))
    a = jnp.exp(log_a)
    u = jnp.sqrt(-jnp.expm1(2.0 * log_a)) * (i * xf)
    a_cum, u_cum = lax.associative_scan(_lin_combine, (a, u), axis=1)
    hs = a_cum * h0.astype(jnp.float32)[:, None] + u_cum
    y = hs.astype(h.dtype) * jax.nn.gelu(gate)
    return y @ w_out, conv_state, hs[:, -1].astype(h.dtype)


def conv_ffn(h, prev, w_up, conv_w, conv_b, w_down):
    g, v = jnp.split(h @ w_up, 2, axis=-1)
    gc, new_prev = causal_dwconv(prev, g, conv_w, conv_b)
    return (jax.nn.gelu(gc) * v) @ w_down, new_prev


def shared_kv(x, pos, g_kv, w_kv, g_k):
    B, T, _ = x.shape
    kv = (rmsnorm(x, g_kv) @ w_kv).reshape(B, T, 2, N_KV_HEADS, HEAD_DIM)
    k = rope(rmsnorm(kv[:, :, 0], g_k), pos)
    return k, kv[:, :, 1]


def queries(h, pos, w_q, g_q):
    B, T, _ = h.shape
    q = (h @ w_q).reshape(B, T, N_GROUPS, N_KV_HEADS, HEAD_DIM)
    return rope(rmsnorm(q, g_q), pos)


def dilated_band_attn(q, k, v, dilation, span):
    B, T, H, E = q.shape
    L = T // dilation
    nb = -(-L // BLOCK)
    Lp = nb * BLOCK

    def to_res(t):
        t = jnp.moveaxis(t.reshape(B, L, dilation, H, E), 2, 1)
        t = jnp.pad(t, ((0, 0), (0, 0), (0, Lp - L), (0, 0), (0, 0)))
        return t.reshape(B, dilation, nb, BLOCK, H, E)

    def with_prev(t):
        prev = jnp.pad(t, ((0, 0), (0, 0), (1, 0), (0, 0), (0, 0), (0, 0)))[:, :, :-1]
        return jnp.concatenate([prev, t], axis=3)

    qb = to_res(q)
    kk = with_prev(to_res(k))
    vv = with_prev(to_res(v))
    s = jnp.einsum('bdnqhe,bdnkhe->bdnhqk', qb, kk, preferred_element_type=jnp.float32) * (HEAD_DIM ** -0.5)
    qi = jnp.arange(BLOCK)[:, None]
    ki = jnp.arange(2 * BLOCK)[None, :]
    dist = BLOCK + qi - ki
    mq = jnp.arange(nb)[:, None, None] * BLOCK + qi
    valid = (dist >= 0) & (dist <= span) & (mq - dist >= 0)
    s = jnp.where(valid[:, None], s, -jnp.inf)
    m = jnp.max(s, axis=-1, keepdims=True)
    p = jnp.exp(s - m)
    den = jnp.sum(p, axis=-1)
    o = jnp.einsum('bdnhqk,bdnkhe->bdnqhe', p, vv, preferred_element_type=jnp.float32)
    o = o / jnp.swapaxes(den, -1, -2)[..., None]
    lse = jnp.swapaxes(m[..., 0] + jnp.log(den), -1, -2)

    def back(t):
        t = t.reshape((B, dilation, Lp) + t.shape[4:])[:, :, :L]
        return jnp.moveaxis(t, 1, 2).reshape((B, T) + t.shape[3:])

    return back(o), back(lse)


def dilated_gather_attn(q, k_all, v_all, n_past, dilation, span):
    S = q.shape[1]
    idx = n_past + jnp.arange(S)[:, None] - dilation * jnp.arange(span + 1)[None, :]
    valid = idx >= 0
    idxc = jnp.maximum(idx, 0)
    kg = k_all[:, idxc]
    vg = v_all[:, idxc]
    s = jnp.einsum('bshe,bsjhe->bshj', q, kg, preferred_element_type=jnp.float32) * (HEAD_DIM ** -0.5)
    s = jnp.where(valid[:, None, :], s, -jnp.inf)
    m = jnp.max(s, axis=-1, keepdims=True)
    p = jnp.exp(s - m)
    den = jnp.sum(p, axis=-1)
    o = jnp.einsum('bshj,bsjhe->bshe', p, vg, preferred_element_type=jnp.float32) / den[..., None]
    return o, m[..., 0] + jnp.log(den)


def combine_groups(outs, lses, dtype):
    w = jax.nn.softmax(jnp.stack(lses), axis=0)
    o = jnp.sum(w[..., None] * jnp.stack(outs), axis=0)
    B, T, H, E = o.shape
    return o.reshape(B, T, H * E).astype(dtype)


def mixture_prompt(q, k, v):
    outs, lses = [], []
    for g in range(N_GROUPS):
        o, l = dilated_band_attn(q[:, :, g], k, v, DILATIONS[g], WINDOWS[g] // DILATIONS[g])
        outs.append(o)
        lses.append(l)
    return combine_groups(outs, lses, q.dtype)


def mixture_sample(q, k_all, v_all, n_past):
    outs, lses = [], []
    for g in range(N_GROUPS):
        o, l = dilated_gather_attn(q[:, :, g], k_all, v_all, n_past, DILATIONS[g], WINDOWS[g] // DILATIONS[g])
        outs.append(o)
        lses.append(l)
    return combine_groups(outs, lses, q.dtype)


def trunk(x, pos, conv_a_prev, h0, ffn_prev, extend_kv, attend, p):
    hs_new, conv_new, ffn_new = [], [], []
    k = v = k_att = v_att = None
    for layer in range(DEPTH):
        if layer < N_A_LAYERS:
            out, cs, hl = recurrent_block(
                rmsnorm(x, p['norm_mix_a'][layer]), conv_a_prev[layer], h0[layer],
                p['w_in_a'][layer], p['conv_a_w'][layer], p['conv_a_b'][layer],
                p['gate_r_w'][layer], p['gate_r_b'][layer], p['gate_i_w'][layer], p['gate_i_b'][layer],
                p['lru_lambda'][layer], p['w_out_a'][layer])
            conv_new.append(cs)
            hs_new.append(hl)
        else:
            j = layer - N_A_LAYERS
            q = queries(rmsnorm(x, p['norm_mix_b'][j]), pos, p['w_q'][j], p['q_norm'][j])
            out = attend(q, k_att, v_att) @ p['w_o'][j]
        x = x + out
        f, fs = conv_ffn(rmsnorm(x, p['norm_ffn'][layer]), ffn_prev[layer], p['w_ffn_up'][layer],
                         p['ffn_conv_w'][layer], p['ffn_conv_b'][layer], p['w_ffn_down'][layer])
        ffn_new.append(fs)
        x = x + f
        if layer == N_A_LAYERS - 1:
            k, v = shared_kv(x, pos, p['norm_kv'], p['w_kv'], p['k_norm'])
            k_att, v_att = extend_kv(k, v)
    return x, k, v, jnp.stack(hs_new), jnp.stack(conv_new), jnp.stack(ffn_new)


def setup_inputs(seed: int = 0) -> dict:
    key = jax.random.key(seed)
    ks = iter(jax.random.split(key, 32))

    def nrm(shape, scale):
        return scale * jax.random.normal(next(ks), shape, jnp.float32)

    w_buf = min(MAX_WINDOW, PAST_LEN)
    a0 = None
    d = {}
    d['x_prompt'] = nrm((BATCH, SEQ, D_MODEL), 1.0)
    d['x_sample'] = nrm((DEC_BATCH, DEC_SEQ, D_MODEL), 1.0)
    d['cache_k'] = nrm((DEC_BATCH, w_buf, N_KV_HEADS, HEAD_DIM), 1.0)
    d['cache_v'] = nrm((DEC_BATCH, w_buf, N_KV_HEADS, HEAD_DIM), 1.0)
    d['state_rglru_h'] = nrm((N_A_LAYERS, DEC_BATCH, D_RNN), 0.5)
    d['state_rglru_conv'] = nrm((N_A_LAYERS, DEC_BATCH, CONV_A_WIDTH - 1, D_RNN), 1.0)
    d['state_ffn_conv'] = nrm((DEPTH, DEC_BATCH, FFN_CONV_WIDTH - 1, D_FF), 1.0)
    d['norm_mix_a'] = 1.0 + nrm((N_A_LAYERS, D_MODEL), 0.1)
    d['w_in_a'] = nrm((N_A_LAYERS, D_MODEL, 2 * D_RNN), D_MODEL ** -0.5)
    d['conv_a_w'] = nrm((N_A_LAYERS, CONV_A_WIDTH, D_RNN), CONV_A_WIDTH ** -0.5)
    d['conv_a_b'] = nrm((N_A_LAYERS, D_RNN), 0.01)
    d['gate_r_w'] = nrm((N_A_LAYERS, N_LRU_BLOCKS, LRU_BLOCK, LRU_BLOCK), LRU_BLOCK ** -0.5)
    d['gate_r_b'] = nrm((N_A_LAYERS, N_LRU_BLOCKS, LRU_BLOCK), 0.01)
    d['gate_i_w'] = nrm((N_A_LAYERS, N_LRU_BLOCKS, LRU_BLOCK, LRU_BLOCK), LRU_BLOCK ** -0.5)
    d['gate_i_b'] = nrm((N_A_LAYERS, N_LRU_BLOCKS, LRU_BLOCK), 0.01)
    a0 = jax.random.uniform(next(ks), (N_A_LAYERS, D_RNN), jnp.float32, minval=0.9, maxval=0.999)
    d['lru_lambda'] = jnp.log(a0) - jnp.log1p(-a0)
    d['w_out_a'] = nrm((N_A_LAYERS, D_RNN, D_MODEL), D_RNN ** -0.5)
    d['norm_kv'] = 1.0 + nrm((D_MODEL,), 0.1)
    d['w_kv'] = nrm((D_MODEL, 2 * N_KV_HEADS * HEAD_DIM), D_MODEL ** -0.5)
    d['k_norm'] = 1.0 + nrm((HEAD_DIM,), 0.1)
    d['norm_mix_b'] = 1.0 + nrm((N_B_LAYERS, D_MODEL), 0.1)
    d['w_q'] = nrm((N_B_LAYERS, D_MODEL, N_GROUPS * N_KV_HEADS * HEAD_DIM), D_MODEL ** -0.5)
    d['q_norm'] = 1.0 + nrm((N_B_LAYERS, HEAD_DIM), 0.1)
    d['w_o'] = nrm((N_B_LAYERS, N_KV_HEADS * HEAD_DIM, D_MODEL), (N_KV_HEADS * HEAD_DIM) ** -0.5)
    d['norm_ffn'] = 1.0 + nrm((DEPTH, D_MODEL), 0.1)
    d['w_ffn_up'] = nrm((DEPTH, D_MODEL, 2 * D_FF), D_MODEL ** -0.5)
    d['ffn_conv_w'] = nrm((DEPTH, FFN_CONV_WIDTH, D_FF), FFN_CONV_WIDTH ** -0.5)
    d['ffn_conv_b'] = nrm((DEPTH, D_FF), 0.01)
    d['w_ffn_down'] = nrm((DEPTH, D_FF, D_MODEL), D_FF ** -0.5)
    return d


def reference(x_prompt, x_sample, cache_k, cache_v, state_rglru_h, state_rglru_conv, state_ffn_conv,
              norm_mix_a, w_in_a, conv_a_w, conv_a_b, gate_r_w, gate_r_b, gate_i_w, gate_i_b,
              lru_lambda, w_out_a, norm_kv, w_kv, k_norm, norm_mix_b, w_q, q_norm, w_o,
              norm_ffn, w_ffn_up, ffn_conv_w, ffn_conv_b, w_ffn_down):
    p = dict(norm_mix_a=norm_mix_a, w_in_a=w_in_a, conv_a_w=conv_a_w, conv_a_b=conv_a_b,
             gate_r_w=gate_r_w, gate_r_b=gate_r_b, gate_i_w=gate_i_w, gate_i_b=gate_i_b,
             lru_lambda=lru_lambda, w_out_a=w_out_a, norm_kv=norm_kv, w_kv=w_kv, k_norm=k_norm,
             norm_mix_b=norm_mix_b, w_q=w_q, q_norm=q_norm, w_o=w_o, norm_ffn=norm_ffn,
             w_ffn_up=w_ffn_up, ffn_conv_w=ffn_conv_w, ffn_conv_b=ffn_conv_b, w_ffn_down=w_ffn_down)

    bp, tp, _ = x_prompt.shape
    dt = x_prompt.dtype
    pos_p = jnp.arange(tp)
    y_prompt, k_p, v_p, p_h, p_conv, p_ffn = trunk(
        x_prompt, pos_p,
        jnp.zeros((N_A_LAYERS, bp, CONV_A_WIDTH - 1, D_RNN), dt),
        jnp.zeros((N_A_LAYERS, bp, D_RNN), jnp.float32),
        jnp.zeros((DEPTH, bp, FFN_CONV_WIDTH - 1, D_FF), dt),
        lambda k, v: (k, v),
        mixture_prompt, p)
    keep = min(MAX_WINDOW, tp)
    p_cache_k = k_p[:, tp - keep:]
    p_cache_v = v_p[:, tp - keep:]

    n_past = cache_k.shape[1]
    pos_s = PAST_LEN + jnp.arange(x_sample.shape[1])
    y_sample, s_cache_k, s_cache_v, s_h, s_conv, s_ffn = trunk(
        x_sample, pos_s, state_rglru_conv, state_rglru_h, state_ffn_conv,
        lambda k, v: (jnp.concatenate([cache_k.astype(k.dtype), k], axis=1),
                      jnp.concatenate([cache_v.astype(v.dtype), v], axis=1)),
        lambda q, ka, va: mixture_sample(q, ka, va, n_past), p)

    return (y_prompt, y_sample, p_h, p_conv, p_ffn, p_cache_k, p_cache_v,
            s_h, s_conv, s_ffn, s_cache_k, s_cache_v)
```

```python
import math
from contextlib import ExitStack
import numpy as np
import concourse.bass as bass
import concourse.mybir as mybir
from concourse.bass_utils import run_bass_kernel_spmd

F32 = mybir.dt.float32
BF16 = mybir.dt.bfloat16
I32 = mybir.dt.int32
AF = mybir.ActivationFunctionType
ALU = mybir.AluOpType

P = 1024
NT = 4
NS = 64
TOK = 4096
KVW = TOK + NS
HB = 1152
NHB = 33
EPS = 1e-6
NB = 16

_po = {}
_n = 0
for _name, _w in [('nma', 16), ('caw', 64), ('cab', 16), ('grb', 16), ('gib', 16), ('lam', 16), ('nkv', 8),
                  ('nmb', 16), ('nff', 32), ('fcw', 288), ('fcb', 96), ('gk', 1), ('gq', 2), ('inv', 1), ('cl', 16)]:
    _po[_name] = _n
    _n += _w
NPAR = _n
C_ID, C_MP, C_MC, C_BD, C_PM, C_ON, C_MS0, C_MN, C_M64 = 0, 128, 256, 384, 512, 640, 768, 772, 784
NCB = 912


class _Space:
    def __init__(self):
        self.segs = []

    def deps(self, lo, hi, is_write, add):
        for a, b, w, r in self.segs:
            if b <= lo or a >= hi:
                continue
            if w is not None:
                add(w)
            if is_write:
                for t in r.values():
                    add(t)

    def apply(self, lo, hi, is_write, tok):
        out = []
        covered = []
        for seg in self.segs:
            a, b, w, r = seg
            if b <= lo or a >= hi:
                out.append(seg)
                continue
            if a < lo:
                out.append([a, lo, w, dict(r)])
            if b > hi:
                out.append([hi, b, w, dict(r)])
            ia, ib = max(a, lo), min(b, hi)
            if not is_write:
                r2 = dict(r)
                r2[(tok[0], tok[1])] = tok
                out.append([ia, ib, w, r2])
                covered.append((ia, ib))
        if is_write:
            out.append([lo, hi, tok, {}])
        else:
            covered.sort()
            cur = lo
            for a, b in covered:
                if a > cur:
                    out.append([cur, a, None, {(tok[0], tok[1]): tok}])
                cur = max(cur, b)
            if cur < hi:
                out.append([cur, hi, None, {(tok[0], tok[1]): tok}])
        out.sort(key=lambda s: s[0])
        self.segs = out


_ESZ = {F32: 4, BF16: 2, I32: 4}


def _extent(ap):
    pat = list(ap.ap)
    es = _ESZ[ap.dtype]
    pstride = pat[0][0]
    off = ap.offset % pstride if pstride > 0 else ap.offset
    lo = off
    hi = off
    for st, cnt in pat[1:]:
        if cnt > 1:
            if st >= 0:
                hi += st * (cnt - 1)
            else:
                lo += st * (cnt - 1)
    return ap.tensor.name, lo * es, (hi + 1) * es


class Sched:
    ENGS = ('pe', 'act', 'dve', 'pool', 'sp')

    def __init__(self, nc, stack):
        self.nc = nc
        self.q = {e: [] for e in self.ENGS}
        self.cnt = {e: 0 for e in self.ENGS}
        self.sem = {e: stack.enter_context(nc.semaphore("s_" + e)) for e in self.ENGS}
        self.waited = {e: {} for e in self.ENGS}
        self.spaces = {}
        self.dpool = {}
        for qe in ('sp', 'pool'):
            self.dpool[qe] = [[stack.enter_context(nc.semaphore("d_%s%d" % (qe, i))), 0] for i in range(24)]
        self.dnext = {'sp': 0, 'pool': 0}
        self.dall = []
        self.psums = []
        self.psn = 0
        self.nops = 0

    def psum(self):
        p = self.psums[self.psn % len(self.psums)]
        self.psn += 1
        return p

    def _semh(self, tok):
        if tok[0] == 'e':
            return self.sem[tok[1]]
        return self.dall[tok[1]][0]

    def _wait(self, eng, tok):
        key = (tok[0], tok[1])
        if self.waited[eng].get(key, 0) >= tok[2]:
            return
        self.waited[eng][key] = tok[2]
        semh = self._semh(tok)
        val = tok[2]
        self.q[eng].append(lambda h, semh=semh, val=val: h.wait_ge(semh, val))

    def _collect(self, eng, reads, writes):
        toks = {}

        def add(t):
            k = (t[0], t[1])
            if k not in toks or toks[k][2] < t[2]:
                toks[k] = t
        acc = []
        for ap, isw in [(a, False) for a in reads] + [(a, True) for a in writes]:
            name, lo, hi = _extent(ap)
            sp = self.spaces.get(name)
            if sp is None:
                sp = self.spaces[name] = _Space()
            acc.append((sp, lo, hi, isw))
            sp.deps(lo, hi, isw or name.startswith('ps'), add)
        for t in toks.values():
            if t[0] == 'e' and t[1] == eng and eng == 'pe':
                continue
            self._wait(eng, t)
        return acc

    def _commit(self, acc, tok):
        for sp, lo, hi, isw in acc:
            if not isw:
                sp.apply(lo, hi, False, tok)
        for sp, lo, hi, isw in acc:
            if isw:
                sp.apply(lo, hi, True, tok)

    def do(self, eng, op):
        fn, reads, writes = op
        acc = self._collect(eng, reads, writes)
        self.cnt[eng] += 1
        idx = self.cnt[eng]
        sem = self.sem[eng]
        self.q[eng].append(lambda h, fn=fn, sem=sem: fn(h).then_inc(sem, 1))
        self._commit(acc, ('e', eng, idx))
        self.nops += 1

    def pe(self, ops):
        reads = []
        writes = []
        for fn, r, w in ops:
            reads += r
            writes += w
        acc = self._collect('pe', reads, writes)
        self.cnt['pe'] += 1
        idx = self.cnt['pe']
        sem = self.sem['pe']
        for fn, r, w in ops[:-1]:
            self.q['pe'].append(lambda h, fn=fn: fn(h))
        fn = ops[-1][0]
        self.q['pe'].append(lambda h, fn=fn, sem=sem: fn(h).then_inc(sem, 1))
        self._commit(acc, ('e', 'pe', idx))
        self.nops += len(ops)

    def dma(self, qe, out, in_, r=(), w=()):
        acc = self._collect(qe, list(r), list(w))
        pool = self.dpool[qe]
        k = self.dnext[qe] % len(pool)
        self.dnext[qe] += 1
        ent = pool[k]
        if len(ent) == 2:
            ent.append(len(self.dall))
            self.dall.append(ent)
        gidx = ent[2]
        if ent[1] > 0:
            self._wait(qe, ('d', gidx, ent[1]))
        ent[1] += 16
        semh = ent[0]
        if qe == 'pool':
            self.q[qe].append(lambda h, out=out, in_=in_, semh=semh:
                              h.dma_start(out=out, in_=in_, max_dma_last_dim=4096).then_inc(semh, 16))
        else:
            self.q[qe].append(lambda h, out=out, in_=in_, semh=semh: h.dma_start(out=out, in_=in_).then_inc(semh, 16))
        self._commit(acc, ('d', gidx, ent[1]))
        self.nops += 1

    def finish(self):
        for ent in self.dall:
            if ent[1] > 0:
                self._wait('sp', ('d', ent[2], ent[1]))
        for e in ('pe', 'act', 'dve', 'pool'):
            if self.cnt[e] > 0:
                self._wait('sp', ('e', e, self.cnt[e]))

    def emit(self, block):
        nc = self.nc
        m = {'pe': block.tensor, 'act': block.scalar, 'dve': block.vector, 'pool': block.gpsimd, 'sp': block.sync}
        for e in self.ENGS:
            lst = self.q[e]
            if not lst:
                continue

            def body(h, lst=lst):
                for f in lst:
                    f(h)
            m[e](body)


def _isap(x):
    return hasattr(x, 'ap') and hasattr(x, 'tensor')


def ACT(out, in_, func, bias=None, scale=None):
    kw = {}
    rd = [in_]
    if bias is not None:
        kw['bias'] = bias
        if _isap(bias):
            rd.append(bias)
    if scale is not None:
        kw['scale'] = scale
        if _isap(scale):
            rd.append(scale)
    return (lambda h: h.activation(out=out, in_=in_, func=func, **kw), rd, [out])


def TS(out, in0, s1, s2, op0, op1=None):
    rd = [in0] + [s for s in (s1, s2) if _isap(s)]
    if op1 is None:
        return (lambda h: h.tensor_scalar(out=out, in0=in0, scalar1=s1, scalar2=None, op0=op0), rd, [out])
    return (lambda h: h.tensor_scalar(out=out, in0=in0, scalar1=s1, scalar2=s2, op0=op0, op1=op1), rd, [out])


def TT(out, a, b, op):
    return (lambda h: h.tensor_tensor(out=out, in0=a, in1=b, op=op), [a, b], [out])


def STT(out, in0, sc, in1, op0, op1):
    rd = [in0, in1] + ([sc] if _isap(sc) else [])
    return (lambda h: h.scalar_tensor_tensor(out=out, in0=in0, scalar=sc, in1=in1, op0=op0, op1=op1), rd, [out])


def SCAN(out, d0, d1, init):
    rd = [d0, d1] + ([init] if _isap(init) else [])
    return (lambda h: h.tensor_tensor_scan(out, d0, d1, init, op0=ALU.mult, op1=ALU.add), rd, [out])


def COPY(out, in_):
    return (lambda h: h.tensor_copy(out, in_), [in_], [out])


def RECIP(out, in_):
    return (lambda h: h.reciprocal(out, in_), [in_], [out])


def MEMSET(out, v):
    return (lambda h: h.memset(out, v), [], [out])


def MM(out, lhsT, rhs, start, stop):
    return (lambda h: h.matmul(out, lhsT, rhs, start=start, stop=stop), [lhsT, rhs], [out])


def TR(out, in_, ident):
    return (lambda h: h.transpose(out, in_, ident), [in_, ident], [out])


class Tile:
    def __init__(self, t):
        self.t = t
        self.has_s = (t == NT - 1)
        self.W = P + (NS if self.has_s else 0)
        self.subs = [(0, 512), (512, 512)] + ([(P, NS)] if self.has_s else [])


def build_program():
    nc = bass.Bass("TRN2", target_bir_lowering=False)
    d = {}

    def din(name, shape):
        d[name] = nc.dram_tensor(name, list(shape), F32, kind="ExternalInput").ap()

    def dout(name, shape):
        d[name] = nc.dram_tensor(name, list(shape), F32, kind="ExternalOutput").ap()

    din('xT', [1024, TOK]); din('xsT', [1024, NS])
    din('ck', [NB, 9, 128, 512]); din('cv', [NB, 9, 128, 512])
    din('sh', [2, 128, 128]); din('sc', [2, 128, 384]); din('sf', [4, 128, 768])
    din('par', [128, NPAR]); din('cst', [128, NCB]); din('pos', [128, KVW])
    din('win', [2, 8, 128, 2048]); din('wg', [2, 4, 128, 1024]); din('wout', [2, 4, 128, 2048])
    din('wup', [4, 24, 128, 2048]); din('wdn', [4, 3, 128, 8192]); din('wkv', [4, 128, 2048])
    din('wq', [2, 12, 128, 1024]); din('wo', [2, 2, 128, 2048])
    dout('yT', [1024, TOK]); dout('ysT', [1024, NS])
    dout('oh', [128, 16]); dout('oc', [128, 48]); dout('of', [128, 192])
    dout('pkT', [512, 2048]); dout('pvT', [512, 2048])
    dout('soh', [2, 128, 128]); dout('soc', [2, 128, 384]); dout('sof', [4, 128, 768])
    dout('skT', [512, NS]); dout('svtok', [NS, 512])

    with ExitStack() as st:
        def sb(name, shape, dt):
            return st.enter_context(nc.sbuf_tensor(name, list(shape), dt))
        X = sb("X", [128, 8, P + NS], F32)
        KT = sb("KT", [128, 4, KVW], BF16)
        VT = sb("VT", [128, 4, KVW], BF16)
        WS = sb("WS", [128, 4, 2048], BF16)
        AR = sb("AR", [128, NHB * HB], BF16)
        PAR = sb("PAR", [128, NPAR], F32)
        CB = sb("CB", [128, NCB], BF16)
        HST = sb("HST", [128, 2, 8], F32)
        CAH = sb("CAH", [128, 2, 8, 3], F32)
        FH = sb("FH", [128, 4, 24, 2], F32)
        SHL = sb("SHL", [128, 8, NB], F32)
        SCL = sb("SCL", [128, 8, NB, 3], F32)
        SFL = sb("SFL", [128, 24, NB, 2], F32)
        TMP16 = sb("TMP16", [128, NB], F32)
        VSNB = sb("VSNB", [NS, 512], BF16)
        VSNF = sb("VSNF", [NS, 512], F32)
        SMALL = sb("SMALL", [128, 2, 16], F32)
        PTS = sb("PTS", [128, 4, 128], BF16)
        VNB = sb("VNB", [4, 2, 512], BF16)
        S = Sched(nc, st)
        S.psums = [st.enter_context(nc.psum_tensor("ps%d" % i, [128, 512], F32)) for i in range(8)]

        def arb(i, n=HB):
            return AR[:, i * HB:i * HB + n]

        def arf(i, n=HB):
            return AR[:, i * HB:(i + 2) * HB].bitcast(F32)[:, 0:n]

        def XN(k):
            return arb(k)

        IDENT = CB[:, C_ID:C_ID + 128]
        MASKP = CB[:, C_MP:C_MP + 128]
        MASKC = CB[:, C_MC:C_MC + 128]
        BD = CB[:, C_BD:C_BD + 128]
        PM = CB[:, C_PM:C_PM + 128]
        ONES = CB[:, C_ON:C_ON + 128]

        def par(name, i=0):
            o = _po[name] + i
            return PAR[:, o:o + 1]

        wsn = [0]

        def wload(src, n=2048, parts=128):
            k = wsn[0] % 4
            wsn[0] += 1
            dst = WS[0:parts, k, 0:n]
            S.dma('pool', dst, src, w=[dst])
            return WS[:, k, :]

        S.dma('sp', PAR[:, :], d['par'], w=[PAR[:, :]])
        S.dma('pool', CB[:, :], d['cst'], w=[CB[:, :]])
        S.do('dve', MEMSET(HST[:, :, :], 0.0))
        S.do('dve', MEMSET(CAH[:, :, :, :], 0.0))
        S.do('dve', MEMSET(FH[:, :, :, :], 0.0))
        lam = PAR[:, _po['lam']:_po['lam'] + 16]
        clv = PAR[:, _po['cl']:_po['cl'] + 16]
        S.do('act', ACT(clv, lam, AF.Exp, scale=-1.0))
        S.do('act', ACT(clv, clv, AF.Ln, bias=1.0))
        S.do('dve', TS(clv, clv, -8.0, None, ALU.mult))

        def load_x(T):
            src = d['xT'].rearrange("(c p) t -> p c t", p=128)[:, :, P * T.t:P * T.t + P]
            S.dma('sp', X[:, :, 0:P], src, w=[X[:, :, 0:P]])
            if T.has_s:
                src = d['xsT'].rearrange("(c p) t -> p c t", p=128)
                S.dma('sp', X[:, :, P:P + NS], src, w=[X[:, :, P:P + NS]])

        def store_y(T):
            dst = d['yT'].rearrange("(c p) t -> p c t", p=128)[:, :, P * T.t:P * T.t + P]
            S.dma('sp', dst, X[:, :, 0:P], r=[X[:, :, 0:P]])
            if T.has_s:
                dst = d['ysT'].rearrange("(c p) t -> p c t", p=128)
                S.dma('sp', dst, X[:, :, P:P + NS], r=[X[:, :, P:P + NS]])

        def rmsnorm(T, gname, gi):
            SQ = arb(32)
            RS = arf(30)
            W = T.W
            for (c0, n) in T.subs:
                ps = S.psum()
                for c in range(8):
                    sq = SQ[:, (c % 2) * 512:(c % 2) * 512 + n]
                    S.do('act', ACT(sq, X[:, c, c0:c0 + n], AF.Square))
                    S.pe([MM(ps[:, 0:n], ONES, sq, c == 0, c == 7)])
                S.do('act', ACT(RS[:, c0:c0 + n], ps[:, 0:n], AF.Sqrt, bias=EPS, scale=1.0 / 1024))
                S.do('dve', RECIP(RS[:, c0:c0 + n], RS[:, c0:c0 + n]))
            for c in range(8):
                S.do('dve', STT(XN(c)[:, 0:W], X[:, c, 0:W], par(gname, gi * 8 + c), RS[:, 0:W], ALU.mult, ALU.mult))

        def sview(ap64, s=4):
            return ap64.rearrange("p (b s) -> p b s", s=s)

        def a_mix(T, L):
            W = T.W
            subs = T.subs
            XBH = arf(16)
            XBHs = XBH[:, 1028:1028 + NB * 7].rearrange("p (b j) -> p b j", j=7)
            XCs = [arf(18), arf(20)]
            XCBs = [arb(22), arb(23)]
            RA = arf(24)
            IU = arf(26)
            TH = arf(28)

            def GG(c):
                return arb(8 + c)
            if T.has_s:
                S.dma('sp', SHL[:, :, :], d['sh'][L].rearrange("p (c b) -> p c b", c=8), w=[SHL[:, :, :]])
                S.dma('sp', SCL[:, :, :, :], d['sc'][L].rearrange("p (c b j) -> p c b j", c=8, b=NB),
                      w=[SCL[:, :, :, :]])
            for n in range(4):
                wgt = wload(d['win'][L, 2 * n]).rearrange("p (c k q) -> p c k q", c=2, k=8)
                for cc in range(2):
                    c = 2 * n + cc
                    for (c0, nn) in subs:
                        ps = S.psum()
                        S.pe([MM(ps[:, 0:nn], wgt[:, cc, k, :], XN(k)[:, c0:c0 + nn], k == 0, k == 7) for k in range(8)])
                        S.do('act', ACT(GG(c)[:, c0:c0 + nn], ps[:, 0:nn], AF.Gelu_apprx_tanh))
                wxb = wload(d['win'][L, 2 * n + 1]).rearrange("p (c k q) -> p c k q", c=2, k=8)
                wrg = wload(d['wg'][L, n], n=1024)[:, 0:1024].rearrange("p (g k o) -> p g k o", g=2, k=2)
                for cc in range(2):
                    c = 2 * n + cc
                    XC = XCs[cc]
                    S.do('dve', COPY(XBH[:, 0:3], CAH[:, L, c, :]))
                    if T.has_s:
                        S.do('dve', COPY(XBHs[:, :, 0:3], SCL[:, c, :, :]))
                    for (c0, nn) in subs:
                        ps = S.psum()
                        S.pe([MM(ps[:, 0:nn], wxb[:, cc, k, :], XN(k)[:, c0:c0 + nn], k == 0, k == 7) for k in range(8)])
                        if c0 < P:
                            S.do('act', ACT(XBH[:, 3 + c0:3 + c0 + nn], ps[:, 0:nn], AF.Copy))
                        else:
                            S.do('act', ACT(XBHs[:, :, 3:7], sview(ps[:, 0:NS]), AF.Copy))
                    S.do('dve', COPY(CAH[:, L, c, :], XBH[:, P:P + 3]))
                    if T.has_s:
                        S.do('dve', COPY(SCL[:, c, :, :], XBHs[:, :, 4:7]))
                    cw = [par('caw', L * 32 + j * 8 + c) for j in range(4)]
                    cbias = par('cab', L * 8 + c)
                    S.do('dve', TS(XC[:, 0:P], XBH[:, 3:3 + P], cw[3], cbias, ALU.mult, ALU.add))
                    for j in (2, 1, 0):
                        S.do('dve', STT(XC[:, 0:P], XBH[:, j:j + P], cw[j], XC[:, 0:P], ALU.mult, ALU.add))
                    if T.has_s:
                        XCv = sview(XC[:, P:P + NS])
                        S.do('dve', TS(XCv, XBHs[:, :, 3:7], cw[3], cbias, ALU.mult, ALU.add))
                        for j in (2, 1, 0):
                            S.do('dve', STT(XCv, XBHs[:, :, j:j + 4], cw[j], XCv, ALU.mult, ALU.add))
                    S.do('act', ACT(XCBs[cc][:, 0:W], XC[:, 0:W], AF.Copy))
                for cc in range(2):
                    c = 2 * n + cc
                    XC = XCs[cc]
                    for gate in range(2):
                        dst = RA if gate == 0 else IU
                        gb = par('grb' if gate == 0 else 'gib', L * 8 + c)
                        for (c0, nn) in subs:
                            ps = S.psum()
                            S.pe([MM(ps[:, 0:nn], wrg[:, gate, k, cc * 128:(cc + 1) * 128], XCBs[k][:, c0:c0 + nn],
                                     k == 0, k == 1) for k in range(2)])
                            S.do('act', ACT(dst[:, c0:c0 + nn], ps[:, 0:nn], AF.Sigmoid, bias=gb))
                    S.do('act', ACT(RA[:, 0:W], RA[:, 0:W], AF.Exp, scale=par('cl', L * 8 + c)))
                    S.do('dve', TT(TH[:, 0:W], RA[:, 0:W], RA[:, 0:W], ALU.mult))
                    S.do('act', ACT(TH[:, 0:W], TH[:, 0:W], AF.Sqrt, bias=1.0, scale=-1.0))
                    S.do('dve', TT(IU[:, 0:W], IU[:, 0:W], XC[:, 0:W], ALU.mult))
                    S.do('dve', TT(IU[:, 0:W], IU[:, 0:W], TH[:, 0:W], ALU.mult))
                    S.do('dve', SCAN(TH[:, 0:P], RA[:, 0:P], IU[:, 0:P], HST[:, L, c:c + 1]))
                    S.do('dve', COPY(HST[:, L, c:c + 1], TH[:, P - 1:P]))
                    if T.has_s:
                        As = sview(RA[:, P:P + NS])
                        Us = sview(IU[:, P:P + NS])
                        S.do('dve', TT(TMP16[:, :], As[:, :, 0], SHL[:, c, :], ALU.mult))
                        S.do('dve', TT(Us[:, :, 0], Us[:, :, 0], TMP16[:, :], ALU.add))
                        S.do('dve', MEMSET(As[:, :, 0], 0.0))
                        S.do('dve', SCAN(TH[:, P:P + NS], RA[:, P:P + NS], IU[:, P:P + NS], 0.0))
                        S.do('dve', COPY(SHL[:, c, :], sview(TH[:, P:P + NS])[:, :, 3]))
                    S.do('dve', TT(GG(c)[:, 0:W], TH[:, 0:W], GG(c)[:, 0:W], ALU.mult))
            if T.has_s:
                S.dma('sp', d['soh'][L].rearrange("p (c b) -> p c b", c=8), SHL[:, :, :], r=[SHL[:, :, :]])
                S.dma('sp', d['soc'][L].rearrange("p (c b j) -> p c b j", c=8, b=NB), SCL[:, :, :, :],
                      r=[SCL[:, :, :, :]])
            for i in range(4):
                wsl = wload(d['wout'][L, i]).rearrange("p (c k q) -> p c k q", c=2, k=8)
                for cc in range(2):
                    m = 2 * i + cc
                    for (c0, nn) in subs:
                        ps = S.psum()
                        S.pe([MM(ps[:, 0:nn], wsl[:, cc, k, :], GG(k)[:, c0:c0 + nn], k == 0, k == 7) for k in range(8)])
                        S.do('dve', TT(X[:, m, c0:c0 + nn], X[:, m, c0:c0 + nn], ps[:, 0:nn], ALU.add))

        def ffn(T, L):
            W = T.W
            subs = T.subs

            def ACTB(jj):
                return arb(8 + jj)
            WD = AR[:, 16 * HB:16 * HB + 8192].rearrange("p (j m) -> p j m", j=8)
            GH = arf(24)
            GHs = GH[:, 1028:1028 + NB * 6].rearrange("p (b j) -> p b j", j=6)
            GC = arf(26)
            VB = arb(28)
            if T.has_s:
                S.dma('sp', SFL[:, :, :, :], d['sf'][L].rearrange("p (j b s) -> p j b s", j=24, b=NB),
                      w=[SFL[:, :, :, :]])
            for G in range(3):
                for jj in range(8):
                    j = 8 * G + jj
                    wsl = wload(d['wup'][L, j]).rearrange("p (k q) -> p k q", k=8)
                    if jj == 2:
                        wdst = AR[:, 16 * HB:16 * HB + 8192].rearrange("p (a b) -> p a b", a=4)
                        S.dma('pool', wdst, d['wdn'][L, G].rearrange("p (a b) -> p a b", a=4), w=[wdst])
                    S.do('dve', COPY(GH[:, 0:2], FH[:, L, j, :]))
                    if T.has_s:
                        S.do('dve', COPY(GHs[:, :, 0:2], SFL[:, j, :, :]))
                    for (c0, nn) in subs:
                        psg = S.psum()
                        S.pe([MM(psg[:, 0:nn], wsl[:, k, 0:128], XN(k)[:, c0:c0 + nn], k == 0, k == 7) for k in range(8)])
                        psv = S.psum()
                        S.pe([MM(psv[:, 0:nn], wsl[:, k, 128:256], XN(k)[:, c0:c0 + nn], k == 0, k == 7) for k in range(8)])
                        if c0 < P:
                            S.do('act', ACT(GH[:, 2 + c0:2 + c0 + nn], psg[:, 0:nn], AF.Copy))
                        else:
                            S.do('act', ACT(GHs[:, :, 2:6], sview(psg[:, 0:NS]), AF.Copy))
                        S.do('act', ACT(VB[:, c0:c0 + nn], psv[:, 0:nn], AF.Copy))
                    S.do('dve', COPY(FH[:, L, j, :], GH[:, P:P + 2]))
                    if T.has_s:
                        S.do('dve', COPY(SFL[:, j, :, :], GHs[:, :, 4:6]))
                    fw = [par('fcw', L * 72 + tap * 24 + j) for tap in range(3)]
                    fb = par('fcb', L * 24 + j)
                    S.do('dve', TS(GC[:, 0:P], GH[:, 2:2 + P], fw[2], fb, ALU.mult, ALU.add))
                    S.do('dve', STT(GC[:, 0:P], GH[:, 1:1 + P], fw[1], GC[:, 0:P], ALU.mult, ALU.add))
                    S.do('dve', STT(GC[:, 0:P], GH[:, 0:P], fw[0], GC[:, 0:P], ALU.mult, ALU.add))
                    if T.has_s:
                        GCv = sview(GC[:, P:P + NS])
                        S.do('dve', TS(GCv, GHs[:, :, 2:6], fw[2], fb, ALU.mult, ALU.add))
                        S.do('dve', STT(GCv, GHs[:, :, 1:5], fw[1], GCv, ALU.mult, ALU.add))
                        S.do('dve', STT(GCv, GHs[:, :, 0:4], fw[0], GCv, ALU.mult, ALU.add))
                    S.do('act', ACT(ACTB(jj)[:, 0:W], GC[:, 0:W], AF.Gelu_apprx_tanh))
                    S.do('dve', TT(ACTB(jj)[:, 0:W], ACTB(jj)[:, 0:W], VB[:, 0:W], ALU.mult))
                for m in range(8):
                    for (c0, nn) in subs:
                        ps = S.psum()
                        S.pe([MM(ps[:, 0:nn], WD[:, jj, 128 * m:128 * m + 128], ACTB(jj)[:, c0:c0 + nn], jj == 0, jj == 7)
                              for jj in range(8)])
                        S.do('dve', TT(X[:, m, c0:c0 + nn], X[:, m, c0:c0 + nn], ps[:, 0:nn], ALU.add))
            if T.has_s:
                S.dma('sp', d['sof'][L].rearrange("p (j b s) -> p j b s", j=24, b=NB), SFL[:, :, :, :],
                      r=[SFL[:, :, :, :]])

        def rope_tables(T):
            W = T.W
            ANG = arf(26)
            KI = AR[:, 28 * HB:30 * HB].bitcast(I32)[:, 0:HB]
            KF = arf(30)
            C = arf(22)
            Sn = arf(24)
            S.dma('sp', ANG[:, 0:P], d['pos'][:, P * T.t:P * T.t + P], w=[ANG[:, 0:P]])
            if T.has_s:
                S.dma('sp', ANG[:, P:P + NS], d['pos'][:, TOK:TOK + NS], w=[ANG[:, P:P + NS]])
            S.do('dve', TS(ANG[:, 0:W], ANG[:, 0:W], par('inv'), None, ALU.mult))
            S.do('dve', TS(KI[:, 0:W], ANG[:, 0:W], 1.0 / (2 * math.pi), None, ALU.mult))
            S.do('dve', COPY(KF[:, 0:W], KI[:, 0:W]))
            S.do('dve', STT(ANG[:, 0:W], KF[:, 0:W], -2.0 * math.pi, ANG[:, 0:W], ALU.mult, ALU.add))
            S2 = arf(28)
            S4 = arf(30)
            S.do('act', ACT(S2[:, 0:W], ANG[:, 0:W], AF.Sin, scale=0.5))
            S.do('act', ACT(S4[:, 0:W], ANG[:, 0:W], AF.Sin, scale=0.25))
            S.do('dve', TT(C[:, 0:W], S2[:, 0:W], S2[:, 0:W], ALU.mult))
            S.do('dve', TS(C[:, 0:W], C[:, 0:W], -2.0, 1.0, ALU.mult, ALU.add))
            S.do('dve', TT(S4[:, 0:W], S4[:, 0:W], S4[:, 0:W], ALU.mult))
            S.do('dve', TS(S4[:, 0:W], S4[:, 0:W], -4.0, 2.0, ALU.mult, ALU.add))
            S.do('dve', TT(Sn[:, 0:W], S2[:, 0:W], S4[:, 0:W], ALU.mult))
            return C, Sn

        def kv(T):
            W = T.W
            subs = T.subs
            t = T.t
            wk = [wload(d['wkv'][0]).rearrange("p (k q) -> p k q", k=4),
                  wload(d['wkv'][1]).rearrange("p (k q) -> p k q", k=4)]
            sets = [dict(KF=arf(16), RS=arf(18), KNb=arb(20), SQ=arb(21)),
                    dict(KF=arf(26), RS=arf(28), KNb=arb(8), SQ=arb(9))]
            C, Sn = None, None
            import os as _os
            _lv = int(_os.environ.get('K_KV', '99'))
            if _lv < 1:
                return
            C, Sn = rope_tables(T)
            if _lv < 2:
                return
            for m in range(4 if _lv >= 3 else 1):
                bs = sets[m % 2]
                KF, RS, KNb, SQ = bs['KF'], bs['RS'], bs['KNb'], bs['SQ']
                for (c0, nn) in subs:
                    ps = S.psum()
                    S.pe([MM(ps[:, 0:nn], wk[k // 4][:, k % 4, 128 * m:128 * m + 128], XN(k)[:, c0:c0 + nn], k == 0, k == 7)
                          for k in range(8)])
                    S.do('act', ACT(SQ[:, c0:c0 + nn], ps[:, 0:nn], AF.Square))
                    S.do('act', ACT(KF[:, c0:c0 + nn], ps[:, 0:nn], AF.Copy))
                    ps2 = S.psum()
                    S.pe([MM(ps2[:, 0:nn], BD, SQ[:, c0:c0 + nn], True, True)])
                    S.do('act', ACT(RS[:, c0:c0 + nn], ps2[:, 0:nn], AF.Sqrt, bias=EPS, scale=1.0 / 64))
                S.do('dve', RECIP(RS[:, 0:W], RS[:, 0:W]))
                S.do('dve', STT(KF[:, 0:W], KF[:, 0:W], par('gk'), RS[:, 0:W], ALU.mult, ALU.mult))
                S.do('act', ACT(KNb[:, 0:W], KF[:, 0:W], AF.Copy))
                for (c0, nn) in subs:
                    ps3 = S.psum()
                    S.pe([MM(ps3[:, 0:nn], PM, KNb[:, c0:c0 + nn], True, True)])
                    S.do('dve', TT(RS[:, c0:c0 + nn], ps3[:, 0:nn], Sn[:, c0:c0 + nn], ALU.mult))
                S.do('dve', TT(KF[:, 0:W], KF[:, 0:W], C[:, 0:W], ALU.mult))
                S.do('dve', TT(KF[:, 0:W], KF[:, 0:W], RS[:, 0:W], ALU.add))
                S.do('act', ACT(KT[:, m, P * t:P * t + P], KF[:, 0:P], AF.Copy))
                if T.has_s:
                    S.do('act', ACT(KT[:, m, TOK:TOK + NS], KF[:, P:P + NS], AF.Copy))
                    S.dma('sp', d['skT'].rearrange("(m p) t -> p m t", p=128)[:, m, :], KF[:, P:P + NS],
                          r=[KF[:, P:P + NS]])
                if t >= 2:
                    dst = d['pkT'].rearrange("(m p) t -> p m t", p=128)[:, m, P * (t - 2):P * (t - 2) + P]
                    S.dma('sp', dst, KF[:, 0:P], r=[KF[:, 0:P]])
            if _lv < 4:
                return
            wv = [wload(d['wkv'][2]).rearrange("p (k q) -> p k q", k=4),
                  wload(d['wkv'][3]).rearrange("p (k q) -> p k q", k=4)]
            for m in range(4):
                VF = sets[m % 2]['KF']
                for (c0, nn) in subs[0:2]:
                    ps = S.psum()
                    S.pe([MM(ps[:, 0:nn], wv[k // 4][:, k % 4, 128 * m:128 * m + 128], XN(k)[:, c0:c0 + nn], k == 0, k == 7)
                          for k in range(8)])
                    _vv = int(_os.environ.get('K_KVV', '3'))
                    if _vv & 1:
                        S.do('act', ACT(VF[:, c0:c0 + nn], ps[:, 0:nn], AF.Copy))
                    if _vv & 2:
                        S.do('dve', COPY(VT[:, m, P * t + c0:P * t + c0 + nn], ps[:, 0:nn]))
                if t >= 2:
                    dst = d['pvT'].rearrange("(m p) t -> p m t", p=128)[:, m, P * (t - 2):P * (t - 2) + P]
                    S.dma('sp', dst, VF[:, 0:P], r=[VF[:, 0:P]])
            if T.has_s:
                ps = S.psum()
                S.pe([MM(ps[0:NS, 0:512], XN(k)[:, P:P + NS], wv[k // 4][:, k % 4, :], k == 0, k == 7) for k in range(8)])
                S.do('act', ACT(VSNF[:, :], ps[0:NS, 0:512], AF.Copy))
                S.do('dve', COPY(VSNB[:, :], ps[0:NS, 0:512]))
                S.dma('sp', d['svtok'], VSNF[:, :], r=[VSNF[:, :]])

        ptn = [0]
        vbn = [0]

        def attention_prompt(T, hp, QT, NUM, DEN):
            t = T.t
            PTbuf = arb(12)
            VBbuf = AR[:, 13 * HB:15 * HB]
            MPC = CB[:, C_MP:C_MP + 256]
            M64 = CB[:, C_M64:C_M64 + 128]
            blocks = []
            for bq in range(8):
                blocks.append((0, 1, 128, 128 * bq))
            for bb in range(2):
                for r in range(4):
                    blocks.append((1, 4, 128, 512 * bb + r))
            for r in range(16):
                blocks.append((2, 16, 64, r))
            ND = AR[:, 26 * HB:30 * HB].bitcast(F32).rearrange("p (a c) -> p a c", a=2)
            for (g, dd, QB, qc) in blocks:
                q0 = P * t + qc
                kbs = []
                if q0 >= 128 * dd:
                    kbs.append((q0 - 128 * dd, 128, MASKP))
                else:
                    pmin = -((q0 - 128 * dd) // dd)
                    if pmin < 128:
                        assert pmin == 64 and QB <= 64
                        kbs.append((q0 - 128 * dd + pmin * dd, 128 - pmin, None))
                kbs.append((q0, QB, MASKC))
                nkb = len(kbs)
                std = (nkb == 2 and kbs[0][1] == 128)
                pi = ptn[0] % 2
                ptn[0] += 1
                PT = PTbuf[:, pi * 512:pi * 512 + 512].rearrange("p (e c) -> p e c", e=2)
                for e in range(2):
                    pss = S.psum()
                    S.pe([MM(pss[0:nk, i * QB:(i + 1) * QB],
                             KT[64 * e:64 * e + 64, hp, k0:k0 + dd * (nk - 1) + 1:dd],
                             QT[g][64 * e:64 * e + 64, qc:qc + dd * (QB - 1) + 1:dd], True, True)
                          for i, (k0, nk, mask) in enumerate(kbs)])
                    if std:
                        S.do('act', ACT(PT[:, e, 0:2 * QB], pss[:, 0:2 * QB], AF.Exp, scale=0.125))
                    else:
                        for i, (k0, nk, mask) in enumerate(kbs):
                            S.do('act', ACT(PT[0:nk, e, i * QB:(i + 1) * QB], pss[0:nk, i * QB:(i + 1) * QB],
                                            AF.Exp, scale=0.125))
                if std:
                    mc = MPC if QB == 128 else M64
                    S.do('dve', TT(PT[:, :, 0:2 * QB], PT[:, :, 0:2 * QB],
                                   mc[:, None, :].broadcast_to([128, 2, 2 * QB]), ALU.mult))
                else:
                    for i, (k0, nk, mask) in enumerate(kbs):
                        if mask is not None:
                            pv3 = PT[0:nk, :, i * QB:(i + 1) * QB]
                            S.do('dve', TT(pv3, pv3, mask[0:nk, None, 0:QB].broadcast_to([nk, 2, QB]), ALU.mult))
                pst = S.psum()
                pstb = pst[:, :].bitcast(BF16)
                S.pe([TR(pstb[0:nk, i * 128:(i + 1) * 128], VT[:, hp, k0:k0 + dd * (nk - 1) + 1:dd], IDENT)
                      for i, (k0, nk, mask) in enumerate(kbs)])
                vi = vbn[0] % 8
                vbn[0] += 1
                VB = VBbuf[:, vi * 256:vi * 256 + 256]
                if std:
                    S.do('dve', COPY(VB[:, 0:256], pstb[:, 0:256]))
                else:
                    for i, (k0, nk, mask) in enumerate(kbs):
                        S.do('dve', COPY(VB[0:nk, i * 128:(i + 1) * 128], pstb[0:nk, i * 128:(i + 1) * 128]))
                psod = S.psum()
                ops = []
                for e in range(2):
                    for i, (k0, nk, mask) in enumerate(kbs):
                        ops.append(MM(psod[64 * e:64 * e + 64, 0:QB], VB[0:nk, i * 128 + 64 * e:i * 128 + 64 * e + 64],
                                      PT[0:nk, e, i * QB:(i + 1) * QB], i == 0, i == nkb - 1))
                for e in range(2):
                    for i, (k0, nk, mask) in enumerate(kbs):
                        ops.append(MM(psod[64 * e:64 * e + 64, 128:128 + QB], ONES[0:nk, 0:64],
                                      PT[0:nk, e, i * QB:(i + 1) * QB], i == 0, i == nkb - 1))
                S.pe(ops)
                ndv = ND[:, :, qc:qc + dd * (QB - 1) + 1:dd]
                src = psod[:, 0:256].rearrange("p (a c) -> p a c", a=2)[:, :, 0:QB]
                if g == 0:
                    S.do('act', ACT(ndv, src, AF.Copy))
                else:
                    S.do('dve', TT(ndv, ndv, src, ALU.add))

        def attention_sample(T, QS, ATBs):
            KSb = AR[:, 16 * HB:16 * HB + 4 * 512].rearrange("p (r q) -> p r q", r=4)
            VSb = AR[:, 18 * HB:18 * HB + 4 * 512].rearrange("p (r q) -> p r q", r=4)
            MS0 = CB[:, C_MS0:C_MS0 + 4]
            MN = CB[0:4, C_MN:C_MN + 12]
            kn = [0]
            for b in range(NB):
                vslot = b % 2
                S.dma('sp', VNB[0:4, vslot, :], VSNB[4 * b:4 * b + 4, :], r=[VSNB[4 * b:4 * b + 4, :]],
                      w=[VNB[0:4, vslot, :]])
                VN = VNB[0:4, vslot, :]
                PTN = PTS[0:4, 3, 0:96]
                for e in range(2):
                    psn_ = S.psum()
                    ops = []
                    for g in range(3):
                        for hp in range(4):
                            ops.append(MM(psn_[0:4, g * 16 + hp * 4:g * 16 + hp * 4 + 4],
                                          KT[64 * e:64 * e + 64, hp, TOK + 4 * b:TOK + 4 * b + 4],
                                          QS[64 * e:64 * e + 64, g * 4 + hp, 4 * b:4 * b + 4], True, True))
                    S.pe(ops)
                    S.do('act', ACT(PTN[:, e * 48:(e + 1) * 48], psn_[0:4, 0:48], AF.Exp, scale=0.125))
                    pn4 = PTN[:, e * 48:(e + 1) * 48].rearrange("p (g h s) -> p g h s", g=3, h=4)
                    mn4 = MN.rearrange("p (g s) -> p g s", g=3)[:, :, None, :].broadcast_to([4, 3, 4, 4])
                    S.do('dve', TT(pn4, pn4, mn4, ALU.mult))
                pn = PTN.rearrange("p (e g c) -> p e g c", e=2, g=3)
                PTNS = PTS[0:4, 3, 96:128]
                PTNS3 = PTNS.rearrange("p (e c) -> p e c", e=2)
                S.do('dve', TT(PTNS3, pn[:, :, 0, :], pn[:, :, 1, :], ALU.add))
                S.do('dve', TT(PTNS3, PTNS3, pn[:, :, 2, :], ALU.add))
                NDs = SMALL[:, :, :]
                NUMs = SMALL[:, 0, 0:16]
                DENs = SMALL[:, 1, 0:16]
                for g in range(3):
                    last = (g == 2)
                    nblk = 1 if g == 0 else 4
                    ks = []
                    vs = []
                    for i in range(nblk):
                        blk = 0 if g == 0 else 1 + 4 * (g - 1) + i
                        ki = kn[0] % 4
                        kn[0] += 1
                        S.dma('pool', KSb[:, ki, :], d['ck'][b, blk], w=[KSb[:, ki, :]])
                        S.dma('pool', VSb[:, ki, :], d['cv'][b, blk], w=[VSb[:, ki, :]])
                        ks.append(KSb[:, ki, :].rearrange("p (a k) -> p a k", a=4))
                        vs.append(VSb[:, ki, :])
                    PT = PTS[:, g, 0:32]
                    for e in range(2):
                        pss = S.psum()
                        ops = []
                        for hp in range(4):
                            if g == 0:
                                ops.append(MM(pss[:, hp * 4:hp * 4 + 4], ks[0][64 * e:64 * e + 64, hp, :],
                                              QS[64 * e:64 * e + 64, hp, 4 * b:4 * b + 4], True, True))
                            else:
                                for s in range(4):
                                    ops.append(MM(pss[:, hp * 4 + s:hp * 4 + s + 1], ks[s][64 * e:64 * e + 64, hp, :],
                                                  QS[64 * e:64 * e + 64, g * 4 + hp, 4 * b + s:4 * b + s + 1], True, True))
                        S.pe(ops)
                        S.do('act', ACT(PT[:, e * 16:(e + 1) * 16], pss[:, 0:16], AF.Exp, scale=0.125))
                    if g == 0:
                        p3 = PT.rearrange("p (h s) -> p h s", h=8)
                        S.do('dve', TT(p3, p3, MS0[:, None, :].broadcast_to([128, 8, 4]), ALU.mult))
                    psod = S.psum()
                    ops = []
                    for e in range(2):
                        for hp in range(4):
                            h = 2 * hp + e
                            for s in range(4):
                                col = hp * 4 + s
                                vblk = vs[0] if g == 0 else vs[s]
                                ops.append(MM(psod[64 * e:64 * e + 64, col:col + 1], vblk[:, h * 64:h * 64 + 64],
                                              PT[:, e * 16 + col:e * 16 + col + 1], True, not last))
                                if last:
                                    ops.append(MM(psod[64 * e:64 * e + 64, col:col + 1], VN[:, h * 64:h * 64 + 64],
                                                  PTNS[:, e * 16 + col:e * 16 + col + 1], False, True))
                    for e in range(2):
                        ops.append(MM(psod[64 * e:64 * e + 64, 16:32], ONES[:, 0:64], PT[:, e * 16:(e + 1) * 16], True, not last))
                        if last:
                            ops.append(MM(psod[64 * e:64 * e + 64, 16:32], ONES[0:4, 0:64],
                                          PTNS[:, e * 16:(e + 1) * 16], False, True))
                    S.pe(ops)
                    src = psod[:, 0:32].rearrange("p (a c) -> p a c", a=2)
                    if g == 0:
                        S.do('act', ACT(NDs, src, AF.Copy))
                    else:
                        S.do('dve', TT(NDs, NDs, src, ALU.add))
                S.do('dve', RECIP(DENs, DENs))
                S.do('dve', TT(ATBs[:, :, 4 * b:4 * b + 4], NUMs.rearrange("p (a c) -> p a c", a=4),
                               DENs.rearrange("p (a c) -> p a c", a=4), ALU.mult))

        def b_mix(T, jB):
            W = T.W
            subs = T.subs
            C, Sn = rope_tables(T)
            QN = arf(16)
            QNb = arb(18)
            QT = [arb(19), arb(20), arb(21)]
            RSq = arf(30)
            SQq = arb(32)
            NUM = arf(26)
            DEN = arf(28)
            QS = arb(15)[:, 0:12 * NS].rearrange("p (m c) -> p m c", m=12)

            def ATB(hp):
                return arb(8 + hp)
            for hp in range(4):
                for g in range(3):
                    wsl = wload(d['wq'][jB, hp * 3 + g], n=1024)[:, 0:1024].rearrange("p (k q) -> p k q", k=8)
                    for (c0, nn) in subs:
                        psq = S.psum()
                        S.pe([MM(psq[:, 0:nn], wsl[:, k, :], XN(k)[:, c0:c0 + nn], k == 0, k == 7) for k in range(8)])
                        S.do('act', ACT(SQq[:, c0:c0 + nn], psq[:, 0:nn], AF.Square))
                        ps2 = S.psum()
                        S.pe([MM(ps2[:, 0:nn], BD, SQq[:, c0:c0 + nn], True, True)])
                        S.do('act', ACT(RSq[:, c0:c0 + nn], ps2[:, 0:nn], AF.Sqrt, bias=EPS, scale=1.0 / 64))
                        S.do('dve', RECIP(RSq[:, c0:c0 + nn], RSq[:, c0:c0 + nn]))
                        S.do('dve', STT(QN[:, c0:c0 + nn], psq[:, 0:nn], par('gq', jB), RSq[:, c0:c0 + nn],
                                        ALU.mult, ALU.mult))
                    S.do('act', ACT(QNb[:, 0:W], QN[:, 0:W], AF.Copy))
                    for (c0, nn) in subs:
                        ps3 = S.psum()
                        S.pe([MM(ps3[:, 0:nn], PM, QNb[:, c0:c0 + nn], True, True)])
                        S.do('dve', TT(RSq[:, c0:c0 + nn], ps3[:, 0:nn], Sn[:, c0:c0 + nn], ALU.mult))
                    S.do('dve', TT(QN[:, 0:W], QN[:, 0:W], C[:, 0:W], ALU.mult))
                    S.do('dve', TT(QT[g][:, 0:W], QN[:, 0:W], RSq[:, 0:W], ALU.add))
                    if T.has_s:
                        S.do('act', ACT(QS[:, g * 4 + hp, :], QT[g][:, P:P + NS], AF.Copy))
                attention_prompt(T, hp, QT, NUM, DEN)
                S.do('dve', RECIP(DEN[:, 0:P], DEN[:, 0:P]))
                S.do('dve', TT(ATB(hp)[:, 0:P], NUM[:, 0:P], DEN[:, 0:P], ALU.mult))
            if T.has_s:
                ATBs = AR[:, 8 * HB:12 * HB].rearrange("p (h w) -> p h w", h=4)[:, :, P:P + NS]
                attention_sample(T, QS, ATBs)
            wos = [wload(d['wo'][jB, i]).rearrange("p (h q) -> p h q", h=4) for i in range(2)]
            for m in range(8):
                for (c0, nn) in subs:
                    ps = S.psum()
                    S.pe([MM(ps[:, 0:nn], wos[m // 4][:, hp, (m % 4) * 128:(m % 4) * 128 + 128], ATB(hp)[:, c0:c0 + nn],
                             hp == 0, hp == 3) for hp in range(4)])
                    S.do('dve', TT(X[:, m, c0:c0 + nn], X[:, m, c0:c0 + nn], ps[:, 0:nn], ALU.add))

        import os as _os
        phases = []
        for t in range(NT):
            T = Tile(t)
            phases.append(lambda T=T: load_x(T))
            for L in range(2):
                phases.append(lambda T=T, L=L: rmsnorm(T, 'nma', L))
                phases.append(lambda T=T, L=L: a_mix(T, L))
                phases.append(lambda T=T, L=L: rmsnorm(T, 'nff', L))
                phases.append(lambda T=T, L=L: ffn(T, L))
            phases.append(lambda T=T: rmsnorm(T, 'nkv', 0))
            phases.append(lambda T=T: kv(T))
            for jB in range(2):
                phases.append(lambda T=T, jB=jB: rmsnorm(T, 'nmb', jB))
                phases.append(lambda T=T, jB=jB: b_mix(T, jB))
                phases.append(lambda T=T, jB=jB: rmsnorm(T, 'nff', 2 + jB))
                phases.append(lambda T=T, jB=jB: ffn(T, 2 + jB))
            phases.append(lambda T=T: store_y(T))
        _stop = int(_os.environ.get('K_STOP', '100000'))
        for _i, _ph in enumerate(phases):
            if _i < _stop:
                _ph()
        S.dma('sp', d['oh'], HST[:, :, :].rearrange("p l c -> p (l c)"), r=[HST[:, :, :]])
        S.dma('sp', d['oc'], CAH[:, :, :, :].rearrange("p l c j -> p (l c j)"), r=[CAH[:, :, :, :]])
        S.dma('sp', d['of'], FH[:, :, :, :].rearrange("p l c j -> p (l c j)"), r=[FH[:, :, :, :]])
        S.finish()
        with nc.Block() as block:
            S.emit(block)
    return nc


def _chunk(v, nchunk):
    return np.ascontiguousarray(np.asarray(v, np.float32).reshape(nchunk, 128).T)


def _consts():
    cst = np.zeros((128, NCB), np.float32)
    p = np.arange(128)[:, None]
    f = np.arange(128)[None, :]
    cst[:, C_ID:C_ID + 128] = np.eye(128)
    cst[:, C_MP:C_MP + 128] = (f <= p)
    cst[:, C_MC:C_MC + 128] = (f >= p)
    cst[:, C_BD:C_BD + 128] = ((p // 64) == (f // 64))
    pm = np.zeros((128, 128), np.float32)
    for m in range(128):
        i = m % 64
        if i < 8:
            pm[m + 8, m] = -1.0
        elif i < 16:
            pm[m - 8, m] = 1.0
    cst[:, C_PM:C_PM + 128] = pm
    cst[:, C_ON:C_ON + 128] = 1.0
    s = np.arange(4)[None, :]
    cst[:, C_MS0:C_MS0 + 4] = (np.arange(128)[:, None] >= s)
    mn = np.zeros((4, 3, 4), np.float32)
    sp = np.arange(4)[:, None]
    mn[:, 0, :] = (sp <= s)
    mn[:, 1, :] = (sp == s)
    mn[:, 2, :] = (sp == s)
    cst[0:4, C_MN:C_MN + 12] = mn.reshape(4, 12)
    cst[:, C_M64:C_M64 + 64] = (f <= p)[:, 0:64]
    cst[:, C_M64 + 64:C_M64 + 128] = (f >= p)[:, 0:64]
    return cst


def _host_layout(inp):
    f = lambda a: np.asarray(a, np.float32)
    com = {}
    par = np.zeros((128, NPAR), np.float32)

    def put(name, off, arr):
        o = _po[name] + off
        par[:, o:o + arr.shape[1]] = arr
    for L in range(2):
        put('nma', 8 * L, _chunk(f(inp['norm_mix_a'])[L], 8))
        for j in range(4):
            put('caw', 32 * L + 8 * j, _chunk(f(inp['conv_a_w'])[L, j], 8))
        put('cab', 8 * L, _chunk(f(inp['conv_a_b'])[L], 8))
        put('grb', 8 * L, _chunk(f(inp['gate_r_b'])[L].reshape(-1), 8))
        put('gib', 8 * L, _chunk(f(inp['gate_i_b'])[L].reshape(-1), 8))
        put('lam', 8 * L, _chunk(f(inp['lru_lambda'])[L], 8))
        put('nmb', 8 * L, _chunk(f(inp['norm_mix_b'])[L], 8))
        par[:, _po['gq'] + L] = np.tile(f(inp['q_norm'])[L], 2)
    put('nkv', 0, _chunk(f(inp['norm_kv']), 8))
    for L in range(4):
        put('nff', 8 * L, _chunk(f(inp['norm_ffn'])[L], 8))
        for tap in range(3):
            put('fcw', 72 * L + 24 * tap, _chunk(f(inp['ffn_conv_w'])[L, tap], 24))
        put('fcb', 24 * L, _chunk(f(inp['ffn_conv_b'])[L], 24))
    par[:, _po['gk']] = np.tile(f(inp['k_norm']), 2)
    half = 8
    inv = (500000.0 ** (-np.arange(half, dtype=np.float32) * np.float32(2.0 / 16))).astype(np.float32)
    invp = np.zeros(128, np.float32)
    for pp in range(128):
        i = pp % 64
        if i < 16:
            invp[pp] = inv[i % 8]
    par[:, _po['inv']] = invp
    com['par'] = par
    com['cst'] = _consts()
    pos = np.zeros((KVW,), np.float32)
    pos[:TOK] = np.arange(TOK)
    pos[TOK:] = np.tile(2048 + np.arange(4), NB)
    com['pos'] = np.ascontiguousarray(np.broadcast_to(pos[None, :], (128, KVW)))
    w_in = f(inp['w_in_a'])
    win = np.empty((2, 8, 128, 2048), np.float32)
    for L in range(2):
        Wk = w_in[L].reshape(8, 128, 2048)
        for i in range(8):
            n, kind = i // 2, i % 2
            c0 = kind * 1024 + 256 * n
            blk = Wk[:, :, c0:c0 + 256].reshape(8, 128, 2, 128)
            win[L, i] = blk.transpose(1, 2, 0, 3).reshape(128, 2048)
    com['win'] = win
    wg = np.empty((2, 4, 128, 1024), np.float32)
    rw, iw = f(inp['gate_r_w']), f(inp['gate_i_w'])
    for L in range(2):
        for n in range(4):
            a = np.stack([rw[L, n], iw[L, n]], 0).reshape(2, 2, 128, 256)
            wg[L, n] = a.transpose(2, 0, 1, 3).reshape(128, 1024)
    com['wg'] = wg
    w_out = f(inp['w_out_a'])
    wout = np.empty((2, 4, 128, 2048), np.float32)
    for L in range(2):
        Wk = w_out[L].reshape(8, 128, 1024)
        for i in range(4):
            blk = Wk[:, :, 256 * i:256 * i + 256].reshape(8, 128, 2, 128)
            wout[L, i] = blk.transpose(1, 2, 0, 3).reshape(128, 2048)
    com['wout'] = wout
    w_up = f(inp['w_ffn_up'])
    wup = np.empty((4, 24, 128, 2048), np.float32)
    for L in range(4):
        Wk = w_up[L].reshape(8, 128, 6144)
        g = Wk[:, :, 0:3072].reshape(8, 128, 24, 128)
        v = Wk[:, :, 3072:6144].reshape(8, 128, 24, 128)
        gv = np.stack([g, v], 3)
        wup[L] = gv.transpose(2, 1, 0, 3, 4).reshape(24, 128, 2048)
    com['wup'] = wup
    w_dn = f(inp['w_ffn_down'])
    wdn = np.empty((4, 3, 128, 8192), np.float32)
    for L in range(4):
        a = w_dn[L].reshape(3, 8, 128, 1024)
        wdn[L] = a.transpose(0, 2, 1, 3).reshape(3, 128, 8192)
    com['wdn'] = wdn
    w_kv = f(inp['w_kv']).reshape(2, 4, 128, 1024)
    wkv = np.empty((4, 128, 2048), np.float32)
    for half_ in range(2):
        for kh in range(2):
            a = w_kv[kh][:, :, 512 * half_:512 * half_ + 512]
            wkv[2 * half_ + kh] = a.transpose(1, 0, 2).reshape(128, 2048)
    com['wkv'] = wkv
    w_q = f(inp['w_q'])
    wq = np.empty((2, 12, 128, 1024), np.float32)
    for j in range(2):
        Wk = w_q[j].reshape(8, 128, 1536)
        for hp in range(4):
            for g in range(3):
                m = 4 * g + hp
                wq[j, hp * 3 + g] = Wk[:, :, 128 * m:128 * m + 128].transpose(1, 0, 2).reshape(128, 1024)
    com['wq'] = wq
    w_o = f(inp['w_o'])
    wo = np.empty((2, 2, 128, 2048), np.float32)
    for j in range(2):
        Wk = w_o[j].reshape(4, 128, 1024)
        for i in range(2):
            wo[j, i] = Wk[:, :, 512 * i:512 * i + 512].transpose(1, 0, 2).reshape(128, 2048)
    com['wo'] = wo
    xp = f(inp['x_prompt'])
    xs = f(inp['x_sample'])
    ck_, cv_ = f(inp['cache_k']), f(inp['cache_v'])
    rows = [1920 + np.arange(128)]
    for s in range(4):
        rows.append(1536 + s + 4 * np.arange(128))
    for s in range(4):
        rows.append(s + 16 * np.arange(128))
    rows = np.stack(rows, 0)
    sh, sc, sf = f(inp['state_rglru_h']), f(inp['state_rglru_conv']), f(inp['state_ffn_conv'])
    maps = []
    for c in range(8):
        m = dict(com)
        m['xT'] = np.ascontiguousarray(xp[c % 4].T)
        b0 = NB * c
        m['xsT'] = np.ascontiguousarray(xs[b0:b0 + NB].reshape(NS, 1024).T)
        kk = ck_[b0:b0 + NB][:, rows]
        kk = kk.reshape(NB, 9, 128, 4, 2, 64).transpose(0, 1, 4, 5, 3, 2)
        m['ck'] = np.ascontiguousarray(kk.reshape(NB, 9, 128, 512))
        m['cv'] = np.ascontiguousarray(cv_[b0:b0 + NB][:, rows].reshape(NB, 9, 128, 512))
        a = sh[:, b0:b0 + NB].reshape(2, NB, 8, 128)
        m['sh'] = np.ascontiguousarray(a.transpose(0, 3, 2, 1).reshape(2, 128, 128))
        a = sc[:, b0:b0 + NB].reshape(2, NB, 3, 8, 128)
        m['sc'] = np.ascontiguousarray(a.transpose(0, 4, 3, 1, 2).reshape(2, 128, 384))
        a = sf[:, b0:b0 + NB].reshape(4, NB, 2, 24, 128)
        m['sf'] = np.ascontiguousarray(a.transpose(0, 4, 3, 1, 2).reshape(4, 128, 768))
        maps.append(m)
    return maps


_NC_CACHE = {}


def kernel(**inputs):
    maps = _host_layout(inputs)
    if 'nc' not in _NC_CACHE:
        _NC_CACHE['nc'] = build_program()
    nc = _NC_CACHE['nc']
    res = run_bass_kernel_spmd(nc, maps, core_ids=list(range(8)))
    R = res.results
    y = np.stack([R[b]['yT'].T for b in range(4)], 0)
    ys = np.concatenate([R[c]['ysT'].T.reshape(NB, 4, 1024) for c in range(8)], 0)
    p_h = np.stack([R[b]['oh'].reshape(128, 2, 8).transpose(1, 2, 0).reshape(2, 1024) for b in range(4)], 1)
    p_c = np.stack([R[b]['oc'].reshape(128, 2, 8, 3).transpose(1, 3, 2, 0).reshape(2, 3, 1024) for b in range(4)], 1)
    p_f = np.stack([R[b]['of'].reshape(128, 4, 24, 2).transpose(1, 3, 2, 0).reshape(4, 2, 3072) for b in range(4)], 1)

    def fm2tok(a, n):
        return a.reshape(4, 2, 64, n).transpose(3, 0, 1, 2).reshape(n, 8, 64)
    p_k = np.stack([fm2tok(R[b]['pkT'], 2048) for b in range(4)], 0)
    p_v = np.stack([fm2tok(R[b]['pvT'], 2048) for b in range(4)], 0)
    s_h = np.concatenate([R[c]['soh'].reshape(2, 128, 8, NB).transpose(0, 3, 2, 1).reshape(2, NB, 1024)
                          for c in range(8)], 1)
    s_c = np.concatenate([R[c]['soc'].reshape(2, 128, 8, NB, 3).transpose(0, 3, 4, 2, 1).reshape(2, NB, 3, 1024)
                          for c in range(8)], 1)
    s_f = np.concatenate([R[c]['sof'].reshape(4, 128, 24, NB, 2).transpose(0, 3, 4, 2, 1).reshape(4, NB, 2, 3072)
                          for c in range(8)], 1)
    s_k = np.concatenate([fm2tok(R[c]['skT'], NS).reshape(NB, 4, 8, 64) for c in range(8)], 0)
    s_v = np.concatenate([R[c]['svtok'].reshape(NB, 4, 8, 64) for c in range(8)], 0)
    outs = (y, ys, p_h, p_c, p_f, p_k, p_v, s_h, s_c, s_f, s_k, s_v)
    return tuple(np.ascontiguousarray(o, dtype=np.float32) for o in outs)
```

```python
import math
from contextlib import ExitStack
import numpy as np
import concourse.bass as bass
import concourse.mybir as mybir
from concourse.bass_utils import run_bass_kernel_spmd

F32 = mybir.dt.float32
BF16 = mybir.dt.bfloat16
I32 = mybir.dt.int32
AF = mybir.ActivationFunctionType
ALU = mybir.AluOpType

P = 1024
NT = 4
NS = 64
TOK = 4096
KVW = TOK + NS
HB = 1152
NHB = 33
EPS = 1e-6
NB = 16

_po = {}
_n = 0
for _name, _w in [('nma', 16), ('caw', 64), ('cab', 16), ('grb', 16), ('gib', 16), ('lam', 16), ('nkv', 8),
                  ('nmb', 16), ('nff', 32), ('fcw', 288), ('fcb', 96), ('gk', 1), ('gq', 2), ('inv', 1), ('cl', 16)]:
    _po[_name] = _n
    _n += _w
NPAR = _n
C_ID, C_MP, C_MC, C_BD, C_PM, C_ON, C_MS0, C_MN, C_M64 = 0, 128, 256, 384, 512, 640, 768, 772, 784
NCB = 912


class _Space:
    def __init__(self):
        self.segs = []

    def deps(self, lo, hi, is_write, add):
        for a, b, w, r in self.segs:
            if b <= lo or a >= hi:
                continue
            if w is not None:
                add(w)
            if is_write:
                for t in r.values():
                    add(t)

    def apply(self, lo, hi, is_write, tok):
        out = []
        covered = []
        for seg in self.segs:
            a, b, w, r = seg
            if b <= lo or a >= hi:
                out.append(seg)
                continue
            if a < lo:
                out.append([a, lo, w, dict(r)])
            if b > hi:
                out.append([hi, b, w, dict(r)])
            ia, ib = max(a, lo), min(b, hi)
            if not is_write:
                r2 = dict(r)
                r2[(tok[0], tok[1])] = tok
                out.append([ia, ib, w, r2])
                covered.append((ia, ib))
        if is_write:
            out.append([lo, hi, tok, {}])
        else:
            covered.sort()
            cur = lo
            for a, b in covered:
                if a > cur:
                    out.append([cur, a, None, {(tok[0], tok[1]): tok}])
                cur = max(cur, b)
            if cur < hi:
                out.append([cur, hi, None, {(tok[0], tok[1]): tok}])
        out.sort(key=lambda s: s[0])
        self.segs = out


_ESZ = {F32: 4, BF16: 2, I32: 4}


def _extent(ap):
    pat = list(ap.ap)
    es = _ESZ[ap.dtype]
    pstride = pat[0][0]
    off = ap.offset % pstride if pstride > 0 else ap.offset
    lo = off
    hi = off
    for st, cnt in pat[1:]:
        if cnt > 1:
            if st >= 0:
                hi += st * (cnt - 1)
            else:
                lo += st * (cnt - 1)
    return ap.tensor.name, lo * es, (hi + 1) * es


class Sched:
    ENGS = ('pe', 'act', 'dve', 'pool', 'sp')

    def __init__(self, nc, stack):
        self.nc = nc
        self.q = {e: [] for e in self.ENGS}
        self.cnt = {e: 0 for e in self.ENGS}
        self.sem = {e: stack.enter_context(nc.semaphore("s_" + e)) for e in self.ENGS}
        self.waited = {e: {} for e in self.ENGS}
        self.spaces = {}
        self.dpool = {}
        for qe in ('sp', 'pool'):
            self.dpool[qe] = [[stack.enter_context(nc.semaphore("d_%s%d" % (qe, i))), 0] for i in range(24)]
        self.dnext = {'sp': 0, 'pool': 0}
        self.dall = []
        self.psums = []
        self.psn = 0
        self.nops = 0

    def psum(self):
        p = self.psums[self.psn % len(self.psums)]
        self.psn += 1
        return p

    def _semh(self, tok):
        if tok[0] == 'e':
            return self.sem[tok[1]]
        return self.dall[tok[1]][0]

    def _wait(self, eng, tok):
        key = (tok[0], tok[1])
        if self.waited[eng].get(key, 0) >= tok[2]:
            return
        self.waited[eng][key] = tok[2]
        semh = self._semh(tok)
        val = tok[2]
        self.q[eng].append(lambda h, semh=semh, val=val: h.wait_ge(semh, val))

    def _collect(self, eng, reads, writes):
        toks = {}

        def add(t):
            k = (t[0], t[1])
            if k not in toks or toks[k][2] < t[2]:
                toks[k] = t
        acc = []
        for ap, isw in [(a, False) for a in reads] + [(a, True) for a in writes]:
            name, lo, hi = _extent(ap)
            sp = self.spaces.get(name)
            if sp is None:
                sp = self.spaces[name] = _Space()
            acc.append((sp, lo, hi, isw))
            sp.deps(lo, hi, isw or name.startswith('ps'), add)
        for t in toks.values():
            if t[0] == 'e' and t[1] == eng and eng == 'pe':
                continue
            self._wait(eng, t)
        return acc

    def _commit(self, acc, tok):
        for sp, lo, hi, isw in acc:
            if not isw:
                sp.apply(lo, hi, False, tok)
        for sp, lo, hi, isw in acc:
            if isw:
                sp.apply(lo, hi, True, tok)

    def do(self, eng, op):
        fn, reads, writes = op
        acc = self._collect(eng, reads, writes)
        self.cnt[eng] += 1
        idx = self.cnt[eng]
        sem = self.sem[eng]
        self.q[eng].append(lambda h, fn=fn, sem=sem: fn(h).then_inc(sem, 1))
        self._commit(acc, ('e', eng, idx))
        self.nops += 1

    def pe(self, ops):
        reads = []
        writes = []
        for fn, r, w in ops:
            reads += r
            writes += w
        acc = self._collect('pe', reads, writes)
        self.cnt['pe'] += 1
        idx = self.cnt['pe']
        sem = self.sem['pe']
        for fn, r, w in ops[:-1]:
            self.q['pe'].append(lambda h, fn=fn: fn(h))
        fn = ops[-1][0]
        self.q['pe'].append(lambda h, fn=fn, sem=sem: fn(h).then_inc(sem, 1))
        self._commit(acc, ('e', 'pe', idx))
        self.nops += len(ops)

    def dma(self, qe, out, in_, r=(), w=()):
        acc = self._collect(qe, list(r), list(w))
        pool = self.dpool[qe]
        k = self.dnext[qe] % len(pool)
        self.dnext[qe] += 1
        ent = pool[k]
        if len(ent) == 2:
            ent.append(len(self.dall))
            self.dall.append(ent)
        gidx = ent[2]
        if ent[1] > 0:
            self._wait(qe, ('d', gidx, ent[1]))
        ent[1] += 16
        semh = ent[0]
        if qe == 'pool':
            self.q[qe].append(lambda h, out=out, in_=in_, semh=semh:
                              h.dma_start(out=out, in_=in_, max_dma_last_dim=4096).then_inc(semh, 16))
        else:
            self.q[qe].append(lambda h, out=out, in_=in_, semh=semh: h.dma_start(out=out, in_=in_).then_inc(semh, 16))
        self._commit(acc, ('d', gidx, ent[1]))
        self.nops += 1

    def finish(self):
        for ent in self.dall:
            if ent[1] > 0:
                self._wait('sp', ('d', ent[2], ent[1]))
        for e in ('pe', 'act', 'dve', 'pool'):
            if self.cnt[e] > 0:
                self._wait('sp', ('e', e, self.cnt[e]))

    def emit(self, block):
        nc = self.nc
        m = {'pe': block.tensor, 'act': block.scalar, 'dve': block.vector, 'pool': block.gpsimd, 'sp': block.sync}
        for e in self.ENGS:
            lst = self.q[e]
            if not lst:
                continue

            def body(h, lst=lst):
                for f in lst:
                    f(h)
            m[e](body)


def _isap(x):
    return hasattr(x, 'ap') and hasattr(x, 'tensor')


def ACT(out, in_, func, bias=None, scale=None):
    kw = {}
    rd = [in_]
    if bias is not None:
        kw['bias'] = bias
        if _isap(bias):
            rd.append(bias)
    if scale is not None:
        kw['scale'] = scale
        if _isap(scale):
            rd.append(scale)
    return (lambda h: h.activation(out=out, in_=in_, func=func, **kw), rd, [out])


def TS(out, in0, s1, s2, op0, op1=None):
    rd = [in0] + [s for s in (s1, s2) if _isap(s)]
    if op1 is None:
        return (lambda h: h.tensor_scalar(out=out, in0=in0, scalar1=s1, scalar2=None, op0=op0), rd, [out])
    return (lambda h: h.tensor_scalar(out=out, in0=in0, scalar1=s1, scalar2=s2, op0=op0, op1=op1), rd, [out])


def TT(out, a, b, op):
    return (lambda h: h.tensor_tensor(out=out, in0=a, in1=b, op=op), [a, b], [out])


def STT(out, in0, sc, in1, op0, op1):
    rd = [in0, in1] + ([sc] if _isap(sc) else [])
    return (lambda h: h.scalar_tensor_tensor(out=out, in0=in0, scalar=sc, in1=in1, op0=op0, op1=op1), rd, [out])


def SCAN(out, d0, d1, init):
    rd = [d0, d1] + ([init] if _isap(init) else [])
    return (lambda h: h.tensor_tensor_scan(out, d0, d1, init, op0=ALU.mult, op1=ALU.add), rd, [out])


def COPY(out, in_):
    return (lambda h: h.tensor_copy(out, in_), [in_], [out])


def RECIP(out, in_):
    return (lambda h: h.reciprocal(out, in_), [in_], [out])


def MEMSET(out, v):
    return (lambda h: h.memset(out, v), [], [out])


def MM(out, lhsT, rhs, start, stop):
    return (lambda h: h.matmul(out, lhsT, rhs, start=start, stop=stop), [lhsT, rhs], [out])


def TR(out, in_, ident):
    return (lambda h: h.transpose(out, in_, ident), [in_, ident], [out])


class Tile:
    def __init__(self, t):
        self.t = t
        self.has_s = (t == NT - 1)
        self.W = P + (NS if self.has_s else 0)
        self.subs = [(0, 512), (512, 512)] + ([(P, NS)] if self.has_s else [])


def build_program():
    nc = bass.Bass("TRN2", target_bir_lowering=False)
    d = {}

    def din(name, shape):
        d[name] = nc.dram_tensor(name, list(shape), F32, kind="ExternalInput").ap()

    def dout(name, shape):
        d[name] = nc.dram_tensor(name, list(shape), F32, kind="ExternalOutput").ap()

    din('xT', [1024, TOK]); din('xsT', [1024, NS])
    din('ck', [NB, 9, 128, 512]); din('cv', [NB, 9, 128, 512])
    din('sh', [2, 128, 128]); din('sc', [2, 128, 384]); din('sf', [4, 128, 768])
    din('par', [128, NPAR]); din('cst', [128, NCB]); din('pos', [128, KVW])
    din('win', [2, 8, 128, 2048]); din('wg', [2, 4, 128, 1024]); din('wout', [2, 4, 128, 2048])
    din('wup', [4, 24, 128, 2048]); din('wdn', [4, 3, 128, 8192]); din('wkv', [4, 128, 2048])
    din('wq', [2, 12, 128, 1024]); din('wo', [2, 2, 128, 2048])
    dout('yT', [1024, TOK]); dout('ysT', [1024, NS])
    dout('oh', [128, 16]); dout('oc', [128, 48]); dout('of', [128, 192])
    dout('pkT', [512, 2048]); dout('pvT', [512, 2048])
    dout('soh', [2, 128, 128]); dout('soc', [2, 128, 384]); dout('sof', [4, 128, 768])
    dout('skT', [512, NS]); dout('svtok', [NS, 512])

    with ExitStack() as st:
        def sb(name, shape, dt):
            return st.enter_context(nc.sbuf_tensor(name, list(shape), dt))
        X = sb("X", [128, 8, P + NS], F32)
        KT = sb("KT", [128, 4, KVW], BF16)
        VT = sb("VT", [128, 4, KVW], BF16)
        WS = sb("WS", [128, 4, 2048], BF16)
        AR = sb("AR", [128, NHB * HB], BF16)
        PAR = sb("PAR", [128, NPAR], F32)
        CB = sb("CB", [128, NCB], BF16)
        HST = sb("HST", [128, 2, 8], F32)
        CAH = sb("CAH", [128, 2, 8, 3], F32)
        FH = sb("FH", [128, 4, 24, 2], F32)
        SHL = sb("SHL", [128, 8, NB], F32)
        SCL = sb("SCL", [128, 8, NB, 3], F32)
        SFL = sb("SFL", [128, 24, NB, 2], F32)
        TMP16 = sb("TMP16", [128, NB], F32)
        VSNB = sb("VSNB", [NS, 512], BF16)
        VSNF = sb("VSNF", [NS, 512], F32)
        SMALL = sb("SMALL", [128, 2, 16], F32)
        PTS = sb("PTS", [128, 4, 128], BF16)
        VNB = sb("VNB", [4, 2, 512], BF16)
        S = Sched(nc, st)
        S.psums = [st.enter_context(nc.psum_tensor("ps%d" % i, [128, 512], F32)) for i in range(8)]

        def arb(i, n=HB):
            return AR[:, i * HB:i * HB + n]

        def arf(i, n=HB):
            return AR[:, i * HB:(i + 2) * HB].bitcast(F32)[:, 0:n]

        def XN(k):
            return arb(k)

        IDENT = CB[:, C_ID:C_ID + 128]
        MASKP = CB[:, C_MP:C_MP + 128]
        MASKC = CB[:, C_MC:C_MC + 128]
        BD = CB[:, C_BD:C_BD + 128]
        PM = CB[:, C_PM:C_PM + 128]
        ONES = CB[:, C_ON:C_ON + 128]

        def par(name, i=0):
            o = _po[name] + i
            return PAR[:, o:o + 1]

        wsn = [0]

        def wload(src, n=2048, parts=128):
            k = wsn[0] % 4
            wsn[0] += 1
            dst = WS[0:parts, k, 0:n]
            S.dma('pool', dst, src, w=[dst])
            return WS[:, k, :]

        S.dma('sp', PAR[:, :], d['par'], w=[PAR[:, :]])
        S.dma('pool', CB[:, :], d['cst'], w=[CB[:, :]])
        S.do('dve', MEMSET(HST[:, :, :], 0.0))
        S.do('dve', MEMSET(CAH[:, :, :, :], 0.0))
        S.do('dve', MEMSET(FH[:, :, :, :], 0.0))
        lam = PAR[:, _po['lam']:_po['lam'] + 16]
        clv = PAR[:, _po['cl']:_po['cl'] + 16]
        S.do('act', ACT(clv, lam, AF.Exp, scale=-1.0))
        S.do('act', ACT(clv, clv, AF.Ln, bias=1.0))
        S.do('dve', TS(clv, clv, -8.0, None, ALU.mult))

        def load_x(T):
            src = d['xT'].rearrange("(c p) t -> p c t", p=128)[:, :, P * T.t:P * T.t + P]
            S.dma('sp', X[:, :, 0:P], src, w=[X[:, :, 0:P]])
            if T.has_s:
                src = d['xsT'].rearrange("(c p) t -> p c t", p=128)
                S.dma('sp', X[:, :, P:P + NS], src, w=[X[:, :, P:P + NS]])

        def store_y(T):
            dst = d['yT'].rearrange("(c p) t -> p c t", p=128)[:, :, P * T.t:P * T.t + P]
            S.dma('sp', dst, X[:, :, 0:P], r=[X[:, :, 0:P]])
            if T.has_s:
                dst = d['ysT'].rearrange("(c p) t -> p c t", p=128)
                S.dma('sp', dst, X[:, :, P:P + NS], r=[X[:, :, P:P + NS]])

        def rmsnorm(T, gname, gi):
            SQ = arb(32)
            RS = arf(30)
            W = T.W
            for (c0, n) in T.subs:
                ps = S.psum()
                for c in range(8):
                    sq = SQ[:, (c % 2) * 512:(c % 2) * 512 + n]
                    S.do('act', ACT(sq, X[:, c, c0:c0 + n], AF.Square))
                    S.pe([MM(ps[:, 0:n], ONES, sq, c == 0, c == 7)])
                S.do('act', ACT(RS[:, c0:c0 + n], ps[:, 0:n], AF.Sqrt, bias=EPS, scale=1.0 / 1024))
                S.do('dve', RECIP(RS[:, c0:c0 + n], RS[:, c0:c0 + n]))
            for c in range(8):
                S.do('dve', STT(XN(c)[:, 0:W], X[:, c, 0:W], par(gname, gi * 8 + c), RS[:, 0:W], ALU.mult, ALU.mult))

        def sview(ap64, s=4):
            return ap64.rearrange("p (b s) -> p b s", s=s)

        def a_mix(T, L):
            W = T.W
            subs = T.subs
            XBH = arf(16)
            XBHs = XBH[:, 1028:1028 + NB * 7].rearrange("p (b j) -> p b j", j=7)
            XCs = [arf(18), arf(20)]
            XCBs = [arb(22), arb(23)]
            RA = arf(24)
            IU = arf(26)
            TH = arf(28)

            def GG(c):
                return arb(8 + c)
            if T.has_s:
                S.dma('sp', SHL[:, :, :], d['sh'][L].rearrange("p (c b) -> p c b", c=8), w=[SHL[:, :, :]])
                S.dma('sp', SCL[:, :, :, :], d['sc'][L].rearrange("p (c b j) -> p c b j", c=8, b=NB),
                      w=[SCL[:, :, :, :]])
            for n in range(4):
                wgt = wload(d['win'][L, 2 * n]).rearrange("p (c k q) -> p c k q", c=2, k=8)
                for cc in range(2):
                    c = 2 * n + cc
                    for (c0, nn) in subs:
                        ps = S.psum()
                        S.pe([MM(ps[:, 0:nn], wgt[:, cc, k, :], XN(k)[:, c0:c0 + nn], k == 0, k == 7) for k in range(8)])
                        S.do('act', ACT(GG(c)[:, c0:c0 + nn], ps[:, 0:nn], AF.Gelu_apprx_tanh))
                wxb = wload(d['win'][L, 2 * n + 1]).rearrange("p (c k q) -> p c k q", c=2, k=8)
                wrg = wload(d['wg'][L, n], n=1024)[:, 0:1024].rearrange("p (g k o) -> p g k o", g=2, k=2)
                for cc in range(2):
                    c = 2 * n + cc
                    XC = XCs[cc]
                    S.do('dve', COPY(XBH[:, 0:3], CAH[:, L, c, :]))
                    if T.has_s:
                        S.do('dve', COPY(XBHs[:, :, 0:3], SCL[:, c, :, :]))
                    for (c0, nn) in subs:
                        ps = S.psum()
                        S.pe([MM(ps[:, 0:nn], wxb[:, cc, k, :], XN(k)[:, c0:c0 + nn], k == 0, k == 7) for k in range(8)])
                        if c0 < P:
                            S.do('act', ACT(XBH[:, 3 + c0:3 + c0 + nn], ps[:, 0:nn], AF.Copy))
                        else:
                            S.do('act', ACT(XBHs[:, :, 3:7], sview(ps[:, 0:NS]), AF.Copy))
                    S.do('dve', COPY(CAH[:, L, c, :], XBH[:, P:P + 3]))
                    if T.has_s:
                        S.do('dve', COPY(SCL[:, c, :, :], XBHs[:, :, 4:7]))
                    cw = [par('caw', L * 32 + j * 8 + c) for j in range(4)]
                    cbias = par('cab', L * 8 + c)
                    S.do('dve', TS(XC[:, 0:P], XBH[:, 3:3 + P], cw[3], cbias, ALU.mult, ALU.add))
                    for j in (2, 1, 0):
                        S.do('dve', STT(XC[:, 0:P], XBH[:, j:j + P], cw[j], XC[:, 0:P], ALU.mult, ALU.add))
                    if T.has_s:
                        XCv = sview(XC[:, P:P + NS])
                        S.do('dve', TS(XCv, XBHs[:, :, 3:7], cw[3], cbias, ALU.mult, ALU.add))
                        for j in (2, 1, 0):
                            S.do('dve', STT(XCv, XBHs[:, :, j:j + 4], cw[j], XCv, ALU.mult, ALU.add))
                    S.do('act', ACT(XCBs[cc][:, 0:W], XC[:, 0:W], AF.Copy))
                for cc in range(2):
                    c = 2 * n + cc
                    XC = XCs[cc]
                    for gate in range(2):
                        dst = RA if gate == 0 else IU
                        gb = par('grb' if gate == 0 else 'gib', L * 8 + c)
                        for (c0, nn) in subs:
                            ps = S.psum()
                            S.pe([MM(ps[:, 0:nn], wrg[:, gate, k, cc * 128:(cc + 1) * 128], XCBs[k][:, c0:c0 + nn],
                                     k == 0, k == 1) for k in range(2)])
                            S.do('act', ACT(dst[:, c0:c0 + nn], ps[:, 0:nn], AF.Sigmoid, bias=gb))
                    S.do('act', ACT(RA[:, 0:W], RA[:, 0:W], AF.Exp, scale=par('cl', L * 8 + c)))
                    S.do('dve', TT(TH[:, 0:W], RA[:, 0:W], RA[:, 0:W], ALU.mult))
                    S.do('act', ACT(TH[:, 0:W], TH[:, 0:W], AF.Sqrt, bias=1.0, scale=-1.0))
                    S.do('dve', TT(IU[:, 0:W], IU[:, 0:W], XC[:, 0:W], ALU.mult))
                    S.do('dve', TT(IU[:, 0:W], IU[:, 0:W], TH[:, 0:W], ALU.mult))
                    S.do('dve', SCAN(TH[:, 0:P], RA[:, 0:P], IU[:, 0:P], HST[:, L, c:c + 1]))
                    S.do('dve', COPY(HST[:, L, c:c + 1], TH[:, P - 1:P]))
                    if T.has_s:
                        As = sview(RA[:, P:P + NS])
                        Us = sview(IU[:, P:P + NS])
                        S.do('dve', TT(TMP16[:, :], As[:, :, 0], SHL[:, c, :], ALU.mult))
                        S.do('dve', TT(Us[:, :, 0], Us[:, :, 0], TMP16[:, :], ALU.add))
                        S.do('dve', MEMSET(As[:, :, 0], 0.0))
                        S.do('dve', SCAN(TH[:, P:P + NS], RA[:, P:P + NS], IU[:, P:P + NS], 0.0))
                        S.do('dve', COPY(SHL[:, c, :], sview(TH[:, P:P + NS])[:, :, 3]))
                    S.do('dve', TT(GG(c)[:, 0:W], TH[:, 0:W], GG(c)[:, 0:W], ALU.mult))
            if T.has_s:
                S.dma('sp', d['soh'][L].rearrange("p (c b) -> p c b", c=8), SHL[:, :, :], r=[SHL[:, :, :]])
                S.dma('sp', d['soc'][L].rearrange("p (c b j) -> p c b j", c=8, b=NB), SCL[:, :, :, :],
                      r=[SCL[:, :, :, :]])
            for i in range(4):
                wsl = wload(d['wout'][L, i]).rearrange("p (c k q) -> p c k q", c=2, k=8)
                for cc in range(2):
                    m = 2 * i + cc
                    for (c0, nn) in subs:
                        ps = S.psum()
                        S.pe([MM(ps[:, 0:nn], wsl[:, cc, k, :], GG(k)[:, c0:c0 + nn], k == 0, k == 7) for k in range(8)])
                        S.do('dve', TT(X[:, m, c0:c0 + nn], X[:, m, c0:c0 + nn], ps[:, 0:nn], ALU.add))

        def ffn(T, L):
            W = T.W
            subs = T.subs

            def ACTB(jj):
                return arb(8 + jj)
            WD = AR[:, 16 * HB:16 * HB + 8192].rearrange("p (j m) -> p j m", j=8)
            GH = arf(24)
            GHs = GH[:, 1028:1028 + NB * 6].rearrange("p (b j) -> p b j", j=6)
            GC = arf(26)
            VB = arb(28)
            if T.has_s:
                S.dma('sp', SFL[:, :, :, :], d['sf'][L].rearrange("p (j b s) -> p j b s", j=24, b=NB),
                      w=[SFL[:, :, :, :]])
            for G in range(3):
                for jj in range(8):
                    j = 8 * G + jj
                    wsl = wload(d['wup'][L, j]).rearrange("p (k q) -> p k q", k=8)
                    if jj == 2:
                        wdst = AR[:, 16 * HB:16 * HB + 8192].rearrange("p (a b) -> p a b", a=4)
                        S.dma('pool', wdst, d['wdn'][L, G].rearrange("p (a b) -> p a b", a=4), w=[wdst])
                    S.do('dve', COPY(GH[:, 0:2], FH[:, L, j, :]))
                    if T.has_s:
                        S.do('dve', COPY(GHs[:, :, 0:2], SFL[:, j, :, :]))
                    for (c0, nn) in subs:
                        psg = S.psum()
                        S.pe([MM(psg[:, 0:nn], wsl[:, k, 0:128], XN(k)[:, c0:c0 + nn], k == 0, k == 7) for k in range(8)])
                        psv = S.psum()
                        S.pe([MM(psv[:, 0:nn], wsl[:, k, 128:256], XN(k)[:, c0:c0 + nn], k == 0, k == 7) for k in range(8)])
                        if c0 < P:
                            S.do('act', ACT(GH[:, 2 + c0:2 + c0 + nn], psg[:, 0:nn], AF.Copy))
                        else:
                            S.do('act', ACT(GHs[:, :, 2:6], sview(psg[:, 0:NS]), AF.Copy))
                        S.do('act', ACT(VB[:, c0:c0 + nn], psv[:, 0:nn], AF.Copy))
                    S.do('dve', COPY(FH[:, L, j, :], GH[:, P:P + 2]))
                    if T.has_s:
                        S.do('dve', COPY(SFL[:, j, :, :], GHs[:, :, 4:6]))
                    fw = [par('fcw', L * 72 + tap * 24 + j) for tap in range(3)]
                    fb = par('fcb', L * 24 + j)
                    S.do('dve', TS(GC[:, 0:P], GH[:, 2:2 + P], fw[2], fb, ALU.mult, ALU.add))
                    S.do('dve', STT(GC[:, 0:P], GH[:, 1:1 + P], fw[1], GC[:, 0:P], ALU.mult, ALU.add))
                    S.do('dve', STT(GC[:, 0:P], GH[:, 0:P], fw[0], GC[:, 0:P], ALU.mult, ALU.add))
                    if T.has_s:
                        GCv = sview(GC[:, P:P + NS])
                        S.do('dve', TS(GCv, GHs[:, :, 2:6], fw[2], fb, ALU.mult, ALU.add))
                        S.do('dve', STT(GCv, GHs[:, :, 1:5], fw[1], GCv, ALU.mult, ALU.add))
                        S.do('dve', STT(GCv, GHs[:, :, 0:4], fw[0], GCv, ALU.mult, ALU.add))
                    S.do('act', ACT(ACTB(jj)[:, 0:W], GC[:, 0:W], AF.Gelu_apprx_tanh))
                    S.do('dve', TT(ACTB(jj)[:, 0:W], ACTB(jj)[:, 0:W], VB[:, 0:W], ALU.mult))
                for m in range(8):
                    for (c0, nn) in subs:
                        ps = S.psum()
                        S.pe([MM(ps[:, 0:nn], WD[:, jj, 128 * m:128 * m + 128], ACTB(jj)[:, c0:c0 + nn], jj == 0, jj == 7)
                              for jj in range(8)])
                        S.do('dve', TT(X[:, m, c0:c0 + nn], X[:, m, c0:c0 + nn], ps[:, 0:nn], ALU.add))
            if T.has_s:
                S.dma('sp', d['sof'][L].rearrange("p (j b s) -> p j b s", j=24, b=NB), SFL[:, :, :, :],
                      r=[SFL[:, :, :, :]])

        def rope_tables(T):
            W = T.W
            ANG = arf(26)
            KI = AR[:, 28 * HB:30 * HB].bitcast(I32)[:, 0:HB]
            KF = arf(30)
            C = arf(22)
            Sn = arf(24)
            S.dma('sp', ANG[:, 0:P], d['pos'][:, P * T.t:P * T.t + P], w=[ANG[:, 0:P]])
            if T.has_s:
                S.dma('sp', ANG[:, P:P + NS], d['pos'][:, TOK:TOK + NS], w=[ANG[:, P:P + NS]])
            S.do('dve', TS(ANG[:, 0:W], ANG[:, 0:W], par('inv'), None, ALU.mult))
            S.do('dve', TS(KI[:, 0:W], ANG[:, 0:W], 1.0 / (2 * math.pi), None, ALU.mult))
            S.do('dve', COPY(KF[:, 0:W], KI[:, 0:W]))
            S.do('dve', STT(ANG[:, 0:W], KF[:, 0:W], -2.0 * math.pi, ANG[:, 0:W], ALU.mult, ALU.add))
            S2 = arf(28)
            S4 = arf(30)
            S.do('act', ACT(S2[:, 0:W], ANG[:, 0:W], AF.Sin, scale=0.5))
            S.do('act', ACT(S4[:, 0:W], ANG[:, 0:W], AF.Sin, scale=0.25))
            S.do('dve', TT(C[:, 0:W], S2[:, 0:W], S2[:, 0:W], ALU.mult))
            S.do('dve', TS(C[:, 0:W], C[:, 0:W], -2.0, 1.0, ALU.mult, ALU.add))
            S.do('dve', TT(S4[:, 0:W], S4[:, 0:W], S4[:, 0:W], ALU.mult))
            S.do('dve', TS(S4[:, 0:W], S4[:, 0:W], -4.0, 2.0, ALU.mult, ALU.add))
            S.do('dve', TT(Sn[:, 0:W], S2[:, 0:W], S4[:, 0:W], ALU.mult))
            return C, Sn

        def kv(T):
            W = T.W
            subs = T.subs
            t = T.t
            wk = [wload(d['wkv'][0]).rearrange("p (k q) -> p k q", k=4),
                  wload(d['wkv'][1]).rearrange("p (k q) -> p k q", k=4)]
            sets = [dict(KF=arf(16), RS=arf(18), KNb=arb(20), SQ=arb(21)),
                    dict(KF=arf(26), RS=arf(28), KNb=arb(8), SQ=arb(9))]
            C, Sn = None, None
            import os as _os
            _lv = int(_os.environ.get('K_KV', '99'))
            if _lv < 1:
                return
            C, Sn = rope_tables(T)
            if _lv < 2:
                return
            for m in range(4 if _lv >= 3 else 1):
                bs = sets[m % 2]
                KF, RS, KNb, SQ = bs['KF'], bs['RS'], bs['KNb'], bs['SQ']
                for (c0, nn) in subs:
                    ps = S.psum()
                    S.pe([MM(ps[:, 0:nn], wk[k // 4][:, k % 4, 128 * m:128 * m + 128], XN(k)[:, c0:c0 + nn], k == 0, k == 7)
                          for k in range(8)])
                    S.do('act', ACT(SQ[:, c0:c0 + nn], ps[:, 0:nn], AF.Square))
                    S.do('act', ACT(KF[:, c0:c0 + nn], ps[:, 0:nn], AF.Copy))
                    ps2 = S.psum()
                    S.pe([MM(ps2[:, 0:nn], BD, SQ[:, c0:c0 + nn], True, True)])
                    S.do('act', ACT(RS[:, c0:c0 + nn], ps2[:, 0:nn], AF.Sqrt, bias=EPS, scale=1.0 / 64))
                S.do('dve', RECIP(RS[:, 0:W], RS[:, 0:W]))
                S.do('dve', STT(KF[:, 0:W], KF[:, 0:W], par('gk'), RS[:, 0:W], ALU.mult, ALU.mult))
                S.do('act', ACT(KNb[:, 0:W], KF[:, 0:W], AF.Copy))
                for (c0, nn) in subs:
                    ps3 = S.psum()
                    S.pe([MM(ps3[:, 0:nn], PM, KNb[:, c0:c0 + nn], True, True)])
                    S.do('dve', TT(RS[:, c0:c0 + nn], ps3[:, 0:nn], Sn[:, c0:c0 + nn], ALU.mult))
                S.do('dve', TT(KF[:, 0:W], KF[:, 0:W], C[:, 0:W], ALU.mult))
                S.do('dve', TT(KF[:, 0:W], KF[:, 0:W], RS[:, 0:W], ALU.add))
                S.do('act', ACT(KT[:, m, P * t:P * t + P], KF[:, 0:P], AF.Copy))
                if T.has_s:
                    S.do('act', ACT(KT[:, m, TOK:TOK + NS], KF[:, P:P + NS], AF.Copy))
                    S.dma('sp', d['skT'].rearrange("(m p) t -> p m t", p=128)[:, m, :], KF[:, P:P + NS],
                          r=[KF[:, P:P + NS]])
                if t >= 2:
                    dst = d['pkT'].rearrange("(m p) t -> p m t", p=128)[:, m, P * (t - 2):P * (t - 2) + P]
                    S.dma('sp', dst, KF[:, 0:P], r=[KF[:, 0:P]])
            if _lv < 4:
                return
            wv = [wload(d['wkv'][2]).rearrange("p (k q) -> p k q", k=4),
                  wload(d['wkv'][3]).rearrange("p (k q) -> p k q", k=4)]
            for m in range(4):
                VF = sets[m % 2]['KF']
                for (c0, nn) in subs[0:2]:
                    ps = S.psum()
                    S.pe([MM(ps[:, 0:nn], wv[k // 4][:, k % 4, 128 * m:128 * m + 128], XN(k)[:, c0:c0 + nn], k == 0, k == 7)
                          for k in range(8)])
                    _vv = int(_os.environ.get('K_KVV', '3'))
                    if _vv & 1:
                        S.do('act', ACT(VF[:, c0:c0 + nn], ps[:, 0:nn], AF.Copy))
                    if _vv & 2:
                        S.do('dve', COPY(VT[:, m, P * t + c0:P * t + c0 + nn], ps[:, 0:nn]))
                if t >= 2:
                    dst = d['pvT'].rearrange("(m p) t -> p m t", p=128)[:, m, P * (t - 2):P * (t - 2) + P]
                    S.dma('sp', dst, VF[:, 0:P], r=[VF[:, 0:P]])
            if T.has_s:
                ps = S.psum()
                S.pe([MM(ps[0:NS, 0:512], XN(k)[:, P:P + NS], wv[k // 4][:, k % 4, :], k == 0, k == 7) for k in range(8)])
                S.do('act', ACT(VSNF[:, :], ps[0:NS, 0:512], AF.Copy))
                S.do('dve', COPY(VSNB[:, :], ps[0:NS, 0:512]))
                S.dma('sp', d['svtok'], VSNF[:, :], r=[VSNF[:, :]])

        ptn = [0]
        vbn = [0]

        def attention_prompt(T, hp, QT, NUM, DEN):
            t = T.t
            PTbuf = arb(12)
            VBbuf = AR[:, 13 * HB:15 * HB]
            MPC = CB[:, C_MP:C_MP + 256]
            M64 = CB[:, C_M64:C_M64 + 128]
            blocks = []
            for bq in range(8):
                blocks.append((0, 1, 128, 128 * bq))
            for bb in range(2):
                for r in range(4):
                    blocks.append((1, 4, 128, 512 * bb + r))
            for r in range(16):
                blocks.append((2, 16, 64, r))
            ND = AR[:, 26 * HB:30 * HB].bitcast(F32).rearrange("p (a c) -> p a c", a=2)
            def stage_a(g, dd, QB, qc):
                q0 = P * t + qc
                kbs = []
                if q0 >= 128 * dd:
                    kbs.append((q0 - 128 * dd, 128, MASKP))
                else:
                    pmin = -((q0 - 128 * dd) // dd)
                    if pmin < 128:
                        assert pmin == 64 and QB <= 64
                        kbs.append((q0 - 128 * dd + pmin * dd, 128 - pmin, None))
                kbs.append((q0, QB, MASKC))
                nkb = len(kbs)
                std = (nkb == 2 and kbs[0][1] == 128)
                pi = ptn[0] % 2
                ptn[0] += 1
                PT = PTbuf[:, pi * 512:pi * 512 + 512].rearrange("p (e c) -> p e c", e=2)
                for e in range(2):
                    pss = S.psum()
                    S.pe([MM(pss[0:nk, i * QB:(i + 1) * QB],
                             KT[64 * e:64 * e + 64, hp, k0:k0 + dd * (nk - 1) + 1:dd],
                             QT[g][64 * e:64 * e + 64, qc:qc + dd * (QB - 1) + 1:dd], True, True)
                          for i, (k0, nk, mask) in enumerate(kbs)])
                    if std:
                        S.do('act', ACT(PT[:, e, 0:2 * QB], pss[:, 0:2 * QB], AF.Exp, scale=0.125))
                    else:
                        for i, (k0, nk, mask) in enumerate(kbs):
                            S.do('act', ACT(PT[0:nk, e, i * QB:(i + 1) * QB], pss[0:nk, i * QB:(i + 1) * QB],
                                            AF.Exp, scale=0.125))
                if std:
                    mc = MPC if QB == 128 else M64
                    S.do('dve', TT(PT[:, :, 0:2 * QB], PT[:, :, 0:2 * QB],
                                   mc[:, None, :].broadcast_to([128, 2, 2 * QB]), ALU.mult))
                else:
                    for i, (k0, nk, mask) in enumerate(kbs):
                        if mask is not None:
                            pv3 = PT[0:nk, :, i * QB:(i + 1) * QB]
                            S.do('dve', TT(pv3, pv3, mask[0:nk, None, 0:QB].broadcast_to([nk, 2, QB]), ALU.mult))
                pst = S.psum()
                pstb = pst[:, :].bitcast(BF16)
                S.pe([TR(pstb[0:nk, i * 128:(i + 1) * 128], VT[:, hp, k0:k0 + dd * (nk - 1) + 1:dd], IDENT)
                      for i, (k0, nk, mask) in enumerate(kbs)])
                vi = vbn[0] % 8
                vbn[0] += 1
                VB = VBbuf[:, vi * 256:vi * 256 + 256]
                if std:
                    S.do('dve', COPY(VB[:, 0:256], pstb[:, 0:256]))
                else:
                    for i, (k0, nk, mask) in enumerate(kbs):
                        S.do('dve', COPY(VB[0:nk, i * 128:(i + 1) * 128], pstb[0:nk, i * 128:(i + 1) * 128]))
                return (g, dd, QB, qc, kbs, nkb, PT, VB)

            def stage_b(ctx):
                (g, dd, QB, qc, kbs, nkb, PT, VB) = ctx
                psod = S.psum()
                ops = []
                for e in range(2):
                    for i, (k0, nk, mask) in enumerate(kbs):
                        ops.append(MM(psod[64 * e:64 * e + 64, 0:QB], VB[0:nk, i * 128 + 64 * e:i * 128 + 64 * e + 64],
                                      PT[0:nk, e, i * QB:(i + 1) * QB], i == 0, i == nkb - 1))
                for e in range(2):
                    for i, (k0, nk, mask) in enumerate(kbs):
                        ops.append(MM(psod[64 * e:64 * e + 64, 128:128 + QB], ONES[0:nk, 0:64],
                                      PT[0:nk, e, i * QB:(i + 1) * QB], i == 0, i == nkb - 1))
                S.pe(ops)
                ndv = ND[:, :, qc:qc + dd * (QB - 1) + 1:dd]
                src = psod[:, 0:256].rearrange("p (a c) -> p a c", a=2)[:, :, 0:QB]
                if g == 0:
                    S.do('act', ACT(ndv, src, AF.Copy))
                else:
                    S.do('dve', TT(ndv, ndv, src, ALU.add))

            prev = None
            for blk in blocks:
                ctx = stage_a(*blk)
                if prev is not None:
                    stage_b(prev)
                prev = ctx
            stage_b(prev)

        def attention_sample(T, QS, ATBs):
            KSb = AR[:, 16 * HB:16 * HB + 4 * 512].rearrange("p (r q) -> p r q", r=4)
            VSb = AR[:, 18 * HB:18 * HB + 8 * 512].rearrange("p (r q) -> p r q", r=8)
            MS0 = CB[:, C_MS0:C_MS0 + 4]
            MN = CB[0:4, C_MN:C_MN + 12]
            kn = [0]
            vn = [0]
            NDs = SMALL[:, :, :]
            NUMs = SMALL[:, 0, 0:16]
            DENs = SMALL[:, 1, 0:16]
            bstate = {}

            def stage_a(b, g):
                if g == 0:
                    vslot = b % 2
                    S.dma('sp', VNB[0:4, vslot, :], VSNB[4 * b:4 * b + 4, :], r=[VSNB[4 * b:4 * b + 4, :]],
                          w=[VNB[0:4, vslot, :]])
                    VN = VNB[0:4, vslot, :]
                    PTN = PTS[0:4, 3, 0:96]
                    for e in range(2):
                        psn_ = S.psum()
                        ops = []
                        for gg in range(3):
                            for hp in range(4):
                                ops.append(MM(psn_[0:4, gg * 16 + hp * 4:gg * 16 + hp * 4 + 4],
                                              KT[64 * e:64 * e + 64, hp, TOK + 4 * b:TOK + 4 * b + 4],
                                              QS[64 * e:64 * e + 64, gg * 4 + hp, 4 * b:4 * b + 4], True, True))
                        S.pe(ops)
                        S.do('act', ACT(PTN[:, e * 48:(e + 1) * 48], psn_[0:4, 0:48], AF.Exp, scale=0.125))
                        pn4 = PTN[:, e * 48:(e + 1) * 48].rearrange("p (g h s) -> p g h s", g=3, h=4)
                        mn4 = MN.rearrange("p (g s) -> p g s", g=3)[:, :, None, :].broadcast_to([4, 3, 4, 4])
                        S.do('dve', TT(pn4, pn4, mn4, ALU.mult))
                    pn = PTN.rearrange("p (e g c) -> p e g c", e=2, g=3)
                    PTNS = PTS[0:4, 3 - (b % 2), 96:128]
                    PTNS3 = PTNS.rearrange("p (e c) -> p e c", e=2)
                    S.do('dve', TT(PTNS3, pn[:, :, 0, :], pn[:, :, 1, :], ALU.add))
                    S.do('dve', TT(PTNS3, PTNS3, pn[:, :, 2, :], ALU.add))
                    bstate[b] = (VN, PTNS)
                nblk = 1 if g == 0 else 4
                ks = []
                vs = []
                for i in range(nblk):
                    blk = 0 if g == 0 else 1 + 4 * (g - 1) + i
                    ki = kn[0] % 4
                    kn[0] += 1
                    vi = vn[0] % 8
                    vn[0] += 1
                    S.dma('pool', KSb[:, ki, :], d['ck'][b, blk], w=[KSb[:, ki, :]])
                    S.dma('pool', VSb[:, vi, :], d['cv'][b, blk], w=[VSb[:, vi, :]])
                    ks.append(KSb[:, ki, :].rearrange("p (a k) -> p a k", a=4))
                    vs.append(VSb[:, vi, :])
                PT = PTS[:, g, 0:32]
                for e in range(2):
                    pss = S.psum()
                    ops = []
                    for hp in range(4):
                        if g == 0:
                            ops.append(MM(pss[:, hp * 4:hp * 4 + 4], ks[0][64 * e:64 * e + 64, hp, :],
                                          QS[64 * e:64 * e + 64, hp, 4 * b:4 * b + 4], True, True))
                        else:
                            for s_ in range(4):
                                ops.append(MM(pss[:, hp * 4 + s_:hp * 4 + s_ + 1], ks[s_][64 * e:64 * e + 64, hp, :],
                                              QS[64 * e:64 * e + 64, g * 4 + hp, 4 * b + s_:4 * b + s_ + 1], True, True))
                    S.pe(ops)
                    S.do('act', ACT(PT[:, e * 16:(e + 1) * 16], pss[:, 0:16], AF.Exp, scale=0.125))
                if g == 0:
                    p3 = PT.rearrange("p (h s) -> p h s", h=8)
                    S.do('dve', TT(p3, p3, MS0[:, None, :].broadcast_to([128, 8, 4]), ALU.mult))
                return (b, g, vs, PT)

            def stage_b(ctx):
                (b, g, vs, PT) = ctx
                (VN, PTNS) = bstate[b]
                last = (g == 2)
                psod = S.psum()
                ops = []
                for e in range(2):
                    for hp in range(4):
                        h = 2 * hp + e
                        for s_ in range(4):
                            col = hp * 4 + s_
                            vblk = vs[0] if g == 0 else vs[s_]
                            ops.append(MM(psod[64 * e:64 * e + 64, col:col + 1], vblk[:, h * 64:h * 64 + 64],
                                          PT[:, e * 16 + col:e * 16 + col + 1], True, not last))
                            if last:
                                ops.append(MM(psod[64 * e:64 * e + 64, col:col + 1], VN[:, h * 64:h * 64 + 64],
                                              PTNS[:, e * 16 + col:e * 16 + col + 1], False, True))
                for e in range(2):
                    ops.append(MM(psod[64 * e:64 * e + 64, 16:32], ONES[:, 0:64], PT[:, e * 16:(e + 1) * 16], True, not last))
                    if last:
                        ops.append(MM(psod[64 * e:64 * e + 64, 16:32], ONES[0:4, 0:64],
                                      PTNS[:, e * 16:(e + 1) * 16], False, True))
                S.pe(ops)
                src = psod[:, 0:32].rearrange("p (a c) -> p a c", a=2)
                if g == 0:
                    S.do('act', ACT(NDs, src, AF.Copy))
                else:
                    S.do('dve', TT(NDs, NDs, src, ALU.add))
                if last:
                    S.do('dve', RECIP(DENs, DENs))
                    S.do('dve', TT(ATBs[:, :, 4 * b:4 * b + 4], NUMs.rearrange("p (a c) -> p a c", a=4),
                                   DENs.rearrange("p (a c) -> p a c", a=4), ALU.mult))

            prev = None
            for b in range(NB):
                for g in range(3):
                    ctx = stage_a(b, g)
                    if prev is not None:
                        stage_b(prev)
                    prev = ctx
            stage_b(prev)

        def b_mix(T, jB):
            W = T.W
            subs = T.subs
            C, Sn = rope_tables(T)
            QN = arf(16)
            QNb = arb(18)
            QT = [arb(19), arb(20), arb(21)]
            RSq = arf(30)
            SQq = arb(32)
            NUM = arf(26)
            DEN = arf(28)
            QS = arb(15)[:, 0:12 * NS].rearrange("p (m c) -> p m c", m=12)

            def ATB(hp):
                return arb(8 + hp)
            for hp in range(4):
                for g in range(3):
                    wsl = wload(d['wq'][jB, hp * 3 + g], n=1024)[:, 0:1024].rearrange("p (k q) -> p k q", k=8)
                    for (c0, nn) in subs:
                        psq = S.psum()
                        S.pe([MM(psq[:, 0:nn], wsl[:, k, :], XN(k)[:, c0:c0 + nn], k == 0, k == 7) for k in range(8)])
                        S.do('act', ACT(SQq[:, c0:c0 + nn], psq[:, 0:nn], AF.Square))
                        ps2 = S.psum()
                        S.pe([MM(ps2[:, 0:nn], BD, SQq[:, c0:c0 + nn], True, True)])
                        S.do('act', ACT(RSq[:, c0:c0 + nn], ps2[:, 0:nn], AF.Sqrt, bias=EPS, scale=1.0 / 64))
                        S.do('dve', RECIP(RSq[:, c0:c0 + nn], RSq[:, c0:c0 + nn]))
                        S.do('dve', STT(QN[:, c0:c0 + nn], psq[:, 0:nn], par('gq', jB), RSq[:, c0:c0 + nn],
                                        ALU.mult, ALU.mult))
                    S.do('act', ACT(QNb[:, 0:W], QN[:, 0:W], AF.Copy))
                    for (c0, nn) in subs:
                        ps3 = S.psum()
                        S.pe([MM(ps3[:, 0:nn], PM, QNb[:, c0:c0 + nn], True, True)])
                        S.do('dve', TT(RSq[:, c0:c0 + nn], ps3[:, 0:nn], Sn[:, c0:c0 + nn], ALU.mult))
                    S.do('dve', TT(QN[:, 0:W], QN[:, 0:W], C[:, 0:W], ALU.mult))
                    S.do('dve', TT(QT[g][:, 0:W], QN[:, 0:W], RSq[:, 0:W], ALU.add))
                    if T.has_s:
                        S.do('act', ACT(QS[:, g * 4 + hp, :], QT[g][:, P:P + NS], AF.Copy))
                attention_prompt(T, hp, QT, NUM, DEN)
                S.do('dve', RECIP(DEN[:, 0:P], DEN[:, 0:P]))
                S.do('dve', TT(ATB(hp)[:, 0:P], NUM[:, 0:P], DEN[:, 0:P], ALU.mult))
            if T.has_s:
                ATBs = AR[:, 8 * HB:12 * HB].rearrange("p (h w) -> p h w", h=4)[:, :, P:P + NS]
                attention_sample(T, QS, ATBs)
            wos = [wload(d['wo'][jB, i]).rearrange("p (h q) -> p h q", h=4) for i in range(2)]
            for m in range(8):
                for (c0, nn) in subs:
                    ps = S.psum()
                    S.pe([MM(ps[:, 0:nn], wos[m // 4][:, hp, (m % 4) * 128:(m % 4) * 128 + 128], ATB(hp)[:, c0:c0 + nn],
                             hp == 0, hp == 3) for hp in range(4)])
                    S.do('dve', TT(X[:, m, c0:c0 + nn], X[:, m, c0:c0 + nn], ps[:, 0:nn], ALU.add))

        import os as _os
        phases = []
        for t in range(NT):
            T = Tile(t)
            phases.append(lambda T=T: load_x(T))
            for L in range(2):
                phases.append(lambda T=T, L=L: rmsnorm(T, 'nma', L))
                phases.append(lambda T=T, L=L: a_mix(T, L))
                phases.append(lambda T=T, L=L: rmsnorm(T, 'nff', L))
                phases.append(lambda T=T, L=L: ffn(T, L))
            phases.append(lambda T=T: rmsnorm(T, 'nkv', 0))
            phases.append(lambda T=T: kv(T))
            for jB in range(2):
                phases.append(lambda T=T, jB=jB: rmsnorm(T, 'nmb', jB))
                phases.append(lambda T=T, jB=jB: b_mix(T, jB))
                phases.append(lambda T=T, jB=jB: rmsnorm(T, 'nff', 2 + jB))
                phases.append(lambda T=T, jB=jB: ffn(T, 2 + jB))
            phases.append(lambda T=T: store_y(T))
        _stop = int(_os.environ.get('K_STOP', '100000'))
        for _i, _ph in enumerate(phases):
            if _i < _stop:
                _ph()
        S.dma('sp', d['oh'], HST[:, :, :].rearrange("p l c -> p (l c)"), r=[HST[:, :, :]])
        S.dma('sp', d['oc'], CAH[:, :, :, :].rearrange("p l c j -> p (l c j)"), r=[CAH[:, :, :, :]])
        S.dma('sp', d['of'], FH[:, :, :, :].rearrange("p l c j -> p (l c j)"), r=[FH[:, :, :, :]])
        S.finish()
        with nc.Block() as block:
            S.emit(block)
    return nc


def _chunk(v, nchunk):
    return np.ascontiguousarray(np.asarray(v, np.float32).reshape(nchunk, 128).T)


def _consts():
    cst = np.zeros((128, NCB), np.float32)
    p = np.arange(128)[:, None]
    f = np.arange(128)[None, :]
    cst[:, C_ID:C_ID + 128] = np.eye(128)
    cst[:, C_MP:C_MP + 128] = (f <= p)
    cst[:, C_MC:C_MC + 128] = (f >= p)
    cst[:, C_BD:C_BD + 128] = ((p // 64) == (f // 64))
    pm = np.zeros((128, 128), np.float32)
    for m in range(128):
        i = m % 64
        if i < 8:
            pm[m + 8, m] = -1.0
        elif i < 16:
            pm[m - 8, m] = 1.0
    cst[:, C_PM:C_PM + 128] = pm
    cst[:, C_ON:C_ON + 128] = 1.0
    s = np.arange(4)[None, :]
    cst[:, C_MS0:C_MS0 + 4] = (np.arange(128)[:, None] >= s)
    mn = np.zeros((4, 3, 4), np.float32)
    sp = np.arange(4)[:, None]
    mn[:, 0, :] = (sp <= s)
    mn[:, 1, :] = (sp == s)
    mn[:, 2, :] = (sp == s)
    cst[0:4, C_MN:C_MN + 12] = mn.reshape(4, 12)
    cst[:, C_M64:C_M64 + 64] = (f <= p)[:, 0:64]
    cst[:, C_M64 + 64:C_M64 + 128] = (f >= p)[:, 0:64]
    return cst


def _host_layout(inp):
    f = lambda a: np.asarray(a, np.float32)
    com = {}
    par = np.zeros((128, NPAR), np.float32)

    def put(name, off, arr):
        o = _po[name] + off
        par[:, o:o + arr.shape[1]] = arr
    for L in range(2):
        put('nma', 8 * L, _chunk(f(inp['norm_mix_a'])[L], 8))
        for j in range(4):
            put('caw', 32 * L + 8 * j, _chunk(f(inp['conv_a_w'])[L, j], 8))
        put('cab', 8 * L, _chunk(f(inp['conv_a_b'])[L], 8))
        put('grb', 8 * L, _chunk(f(inp['gate_r_b'])[L].reshape(-1), 8))
        put('gib', 8 * L, _chunk(f(inp['gate_i_b'])[L].reshape(-1), 8))
        put('lam', 8 * L, _chunk(f(inp['lru_lambda'])[L], 8))
        put('nmb', 8 * L, _chunk(f(inp['norm_mix_b'])[L], 8))
        par[:, _po['gq'] + L] = np.tile(f(inp['q_norm'])[L], 2)
    put('nkv', 0, _chunk(f(inp['norm_kv']), 8))
    for L in range(4):
        put('nff', 8 * L, _chunk(f(inp['norm_ffn'])[L], 8))
        for tap in range(3):
            put('fcw', 72 * L + 24 * tap, _chunk(f(inp['ffn_conv_w'])[L, tap], 24))
        put('fcb', 24 * L, _chunk(f(inp['ffn_conv_b'])[L], 24))
    par[:, _po['gk']] = np.tile(f(inp['k_norm']), 2)
    half = 8
    inv = (500000.0 ** (-np.arange(half, dtype=np.float32) * np.float32(2.0 / 16))).astype(np.float32)
    invp = np.zeros(128, np.float32)
    for pp in range(128):
        i = pp % 64
        if i < 16:
            invp[pp] = inv[i % 8]
    par[:, _po['inv']] = invp
    com['par'] = par
    com['cst'] = _consts()
    pos = np.zeros((KVW,), np.float32)
    pos[:TOK] = np.arange(TOK)
    pos[TOK:] = np.tile(2048 + np.arange(4), NB)
    com['pos'] = np.ascontiguousarray(np.broadcast_to(pos[None, :], (128, KVW)))
    w_in = f(inp['w_in_a'])
    win = np.empty((2, 8, 128, 2048), np.float32)
    for L in range(2):
        Wk = w_in[L].reshape(8, 128, 2048)
        for i in range(8):
            n, kind = i // 2, i % 2
            c0 = kind * 1024 + 256 * n
            blk = Wk[:, :, c0:c0 + 256].reshape(8, 128, 2, 128)
            win[L, i] = blk.transpose(1, 2, 0, 3).reshape(128, 2048)
    com['win'] = win
    wg = np.empty((2, 4, 128, 1024), np.float32)
    rw, iw = f(inp['gate_r_w']), f(inp['gate_i_w'])
    for L in range(2):
        for n in range(4):
            a = np.stack([rw[L, n], iw[L, n]], 0).reshape(2, 2, 128, 256)
            wg[L, n] = a.transpose(2, 0, 1, 3).reshape(128, 1024)
    com['wg'] = wg
    w_out = f(inp['w_out_a'])
    wout = np.empty((2, 4, 128, 2048), np.float32)
    for L in range(2):
        Wk = w_out[L].reshape(8, 128, 1024)
        for i in range(4):
            blk = Wk[:, :, 256 * i:256 * i + 256].reshape(8, 128, 2, 128)
            wout[L, i] = blk.transpose(1, 2, 0, 3).reshape(128, 2048)
    com['wout'] = wout
    w_up = f(inp['w_ffn_up'])
    wup = np.empty((4, 24, 128, 2048), np.float32)
    for L in range(4):
        Wk = w_up[L].reshape(8, 128, 6144)
        g = Wk[:, :, 0:3072].reshape(8, 128, 24, 128)
        v = Wk[:, :, 3072:6144].reshape(8, 128, 24, 128)
        gv = np.stack([g, v], 3)
        wup[L] = gv.transpose(2, 1, 0, 3, 4).reshape(24, 128, 2048)
    com['wup'] = wup
    w_dn = f(inp['w_ffn_down'])
    wdn = np.empty((4, 3, 128, 8192), np.float32)
    for L in range(4):
        a = w_dn[L].reshape(3, 8, 128, 1024)
        wdn[L] = a.transpose(0, 2, 1, 3).reshape(3, 128, 8192)
    com['wdn'] = wdn
    w_kv = f(inp['w_kv']).reshape(2, 4, 128, 1024)
    wkv = np.empty((4, 128, 2048), np.float32)
    for half_ in range(2):
        for kh in range(2):
            a = w_kv[kh][:, :, 512 * half_:512 * half_ + 512]
            wkv[2 * half_ + kh] = a.transpose(1, 0, 2).reshape(128, 2048)
    com['wkv'] = wkv
    w_q = f(inp['w_q'])
    wq = np.empty((2, 12, 128, 1024), np.float32)
    for j in range(2):
        Wk = w_q[j].reshape(8, 128, 1536)
        for hp in range(4):
            for g in range(3):
                m = 4 * g + hp
                wq[j, hp * 3 + g] = Wk[:, :, 128 * m:128 * m + 128].transpose(1, 0, 2).reshape(128, 1024)
    com['wq'] = wq
    w_o = f(inp['w_o'])
    wo = np.empty((2, 2, 128, 2048), np.float32)
    for j in range(2):
        Wk = w_o[j].reshape(4, 128, 1024)
        for i in range(2):
            wo[j, i] = Wk[:, :, 512 * i:512 * i + 512].transpose(1, 0, 2).reshape(128, 2048)
    com['wo'] = wo
    xp = f(inp['x_prompt'])
    xs = f(inp['x_sample'])
    ck_, cv_ = f(inp['cache_k']), f(inp['cache_v'])
    rows = [1920 + np.arange(128)]
    for s in range(4):
        rows.append(1536 + s + 4 * np.arange(128))
    for s in range(4):
        rows.append(s + 16 * np.arange(128))
    rows = np.stack(rows, 0)
    sh, sc, sf = f(inp['state_rglru_h']), f(inp['state_rglru_conv']), f(inp['state_ffn_conv'])
    maps = []
    for c in range(8):
        m = dict(com)
        m['xT'] = np.ascontiguousarray(xp[c % 4].T)
        b0 = NB * c
        m['xsT'] = np.ascontiguousarray(xs[b0:b0 + NB].reshape(NS, 1024).T)
        kk = ck_[b0:b0 + NB][:, rows]
        kk = kk.reshape(NB, 9, 128, 4, 2, 64).transpose(0, 1, 4, 5, 3, 2)
        m['ck'] = np.ascontiguousarray(kk.reshape(NB, 9, 128, 512))
        m['cv'] = np.ascontiguousarray(cv_[b0:b0 + NB][:, rows].reshape(NB, 9, 128, 512))
        a = sh[:, b0:b0 + NB].reshape(2, NB, 8, 128)
        m['sh'] = np.ascontiguousarray(a.transpose(0, 3, 2, 1).reshape(2, 128, 128))
        a = sc[:, b0:b0 + NB].reshape(2, NB, 3, 8, 128)
        m['sc'] = np.ascontiguousarray(a.transpose(0, 4, 3, 1, 2).reshape(2, 128, 384))
        a = sf[:, b0:b0 + NB].reshape(4, NB, 2, 24, 128)
        m['sf'] = np.ascontiguousarray(a.transpose(0, 4, 3, 1, 2).reshape(4, 128, 768))
        maps.append(m)
    return maps


_NC_CACHE = {}


def kernel(**inputs):
    maps = _host_layout(inputs)
    if 'nc' not in _NC_CACHE:
        _NC_CACHE['nc'] = build_program()
    nc = _NC_CACHE['nc']
    res = run_bass_kernel_spmd(nc, maps, core_ids=list(range(8)))
    R = res.results
    y = np.stack([R[b]['yT'].T for b in range(4)], 0)
    ys = np.concatenate([R[c]['ysT'].T.reshape(NB, 4, 1024) for c in range(8)], 0)
    p_h = np.stack([R[b]['oh'].reshape(128, 2, 8).transpose(1, 2, 0).reshape(2, 1024) for b in range(4)], 1)
    p_c = np.stack([R[b]['oc'].reshape(128, 2, 8, 3).transpose(1, 3, 2, 0).reshape(2, 3, 1024) for b in range(4)], 1)
    p_f = np.stack([R[b]['of'].reshape(128, 4, 24, 2).transpose(1, 3, 2, 0).reshape(4, 2, 3072) for b in range(4)], 1)

    def fm2tok(a, n):
        return a.reshape(4, 2, 64, n).transpose(3, 0, 1, 2).reshape(n, 8, 64)
    p_k = np.stack([fm2tok(R[b]['pkT'], 2048) for b in range(4)], 0)
    p_v = np.stack([fm2tok(R[b]['pvT'], 2048) for b in range(4)], 0)
    s_h = np.concatenate([R[c]['soh'].reshape(2, 128, 8, NB).transpose(0, 3, 2, 1).reshape(2, NB, 1024)
                          for c in range(8)], 1)
    s_c = np.concatenate([R[c]['soc'].reshape(2, 128, 8, NB, 3).transpose(0, 3, 4, 2, 1).reshape(2, NB, 3, 1024)
                          for c in range(8)], 1)
    s_f = np.concatenate([R[c]['sof'].reshape(4, 128, 24, NB, 2).transpose(0, 3, 4, 2, 1).reshape(4, NB, 2, 3072)
                          for c in range(8)], 1)
    s_k = np.concatenate([fm2tok(R[c]['skT'], NS).reshape(NB, 4, 8, 64) for c in range(8)], 0)
    s_v = np.concatenate([R[c]['svtok'].reshape(NB, 4, 8, 64) for c in range(8)], 0)
    outs = (y, ys, p_h, p_c, p_f, p_k, p_v, s_h, s_c, s_f, s_k, s_v)
    return tuple(np.ascontiguousarray(o, dtype=np.float32) for o in outs)
```

```python
import math
from contextlib import ExitStack
import numpy as np
import concourse.bass as bass
import concourse.mybir as mybir
from concourse.bass_utils import run_bass_kernel_spmd

F32 = mybir.dt.float32
BF16 = mybir.dt.bfloat16
I32 = mybir.dt.int32
AF = mybir.ActivationFunctionType
ALU = mybir.AluOpType

P = 1024
NT = 4
NS = 64
TOK = 4096
KVW = TOK + NS
HB = 1152
NHB = 33
EPS = 1e-6
NB = 16

_po = {}
_n = 0
for _name, _w in [('nma', 16), ('caw', 64), ('cab', 16), ('grb', 16), ('gib', 16), ('lam', 16), ('nkv', 8),
                  ('nmb', 16), ('nff', 32), ('fcw', 288), ('fcb', 96), ('gk', 1), ('gq', 2), ('inv', 1), ('cl', 16), ('vh', 1)]:
    _po[_name] = _n
    _n += _w
NPAR = _n
C_ID, C_MP, C_MC, C_BD, C_PM, C_ON, C_MS0, C_MN, C_M64 = 0, 128, 256, 384, 512, 640, 768, 772, 784
NCB = 912


class _Space:
    def __init__(self):
        self.segs = []

    def deps(self, lo, hi, is_write, add):
        for a, b, w, r in self.segs:
            if b <= lo or a >= hi:
                continue
            if w is not None:
                add(w)
            if is_write:
                for t in r.values():
                    add(t)

    def apply(self, lo, hi, is_write, tok):
        out = []
        covered = []
        for seg in self.segs:
            a, b, w, r = seg
            if b <= lo or a >= hi:
                out.append(seg)
                continue
            if a < lo:
                out.append([a, lo, w, dict(r)])
            if b > hi:
                out.append([hi, b, w, dict(r)])
            ia, ib = max(a, lo), min(b, hi)
            if not is_write:
                r2 = dict(r)
                r2[(tok[0], tok[1])] = tok
                out.append([ia, ib, w, r2])
                covered.append((ia, ib))
        if is_write:
            out.append([lo, hi, tok, {}])
        else:
            covered.sort()
            cur = lo
            for a, b in covered:
                if a > cur:
                    out.append([cur, a, None, {(tok[0], tok[1]): tok}])
                cur = max(cur, b)
            if cur < hi:
                out.append([cur, hi, None, {(tok[0], tok[1]): tok}])
        out.sort(key=lambda s: s[0])
        self.segs = out


_ESZ = {F32: 4, BF16: 2, I32: 4}


def _extent(ap):
    pat = list(ap.ap)
    es = _ESZ[ap.dtype]
    pstride = pat[0][0]
    off = ap.offset % pstride if pstride > 0 else ap.offset
    lo = off
    hi = off
    for st, cnt in pat[1:]:
        if cnt > 1:
            if st >= 0:
                hi += st * (cnt - 1)
            else:
                lo += st * (cnt - 1)
    return ap.tensor.name, lo * es, (hi + 1) * es


class Sched:
    ENGS = ('pe', 'act', 'dve', 'pool', 'sp')

    def __init__(self, nc, stack):
        self.nc = nc
        self.q = {e: [] for e in self.ENGS}
        self.cnt = {e: 0 for e in self.ENGS}
        self.sem = {e: stack.enter_context(nc.semaphore("s_" + e)) for e in self.ENGS}
        self.waited = {e: {} for e in self.ENGS}
        self.spaces = {}
        self.dpool = {}
        for qe in ('sp', 'pool'):
            self.dpool[qe] = [[stack.enter_context(nc.semaphore("d_%s%d" % (qe, i))), 0] for i in range(24)]
        self.dnext = {'sp': 0, 'pool': 0}
        self.dall = []
        self.psums = []
        self.psn = 0
        self.nops = 0

    def psum(self):
        p = self.psums[self.psn % len(self.psums)]
        self.psn += 1
        return p

    def _semh(self, tok):
        if tok[0] == 'e':
            return self.sem[tok[1]]
        return self.dall[tok[1]][0]

    def _wait(self, eng, tok):
        key = (tok[0], tok[1])
        if self.waited[eng].get(key, 0) >= tok[2]:
            return
        self.waited[eng][key] = tok[2]
        semh = self._semh(tok)
        val = tok[2]
        self.q[eng].append(lambda h, semh=semh, val=val: h.wait_ge(semh, val))

    def _collect(self, eng, reads, writes):
        toks = {}

        def add(t):
            k = (t[0], t[1])
            if k not in toks or toks[k][2] < t[2]:
                toks[k] = t
        acc = []
        for ap, isw in [(a, False) for a in reads] + [(a, True) for a in writes]:
            name, lo, hi = _extent(ap)
            sp = self.spaces.get(name)
            if sp is None:
                sp = self.spaces[name] = _Space()
            acc.append((sp, lo, hi, isw))
            sp.deps(lo, hi, isw or name.startswith('ps'), add)
        for t in toks.values():
            if t[0] == 'e' and t[1] == eng and eng == 'pe':
                continue
            self._wait(eng, t)
        return acc

    def _commit(self, acc, tok):
        for sp, lo, hi, isw in acc:
            if not isw:
                sp.apply(lo, hi, False, tok)
        for sp, lo, hi, isw in acc:
            if isw:
                sp.apply(lo, hi, True, tok)

    def do(self, eng, op):
        fn, reads, writes = op
        acc = self._collect(eng, reads, writes)
        self.cnt[eng] += 1
        idx = self.cnt[eng]
        sem = self.sem[eng]
        self.q[eng].append(lambda h, fn=fn, sem=sem: fn(h).then_inc(sem, 1))
        self._commit(acc, ('e', eng, idx))
        self.nops += 1

    def pe(self, ops):
        reads = []
        writes = []
        for fn, r, w in ops:
            reads += r
            writes += w
        acc = self._collect('pe', reads, writes)
        self.cnt['pe'] += 1
        idx = self.cnt['pe']
        sem = self.sem['pe']
        for fn, r, w in ops[:-1]:
            self.q['pe'].append(lambda h, fn=fn: fn(h))
        fn = ops[-1][0]
        self.q['pe'].append(lambda h, fn=fn, sem=sem: fn(h).then_inc(sem, 1))
        self._commit(acc, ('e', 'pe', idx))
        self.nops += len(ops)

    def dma(self, qe, out, in_, r=(), w=()):
        acc = self._collect(qe, list(r), list(w))
        pool = self.dpool[qe]
        k = self.dnext[qe] % len(pool)
        self.dnext[qe] += 1
        ent = pool[k]
        if len(ent) == 2:
            ent.append(len(self.dall))
            self.dall.append(ent)
        gidx = ent[2]
        if ent[1] > 0:
            self._wait(qe, ('d', gidx, ent[1]))
        ent[1] += 16
        semh = ent[0]
        if qe == 'pool':
            self.q[qe].append(lambda h, out=out, in_=in_, semh=semh:
                              h.dma_start(out=out, in_=in_, max_dma_last_dim=4096).then_inc(semh, 16))
        else:
            self.q[qe].append(lambda h, out=out, in_=in_, semh=semh: h.dma_start(out=out, in_=in_).then_inc(semh, 16))
        self._commit(acc, ('d', gidx, ent[1]))
        self.nops += 1

    def finish(self):
        for ent in self.dall:
            if ent[1] > 0:
                self._wait('sp', ('d', ent[2], ent[1]))
        for e in ('pe', 'act', 'dve', 'pool'):
            if self.cnt[e] > 0:
                self._wait('sp', ('e', e, self.cnt[e]))

    def emit(self, block):
        nc = self.nc
        m = {'pe': block.tensor, 'act': block.scalar, 'dve': block.vector, 'pool': block.gpsimd, 'sp': block.sync}
        for e in self.ENGS:
            lst = self.q[e]
            if not lst:
                continue

            def body(h, lst=lst):
                for f in lst:
                    f(h)
            m[e](body)


def _isap(x):
    return hasattr(x, 'ap') and hasattr(x, 'tensor')


def ACT(out, in_, func, bias=None, scale=None):
    kw = {}
    rd = [in_]
    if bias is not None:
        kw['bias'] = bias
        if _isap(bias):
            rd.append(bias)
    if scale is not None:
        kw['scale'] = scale
        if _isap(scale):
            rd.append(scale)
    return (lambda h: h.activation(out=out, in_=in_, func=func, **kw), rd, [out])


def TS(out, in0, s1, s2, op0, op1=None):
    rd = [in0] + [s for s in (s1, s2) if _isap(s)]
    if op1 is None:
        return (lambda h: h.tensor_scalar(out=out, in0=in0, scalar1=s1, scalar2=None, op0=op0), rd, [out])
    return (lambda h: h.tensor_scalar(out=out, in0=in0, scalar1=s1, scalar2=s2, op0=op0, op1=op1), rd, [out])


def TT(out, a, b, op):
    return (lambda h: h.tensor_tensor(out=out, in0=a, in1=b, op=op), [a, b], [out])


def STT(out, in0, sc, in1, op0, op1):
    rd = [in0, in1] + ([sc] if _isap(sc) else [])
    return (lambda h: h.scalar_tensor_tensor(out=out, in0=in0, scalar=sc, in1=in1, op0=op0, op1=op1), rd, [out])


def SCAN(out, d0, d1, init):
    rd = [d0, d1] + ([init] if _isap(init) else [])
    return (lambda h: h.tensor_tensor_scan(out, d0, d1, init, op0=ALU.mult, op1=ALU.add), rd, [out])


def COPY(out, in_):
    return (lambda h: h.tensor_copy(out, in_), [in_], [out])


def RECIP(out, in_):
    return (lambda h: h.reciprocal(out, in_), [in_], [out])


def MEMSET(out, v):
    return (lambda h: h.memset(out, v), [], [out])


def MM(out, lhsT, rhs, start, stop):
    return (lambda h: h.matmul(out, lhsT, rhs, start=start, stop=stop), [lhsT, rhs], [out])


def TR(out, in_, ident):
    return (lambda h: h.transpose(out, in_, ident), [in_, ident], [out])


HALO = 128


class Tile:
    def __init__(self, t, halo=False):
        self.t = t
        self.has_s = (t == NT - 1)
        self.halo = halo
        self.W = P + (NS if self.has_s else 0) + (HALO if halo else 0)
        self.subs = [(0, 512), (512, 512)] + ([(P, NS)] if self.has_s else []) + ([(P, HALO)] if halo else [])


def build_program():
    nc = bass.Bass("TRN2", target_bir_lowering=False)
    d = {}

    def din(name, shape):
        d[name] = nc.dram_tensor(name, list(shape), F32, kind="ExternalInput").ap()

    def dout(name, shape):
        d[name] = nc.dram_tensor(name, list(shape), F32, kind="ExternalOutput").ap()

    din('xT', [1024, TOK]); din('xsT', [1024, NS])
    din('ck', [NB, 9, 128, 512]); din('cv', [NB, 9, 128, 512])
    din('sh', [2, 128, 128]); din('sc', [2, 128, 384]); din('sf', [4, 128, 768])
    din('par', [128, NPAR]); din('cst', [128, NCB]); din('pos', [128, KVW])
    din('win', [2, 8, 128, 2048]); din('wg', [2, 4, 128, 1024]); din('wout', [2, 4, 128, 2048])
    din('wup', [4, 24, 128, 2048]); din('wdn', [4, 3, 128, 8192]); din('wkv', [4, 128, 2048])
    din('wq', [2, 12, 128, 1024]); din('wo', [2, 2, 128, 2048])
    dout('yT', [1024, 2 * P]); dout('ysT', [1024, NS])
    dout('oh', [128, 16]); dout('oc', [128, 48]); dout('of', [128, 192])
    dout('pkT', [512, 2048]); dout('pvT', [512, 2048])
    dout('soh', [2, 128, 128]); dout('soc', [2, 128, 384]); dout('sof', [4, 128, 768])
    dout('skT', [512, NS]); dout('svtok', [NS, 512])

    with ExitStack() as st:
        def sb(name, shape, dt):
            return st.enter_context(nc.sbuf_tensor(name, list(shape), dt))
        X = sb("X", [128, 8, P + HALO], F32)
        KT = sb("KT", [128, 4, KVW], BF16)
        VT = sb("VT", [128, 4, KVW], BF16)
        WS = sb("WS", [128, 4, 2048], BF16)
        AR = sb("AR", [128, NHB * HB], BF16)
        PAR = sb("PAR", [128, NPAR], F32)
        CB = sb("CB", [128, NCB], BF16)
        HST = sb("HST", [128, 2, 8], F32)
        CAH = sb("CAH", [128, 2, 8, 3], F32)
        FH = sb("FH", [128, 4, 24, 2], F32)
        SHL = sb("SHL", [128, 8, NB], F32)
        SCL = sb("SCL", [128, 8, NB, 3], F32)
        SFL = sb("SFL", [128, 24, NB, 2], F32)
        TMP16 = sb("TMP16", [128, NB], F32)
        VSNB = sb("VSNB", [NS, 512], BF16)
        VSNF = sb("VSNF", [NS, 512], F32)
        SMALL = sb("SMALL", [128, 2, 16], F32)
        PTS = sb("PTS", [128, 4, 128], BF16)
        VNB = sb("VNB", [4, 2, 512], BF16)
        S = Sched(nc, st)
        S.psums = [st.enter_context(nc.psum_tensor("ps%d" % i, [128, 512], F32)) for i in range(8)]

        def arb(i, n=HB):
            return AR[:, i * HB:i * HB + n]

        def arf(i, n=HB):
            return AR[:, i * HB:(i + 2) * HB].bitcast(F32)[:, 0:n]

        def XN(k):
            return arb(k)

        IDENT = CB[:, C_ID:C_ID + 128]
        MASKP = CB[:, C_MP:C_MP + 128]
        MASKC = CB[:, C_MC:C_MC + 128]
        BD = CB[:, C_BD:C_BD + 128]
        PM = CB[:, C_PM:C_PM + 128]
        ONES = CB[:, C_ON:C_ON + 128]

        def par(name, i=0):
            o = _po[name] + i
            return PAR[:, o:o + 1]

        wsn = [0]

        def wload(src, n=2048, parts=128):
            k = wsn[0] % 4
            wsn[0] += 1
            dst = WS[0:parts, k, 0:n]
            S.dma('pool', dst, src, w=[dst])
            return WS[:, k, :]

        S.dma('sp', PAR[:, :], d['par'], w=[PAR[:, :]])
        S.dma('pool', CB[:, :], d['cst'], w=[CB[:, :]])
        S.do('dve', MEMSET(HST[:, :, :], 0.0))
        S.do('dve', MEMSET(CAH[:, :, :, :], 0.0))
        S.do('dve', MEMSET(FH[:, :, :, :], 0.0))
        HV = PTS[:, 0, 64:128]
        HV2 = PTS[:, 1, 64:128]
        vhp = PAR[:, _po['vh']:_po['vh'] + 1]
        S.do('dve', TS(HV, ONES[:, 0:64], vhp, None, ALU.mult))
        S.do('dve', TS(HV2[0:64, :], ONES[0:64, 0:64], PAR[0:64, _po['vh']:_po['vh'] + 1], None, ALU.mult))
        S.do('dve', COPY(HV2[64:128, :], ONES[64:128, 0:64]))
        lam = PAR[:, _po['lam']:_po['lam'] + 16]
        clv = PAR[:, _po['cl']:_po['cl'] + 16]
        S.do('act', ACT(clv, lam, AF.Exp, scale=-1.0))
        S.do('act', ACT(clv, clv, AF.Ln, bias=1.0))
        S.do('dve', TS(clv, clv, -8.0, None, ALU.mult))

        def load_x(T):
            src = d['xT'].rearrange("(c p) t -> p c t", p=128)[:, :, P * T.t:P * T.t + P]
            S.dma('sp', X[:, :, 0:P], src, w=[X[:, :, 0:P]])
            if T.has_s:
                src = d['xsT'].rearrange("(c p) t -> p c t", p=128)
                S.dma('sp', X[:, :, P:P + NS], src, w=[X[:, :, P:P + NS]])

        def store_y(T):
            dst = d['yT'].rearrange("(c p) t -> p c t", p=128)[:, :, P * (T.t - 2):P * (T.t - 2) + P]
            S.dma('sp', dst, X[:, :, 0:P], r=[X[:, :, 0:P]])
            if T.has_s:
                dst = d['ysT'].rearrange("(c p) t -> p c t", p=128)
                S.dma('sp', dst, X[:, :, P:P + NS], r=[X[:, :, P:P + NS]])

        def rmsnorm(T, gname, gi):
            SQ = arb(32)
            RS = arf(30)
            W = T.W
            for (c0, n) in T.subs:
                ps = S.psum()
                for c in range(8):
                    sq = SQ[:, (c % 2) * 512:(c % 2) * 512 + n]
                    S.do('act', ACT(sq, X[:, c, c0:c0 + n], AF.Square))
                    S.pe([MM(ps[:, 0:n], ONES, sq, c == 0, c == 7)])
                S.do('act', ACT(RS[:, c0:c0 + n], ps[:, 0:n], AF.Sqrt, bias=EPS, scale=1.0 / 1024))
                S.do('dve', RECIP(RS[:, c0:c0 + n], RS[:, c0:c0 + n]))
            for c in range(8):
                S.do('dve', STT(XN(c)[:, 0:W], X[:, c, 0:W], par(gname, gi * 8 + c), RS[:, 0:W], ALU.mult, ALU.mult))

        def sview(ap64, s=4):
            return ap64.rearrange("p (b s) -> p b s", s=s)

        def a_mix(T, L):
            W = T.W
            subs = T.subs
            XBH = arf(16)
            XBHs = XBH[:, 1028:1028 + NB * 7].rearrange("p (b j) -> p b j", j=7)
            XCs = [arf(18), arf(20)]
            XCBs = [arb(22), arb(23)]
            RA = arf(24)
            IU = arf(26)
            TH = arf(28)

            def GG(c):
                return arb(8 + c)
            if T.has_s:
                S.dma('sp', SHL[:, :, :], d['sh'][L].rearrange("p (c b) -> p c b", c=8), w=[SHL[:, :, :]])
                S.dma('sp', SCL[:, :, :, :], d['sc'][L].rearrange("p (c b j) -> p c b j", c=8, b=NB),
                      w=[SCL[:, :, :, :]])
            for n in range(4):
                wgt = wload(d['win'][L, 2 * n]).rearrange("p (c k q) -> p c k q", c=2, k=8)
                for cc in range(2):
                    c = 2 * n + cc
                    for (c0, nn) in subs:
                        ps = S.psum()
                        S.pe([MM(ps[:, 0:nn], wgt[:, cc, k, :], XN(k)[:, c0:c0 + nn], k == 0, k == 7) for k in range(8)])
                        S.do('act', ACT(GG(c)[:, c0:c0 + nn], ps[:, 0:nn], AF.Gelu_apprx_tanh))
                wxb = wload(d['win'][L, 2 * n + 1]).rearrange("p (c k q) -> p c k q", c=2, k=8)
                wrg = wload(d['wg'][L, n], n=1024)[:, 0:1024].rearrange("p (g k o) -> p g k o", g=2, k=2)
                for cc in range(2):
                    c = 2 * n + cc
                    XC = XCs[cc]
                    S.do('dve', COPY(XBH[:, 0:3], CAH[:, L, c, :]))
                    if T.has_s:
                        S.do('dve', COPY(XBHs[:, :, 0:3], SCL[:, c, :, :]))
                    for (c0, nn) in subs:
                        ps = S.psum()
                        S.pe([MM(ps[:, 0:nn], wxb[:, cc, k, :], XN(k)[:, c0:c0 + nn], k == 0, k == 7) for k in range(8)])
                        if c0 < P:
                            S.do('act', ACT(XBH[:, 3 + c0:3 + c0 + nn], ps[:, 0:nn], AF.Copy))
                        else:
                            S.do('act', ACT(XBHs[:, :, 3:7], sview(ps[:, 0:NS]), AF.Copy))
                    S.do('dve', COPY(CAH[:, L, c, :], XBH[:, P:P + 3]))
                    if T.has_s:
                        S.do('dve', COPY(SCL[:, c, :, :], XBHs[:, :, 4:7]))
                    cw = [par('caw', L * 32 + j * 8 + c) for j in range(4)]
                    cbias = par('cab', L * 8 + c)
                    S.do('dve', TS(XC[:, 0:P], XBH[:, 3:3 + P], cw[3], cbias, ALU.mult, ALU.add))
                    for j in (2, 1, 0):
                        S.do('dve', STT(XC[:, 0:P], XBH[:, j:j + P], cw[j], XC[:, 0:P], ALU.mult, ALU.add))
                    if T.has_s:
                        XCv = sview(XC[:, P:P + NS])
                        S.do('dve', TS(XCv, XBHs[:, :, 3:7], cw[3], cbias, ALU.mult, ALU.add))
                        for j in (2, 1, 0):
                            S.do('dve', STT(XCv, XBHs[:, :, j:j + 4], cw[j], XCv, ALU.mult, ALU.add))
                    S.do('act', ACT(XCBs[cc][:, 0:W], XC[:, 0:W], AF.Copy))
                for cc in range(2):
                    c = 2 * n + cc
                    XC = XCs[cc]
                    for gate in range(2):
                        dst = RA if gate == 0 else IU
                        gb = par('grb' if gate == 0 else 'gib', L * 8 + c)
                        for (c0, nn) in subs:
                            ps = S.psum()
                            S.pe([MM(ps[:, 0:nn], wrg[:, gate, k, cc * 128:(cc + 1) * 128], XCBs[k][:, c0:c0 + nn],
                                     k == 0, k == 1) for k in range(2)])
                            S.do('act', ACT(dst[:, c0:c0 + nn], ps[:, 0:nn], AF.Sigmoid, bias=gb))
                    S.do('act', ACT(RA[:, 0:W], RA[:, 0:W], AF.Exp, scale=par('cl', L * 8 + c)))
                    S.do('dve', TT(TH[:, 0:W], RA[:, 0:W], RA[:, 0:W], ALU.mult))
                    S.do('act', ACT(TH[:, 0:W], TH[:, 0:W], AF.Sqrt, bias=1.0, scale=-1.0))
                    S.do('dve', TT(IU[:, 0:W], IU[:, 0:W], XC[:, 0:W], ALU.mult))
                    S.do('dve', TT(IU[:, 0:W], IU[:, 0:W], TH[:, 0:W], ALU.mult))
                    if T.t < 2:
                        S.do('dve', TS(IU[:, 0:W], IU[:, 0:W], par('vh'), None, ALU.mult))
                    S.do('dve', SCAN(TH[:, 0:P], RA[:, 0:P], IU[:, 0:P], HST[:, L, c:c + 1]))
                    S.do('dve', COPY(HST[:, L, c:c + 1], TH[:, P - 1:P]))
                    if T.has_s:
                        As = sview(RA[:, P:P + NS])
                        Us = sview(IU[:, P:P + NS])
                        S.do('dve', TT(TMP16[:, :], As[:, :, 0], SHL[:, c, :], ALU.mult))
                        S.do('dve', TT(Us[:, :, 0], Us[:, :, 0], TMP16[:, :], ALU.add))
                        S.do('dve', MEMSET(As[:, :, 0], 0.0))
                        S.do('dve', SCAN(TH[:, P:P + NS], RA[:, P:P + NS], IU[:, P:P + NS], 0.0))
                        S.do('dve', COPY(SHL[:, c, :], sview(TH[:, P:P + NS])[:, :, 3]))
                    S.do('dve', TT(GG(c)[:, 0:W], TH[:, 0:W], GG(c)[:, 0:W], ALU.mult))
            if T.has_s:
                S.dma('sp', d['soh'][L].rearrange("p (c b) -> p c b", c=8), SHL[:, :, :], r=[SHL[:, :, :]])
                S.dma('sp', d['soc'][L].rearrange("p (c b j) -> p c b j", c=8, b=NB), SCL[:, :, :, :],
                      r=[SCL[:, :, :, :]])
            for i in range(4):
                wsl = wload(d['wout'][L, i]).rearrange("p (c k q) -> p c k q", c=2, k=8)
                for cc in range(2):
                    m = 2 * i + cc
                    for (c0, nn) in subs:
                        ps = S.psum()
                        S.pe([MM(ps[:, 0:nn], wsl[:, cc, k, :], GG(k)[:, c0:c0 + nn], k == 0, k == 7) for k in range(8)])
                        S.do('dve', TT(X[:, m, c0:c0 + nn], X[:, m, c0:c0 + nn], ps[:, 0:nn], ALU.add))

        def ffn(T, L):
            W = T.W
            subs = T.subs

            def ACTB(jj):
                return arb(8 + jj)
            WD = AR[:, 16 * HB:16 * HB + 8192].rearrange("p (j m) -> p j m", j=8)
            GH = arf(24)
            GHs = GH[:, 1028:1028 + NB * 6].rearrange("p (b j) -> p b j", j=6)
            GC = arf(26)
            VB = arb(28)
            if T.has_s:
                S.dma('sp', SFL[:, :, :, :], d['sf'][L].rearrange("p (j b s) -> p j b s", j=24, b=NB),
                      w=[SFL[:, :, :, :]])
            for G in range(3):
                for jj in range(8):
                    j = 8 * G + jj
                    wsl = wload(d['wup'][L, j]).rearrange("p (k q) -> p k q", k=8)
                    if jj == 2:
                        wdst = AR[:, 16 * HB:16 * HB + 8192].rearrange("p (a b) -> p a b", a=4)
                        S.dma('pool', wdst, d['wdn'][L, G].rearrange("p (a b) -> p a b", a=4), w=[wdst])
                    if not T.halo:
                        S.do('dve', COPY(GH[:, 0:2], FH[:, L, j, :]))
                    if T.has_s:
                        S.do('dve', COPY(GHs[:, :, 0:2], SFL[:, j, :, :]))
                    for (c0, nn) in subs:
                        psg = S.psum()
                        S.pe([MM(psg[:, 0:nn], wsl[:, k, 0:128], XN(k)[:, c0:c0 + nn], k == 0, k == 7) for k in range(8)])
                        psv = S.psum()
                        S.pe([MM(psv[:, 0:nn], wsl[:, k, 128:256], XN(k)[:, c0:c0 + nn], k == 0, k == 7) for k in range(8)])
                        if T.halo:
                            if c0 < P:
                                S.do('act', ACT(GH[:, HALO + c0:HALO + c0 + nn], psg[:, 0:nn], AF.Copy))
                            else:
                                S.do('act', ACT(GH[:, 0:HALO], psg[:, 0:nn], AF.Copy))
                        elif c0 < P:
                            S.do('act', ACT(GH[:, 2 + c0:2 + c0 + nn], psg[:, 0:nn], AF.Copy))
                        else:
                            S.do('act', ACT(GHs[:, :, 2:6], sview(psg[:, 0:NS]), AF.Copy))
                        S.do('act', ACT(VB[:, c0:c0 + nn], psv[:, 0:nn], AF.Copy))
                    if T.halo:
                        S.do('dve', COPY(FH[:, L, j, :], GH[:, HALO + P - 2:HALO + P]))
                    else:
                        S.do('dve', COPY(FH[:, L, j, :], GH[:, P:P + 2]))
                    if T.has_s:
                        S.do('dve', COPY(SFL[:, j, :, :], GHs[:, :, 4:6]))
                    fw = [par('fcw', L * 72 + tap * 24 + j) for tap in range(3)]
                    fb = par('fcb', L * 24 + j)
                    if T.halo:
                        S.do('dve', TS(GC[:, 0:P], GH[:, HALO:HALO + P], fw[2], fb, ALU.mult, ALU.add))
                        S.do('dve', STT(GC[:, 0:P], GH[:, HALO - 1:HALO - 1 + P], fw[1], GC[:, 0:P], ALU.mult, ALU.add))
                        S.do('dve', STT(GC[:, 0:P], GH[:, HALO - 2:HALO - 2 + P], fw[0], GC[:, 0:P], ALU.mult, ALU.add))
                        S.do('dve', MEMSET(GC[:, P:P + 2], 0.0))
                        gch = GC[:, P + 2:P + HALO]
                        S.do('dve', TS(gch, GH[:, 2:HALO], fw[2], fb, ALU.mult, ALU.add))
                        S.do('dve', STT(gch, GH[:, 1:HALO - 1], fw[1], gch, ALU.mult, ALU.add))
                        S.do('dve', STT(gch, GH[:, 0:HALO - 2], fw[0], gch, ALU.mult, ALU.add))
                    else:
                        S.do('dve', TS(GC[:, 0:P], GH[:, 2:2 + P], fw[2], fb, ALU.mult, ALU.add))
                        S.do('dve', STT(GC[:, 0:P], GH[:, 1:1 + P], fw[1], GC[:, 0:P], ALU.mult, ALU.add))
                        S.do('dve', STT(GC[:, 0:P], GH[:, 0:P], fw[0], GC[:, 0:P], ALU.mult, ALU.add))
                    if T.has_s:
                        GCv = sview(GC[:, P:P + NS])
                        S.do('dve', TS(GCv, GHs[:, :, 2:6], fw[2], fb, ALU.mult, ALU.add))
                        S.do('dve', STT(GCv, GHs[:, :, 1:5], fw[1], GCv, ALU.mult, ALU.add))
                        S.do('dve', STT(GCv, GHs[:, :, 0:4], fw[0], GCv, ALU.mult, ALU.add))
                    S.do('act', ACT(ACTB(jj)[:, 0:W], GC[:, 0:W], AF.Gelu_apprx_tanh))
                    S.do('dve', TT(ACTB(jj)[:, 0:W], ACTB(jj)[:, 0:W], VB[:, 0:W], ALU.mult))
                for m in range(8):
                    for (c0, nn) in subs:
                        ps = S.psum()
                        S.pe([MM(ps[:, 0:nn], WD[:, jj, 128 * m:128 * m + 128], ACTB(jj)[:, c0:c0 + nn], jj == 0, jj == 7)
                              for jj in range(8)])
                        S.do('dve', TT(X[:, m, c0:c0 + nn], X[:, m, c0:c0 + nn], ps[:, 0:nn], ALU.add))
            if T.has_s:
                S.dma('sp', d['sof'][L].rearrange("p (j b s) -> p j b s", j=24, b=NB), SFL[:, :, :, :],
                      r=[SFL[:, :, :, :]])

        def rope_tables(T):
            W = T.W
            ANG = arf(26)
            KI = AR[:, 28 * HB:30 * HB].bitcast(I32)[:, 0:HB]
            KF = arf(30)
            C = arf(22)
            Sn = arf(24)
            S.dma('sp', ANG[:, 0:P], d['pos'][:, P * T.t:P * T.t + P], w=[ANG[:, 0:P]])
            if T.has_s:
                S.dma('sp', ANG[:, P:P + NS], d['pos'][:, TOK:TOK + NS], w=[ANG[:, P:P + NS]])
            if T.halo:
                S.dma('sp', ANG[:, P:P + HALO], d['pos'][:, P * T.t - HALO:P * T.t], w=[ANG[:, P:P + HALO]])
            S.do('dve', TS(ANG[:, 0:W], ANG[:, 0:W], par('inv'), None, ALU.mult))
            S.do('dve', TS(KI[:, 0:W], ANG[:, 0:W], 1.0 / (2 * math.pi), None, ALU.mult))
            S.do('dve', COPY(KF[:, 0:W], KI[:, 0:W]))
            S.do('dve', STT(ANG[:, 0:W], KF[:, 0:W], -2.0 * math.pi, ANG[:, 0:W], ALU.mult, ALU.add))
            S2 = arf(28)
            S4 = arf(30)
            S.do('act', ACT(S2[:, 0:W], ANG[:, 0:W], AF.Sin, scale=0.5))
            S.do('act', ACT(S4[:, 0:W], ANG[:, 0:W], AF.Sin, scale=0.25))
            S.do('dve', TT(C[:, 0:W], S2[:, 0:W], S2[:, 0:W], ALU.mult))
            S.do('dve', TS(C[:, 0:W], C[:, 0:W], -2.0, 1.0, ALU.mult, ALU.add))
            S.do('dve', TT(S4[:, 0:W], S4[:, 0:W], S4[:, 0:W], ALU.mult))
            S.do('dve', TS(S4[:, 0:W], S4[:, 0:W], -4.0, 2.0, ALU.mult, ALU.add))
            S.do('dve', TT(Sn[:, 0:W], S2[:, 0:W], S4[:, 0:W], ALU.mult))
            return C, Sn

        def kv(T):
            W = T.W
            subs = T.subs
            t = T.t
            wk = [wload(d['wkv'][0]).rearrange("p (k q) -> p k q", k=4),
                  wload(d['wkv'][1]).rearrange("p (k q) -> p k q", k=4)]
            sets = [dict(KF=arf(16), RS=arf(18), KNb=arb(20), SQ=arb(21)),
                    dict(KF=arf(26), RS=arf(28), KNb=arb(8), SQ=arb(9))]
            C, Sn = None, None
            import os as _os
            _lv = int(_os.environ.get('K_KV', '99'))
            if _lv < 1:
                return
            C, Sn = rope_tables(T)
            if _lv < 2:
                return
            for m in range(4 if _lv >= 3 else 1):
                bs = sets[m % 2]
                KF, RS, KNb, SQ = bs['KF'], bs['RS'], bs['KNb'], bs['SQ']
                for (c0, nn) in subs:
                    ps = S.psum()
                    S.pe([MM(ps[:, 0:nn], wk[k // 4][:, k % 4, 128 * m:128 * m + 128], XN(k)[:, c0:c0 + nn], k == 0, k == 7)
                          for k in range(8)])
                    S.do('act', ACT(SQ[:, c0:c0 + nn], ps[:, 0:nn], AF.Square))
                    S.do('act', ACT(KF[:, c0:c0 + nn], ps[:, 0:nn], AF.Copy))
                    ps2 = S.psum()
                    S.pe([MM(ps2[:, 0:nn], BD, SQ[:, c0:c0 + nn], True, True)])
                    S.do('act', ACT(RS[:, c0:c0 + nn], ps2[:, 0:nn], AF.Sqrt, bias=EPS, scale=1.0 / 64))
                S.do('dve', RECIP(RS[:, 0:W], RS[:, 0:W]))
                S.do('dve', STT(KF[:, 0:W], KF[:, 0:W], par('gk'), RS[:, 0:W], ALU.mult, ALU.mult))
                S.do('act', ACT(KNb[:, 0:W], KF[:, 0:W], AF.Copy))
                for (c0, nn) in subs:
                    ps3 = S.psum()
                    S.pe([MM(ps3[:, 0:nn], PM, KNb[:, c0:c0 + nn], True, True)])
                    S.do('dve', TT(RS[:, c0:c0 + nn], ps3[:, 0:nn], Sn[:, c0:c0 + nn], ALU.mult))
                S.do('dve', TT(KF[:, 0:W], KF[:, 0:W], C[:, 0:W], ALU.mult))
                S.do('dve', TT(KF[:, 0:W], KF[:, 0:W], RS[:, 0:W], ALU.add))
                S.do('act', ACT(KT[:, m, P * t:P * t + P], KF[:, 0:P], AF.Copy))
                if T.has_s:
                    S.do('act', ACT(KT[:, m, TOK:TOK + NS], KF[:, P:P + NS], AF.Copy))
                    S.dma('sp', d['skT'].rearrange("(m p) t -> p m t", p=128)[:, m, :], KF[:, P:P + NS],
                          r=[KF[:, P:P + NS]])
                if t >= 2:
                    dst = d['pkT'].rearrange("(m p) t -> p m t", p=128)[:, m, P * (t - 2):P * (t - 2) + P]
                    S.dma('sp', dst, KF[:, 0:P], r=[KF[:, 0:P]])
            if _lv < 4:
                return
            wv = [wload(d['wkv'][2]).rearrange("p (k q) -> p k q", k=4),
                  wload(d['wkv'][3]).rearrange("p (k q) -> p k q", k=4)]
            for m in range(4):
                VF = sets[m % 2]['KF']
                for (c0, nn) in subs[0:2]:
                    ps = S.psum()
                    S.pe([MM(ps[:, 0:nn], wv[k // 4][:, k % 4, 128 * m:128 * m + 128], XN(k)[:, c0:c0 + nn], k == 0, k == 7)
                          for k in range(8)])
                    _vv = int(_os.environ.get('K_KVV', '3'))
                    if _vv & 1:
                        S.do('act', ACT(VF[:, c0:c0 + nn], ps[:, 0:nn], AF.Copy))
                    if _vv & 2:
                        S.do('dve', COPY(VT[:, m, P * t + c0:P * t + c0 + nn], ps[:, 0:nn]))
                if t >= 2:
                    dst = d['pvT'].rearrange("(m p) t -> p m t", p=128)[:, m, P * (t - 2):P * (t - 2) + P]
                    S.dma('sp', dst, VF[:, 0:P], r=[VF[:, 0:P]])
            if T.has_s:
                ps = S.psum()
                S.pe([MM(ps[0:NS, 0:512], XN(k)[:, P:P + NS], wv[k // 4][:, k % 4, :], k == 0, k == 7) for k in range(8)])
                S.do('act', ACT(VSNF[:, :], ps[0:NS, 0:512], AF.Copy))
                S.do('dve', COPY(VSNB[:, :], ps[0:NS, 0:512]))
                S.dma('sp', d['svtok'], VSNF[:, :], r=[VSNF[:, :]])

        ptn = [0]
        vbn = [0]

        def attention_prompt(T, hp, QT, NUM, DEN):
            t = T.t
            PTbuf = arb(12)
            VBbuf = AR[:, 13 * HB:15 * HB]
            MPC = CB[:, C_MP:C_MP + 256]
            M64 = CB[:, C_M64:C_M64 + 128]
            blocks = []
            hq0 = P * t - HALO
            for bq in range(8):
                blocks.append((0, 1, 128, 128 * bq, P * t + 128 * bq))
            if T.halo:
                blocks.append((0, 1, 128, P, hq0))
            for bb in range(2):
                for r in range(4):
                    blocks.append((1, 4, 128, 512 * bb + r, P * t + 512 * bb + r))
            if T.halo:
                for r in range(4):
                    blocks.append((1, 4, 32, P + r, hq0 + r))
            for r in range(16):
                blocks.append((2, 16, 64, r, P * t + r))
            if T.halo:
                for r in range(16):
                    blocks.append((2, 16, 8, P + r, hq0 + r))
            ND = AR[:, 26 * HB:30 * HB].bitcast(F32).rearrange("p (a c) -> p a c", a=2)
            def stage_a(g, dd, QB, qc, q0):
                kbs = []
                if q0 >= 128 * dd:
                    kbs.append((q0 - 128 * dd, 128, MASKP))
                else:
                    pmin = -((q0 - 128 * dd) // dd)
                    if pmin < 128:
                        assert QB <= pmin
                        kbs.append((q0 - 128 * dd + pmin * dd, 128 - pmin, None))
                kbs.append((q0, QB, MASKC))
                nkb = len(kbs)
                std = (nkb == 2 and kbs[0][1] == 128 and QB in (64, 128))
                pi = ptn[0] % 2
                ptn[0] += 1
                PT = PTbuf[:, pi * 512:pi * 512 + 512].rearrange("p (e c) -> p e c", e=2)
                for e in range(2):
                    pss = S.psum()
                    S.pe([MM(pss[0:nk, i * QB:(i + 1) * QB],
                             KT[64 * e:64 * e + 64, hp, k0:k0 + dd * (nk - 1) + 1:dd],
                             QT[g][64 * e:64 * e + 64, qc:qc + dd * (QB - 1) + 1:dd], True, True)
                          for i, (k0, nk, mask) in enumerate(kbs)])
                    if std:
                        S.do('act', ACT(PT[:, e, 0:2 * QB], pss[:, 0:2 * QB], AF.Exp, scale=0.125))
                    else:
                        for i, (k0, nk, mask) in enumerate(kbs):
                            S.do('act', ACT(PT[0:nk, e, i * QB:(i + 1) * QB], pss[0:nk, i * QB:(i + 1) * QB],
                                            AF.Exp, scale=0.125))
                if std:
                    mc = MPC if QB == 128 else M64
                    S.do('dve', TT(PT[:, :, 0:2 * QB], PT[:, :, 0:2 * QB],
                                   mc[:, None, :].broadcast_to([128, 2, 2 * QB]), ALU.mult))
                else:
                    for i, (k0, nk, mask) in enumerate(kbs):
                        if mask is not None:
                            pv3 = PT[0:nk, :, i * QB:(i + 1) * QB]
                            S.do('dve', TT(pv3, pv3, mask[0:nk, None, 0:QB].broadcast_to([nk, 2, QB]), ALU.mult))
                pst = S.psum()
                pstb = pst[:, :].bitcast(BF16)
                S.pe([TR(pstb[0:nk, i * 128:(i + 1) * 128], VT[:, hp, k0:k0 + dd * (nk - 1) + 1:dd], IDENT)
                      for i, (k0, nk, mask) in enumerate(kbs)])
                vi = vbn[0] % 8
                vbn[0] += 1
                VB = VBbuf[:, vi * 256:vi * 256 + 256]
                if std:
                    S.do('dve', COPY(VB[:, 0:256], pstb[:, 0:256]))
                else:
                    for i, (k0, nk, mask) in enumerate(kbs):
                        S.do('dve', COPY(VB[0:nk, i * 128:(i + 1) * 128], pstb[0:nk, i * 128:(i + 1) * 128]))
                return (g, dd, QB, qc, kbs, nkb, PT, VB)

            def stage_b(ctx):
                (g, dd, QB, qc, kbs, nkb, PT, VB) = ctx
                psod = S.psum()
                ops = []
                for e in range(2):
                    for i, (k0, nk, mask) in enumerate(kbs):
                        ops.append(MM(psod[64 * e:64 * e + 64, 0:QB], VB[0:nk, i * 128 + 64 * e:i * 128 + 64 * e + 64],
                                      PT[0:nk, e, i * QB:(i + 1) * QB], i == 0, i == nkb - 1))
                for e in range(2):
                    for i, (k0, nk, mask) in enumerate(kbs):
                        nh = min(nk, max(0, -((k0 - 2 * P) // dd)))
                        if nh == 0:
                            dl = ONES[0:nk, 0:64]
                        elif nh == nk:
                            dl = HV[0:nk, :]
                        else:
                            assert nh == 64 and nk == 128
                            dl = HV2[0:nk, :]
                        ops.append(MM(psod[64 * e:64 * e + 64, 128:128 + QB], dl,
                                      PT[0:nk, e, i * QB:(i + 1) * QB], i == 0, i == nkb - 1))
                S.pe(ops)
                ndv = ND[:, :, qc:qc + dd * (QB - 1) + 1:dd]
                src = psod[:, 0:256].rearrange("p (a c) -> p a c", a=2)[:, :, 0:QB]
                if g == 0:
                    S.do('act', ACT(ndv, src, AF.Copy))
                else:
                    S.do('dve', TT(ndv, ndv, src, ALU.add))

            prev = None
            for blk in blocks:
                ctx = stage_a(*blk)
                if prev is not None:
                    stage_b(prev)
                prev = ctx
            stage_b(prev)

        def attention_sample(T, QS, ATBs):
            KSb = AR[:, 16 * HB:16 * HB + 4 * 512].rearrange("p (r q) -> p r q", r=4)
            VSb = AR[:, 18 * HB:18 * HB + 8 * 512].rearrange("p (r q) -> p r q", r=8)
            MS0 = CB[:, C_MS0:C_MS0 + 4]
            MN = CB[0:4, C_MN:C_MN + 12]
            kn = [0]
            vn = [0]
            NDs = SMALL[:, :, :]
            NUMs = SMALL[:, 0, 0:16]
            DENs = SMALL[:, 1, 0:16]
            bstate = {}

            def stage_a(b, g):
                if g == 0:
                    vslot = b % 2
                    S.dma('sp', VNB[0:4, vslot, :], VSNB[4 * b:4 * b + 4, :], r=[VSNB[4 * b:4 * b + 4, :]],
                          w=[VNB[0:4, vslot, :]])
                    VN = VNB[0:4, vslot, :]
                    PTN = PTS[0:4, 3, 0:96]
                    for e in range(2):
                        psn_ = S.psum()
                        ops = []
                        for gg in range(3):
                            for hp in range(4):
                                ops.append(MM(psn_[0:4, gg * 16 + hp * 4:gg * 16 + hp * 4 + 4],
                                              KT[64 * e:64 * e + 64, hp, TOK + 4 * b:TOK + 4 * b + 4],
                                              QS[64 * e:64 * e + 64, gg * 4 + hp, 4 * b:4 * b + 4], True, True))
                        S.pe(ops)
                        S.do('act', ACT(PTN[:, e * 48:(e + 1) * 48], psn_[0:4, 0:48], AF.Exp, scale=0.125))
                        pn4 = PTN[:, e * 48:(e + 1) * 48].rearrange("p (g h s) -> p g h s", g=3, h=4)
                        mn4 = MN.rearrange("p (g s) -> p g s", g=3)[:, :, None, :].broadcast_to([4, 3, 4, 4])
                        S.do('dve', TT(pn4, pn4, mn4, ALU.mult))
                    pn = PTN.rearrange("p (e g c) -> p e g c", e=2, g=3)
                    PTNS = PTS[0:4, 3 - (b % 2), 96:128]
                    PTNS3 = PTNS.rearrange("p (e c) -> p e c", e=2)
                    S.do('dve', TT(PTNS3, pn[:, :, 0, :], pn[:, :, 1, :], ALU.add))
                    S.do('dve', TT(PTNS3, PTNS3, pn[:, :, 2, :], ALU.add))
                    bstate[b] = (VN, PTNS)
                nblk = 1 if g == 0 else 4
                ks = []
                vs = []
                for i in range(nblk):
                    blk = 0 if g == 0 else 1 + 4 * (g - 1) + i
                    ki = kn[0] % 4
                    kn[0] += 1
                    vi = vn[0] % 8
                    vn[0] += 1
                    S.dma('pool', KSb[:, ki, :], d['ck'][b, blk], w=[KSb[:, ki, :]])
                    S.dma('pool', VSb[:, vi, :], d['cv'][b, blk], w=[VSb[:, vi, :]])
                    ks.append(KSb[:, ki, :].rearrange("p (a k) -> p a k", a=4))
                    vs.append(VSb[:, vi, :])
                PT = PTS[:, g, 0:32]
                for e in range(2):
                    pss = S.psum()
                    ops = []
                    for hp in range(4):
                        if g == 0:
                            ops.append(MM(pss[:, hp * 4:hp * 4 + 4], ks[0][64 * e:64 * e + 64, hp, :],
                                          QS[64 * e:64 * e + 64, hp, 4 * b:4 * b + 4], True, True))
                        else:
                            for s_ in range(4):
                                ops.append(MM(pss[:, hp * 4 + s_:hp * 4 + s_ + 1], ks[s_][64 * e:64 * e + 64, hp, :],
                                              QS[64 * e:64 * e + 64, g * 4 + hp, 4 * b + s_:4 * b + s_ + 1], True, True))
                    S.pe(ops)
                    S.do('act', ACT(PT[:, e * 16:(e + 1) * 16], pss[:, 0:16], AF.Exp, scale=0.125))
                if g == 0:
                    p3 = PT.rearrange("p (h s) -> p h s", h=8)
                    S.do('dve', TT(p3, p3, MS0[:, None, :].broadcast_to([128, 8, 4]), ALU.mult))
                return (b, g, vs, PT)

            def stage_b(ctx):
                (b, g, vs, PT) = ctx
                (VN, PTNS) = bstate[b]
                last = (g == 2)
                psod = S.psum()
                ops = []
                for e in range(2):
                    for hp in range(4):
                        h = 2 * hp + e
                        for s_ in range(4):
                            col = hp * 4 + s_
                            vblk = vs[0] if g == 0 else vs[s_]
                            ops.append(MM(psod[64 * e:64 * e + 64, col:col + 1], vblk[:, h * 64:h * 64 + 64],
                                          PT[:, e * 16 + col:e * 16 + col + 1], True, not last))
                            if last:
                                ops.append(MM(psod[64 * e:64 * e + 64, col:col + 1], VN[:, h * 64:h * 64 + 64],
                                              PTNS[:, e * 16 + col:e * 16 + col + 1], False, True))
                for e in range(2):
                    ops.append(MM(psod[64 * e:64 * e + 64, 16:32], ONES[:, 0:64], PT[:, e * 16:(e + 1) * 16], True, not last))
                    if last:
                        ops.append(MM(psod[64 * e:64 * e + 64, 16:32], ONES[0:4, 0:64],
                                      PTNS[:, e * 16:(e + 1) * 16], False, True))
                S.pe(ops)
                src = psod[:, 0:32].rearrange("p (a c) -> p a c", a=2)
                if g == 0:
                    S.do('act', ACT(NDs, src, AF.Copy))
                else:
                    S.do('dve', TT(NDs, NDs, src, ALU.add))
                if last:
                    S.do('dve', RECIP(DENs, DENs))
                    S.do('dve', TT(ATBs[:, :, 4 * b:4 * b + 4], NUMs.rearrange("p (a c) -> p a c", a=4),
                                   DENs.rearrange("p (a c) -> p a c", a=4), ALU.mult))

            prev = None
            for b in range(NB):
                for g in range(3):
                    ctx = stage_a(b, g)
                    if prev is not None:
                        stage_b(prev)
                    prev = ctx
            stage_b(prev)

        def b_mix(T, jB):
            W = T.W
            subs = T.subs
            C, Sn = rope_tables(T)
            QN = arf(16)
            QNb = arb(18)
            QT = [arb(19), arb(20), arb(21)]
            RSq = arf(30)
            SQq = arb(32)
            NUM = arf(26)
            DEN = arf(28)
            QS = arb(15)[:, 0:12 * NS].rearrange("p (m c) -> p m c", m=12)

            def ATB(hp):
                return arb(8 + hp)
            for hp in range(4):
                for g in range(3):
                    wsl = wload(d['wq'][jB, hp * 3 + g], n=1024)[:, 0:1024].rearrange("p (k q) -> p k q", k=8)
                    for (c0, nn) in subs:
                        psq = S.psum()
                        S.pe([MM(psq[:, 0:nn], wsl[:, k, :], XN(k)[:, c0:c0 + nn], k == 0, k == 7) for k in range(8)])
                        S.do('act', ACT(SQq[:, c0:c0 + nn], psq[:, 0:nn], AF.Square))
                        ps2 = S.psum()
                        S.pe([MM(ps2[:, 0:nn], BD, SQq[:, c0:c0 + nn], True, True)])
                        S.do('act', ACT(RSq[:, c0:c0 + nn], ps2[:, 0:nn], AF.Sqrt, bias=EPS, scale=1.0 / 64))
                        S.do('dve', RECIP(RSq[:, c0:c0 + nn], RSq[:, c0:c0 + nn]))
                        S.do('dve', STT(QN[:, c0:c0 + nn], psq[:, 0:nn], par('gq', jB), RSq[:, c0:c0 + nn],
                                        ALU.mult, ALU.mult))
                    S.do('act', ACT(QNb[:, 0:W], QN[:, 0:W], AF.Copy))
                    for (c0, nn) in subs:
                        ps3 = S.psum()
                        S.pe([MM(ps3[:, 0:nn], PM, QNb[:, c0:c0 + nn], True, True)])
                        S.do('dve', TT(RSq[:, c0:c0 + nn], ps3[:, 0:nn], Sn[:, c0:c0 + nn], ALU.mult))
                    S.do('dve', TT(QN[:, 0:W], QN[:, 0:W], C[:, 0:W], ALU.mult))
                    S.do('dve', TT(QT[g][:, 0:W], QN[:, 0:W], RSq[:, 0:W], ALU.add))
                    if T.has_s:
                        S.do('act', ACT(QS[:, g * 4 + hp, :], QT[g][:, P:P + NS], AF.Copy))
                attention_prompt(T, hp, QT, NUM, DEN)
                Wa = P + (HALO if T.halo else 0)
                S.do('dve', TS(DEN[:, 0:Wa], DEN[:, 0:Wa], 1e-30, None, ALU.max))
                S.do('dve', RECIP(DEN[:, 0:Wa], DEN[:, 0:Wa]))
                S.do('dve', TT(ATB(hp)[:, 0:Wa], NUM[:, 0:Wa], DEN[:, 0:Wa], ALU.mult))
            if T.has_s:
                ATBs = AR[:, 8 * HB:12 * HB].rearrange("p (h w) -> p h w", h=4)[:, :, P:P + NS]
                attention_sample(T, QS, ATBs)
            wos = [wload(d['wo'][jB, i]).rearrange("p (h q) -> p h q", h=4) for i in range(2)]
            for m in range(8):
                for (c0, nn) in subs:
                    ps = S.psum()
                    S.pe([MM(ps[:, 0:nn], wos[m // 4][:, hp, (m % 4) * 128:(m % 4) * 128 + 128], ATB(hp)[:, c0:c0 + nn],
                             hp == 0, hp == 3) for hp in range(4)])
                    S.do('dve', TT(X[:, m, c0:c0 + nn], X[:, m, c0:c0 + nn], ps[:, 0:nn], ALU.add))

        for t in range(NT):
            T = Tile(t)
            TB = Tile(t, halo=(t == 2))
            load_x(T)
            for L in range(2):
                rmsnorm(T, 'nma', L)
                a_mix(T, L)
                rmsnorm(T, 'nff', L)
                ffn(T, L)
            rmsnorm(T, 'nkv', 0)
            kv(T)
            if t == 1:
                S.do('act', ACT(X[:, :, P:P + HALO], X[:, :, P - HALO:P], AF.Copy))
            if t >= 2:
                for jB in range(2):
                    rmsnorm(TB, 'nmb', jB)
                    b_mix(TB, jB)
                    rmsnorm(TB, 'nff', 2 + jB)
                    ffn(TB, 2 + jB)
                store_y(T)
        S.dma('sp', d['oh'], HST[:, :, :].rearrange("p l c -> p (l c)"), r=[HST[:, :, :]])
        S.dma('sp', d['oc'], CAH[:, :, :, :].rearrange("p l c j -> p (l c j)"), r=[CAH[:, :, :, :]])
        S.dma('sp', d['of'], FH[:, :, :, :].rearrange("p l c j -> p (l c j)"), r=[FH[:, :, :, :]])
        S.finish()
        with nc.Block() as block:
            S.emit(block)
    return nc


def _chunk(v, nchunk):
    return np.ascontiguousarray(np.asarray(v, np.float32).reshape(nchunk, 128).T)


def _consts():
    cst = np.zeros((128, NCB), np.float32)
    p = np.arange(128)[:, None]
    f = np.arange(128)[None, :]
    cst[:, C_ID:C_ID + 128] = np.eye(128)
    cst[:, C_MP:C_MP + 128] = (f <= p)
    cst[:, C_MC:C_MC + 128] = (f >= p)
    cst[:, C_BD:C_BD + 128] = ((p // 64) == (f // 64))
    pm = np.zeros((128, 128), np.float32)
    for m in range(128):
        i = m % 64
        if i < 8:
            pm[m + 8, m] = -1.0
        elif i < 16:
            pm[m - 8, m] = 1.0
    cst[:, C_PM:C_PM + 128] = pm
    cst[:, C_ON:C_ON + 128] = 1.0
    s = np.arange(4)[None, :]
    cst[:, C_MS0:C_MS0 + 4] = (np.arange(128)[:, None] >= s)
    mn = np.zeros((4, 3, 4), np.float32)
    sp = np.arange(4)[:, None]
    mn[:, 0, :] = (sp <= s)
    mn[:, 1, :] = (sp == s)
    mn[:, 2, :] = (sp == s)
    cst[0:4, C_MN:C_MN + 12] = mn.reshape(4, 12)
    cst[:, C_M64:C_M64 + 64] = (f <= p)[:, 0:64]
    cst[:, C_M64 + 64:C_M64 + 128] = (f >= p)[:, 0:64]
    return cst


def _host_layout(inp):
    f = lambda a: np.asarray(a, np.float32)
    com = {}
    par = np.zeros((128, NPAR), np.float32)

    def put(name, off, arr):
        o = _po[name] + off
        par[:, o:o + arr.shape[1]] = arr
    for L in range(2):
        put('nma', 8 * L, _chunk(f(inp['norm_mix_a'])[L], 8))
        for j in range(4):
            put('caw', 32 * L + 8 * j, _chunk(f(inp['conv_a_w'])[L, j], 8))
        put('cab', 8 * L, _chunk(f(inp['conv_a_b'])[L], 8))
        put('grb', 8 * L, _chunk(f(inp['gate_r_b'])[L].reshape(-1), 8))
        put('gib', 8 * L, _chunk(f(inp['gate_i_b'])[L].reshape(-1), 8))
        put('lam', 8 * L, _chunk(f(inp['lru_lambda'])[L], 8))
        put('nmb', 8 * L, _chunk(f(inp['norm_mix_b'])[L], 8))
        par[:, _po['gq'] + L] = np.tile(f(inp['q_norm'])[L], 2)
    put('nkv', 0, _chunk(f(inp['norm_kv']), 8))
    for L in range(4):
        put('nff', 8 * L, _chunk(f(inp['norm_ffn'])[L], 8))
        for tap in range(3):
            put('fcw', 72 * L + 24 * tap, _chunk(f(inp['ffn_conv_w'])[L, tap], 24))
        put('fcb', 24 * L, _chunk(f(inp['ffn_conv_b'])[L], 24))
    par[:, _po['gk']] = np.tile(f(inp['k_norm']), 2)
    half = 8
    inv = (500000.0 ** (-np.arange(half, dtype=np.float32) * np.float32(2.0 / 16))).astype(np.float32)
    invp = np.zeros(128, np.float32)
    for pp in range(128):
        i = pp % 64
        if i < 16:
            invp[pp] = inv[i % 8]
    par[:, _po['inv']] = invp
    com['par'] = par
    com['cst'] = _consts()
    pos = np.zeros((KVW,), np.float32)
    pos[:TOK] = np.arange(TOK)
    pos[TOK:] = np.tile(2048 + np.arange(4), NB)
    com['pos'] = np.ascontiguousarray(np.broadcast_to(pos[None, :], (128, KVW)))
    w_in = f(inp['w_in_a'])
    win = np.empty((2, 8, 128, 2048), np.float32)
    for L in range(2):
        Wk = w_in[L].reshape(8, 128, 2048)
        for i in range(8):
            n, kind = i // 2, i % 2
            c0 = kind * 1024 + 256 * n
            blk = Wk[:, :, c0:c0 + 256].reshape(8, 128, 2, 128)
            win[L, i] = blk.transpose(1, 2, 0, 3).reshape(128, 2048)
    com['win'] = win
    wg = np.empty((2, 4, 128, 1024), np.float32)
    rw, iw = f(inp['gate_r_w']), f(inp['gate_i_w'])
    for L in range(2):
        for n in range(4):
            a = np.stack([rw[L, n], iw[L, n]], 0).reshape(2, 2, 128, 256)
            wg[L, n] = a.transpose(2, 0, 1, 3).reshape(128, 1024)
    com['wg'] = wg
    w_out = f(inp['w_out_a'])
    wout = np.empty((2, 4, 128, 2048), np.float32)
    for L in range(2):
        Wk = w_out[L].reshape(8, 128, 1024)
        for i in range(4):
            blk = Wk[:, :, 256 * i:256 * i + 256].reshape(8, 128, 2, 128)
            wout[L, i] = blk.transpose(1, 2, 0, 3).reshape(128, 2048)
    com['wout'] = wout
    w_up = f(inp['w_ffn_up'])
    wup = np.empty((4, 24, 128, 2048), np.float32)
    for L in range(4):
        Wk = w_up[L].reshape(8, 128, 6144)
        g = Wk[:, :, 0:3072].reshape(8, 128, 24, 128)
        v = Wk[:, :, 3072:6144].reshape(8, 128, 24, 128)
        gv = np.stack([g, v], 3)
        wup[L] = gv.transpose(2, 1, 0, 3, 4).reshape(24, 128, 2048)
    com['wup'] = wup
    w_dn = f(inp['w_ffn_down'])
    wdn = np.empty((4, 3, 128, 8192), np.float32)
    for L in range(4):
        a = w_dn[L].reshape(3, 8, 128, 1024)
        wdn[L] = a.transpose(0, 2, 1, 3).reshape(3, 128, 8192)
    com['wdn'] = wdn
    w_kv = f(inp['w_kv']).reshape(2, 4, 128, 1024)
    wkv = np.empty((4, 128, 2048), np.float32)
    for half_ in range(2):
        for kh in range(2):
            a = w_kv[kh][:, :, 512 * half_:512 * half_ + 512]
            wkv[2 * half_ + kh] = a.transpose(1, 0, 2).reshape(128, 2048)
    com['wkv'] = wkv
    w_q = f(inp['w_q'])
    wq = np.empty((2, 12, 128, 1024), np.float32)
    for j in range(2):
        Wk = w_q[j].reshape(8, 128, 1536)
        for hp in range(4):
            for g in range(3):
                m = 4 * g + hp
                wq[j, hp * 3 + g] = Wk[:, :, 128 * m:128 * m + 128].transpose(1, 0, 2).reshape(128, 1024)
    com['wq'] = wq
    w_o = f(inp['w_o'])
    wo = np.empty((2, 2, 128, 2048), np.float32)
    for j in range(2):
        Wk = w_o[j].reshape(4, 128, 1024)
        for i in range(2):
            wo[j, i] = Wk[:, :, 512 * i:512 * i + 512].transpose(1, 0, 2).reshape(128, 2048)
    com['wo'] = wo
    xp = f(inp['x_prompt'])
    xs = f(inp['x_sample'])
    ck_, cv_ = f(inp['cache_k']), f(inp['cache_v'])
    rows = [1920 + np.arange(128)]
    for s in range(4):
        rows.append(1536 + s + 4 * np.arange(128))
    for s in range(4):
        rows.append(s + 16 * np.arange(128))
    rows = np.stack(rows, 0)
    sh, sc, sf = f(inp['state_rglru_h']), f(inp['state_rglru_conv']), f(inp['state_ffn_conv'])
    maps = []
    posv = com.pop('pos')
    for c in range(8):
        m = dict(com)
        bq, hf = c // 2, c % 2
        if hf == 1:
            m['xT'] = np.ascontiguousarray(xp[bq].T)
            m['pos'] = posv
        else:
            xt_ = np.zeros((1024, TOK), np.float32)
            xt_[:, 2 * P:] = xp[bq, 0:2 * P].T
            m['xT'] = xt_
            pz = posv.copy()
            pz[:, 0:2 * P] = 0.0
            pz[:, 2 * P:TOK] = posv[:, 0:2 * P]
            m['pos'] = pz
        pc = par.copy()
        pc[:, _po['vh']] = float(hf)
        m['par'] = pc
        b0 = NB * c
        m['xsT'] = np.ascontiguousarray(xs[b0:b0 + NB].reshape(NS, 1024).T)
        kk = ck_[b0:b0 + NB][:, rows]
        kk = kk.reshape(NB, 9, 128, 4, 2, 64).transpose(0, 1, 4, 5, 3, 2)
        m['ck'] = np.ascontiguousarray(kk.reshape(NB, 9, 128, 512))
        m['cv'] = np.ascontiguousarray(cv_[b0:b0 + NB][:, rows].reshape(NB, 9, 128, 512))
        a = sh[:, b0:b0 + NB].reshape(2, NB, 8, 128)
        m['sh'] = np.ascontiguousarray(a.transpose(0, 3, 2, 1).reshape(2, 128, 128))
        a = sc[:, b0:b0 + NB].reshape(2, NB, 3, 8, 128)
        m['sc'] = np.ascontiguousarray(a.transpose(0, 4, 3, 1, 2).reshape(2, 128, 384))
        a = sf[:, b0:b0 + NB].reshape(4, NB, 2, 24, 128)
        m['sf'] = np.ascontiguousarray(a.transpose(0, 4, 3, 1, 2).reshape(4, 128, 768))
        maps.append(m)
    return maps


_NC_CACHE = {}


def kernel(**inputs):
    maps = _host_layout(inputs)
    if 'nc' not in _NC_CACHE:
        _NC_CACHE['nc'] = build_program()
    nc = _NC_CACHE['nc']
    res = run_bass_kernel_spmd(nc, maps, core_ids=list(range(8)))
    R = res.results
    y = np.stack([np.concatenate([R[2 * b]['yT'].T, R[2 * b + 1]['yT'].T], 0) for b in range(4)], 0)
    ys = np.concatenate([R[c]['ysT'].T.reshape(NB, 4, 1024) for c in range(8)], 0)
    p_h = np.stack([R[2 * b + 1]['oh'].reshape(128, 2, 8).transpose(1, 2, 0).reshape(2, 1024) for b in range(4)], 1)
    p_c = np.stack([R[2 * b + 1]['oc'].reshape(128, 2, 8, 3).transpose(1, 3, 2, 0).reshape(2, 3, 1024) for b in range(4)], 1)
    p_f = np.stack([R[2 * b + 1]['of'].reshape(128, 4, 24, 2).transpose(1, 3, 2, 0).reshape(4, 2, 3072) for b in range(4)], 1)

    def fm2tok(a, n):
        return a.reshape(4, 2, 64, n).transpose(3, 0, 1, 2).reshape(n, 8, 64)
    p_k = np.stack([fm2tok(R[2 * b + 1]['pkT'], 2048) for b in range(4)], 0)
    p_v = np.stack([fm2tok(R[2 * b + 1]['pvT'], 2048) for b in range(4)], 0)
    s_h = np.concatenate([R[c]['soh'].reshape(2, 128, 8, NB).transpose(0, 3, 2, 1).reshape(2, NB, 1024)
                          for c in range(8)], 1)
    s_c = np.concatenate([R[c]['soc'].reshape(2, 128, 8, NB, 3).transpose(0, 3, 4, 2, 1).reshape(2, NB, 3, 1024)
                          for c in range(8)], 1)
    s_f = np.concatenate([R[c]['sof'].reshape(4, 128, 24, NB, 2).transpose(0, 3, 4, 2, 1).reshape(4, NB, 2, 3072)
                          for c in range(8)], 1)
    s_k = np.concatenate([fm2tok(R[c]['skT'], NS).reshape(NB, 4, 8, 64) for c in range(8)], 0)
    s_v = np.concatenate([R[c]['svtok'].reshape(NB, 4, 8, 64) for c in range(8)], 0)
    outs = (y, ys, p_h, p_c, p_f, p_k, p_v, s_h, s_c, s_f, s_k, s_v)
    return tuple(np.ascontiguousarray(o, dtype=np.float32) for o in outs)
```

```python
import math
from contextlib import ExitStack
import numpy as np
import concourse.bass as bass
import concourse.mybir as mybir
from concourse.bass_utils import run_bass_kernel_spmd

F32 = mybir.dt.float32
BF16 = mybir.dt.bfloat16
I32 = mybir.dt.int32
AF = mybir.ActivationFunctionType
ALU = mybir.AluOpType

P = 1024
NT = 4
NS = 64
TOK = 4096
KVW = TOK + NS
HB = 1152
NHB = 33
EPS = 1e-6
NB = 16

_po = {}
_n = 0
for _name, _w in [('nma', 16), ('caw', 64), ('cab', 16), ('grb', 16), ('gib', 16), ('lam', 16), ('nkv', 8),
                  ('nmb', 16), ('nff', 32), ('fcw', 288), ('fcb', 96), ('gk', 1), ('gq', 2), ('inv', 1), ('cl', 16), ('vh', 1)]:
    _po[_name] = _n
    _n += _w
NPAR = _n
C_ID, C_MP, C_MC, C_BD, C_PM, C_ON, C_MS0, C_MN, C_M64 = 0, 128, 256, 384, 512, 640, 768, 772, 784
NCB = 912


class _Space:
    def __init__(self):
        self.segs = []

    def deps(self, lo, hi, is_write, add):
        for a, b, w, r in self.segs:
            if b <= lo or a >= hi:
                continue
            if w is not None:
                add(w)
            if is_write:
                for t in r.values():
                    add(t)

    def apply(self, lo, hi, is_write, tok):
        out = []
        covered = []
        for seg in self.segs:
            a, b, w, r = seg
            if b <= lo or a >= hi:
                out.append(seg)
                continue
            if a < lo:
                out.append([a, lo, w, dict(r)])
            if b > hi:
                out.append([hi, b, w, dict(r)])
            ia, ib = max(a, lo), min(b, hi)
            if not is_write:
                r2 = dict(r)
                r2[(tok[0], tok[1])] = tok
                out.append([ia, ib, w, r2])
                covered.append((ia, ib))
        if is_write:
            out.append([lo, hi, tok, {}])
        else:
            covered.sort()
            cur = lo
            for a, b in covered:
                if a > cur:
                    out.append([cur, a, None, {(tok[0], tok[1]): tok}])
                cur = max(cur, b)
            if cur < hi:
                out.append([cur, hi, None, {(tok[0], tok[1]): tok}])
        out.sort(key=lambda s: s[0])
        self.segs = out


_ESZ = {F32: 4, BF16: 2, I32: 4}


def _extent(ap):
    pat = list(ap.ap)
    es = _ESZ[ap.dtype]
    pstride = pat[0][0]
    off = ap.offset % pstride if pstride > 0 else ap.offset
    lo = off
    hi = off
    for st, cnt in pat[1:]:
        if cnt > 1:
            if st >= 0:
                hi += st * (cnt - 1)
            else:
                lo += st * (cnt - 1)
    return ap.tensor.name, lo * es, (hi + 1) * es


class Sched:
    ENGS = ('pe', 'act', 'dve', 'pool', 'sp')

    def __init__(self, nc, stack):
        self.nc = nc
        self.q = {e: [] for e in self.ENGS}
        self.cnt = {e: 0 for e in self.ENGS}
        self.sem = {e: stack.enter_context(nc.semaphore("s_" + e)) for e in self.ENGS}
        self.waited = {e: {} for e in self.ENGS}
        self.spaces = {}
        self.dpool = {}
        for qe in ('sp', 'pool'):
            self.dpool[qe] = [[stack.enter_context(nc.semaphore("d_%s%d" % (qe, i))), 0] for i in range(24)]
        self.dnext = {'sp': 0, 'pool': 0}
        self.dall = []
        self.psums = []
        self.psn = 0
        self.nops = 0

    def psum(self):
        p = self.psums[self.psn % len(self.psums)]
        self.psn += 1
        return p

    def _semh(self, tok):
        if tok[0] == 'e':
            return self.sem[tok[1]]
        return self.dall[tok[1]][0]

    def _wait(self, eng, tok):
        key = (tok[0], tok[1])
        if self.waited[eng].get(key, 0) >= tok[2]:
            return
        self.waited[eng][key] = tok[2]
        semh = self._semh(tok)
        val = tok[2]
        self.q[eng].append(lambda h, semh=semh, val=val: h.wait_ge(semh, val))

    def _collect(self, eng, reads, writes):
        toks = {}

        def add(t):
            k = (t[0], t[1])
            if k not in toks or toks[k][2] < t[2]:
                toks[k] = t
        acc = []
        for ap, isw in [(a, False) for a in reads] + [(a, True) for a in writes]:
            name, lo, hi = _extent(ap)
            sp = self.spaces.get(name)
            if sp is None:
                sp = self.spaces[name] = _Space()
            acc.append((sp, lo, hi, isw))
            sp.deps(lo, hi, isw or name.startswith('ps'), add)
        for t in toks.values():
            if t[0] == 'e' and t[1] == eng and eng == 'pe':
                continue
            self._wait(eng, t)
        return acc

    def _commit(self, acc, tok):
        for sp, lo, hi, isw in acc:
            if not isw:
                sp.apply(lo, hi, False, tok)
        for sp, lo, hi, isw in acc:
            if isw:
                sp.apply(lo, hi, True, tok)

    def do(self, eng, op):
        fn, reads, writes = op
        acc = self._collect(eng, reads, writes)
        self.cnt[eng] += 1
        idx = self.cnt[eng]
        sem = self.sem[eng]
        self.q[eng].append(lambda h, fn=fn, sem=sem: fn(h).then_inc(sem, 1))
        self._commit(acc, ('e', eng, idx))
        self.nops += 1

    def pe(self, ops):
        reads = []
        writes = []
        for fn, r, w in ops:
            reads += r
            writes += w
        acc = self._collect('pe', reads, writes)
        self.cnt['pe'] += 1
        idx = self.cnt['pe']
        sem = self.sem['pe']
        for fn, r, w in ops[:-1]:
            self.q['pe'].append(lambda h, fn=fn: fn(h))
        fn = ops[-1][0]
        self.q['pe'].append(lambda h, fn=fn, sem=sem: fn(h).then_inc(sem, 1))
        self._commit(acc, ('e', 'pe', idx))
        self.nops += len(ops)

    def dma(self, qe, out, in_, r=(), w=()):
        acc = self._collect(qe, list(r), list(w))
        pool = self.dpool[qe]
        k = self.dnext[qe] % len(pool)
        self.dnext[qe] += 1
        ent = pool[k]
        if len(ent) == 2:
            ent.append(len(self.dall))
            self.dall.append(ent)
        gidx = ent[2]
        if ent[1] > 0:
            self._wait(qe, ('d', gidx, ent[1]))
        ent[1] += 16
        semh = ent[0]
        if qe == 'pool':
            self.q[qe].append(lambda h, out=out, in_=in_, semh=semh:
                              h.dma_start(out=out, in_=in_, max_dma_last_dim=4096).then_inc(semh, 16))
        else:
            self.q[qe].append(lambda h, out=out, in_=in_, semh=semh: h.dma_start(out=out, in_=in_).then_inc(semh, 16))
        self._commit(acc, ('d', gidx, ent[1]))
        self.nops += 1

    def finish(self):
        for ent in self.dall:
            if ent[1] > 0:
                self._wait('sp', ('d', ent[2], ent[1]))
        for e in ('pe', 'act', 'dve', 'pool'):
            if self.cnt[e] > 0:
                self._wait('sp', ('e', e, self.cnt[e]))

    def emit(self, block):
        nc = self.nc
        m = {'pe': block.tensor, 'act': block.scalar, 'dve': block.vector, 'pool': block.gpsimd, 'sp': block.sync}
        for e in self.ENGS:
            lst = self.q[e]
            if not lst:
                continue

            def body(h, lst=lst):
                for f in lst:
                    f(h)
            m[e](body)


def _isap(x):
    return hasattr(x, 'ap') and hasattr(x, 'tensor')


def ACT(out, in_, func, bias=None, scale=None):
    kw = {}
    rd = [in_]
    if bias is not None:
        kw['bias'] = bias
        if _isap(bias):
            rd.append(bias)
    if scale is not None:
        kw['scale'] = scale
        if _isap(scale):
            rd.append(scale)
    return (lambda h: h.activation(out=out, in_=in_, func=func, **kw), rd, [out])


def TS(out, in0, s1, s2, op0, op1=None):
    rd = [in0] + [s for s in (s1, s2) if _isap(s)]
    if op1 is None:
        return (lambda h: h.tensor_scalar(out=out, in0=in0, scalar1=s1, scalar2=None, op0=op0), rd, [out])
    return (lambda h: h.tensor_scalar(out=out, in0=in0, scalar1=s1, scalar2=s2, op0=op0, op1=op1), rd, [out])


def TT(out, a, b, op):
    return (lambda h: h.tensor_tensor(out=out, in0=a, in1=b, op=op), [a, b], [out])


def STT(out, in0, sc, in1, op0, op1):
    rd = [in0, in1] + ([sc] if _isap(sc) else [])
    return (lambda h: h.scalar_tensor_tensor(out=out, in0=in0, scalar=sc, in1=in1, op0=op0, op1=op1), rd, [out])


def SCAN(out, d0, d1, init):
    rd = [d0, d1] + ([init] if _isap(init) else [])
    return (lambda h: h.tensor_tensor_scan(out, d0, d1, init, op0=ALU.mult, op1=ALU.add), rd, [out])


def COPY(out, in_):
    return (lambda h: h.tensor_copy(out, in_), [in_], [out])


def RECIP(out, in_):
    return (lambda h: h.reciprocal(out, in_), [in_], [out])


def MEMSET(out, v):
    return (lambda h: h.memset(out, v), [], [out])


def MM(out, lhsT, rhs, start, stop):
    return (lambda h: h.matmul(out, lhsT, rhs, start=start, stop=stop), [lhsT, rhs], [out])


def TR(out, in_, ident):
    return (lambda h: h.transpose(out, in_, ident), [in_, ident], [out])


HALO = 128


class Tile:
    def __init__(self, t, halo=False):
        self.t = t
        self.has_s = (t == NT - 1)
        self.halo = halo
        self.W = P + (NS if self.has_s else 0) + (HALO if halo else 0)
        self.subs = [(0, 512), (512, 512)] + ([(P, NS)] if self.has_s else []) + ([(P, HALO)] if halo else [])


def build_program():
    nc = bass.Bass("TRN2", target_bir_lowering=False)
    d = {}

    def din(name, shape):
        d[name] = nc.dram_tensor(name, list(shape), F32, kind="ExternalInput").ap()

    def dout(name, shape):
        d[name] = nc.dram_tensor(name, list(shape), F32, kind="ExternalOutput").ap()

    din('xT', [1024, TOK]); din('xsT', [1024, NS])
    din('ck', [NB, 9, 128, 512]); din('cv', [NB, 9, 128, 512])
    din('sh', [2, 128, 128]); din('sc', [2, 128, 384]); din('sf', [4, 128, 768])
    din('par', [128, NPAR]); din('cst', [128, NCB]); din('pos', [128, KVW])
    din('win', [2, 8, 128, 2048]); din('wg', [2, 4, 128, 1024]); din('wout', [2, 4, 128, 2048])
    din('wup', [4, 24, 128, 2048]); din('wdn', [4, 3, 128, 8192]); din('wkv', [4, 128, 2048])
    din('wq', [2, 12, 128, 1024]); din('wo', [2, 2, 128, 2048])
    dout('yT', [1024, 2 * P]); dout('ysT', [1024, NS])
    dout('oh', [128, 16]); dout('oc', [128, 48]); dout('of', [128, 192])
    dout('pkT', [512, 2048]); dout('pvT', [512, 2048])
    dout('soh', [2, 128, 128]); dout('soc', [2, 128, 384]); dout('sof', [4, 128, 768])
    dout('skT', [512, NS]); dout('svtok', [NS, 512])

    with ExitStack() as st:
        def sb(name, shape, dt):
            return st.enter_context(nc.sbuf_tensor(name, list(shape), dt))
        X = sb("X", [128, 8, P + HALO], F32)
        KT = sb("KT", [128, 4, KVW], BF16)
        VT = sb("VT", [128, 4, KVW], BF16)
        WS = sb("WS", [128, 4, 2048], BF16)
        AR = sb("AR", [128, NHB * HB], BF16)
        PAR = sb("PAR", [128, NPAR], F32)
        CB = sb("CB", [128, NCB], BF16)
        HST = sb("HST", [128, 2, 8], F32)
        CAH = sb("CAH", [128, 2, 8, 3], F32)
        FH = sb("FH", [128, 4, 24, 2], F32)
        SHL = sb("SHL", [128, 8, NB], F32)
        SCL = sb("SCL", [128, 8, NB, 3], F32)
        SFL = sb("SFL", [128, 24, NB, 2], F32)
        TMP16 = sb("TMP16", [128, NB], F32)
        VSNB = sb("VSNB", [NS, 512], BF16)
        VSNF = sb("VSNF", [NS, 512], F32)
        SMALL = sb("SMALL", [128, 2, 16], F32)
        PTS = sb("PTS", [128, 4, 128], BF16)
        VNB = sb("VNB", [4, 2, 512], BF16)
        S = Sched(nc, st)
        S.psums = [st.enter_context(nc.psum_tensor("ps%d" % i, [128, 512], F32)) for i in range(8)]

        def arb(i, n=HB):
            return AR[:, i * HB:i * HB + n]

        def arf(i, n=HB):
            return AR[:, i * HB:(i + 2) * HB].bitcast(F32)[:, 0:n]

        def XN(k):
            return arb(k)

        IDENT = CB[:, C_ID:C_ID + 128]
        MASKP = CB[:, C_MP:C_MP + 128]
        MASKC = CB[:, C_MC:C_MC + 128]
        BD = CB[:, C_BD:C_BD + 128]
        PM = CB[:, C_PM:C_PM + 128]
        ONES = CB[:, C_ON:C_ON + 128]

        def par(name, i=0):
            o = _po[name] + i
            return PAR[:, o:o + 1]

        wsn = [0]

        def wload(src, n=2048, parts=128):
            k = wsn[0] % 4
            wsn[0] += 1
            dst = WS[0:parts, k, 0:n]
            S.dma('pool', dst, src, w=[dst])
            return WS[:, k, :]

        S.dma('sp', PAR[:, :], d['par'], w=[PAR[:, :]])
        S.dma('pool', CB[:, :], d['cst'], w=[CB[:, :]])
        S.do('dve', MEMSET(HST[:, :, :], 0.0))
        S.do('dve', MEMSET(CAH[:, :, :, :], 0.0))
        S.do('dve', MEMSET(FH[:, :, :, :], 0.0))
        HV = PTS[:, 0, 64:128]
        HV2 = PTS[:, 1, 64:128]
        vhp = PAR[:, _po['vh']:_po['vh'] + 1]
        S.do('dve', TS(HV, ONES[:, 0:64], vhp, None, ALU.mult))
        S.do('dve', TS(HV2[0:64, :], ONES[0:64, 0:64], PAR[0:64, _po['vh']:_po['vh'] + 1], None, ALU.mult))
        S.do('dve', COPY(HV2[64:128, :], ONES[64:128, 0:64]))
        lam = PAR[:, _po['lam']:_po['lam'] + 16]
        clv = PAR[:, _po['cl']:_po['cl'] + 16]
        S.do('act', ACT(clv, lam, AF.Exp, scale=-1.0))
        S.do('act', ACT(clv, clv, AF.Ln, bias=1.0))
        S.do('dve', TS(clv, clv, -8.0, None, ALU.mult))
        S.do('dve', TS(lam, clv, 2.0, None, ALU.mult))

        def load_x(T):
            src = d['xT'].rearrange("(c p) t -> p c t", p=128)[:, :, P * T.t:P * T.t + P]
            S.dma('sp', X[:, :, 0:P], src, w=[X[:, :, 0:P]])
            if T.has_s:
                src = d['xsT'].rearrange("(c p) t -> p c t", p=128)
                S.dma('sp', X[:, :, P:P + NS], src, w=[X[:, :, P:P + NS]])

        def store_y(T):
            dst = d['yT'].rearrange("(c p) t -> p c t", p=128)[:, :, P * (T.t - 2):P * (T.t - 2) + P]
            S.dma('sp', dst, X[:, :, 0:P], r=[X[:, :, 0:P]])
            if T.has_s:
                dst = d['ysT'].rearrange("(c p) t -> p c t", p=128)
                S.dma('sp', dst, X[:, :, P:P + NS], r=[X[:, :, P:P + NS]])

        def rmsnorm(T, gname, gi):
            SQ = arb(32)
            RS = arf(30)
            W = T.W
            for (c0, n) in T.subs:
                ps = S.psum()
                for c in range(8):
                    sq = SQ[:, (c % 2) * 512:(c % 2) * 512 + n]
                    S.do('act', ACT(sq, X[:, c, c0:c0 + n], AF.Square))
                    S.pe([MM(ps[:, 0:n], ONES, sq, c == 0, c == 7)])
                S.do('act', ACT(RS[:, c0:c0 + n], ps[:, 0:n], AF.Sqrt, bias=EPS, scale=1.0 / 1024))
                S.do('dve', RECIP(RS[:, c0:c0 + n], RS[:, c0:c0 + n]))
            for c in range(8):
                S.do('dve', STT(XN(c)[:, 0:W], X[:, c, 0:W], par(gname, gi * 8 + c), RS[:, 0:W], ALU.mult, ALU.mult))

        def sview(ap64, s=4):
            return ap64.rearrange("p (b s) -> p b s", s=s)

        def a_mix(T, L):
            W = T.W
            subs = T.subs
            XBH = arf(16)
            XBHs = XBH[:, 1028:1028 + NB * 7].rearrange("p (b j) -> p b j", j=7)
            XCs = [arf(18), arf(20)]
            XCBs = [arb(22), arb(23)]
            RAs = [arf(24), arf(30)]
            IU = arf(26)
            TH = arf(28)

            def GG(c):
                return arb(8 + c)
            if T.has_s:
                S.dma('sp', SHL[:, :, :], d['sh'][L].rearrange("p (c b) -> p c b", c=8), w=[SHL[:, :, :]])
                S.dma('sp', SCL[:, :, :, :], d['sc'][L].rearrange("p (c b j) -> p c b j", c=8, b=NB),
                      w=[SCL[:, :, :, :]])
            for n in range(4):
                wxb = wload(d['win'][L, 2 * n + 1]).rearrange("p (c k q) -> p c k q", c=2, k=8)
                wrg = wload(d['wg'][L, n], n=1024)[:, 0:1024].rearrange("p (g k o) -> p g k o", g=2, k=2)
                for cc in range(2):
                    c = 2 * n + cc
                    XC = XCs[cc]
                    S.do('dve', COPY(XBH[:, 0:3], CAH[:, L, c, :]))
                    if T.has_s:
                        S.do('dve', COPY(XBHs[:, :, 0:3], SCL[:, c, :, :]))
                    for (c0, nn) in subs:
                        ps = S.psum()
                        S.pe([MM(ps[:, 0:nn], wxb[:, cc, k, :], XN(k)[:, c0:c0 + nn], k == 0, k == 7) for k in range(8)])
                        if c0 < P:
                            S.do('act', ACT(XBH[:, 3 + c0:3 + c0 + nn], ps[:, 0:nn], AF.Copy))
                        else:
                            S.do('act', ACT(XBHs[:, :, 3:7], sview(ps[:, 0:NS]), AF.Copy))
                    S.do('dve', COPY(CAH[:, L, c, :], XBH[:, P:P + 3]))
                    if T.has_s:
                        S.do('dve', COPY(SCL[:, c, :, :], XBHs[:, :, 4:7]))
                    cw = [par('caw', L * 32 + j * 8 + c) for j in range(4)]
                    cbias = par('cab', L * 8 + c)
                    S.do('dve', TS(XC[:, 0:P], XBH[:, 3:3 + P], cw[3], cbias, ALU.mult, ALU.add))
                    for j in (2, 1, 0):
                        S.do('dve', STT(XC[:, 0:P], XBH[:, j:j + P], cw[j], XC[:, 0:P], ALU.mult, ALU.add))
                    if T.has_s:
                        XCv = sview(XC[:, P:P + NS])
                        S.do('dve', TS(XCv, XBHs[:, :, 3:7], cw[3], cbias, ALU.mult, ALU.add))
                        for j in (2, 1, 0):
                            S.do('dve', STT(XCv, XBHs[:, :, j:j + 4], cw[j], XCv, ALU.mult, ALU.add))
                    S.do('act', ACT(XCBs[cc][:, 0:W], XC[:, 0:W], AF.Copy))
                wgt = wload(d['win'][L, 2 * n]).rearrange("p (c k q) -> p c k q", c=2, k=8)
                for cc in range(2):
                    c = 2 * n + cc
                    for (c0, nn) in subs:
                        ps = S.psum()
                        S.pe([MM(ps[:, 0:nn], wgt[:, cc, k, :], XN(k)[:, c0:c0 + nn], k == 0, k == 7) for k in range(8)])
                        S.do('act', ACT(GG(c)[:, c0:c0 + nn], ps[:, 0:nn], AF.Gelu_apprx_tanh))
                for cc in range(2):
                    c = 2 * n + cc
                    XC = XCs[cc]
                    RA = RAs[cc]
                    for gate in range(2):
                        dst = RA if gate == 0 else IU
                        gb = par('grb' if gate == 0 else 'gib', L * 8 + c)
                        for (c0, nn) in subs:
                            ps = S.psum()
                            S.pe([MM(ps[:, 0:nn], wrg[:, gate, k, cc * 128:(cc + 1) * 128], XCBs[k][:, c0:c0 + nn],
                                     k == 0, k == 1) for k in range(2)])
                            S.do('act', ACT(dst[:, c0:c0 + nn], ps[:, 0:nn], AF.Sigmoid, bias=gb))
                    S.do('act', ACT(TH[:, 0:W], RA[:, 0:W], AF.Exp, scale=par('lam', L * 8 + c)))
                    S.do('act', ACT(RA[:, 0:W], RA[:, 0:W], AF.Exp, scale=par('cl', L * 8 + c)))
                    S.do('dve', TT(IU[:, 0:W], IU[:, 0:W], XC[:, 0:W], ALU.mult))
                    S.do('act', ACT(TH[:, 0:W], TH[:, 0:W], AF.Sqrt, bias=1.0, scale=-1.0))
                    S.do('dve', TT(IU[:, 0:W], IU[:, 0:W], TH[:, 0:W], ALU.mult))
                    if T.t < 2:
                        S.do('dve', TS(IU[:, 0:W], IU[:, 0:W], par('vh'), None, ALU.mult))
                    S.do('dve', SCAN(TH[:, 0:P], RA[:, 0:P], IU[:, 0:P], HST[:, L, c:c + 1]))
                    S.do('dve', COPY(HST[:, L, c:c + 1], TH[:, P - 1:P]))
                    if T.has_s:
                        As = sview(RA[:, P:P + NS])
                        Us = sview(IU[:, P:P + NS])
                        S.do('dve', TT(TMP16[:, :], As[:, :, 0], SHL[:, c, :], ALU.mult))
                        S.do('dve', TT(Us[:, :, 0], Us[:, :, 0], TMP16[:, :], ALU.add))
                        S.do('dve', MEMSET(As[:, :, 0], 0.0))
                        S.do('dve', SCAN(TH[:, P:P + NS], RA[:, P:P + NS], IU[:, P:P + NS], 0.0))
                        S.do('dve', COPY(SHL[:, c, :], sview(TH[:, P:P + NS])[:, :, 3]))
                    S.do('dve', TT(GG(c)[:, 0:W], TH[:, 0:W], GG(c)[:, 0:W], ALU.mult))
            if T.has_s:
                S.dma('sp', d['soh'][L].rearrange("p (c b) -> p c b", c=8), SHL[:, :, :], r=[SHL[:, :, :]])
                S.dma('sp', d['soc'][L].rearrange("p (c b j) -> p c b j", c=8, b=NB), SCL[:, :, :, :],
                      r=[SCL[:, :, :, :]])
            for i in range(4):
                wsl = wload(d['wout'][L, i]).rearrange("p (c k q) -> p c k q", c=2, k=8)
                for cc in range(2):
                    m = 2 * i + cc
                    for (c0, nn) in subs:
                        ps = S.psum()
                        S.pe([MM(ps[:, 0:nn], wsl[:, cc, k, :], GG(k)[:, c0:c0 + nn], k == 0, k == 7) for k in range(8)])
                        S.do('dve', TT(X[:, m, c0:c0 + nn], X[:, m, c0:c0 + nn], ps[:, 0:nn], ALU.add))

        def ffn(T, L):
            W = T.W
            subs = T.subs

            def ACTB(jj):
                return arb(8 + jj)
            WD = AR[:, 16 * HB:16 * HB + 8192].rearrange("p (j m) -> p j m", j=8)
            GH = arf(24)
            GHs = GH[:, 1028:1028 + NB * 6].rearrange("p (b j) -> p b j", j=6)
            GC = arf(26)
            VB = arb(28)
            if T.has_s:
                S.dma('sp', SFL[:, :, :, :], d['sf'][L].rearrange("p (j b s) -> p j b s", j=24, b=NB),
                      w=[SFL[:, :, :, :]])
            for G in range(3):
                for jj in range(8):
                    j = 8 * G + jj
                    wsl = wload(d['wup'][L, j]).rearrange("p (k q) -> p k q", k=8)
                    if jj == 2:
                        wdst = AR[:, 16 * HB:16 * HB + 8192].rearrange("p (a b) -> p a b", a=4)
                        S.dma('pool', wdst, d['wdn'][L, G].rearrange("p (a b) -> p a b", a=4), w=[wdst])
                    if not T.halo:
                        S.do('dve', COPY(GH[:, 0:2], FH[:, L, j, :]))
                    if T.has_s:
                        S.do('dve', COPY(GHs[:, :, 0:2], SFL[:, j, :, :]))
                    for (c0, nn) in subs:
                        psg = S.psum()
                        S.pe([MM(psg[:, 0:nn], wsl[:, k, 0:128], XN(k)[:, c0:c0 + nn], k == 0, k == 7) for k in range(8)])
                        psv = S.psum()
                        S.pe([MM(psv[:, 0:nn], wsl[:, k, 128:256], XN(k)[:, c0:c0 + nn], k == 0, k == 7) for k in range(8)])
                        if T.halo:
                            if c0 < P:
                                S.do('act', ACT(GH[:, HALO + c0:HALO + c0 + nn], psg[:, 0:nn], AF.Copy))
                            else:
                                S.do('act', ACT(GH[:, 0:HALO], psg[:, 0:nn], AF.Copy))
                        elif c0 < P:
                            S.do('act', ACT(GH[:, 2 + c0:2 + c0 + nn], psg[:, 0:nn], AF.Copy))
                        else:
                            S.do('act', ACT(GHs[:, :, 2:6], sview(psg[:, 0:NS]), AF.Copy))
                        S.do('act', ACT(VB[:, c0:c0 + nn], psv[:, 0:nn], AF.Copy))
                    if T.halo:
                        S.do('dve', COPY(FH[:, L, j, :], GH[:, HALO + P - 2:HALO + P]))
                    else:
                        S.do('dve', COPY(FH[:, L, j, :], GH[:, P:P + 2]))
                    if T.has_s:
                        S.do('dve', COPY(SFL[:, j, :, :], GHs[:, :, 4:6]))
                    fw = [par('fcw', L * 72 + tap * 24 + j) for tap in range(3)]
                    fb = par('fcb', L * 24 + j)
                    if T.halo:
                        S.do('dve', TS(GC[:, 0:P], GH[:, HALO:HALO + P], fw[2], fb, ALU.mult, ALU.add))
                        S.do('dve', STT(GC[:, 0:P], GH[:, HALO - 1:HALO - 1 + P], fw[1], GC[:, 0:P], ALU.mult, ALU.add))
                        S.do('dve', STT(GC[:, 0:P], GH[:, HALO - 2:HALO - 2 + P], fw[0], GC[:, 0:P], ALU.mult, ALU.add))
                        S.do('dve', MEMSET(GC[:, P:P + 2], 0.0))
                        gch = GC[:, P + 2:P + HALO]
                        S.do('dve', TS(gch, GH[:, 2:HALO], fw[2], fb, ALU.mult, ALU.add))
                        S.do('dve', STT(gch, GH[:, 1:HALO - 1], fw[1], gch, ALU.mult, ALU.add))
                        S.do('dve', STT(gch, GH[:, 0:HALO - 2], fw[0], gch, ALU.mult, ALU.add))
                    else:
                        S.do('dve', TS(GC[:, 0:P], GH[:, 2:2 + P], fw[2], fb, ALU.mult, ALU.add))
                        S.do('dve', STT(GC[:, 0:P], GH[:, 1:1 + P], fw[1], GC[:, 0:P], ALU.mult, ALU.add))
                        S.do('dve', STT(GC[:, 0:P], GH[:, 0:P], fw[0], GC[:, 0:P], ALU.mult, ALU.add))
                    if T.has_s:
                        GCv = sview(GC[:, P:P + NS])
                        S.do('dve', TS(GCv, GHs[:, :, 2:6], fw[2], fb, ALU.mult, ALU.add))
                        S.do('dve', STT(GCv, GHs[:, :, 1:5], fw[1], GCv, ALU.mult, ALU.add))
                        S.do('dve', STT(GCv, GHs[:, :, 0:4], fw[0], GCv, ALU.mult, ALU.add))
                    S.do('act', ACT(ACTB(jj)[:, 0:W], GC[:, 0:W], AF.Gelu_apprx_tanh))
                    S.do('dve', TT(ACTB(jj)[:, 0:W], ACTB(jj)[:, 0:W], VB[:, 0:W], ALU.mult))
                for m in range(8):
                    for (c0, nn) in subs:
                        ps = S.psum()
                        S.pe([MM(ps[:, 0:nn], WD[:, jj, 128 * m:128 * m + 128], ACTB(jj)[:, c0:c0 + nn], jj == 0, jj == 7)
                              for jj in range(8)])
                        S.do('dve', TT(X[:, m, c0:c0 + nn], X[:, m, c0:c0 + nn], ps[:, 0:nn], ALU.add))
            if T.has_s:
                S.dma('sp', d['sof'][L].rearrange("p (j b s) -> p j b s", j=24, b=NB), SFL[:, :, :, :],
                      r=[SFL[:, :, :, :]])

        def rope_tables(T):
            W = T.W
            ANG = arf(26)
            KI = AR[:, 28 * HB:30 * HB].bitcast(I32)[:, 0:HB]
            KF = arf(30)
            C = arf(22)
            Sn = arf(24)
            S.dma('sp', ANG[:, 0:P], d['pos'][:, P * T.t:P * T.t + P], w=[ANG[:, 0:P]])
            if T.has_s:
                S.dma('sp', ANG[:, P:P + NS], d['pos'][:, TOK:TOK + NS], w=[ANG[:, P:P + NS]])
            if T.halo:
                S.dma('sp', ANG[:, P:P + HALO], d['pos'][:, P * T.t - HALO:P * T.t], w=[ANG[:, P:P + HALO]])
            S.do('dve', TS(ANG[:, 0:W], ANG[:, 0:W], par('inv'), None, ALU.mult))
            S.do('dve', TS(KI[:, 0:W], ANG[:, 0:W], 1.0 / (2 * math.pi), None, ALU.mult))
            S.do('dve', COPY(KF[:, 0:W], KI[:, 0:W]))
            S.do('dve', STT(ANG[:, 0:W], KF[:, 0:W], -2.0 * math.pi, ANG[:, 0:W], ALU.mult, ALU.add))
            S2 = arf(28)
            S4 = arf(30)
            S.do('act', ACT(S2[:, 0:W], ANG[:, 0:W], AF.Sin, scale=0.5))
            S.do('act', ACT(S4[:, 0:W], ANG[:, 0:W], AF.Sin, scale=0.25))
            S.do('dve', TT(C[:, 0:W], S2[:, 0:W], S2[:, 0:W], ALU.mult))
            S.do('dve', TS(C[:, 0:W], C[:, 0:W], -2.0, 1.0, ALU.mult, ALU.add))
            S.do('dve', TT(S4[:, 0:W], S4[:, 0:W], S4[:, 0:W], ALU.mult))
            S.do('dve', TS(S4[:, 0:W], S4[:, 0:W], -4.0, 2.0, ALU.mult, ALU.add))
            S.do('dve', TT(Sn[:, 0:W], S2[:, 0:W], S4[:, 0:W], ALU.mult))
            return C, Sn

        def kv(T):
            W = T.W
            subs = T.subs
            t = T.t
            wk = [wload(d['wkv'][0]).rearrange("p (k q) -> p k q", k=4),
                  wload(d['wkv'][1]).rearrange("p (k q) -> p k q", k=4)]
            sets = [dict(KF=arf(16), RS=arf(18), KNb=arb(20), SQ=arb(21)),
                    dict(KF=arf(26), RS=arf(28), KNb=arb(8), SQ=arb(9))]
            C, Sn = None, None
            import os as _os
            _lv = int(_os.environ.get('K_KV', '99'))
            if _lv < 1:
                return
            C, Sn = rope_tables(T)
            if _lv < 2:
                return
            for m in range(4 if _lv >= 3 else 1):
                bs = sets[m % 2]
                KF, RS, KNb, SQ = bs['KF'], bs['RS'], bs['KNb'], bs['SQ']
                for (c0, nn) in subs:
                    ps = S.psum()
                    S.pe([MM(ps[:, 0:nn], wk[k // 4][:, k % 4, 128 * m:128 * m + 128], XN(k)[:, c0:c0 + nn], k == 0, k == 7)
                          for k in range(8)])
                    S.do('act', ACT(SQ[:, c0:c0 + nn], ps[:, 0:nn], AF.Square))
                    S.do('act', ACT(KF[:, c0:c0 + nn], ps[:, 0:nn], AF.Copy))
                    ps2 = S.psum()
                    S.pe([MM(ps2[:, 0:nn], BD, SQ[:, c0:c0 + nn], True, True)])
                    S.do('act', ACT(RS[:, c0:c0 + nn], ps2[:, 0:nn], AF.Sqrt, bias=EPS, scale=1.0 / 64))
                S.do('dve', RECIP(RS[:, 0:W], RS[:, 0:W]))
                S.do('dve', STT(KF[:, 0:W], KF[:, 0:W], par('gk'), RS[:, 0:W], ALU.mult, ALU.mult))
                S.do('act', ACT(KNb[:, 0:W], KF[:, 0:W], AF.Copy))
                for (c0, nn) in subs:
                    ps3 = S.psum()
                    S.pe([MM(ps3[:, 0:nn], PM, KNb[:, c0:c0 + nn], True, True)])
                    S.do('dve', TT(RS[:, c0:c0 + nn], ps3[:, 0:nn], Sn[:, c0:c0 + nn], ALU.mult))
                S.do('dve', TT(KF[:, 0:W], KF[:, 0:W], C[:, 0:W], ALU.mult))
                S.do('dve', TT(KF[:, 0:W], KF[:, 0:W], RS[:, 0:W], ALU.add))
                S.do('act', ACT(KT[:, m, P * t:P * t + P], KF[:, 0:P], AF.Copy))
                if T.has_s:
                    S.do('act', ACT(KT[:, m, TOK:TOK + NS], KF[:, P:P + NS], AF.Copy))
                    S.dma('sp', d['skT'].rearrange("(m p) t -> p m t", p=128)[:, m, :], KF[:, P:P + NS],
                          r=[KF[:, P:P + NS]])
                if t >= 2:
                    dst = d['pkT'].rearrange("(m p) t -> p m t", p=128)[:, m, P * (t - 2):P * (t - 2) + P]
                    S.dma('sp', dst, KF[:, 0:P], r=[KF[:, 0:P]])
            if _lv < 4:
                return
            wv = [wload(d['wkv'][2]).rearrange("p (k q) -> p k q", k=4),
                  wload(d['wkv'][3]).rearrange("p (k q) -> p k q", k=4)]
            for m in range(4):
                VF = sets[m % 2]['KF']
                for (c0, nn) in subs[0:2]:
                    ps = S.psum()
                    S.pe([MM(ps[:, 0:nn], wv[k // 4][:, k % 4, 128 * m:128 * m + 128], XN(k)[:, c0:c0 + nn], k == 0, k == 7)
                          for k in range(8)])
                    _vv = int(_os.environ.get('K_KVV', '3'))
                    if _vv & 1:
                        S.do('act', ACT(VF[:, c0:c0 + nn], ps[:, 0:nn], AF.Copy))
                    if _vv & 2:
                        S.do('dve', COPY(VT[:, m, P * t + c0:P * t + c0 + nn], ps[:, 0:nn]))
                if t >= 2:
                    dst = d['pvT'].rearrange("(m p) t -> p m t", p=128)[:, m, P * (t - 2):P * (t - 2) + P]
                    S.dma('sp', dst, VF[:, 0:P], r=[VF[:, 0:P]])
            if T.has_s:
                ps = S.psum()
                S.pe([MM(ps[0:NS, 0:512], XN(k)[:, P:P + NS], wv[k // 4][:, k % 4, :], k == 0, k == 7) for k in range(8)])
                S.do('act', ACT(VSNF[:, :], ps[0:NS, 0:512], AF.Copy))
                S.do('dve', COPY(VSNB[:, :], ps[0:NS, 0:512]))
                S.dma('sp', d['svtok'], VSNF[:, :], r=[VSNF[:, :]])

        ptn = [0]
        vbn = [0]

        def attention_prompt(T, hp, QT, NUM, DEN):
            t = T.t
            PTbuf = arb(12)
            VBbuf = AR[:, 13 * HB:15 * HB]
            MPC = CB[:, C_MP:C_MP + 256]
            M64 = CB[:, C_M64:C_M64 + 128]
            blocks = []
            hq0 = P * t - HALO
            for bq in range(8):
                blocks.append((0, 1, 128, 128 * bq, P * t + 128 * bq))
            if T.halo:
                blocks.append((0, 1, 128, P, hq0))
            for bb in range(2):
                for r in range(4):
                    blocks.append((1, 4, 128, 512 * bb + r, P * t + 512 * bb + r))
            if T.halo:
                for r in range(4):
                    blocks.append((1, 4, 32, P + r, hq0 + r))
            for r in range(16):
                blocks.append((2, 16, 64, r, P * t + r))
            if T.halo:
                for r in range(16):
                    blocks.append((2, 16, 8, P + r, hq0 + r))
            ND = AR[:, 26 * HB:30 * HB].bitcast(F32).rearrange("p (a c) -> p a c", a=2)
            def stage_a(g, dd, QB, qc, q0):
                kbs = []
                if q0 >= 128 * dd:
                    kbs.append((q0 - 128 * dd, 128, MASKP))
                else:
                    pmin = -((q0 - 128 * dd) // dd)
                    if pmin < 128:
                        assert QB <= pmin
                        kbs.append((q0 - 128 * dd + pmin * dd, 128 - pmin, None))
                kbs.append((q0, QB, MASKC))
                nkb = len(kbs)
                std = (nkb == 2 and kbs[0][1] == 128 and QB in (64, 128))
                pi = ptn[0] % 2
                ptn[0] += 1
                PT = PTbuf[:, pi * 512:pi * 512 + 512].rearrange("p (e c) -> p e c", e=2)
                for e in range(2):
                    pss = S.psum()
                    S.pe([MM(pss[0:nk, i * QB:(i + 1) * QB],
                             KT[64 * e:64 * e + 64, hp, k0:k0 + dd * (nk - 1) + 1:dd],
                             QT[g][64 * e:64 * e + 64, qc:qc + dd * (QB - 1) + 1:dd], True, True)
                          for i, (k0, nk, mask) in enumerate(kbs)])
                    if std:
                        S.do('act', ACT(PT[:, e, 0:2 * QB], pss[:, 0:2 * QB], AF.Exp, scale=0.125))
                    else:
                        for i, (k0, nk, mask) in enumerate(kbs):
                            S.do('act', ACT(PT[0:nk, e, i * QB:(i + 1) * QB], pss[0:nk, i * QB:(i + 1) * QB],
                                            AF.Exp, scale=0.125))
                if std:
                    mc = MPC if QB == 128 else M64
                    S.do('dve', TT(PT[:, :, 0:2 * QB], PT[:, :, 0:2 * QB],
                                   mc[:, None, :].broadcast_to([128, 2, 2 * QB]), ALU.mult))
                else:
                    for i, (k0, nk, mask) in enumerate(kbs):
                        if mask is not None:
                            pv3 = PT[0:nk, :, i * QB:(i + 1) * QB]
                            S.do('dve', TT(pv3, pv3, mask[0:nk, None, 0:QB].broadcast_to([nk, 2, QB]), ALU.mult))
                pst = S.psum()
                pstb = pst[:, :].bitcast(BF16)
                S.pe([TR(pstb[0:nk, i * 128:(i + 1) * 128], VT[:, hp, k0:k0 + dd * (nk - 1) + 1:dd], IDENT)
                      for i, (k0, nk, mask) in enumerate(kbs)])
                vi = vbn[0] % 8
                vbn[0] += 1
                VB = VBbuf[:, vi * 256:vi * 256 + 256]
                if std:
                    S.do('act', ACT(VB[:, 0:256], pstb[:, 0:256], AF.Copy))
                else:
                    for i, (k0, nk, mask) in enumerate(kbs):
                        S.do('dve', COPY(VB[0:nk, i * 128:(i + 1) * 128], pstb[0:nk, i * 128:(i + 1) * 128]))
                return (g, dd, QB, qc, kbs, nkb, PT, VB)

            def stage_b(ctx):
                (g, dd, QB, qc, kbs, nkb, PT, VB) = ctx
                psod = S.psum()
                ops = []
                for e in range(2):
                    for i, (k0, nk, mask) in enumerate(kbs):
                        ops.append(MM(psod[64 * e:64 * e + 64, 0:QB], VB[0:nk, i * 128 + 64 * e:i * 128 + 64 * e + 64],
                                      PT[0:nk, e, i * QB:(i + 1) * QB], i == 0, i == nkb - 1))
                for e in range(2):
                    for i, (k0, nk, mask) in enumerate(kbs):
                        nh = min(nk, max(0, -((k0 - 2 * P) // dd)))
                        if nh == 0:
                            dl = ONES[0:nk, 0:64]
                        elif nh == nk:
                            dl = HV[0:nk, :]
                        else:
                            assert nh == 64 and nk == 128
                            dl = HV2[0:nk, :]
                        ops.append(MM(psod[64 * e:64 * e + 64, 128:128 + QB], dl,
                                      PT[0:nk, e, i * QB:(i + 1) * QB], i == 0, i == nkb - 1))
                S.pe(ops)
                ndv = ND[:, :, qc:qc + dd * (QB - 1) + 1:dd]
                src = psod[:, 0:256].rearrange("p (a c) -> p a c", a=2)[:, :, 0:QB]
                if g == 0:
                    S.do('act', ACT(ndv, src, AF.Copy))
                else:
                    S.do('dve', TT(ndv, ndv, src, ALU.add))

            prev = None
            for blk in blocks:
                ctx = stage_a(*blk)
                if prev is not None:
                    stage_b(prev)
                prev = ctx
            stage_b(prev)

        def attention_sample(T, QS, ATBs):
            KSb = AR[:, 16 * HB:16 * HB + 4 * 512].rearrange("p (r q) -> p r q", r=4)
            VSb = AR[:, 18 * HB:18 * HB + 8 * 512].rearrange("p (r q) -> p r q", r=8)
            MS0 = CB[:, C_MS0:C_MS0 + 4]
            MN = CB[0:4, C_MN:C_MN + 12]
            kn = [0]
            vn = [0]
            NDs = SMALL[:, :, :]
            NUMs = SMALL[:, 0, 0:16]
            DENs = SMALL[:, 1, 0:16]
            bstate = {}

            def stage_a(b, g):
                if g == 0:
                    vslot = b % 2
                    S.dma('sp', VNB[0:4, vslot, :], VSNB[4 * b:4 * b + 4, :], r=[VSNB[4 * b:4 * b + 4, :]],
                          w=[VNB[0:4, vslot, :]])
                    VN = VNB[0:4, vslot, :]
                    PTN = PTS[0:4, 3, 0:96]
                    for e in range(2):
                        psn_ = S.psum()
                        ops = []
                        for gg in range(3):
                            for hp in range(4):
                                ops.append(MM(psn_[0:4, gg * 16 + hp * 4:gg * 16 + hp * 4 + 4],
                                              KT[64 * e:64 * e + 64, hp, TOK + 4 * b:TOK + 4 * b + 4],
                                              QS[64 * e:64 * e + 64, gg * 4 + hp, 4 * b:4 * b + 4], True, True))
                        S.pe(ops)
                        S.do('act', ACT(PTN[:, e * 48:(e + 1) * 48], psn_[0:4, 0:48], AF.Exp, scale=0.125))
                        pn4 = PTN[:, e * 48:(e + 1) * 48].rearrange("p (g h s) -> p g h s", g=3, h=4)
                        mn4 = MN.rearrange("p (g s) -> p g s", g=3)[:, :, None, :].broadcast_to([4, 3, 4, 4])
                        S.do('dve', TT(pn4, pn4, mn4, ALU.mult))
                    pn = PTN.rearrange("p (e g c) -> p e g c", e=2, g=3)
                    PTNS = PTS[0:4, 3 - (b % 2), 96:128]
                    PTNS3 = PTNS.rearrange("p (e c) -> p e c", e=2)
                    S.do('dve', TT(PTNS3, pn[:, :, 0, :], pn[:, :, 1, :], ALU.add))
                    S.do('dve', TT(PTNS3, PTNS3, pn[:, :, 2, :], ALU.add))
                    bstate[b] = (VN, PTNS)
                nblk = 1 if g == 0 else 4
                ks = []
                vs = []
                for i in range(nblk):
                    blk = 0 if g == 0 else 1 + 4 * (g - 1) + i
                    ki = kn[0] % 4
                    kn[0] += 1
                    vi = vn[0] % 8
                    vn[0] += 1
                    S.dma('pool', KSb[:, ki, :], d['ck'][b, blk], w=[KSb[:, ki, :]])
                    S.dma('pool', VSb[:, vi, :], d['cv'][b, blk], w=[VSb[:, vi, :]])
                    ks.append(KSb[:, ki, :].rearrange("p (a k) -> p a k", a=4))
                    vs.append(VSb[:, vi, :])
                PT = PTS[:, g, 0:32]
                for e in range(2):
                    pss = S.psum()
                    ops = []
                    for hp in range(4):
                        if g == 0:
                            ops.append(MM(pss[:, hp * 4:hp * 4 + 4], ks[0][64 * e:64 * e + 64, hp, :],
                                          QS[64 * e:64 * e + 64, hp, 4 * b:4 * b + 4], True, True))
                        else:
                            for s_ in range(4):
                                ops.append(MM(pss[:, hp * 4 + s_:hp * 4 + s_ + 1], ks[s_][64 * e:64 * e + 64, hp, :],
                                              QS[64 * e:64 * e + 64, g * 4 + hp, 4 * b + s_:4 * b + s_ + 1], True, True))
                    S.pe(ops)
                    S.do('act', ACT(PT[:, e * 16:(e + 1) * 16], pss[:, 0:16], AF.Exp, scale=0.125))
                if g == 0:
                    p3 = PT.rearrange("p (h s) -> p h s", h=8)
                    S.do('dve', TT(p3, p3, MS0[:, None, :].broadcast_to([128, 8, 4]), ALU.mult))
                return (b, g, vs, PT)

            def stage_b(ctx):
                (b, g, vs, PT) = ctx
                (VN, PTNS) = bstate[b]
                last = (g == 2)
                psod = S.psum()
                ops = []
                for e in range(2):
                    for hp in range(4):
                        h = 2 * hp + e
                        for s_ in range(4):
                            col = hp * 4 + s_
                            vblk = vs[0] if g == 0 else vs[s_]
                            ops.append(MM(psod[64 * e:64 * e + 64, col:col + 1], vblk[:, h * 64:h * 64 + 64],
                                          PT[:, e * 16 + col:e * 16 + col + 1], True, not last))
                            if last:
                                ops.append(MM(psod[64 * e:64 * e + 64, col:col + 1], VN[:, h * 64:h * 64 + 64],
                                              PTNS[:, e * 16 + col:e * 16 + col + 1], False, True))
                for e in range(2):
                    ops.append(MM(psod[64 * e:64 * e + 64, 16:32], ONES[:, 0:64], PT[:, e * 16:(e + 1) * 16], True, not last))
                    if last:
                        ops.append(MM(psod[64 * e:64 * e + 64, 16:32], ONES[0:4, 0:64],
                                      PTNS[:, e * 16:(e + 1) * 16], False, True))
                S.pe(ops)
                src = psod[:, 0:32].rearrange("p (a c) -> p a c", a=2)
                if g == 0:
                    S.do('act', ACT(NDs, src, AF.Copy))
                else:
                    S.do('dve', TT(NDs, NDs, src, ALU.add))
                if last:
                    S.do('dve', RECIP(DENs, DENs))
                    S.do('dve', TT(ATBs[:, :, 4 * b:4 * b + 4], NUMs.rearrange("p (a c) -> p a c", a=4),
                                   DENs.rearrange("p (a c) -> p a c", a=4), ALU.mult))

            prev = None
            for b in range(NB):
                for g in range(3):
                    ctx = stage_a(b, g)
                    if prev is not None:
                        stage_b(prev)
                    prev = ctx
            stage_b(prev)

        def b_mix(T, jB):
            W = T.W
            subs = T.subs
            C, Sn = rope_tables(T)
            QN = arf(16)
            QNb = arb(18)
            QT = [arb(19), arb(20), arb(21)]
            RSq = arf(30)
            SQq = arb(32)
            NUM = arf(26)
            DEN = arf(28)
            QS = arb(15)[:, 0:12 * NS].rearrange("p (m c) -> p m c", m=12)

            def ATB(hp):
                return arb(8 + hp)
            for hp in range(4):
                for g in range(3):
                    wsl = wload(d['wq'][jB, hp * 3 + g], n=1024)[:, 0:1024].rearrange("p (k q) -> p k q", k=8)
                    for (c0, nn) in subs:
                        psq = S.psum()
                        S.pe([MM(psq[:, 0:nn], wsl[:, k, :], XN(k)[:, c0:c0 + nn], k == 0, k == 7) for k in range(8)])
                        S.do('act', ACT(SQq[:, c0:c0 + nn], psq[:, 0:nn], AF.Square))
                        ps2 = S.psum()
                        S.pe([MM(ps2[:, 0:nn], BD, SQq[:, c0:c0 + nn], True, True)])
                        S.do('act', ACT(RSq[:, c0:c0 + nn], ps2[:, 0:nn], AF.Sqrt, bias=EPS, scale=1.0 / 64))
                        S.do('dve', RECIP(RSq[:, c0:c0 + nn], RSq[:, c0:c0 + nn]))
                        S.do('dve', STT(QN[:, c0:c0 + nn], psq[:, 0:nn], par('gq', jB), RSq[:, c0:c0 + nn],
                                        ALU.mult, ALU.mult))
                    S.do('act', ACT(QNb[:, 0:W], QN[:, 0:W], AF.Copy))
                    for (c0, nn) in subs:
                        ps3 = S.psum()
                        S.pe([MM(ps3[:, 0:nn], PM, QNb[:, c0:c0 + nn], True, True)])
                        S.do('dve', TT(RSq[:, c0:c0 + nn], ps3[:, 0:nn], Sn[:, c0:c0 + nn], ALU.mult))
                    S.do('dve', TT(QN[:, 0:W], QN[:, 0:W], C[:, 0:W], ALU.mult))
                    S.do('dve', TT(QT[g][:, 0:W], QN[:, 0:W], RSq[:, 0:W], ALU.add))
                    if T.has_s:
                        S.do('act', ACT(QS[:, g * 4 + hp, :], QT[g][:, P:P + NS], AF.Copy))
                attention_prompt(T, hp, QT, NUM, DEN)
                Wa = P + (HALO if T.halo else 0)
                S.do('dve', TS(DEN[:, 0:Wa], DEN[:, 0:Wa], 1e-30, None, ALU.max))
                S.do('dve', RECIP(DEN[:, 0:Wa], DEN[:, 0:Wa]))
                S.do('dve', TT(ATB(hp)[:, 0:Wa], NUM[:, 0:Wa], DEN[:, 0:Wa], ALU.mult))
            if T.has_s:
                ATBs = AR[:, 8 * HB:12 * HB].rearrange("p (h w) -> p h w", h=4)[:, :, P:P + NS]
                attention_sample(T, QS, ATBs)
            wos = [wload(d['wo'][jB, i]).rearrange("p (h q) -> p h q", h=4) for i in range(2)]
            for m in range(8):
                for (c0, nn) in subs:
                    ps = S.psum()
                    S.pe([MM(ps[:, 0:nn], wos[m // 4][:, hp, (m % 4) * 128:(m % 4) * 128 + 128], ATB(hp)[:, c0:c0 + nn],
                             hp == 0, hp == 3) for hp in range(4)])
                    S.do('dve', TT(X[:, m, c0:c0 + nn], X[:, m, c0:c0 + nn], ps[:, 0:nn], ALU.add))

        for t in range(NT):
            T = Tile(t)
            TB = Tile(t, halo=(t == 2))
            load_x(T)
            for L in range(2):
                rmsnorm(T, 'nma', L)
                a_mix(T, L)
                rmsnorm(T, 'nff', L)
                ffn(T, L)
            rmsnorm(T, 'nkv', 0)
            kv(T)
            if t == 1:
                S.do('act', ACT(X[:, :, P:P + HALO], X[:, :, P - HALO:P], AF.Copy))
            if t >= 2:
                for jB in range(2):
                    rmsnorm(TB, 'nmb', jB)
                    b_mix(TB, jB)
                    rmsnorm(TB, 'nff', 2 + jB)
                    ffn(TB, 2 + jB)
                store_y(T)
        S.dma('sp', d['oh'], HST[:, :, :].rearrange("p l c -> p (l c)"), r=[HST[:, :, :]])
        S.dma('sp', d['oc'], CAH[:, :, :, :].rearrange("p l c j -> p (l c j)"), r=[CAH[:, :, :, :]])
        S.dma('sp', d['of'], FH[:, :, :, :].rearrange("p l c j -> p (l c j)"), r=[FH[:, :, :, :]])
        S.finish()
        with nc.Block() as block:
            S.emit(block)
    return nc


def _chunk(v, nchunk):
    return np.ascontiguousarray(np.asarray(v, np.float32).reshape(nchunk, 128).T)


def _consts():
    cst = np.zeros((128, NCB), np.float32)
    p = np.arange(128)[:, None]
    f = np.arange(128)[None, :]
    cst[:, C_ID:C_ID + 128] = np.eye(128)
    cst[:, C_MP:C_MP + 128] = (f <= p)
    cst[:, C_MC:C_MC + 128] = (f >= p)
    cst[:, C_BD:C_BD + 128] = ((p // 64) == (f // 64))
    pm = np.zeros((128, 128), np.float32)
    for m in range(128):
        i = m % 64
        if i < 8:
            pm[m + 8, m] = -1.0
        elif i < 16:
            pm[m - 8, m] = 1.0
    cst[:, C_PM:C_PM + 128] = pm
    cst[:, C_ON:C_ON + 128] = 1.0
    s = np.arange(4)[None, :]
    cst[:, C_MS0:C_MS0 + 4] = (np.arange(128)[:, None] >= s)
    mn = np.zeros((4, 3, 4), np.float32)
    sp = np.arange(4)[:, None]
    mn[:, 0, :] = (sp <= s)
    mn[:, 1, :] = (sp == s)
    mn[:, 2, :] = (sp == s)
    cst[0:4, C_MN:C_MN + 12] = mn.reshape(4, 12)
    cst[:, C_M64:C_M64 + 64] = (f <= p)[:, 0:64]
    cst[:, C_M64 + 64:C_M64 + 128] = (f >= p)[:, 0:64]
    return cst


def _host_layout(inp):
    f = lambda a: np.asarray(a, np.float32)
    com = {}
    par = np.zeros((128, NPAR), np.float32)

    def put(name, off, arr):
        o = _po[name] + off
        par[:, o:o + arr.shape[1]] = arr
    for L in range(2):
        put('nma', 8 * L, _chunk(f(inp['norm_mix_a'])[L], 8))
        for j in range(4):
            put('caw', 32 * L + 8 * j, _chunk(f(inp['conv_a_w'])[L, j], 8))
        put('cab', 8 * L, _chunk(f(inp['conv_a_b'])[L], 8))
        put('grb', 8 * L, _chunk(f(inp['gate_r_b'])[L].reshape(-1), 8))
        put('gib', 8 * L, _chunk(f(inp['gate_i_b'])[L].reshape(-1), 8))
        put('lam', 8 * L, _chunk(f(inp['lru_lambda'])[L], 8))
        put('nmb', 8 * L, _chunk(f(inp['norm_mix_b'])[L], 8))
        par[:, _po['gq'] + L] = np.tile(f(inp['q_norm'])[L], 2)
    put('nkv', 0, _chunk(f(inp['norm_kv']), 8))
    for L in range(4):
        put('nff', 8 * L, _chunk(f(inp['norm_ffn'])[L], 8))
        for tap in range(3):
            put('fcw', 72 * L + 24 * tap, _chunk(f(inp['ffn_conv_w'])[L, tap], 24))
        put('fcb', 24 * L, _chunk(f(inp['ffn_conv_b'])[L], 24))
    par[:, _po['gk']] = np.tile(f(inp['k_norm']), 2)
    half = 8
    inv = (500000.0 ** (-np.arange(half, dtype=np.float32) * np.float32(2.0 / 16))).astype(np.float32)
    invp = np.zeros(128, np.float32)
    for pp in range(128):
        i = pp % 64
        if i < 16:
            invp[pp] = inv[i % 8]
    par[:, _po['inv']] = invp
    com['par'] = par
    com['cst'] = _consts()
    pos = np.zeros((KVW,), np.float32)
    pos[:TOK] = np.arange(TOK)
    pos[TOK:] = np.tile(2048 + np.arange(4), NB)
    com['pos'] = np.ascontiguousarray(np.broadcast_to(pos[None, :], (128, KVW)))
    w_in = f(inp['w_in_a'])
    win = np.empty((2, 8, 128, 2048), np.float32)
    for L in range(2):
        Wk = w_in[L].reshape(8, 128, 2048)
        for i in range(8):
            n, kind = i // 2, i % 2
            c0 = kind * 1024 + 256 * n
            blk = Wk[:, :, c0:c0 + 256].reshape(8, 128, 2, 128)
            win[L, i] = blk.transpose(1, 2, 0, 3).reshape(128, 2048)
    com['win'] = win
    wg = np.empty((2, 4, 128, 1024), np.float32)
    rw, iw = f(inp['gate_r_w']), f(inp['gate_i_w'])
    for L in range(2):
        for n in range(4):
            a = np.stack([rw[L, n], iw[L, n]], 0).reshape(2, 2, 128, 256)
            wg[L, n] = a.transpose(2, 0, 1, 3).reshape(128, 1024)
    com['wg'] = wg
    w_out = f(inp['w_out_a'])
    wout = np.empty((2, 4, 128, 2048), np.float32)
    for L in range(2):
        Wk = w_out[L].reshape(8, 128, 1024)
        for i in range(4):
            blk = Wk[:, :, 256 * i:256 * i + 256].reshape(8, 128, 2, 128)
            wout[L, i] = blk.transpose(1, 2, 0, 3).reshape(128, 2048)
    com['wout'] = wout
    w_up = f(inp['w_ffn_up'])
    wup = np.empty((4, 24, 128, 2048), np.float32)
    for L in range(4):
        Wk = w_up[L].reshape(8, 128, 6144)
        g = Wk[:, :, 0:3072].reshape(8, 128, 24, 128)
        v = Wk[:, :, 3072:6144].reshape(8, 128, 24, 128)
        gv = np.stack([g, v], 3)
        wup[L] = gv.transpose(2, 1, 0, 3, 4).reshape(24, 128, 2048)
    com['wup'] = wup
    w_dn = f(inp['w_ffn_down'])
    wdn = np.empty((4, 3, 128, 8192), np.float32)
    for L in range(4):
        a = w_dn[L].reshape(3, 8, 128, 1024)
        wdn[L] = a.transpose(0, 2, 1, 3).reshape(3, 128, 8192)
    com['wdn'] = wdn
    w_kv = f(inp['w_kv']).reshape(2, 4, 128, 1024)
    wkv = np.empty((4, 128, 2048), np.float32)
    for half_ in range(2):
        for kh in range(2):
            a = w_kv[kh][:, :, 512 * half_:512 * half_ + 512]
            wkv[2 * half_ + kh] = a.transpose(1, 0, 2).reshape(128, 2048)
    com['wkv'] = wkv
    w_q = f(inp['w_q'])
    wq = np.empty((2, 12, 128, 1024), np.float32)
    for j in range(2):
        Wk = w_q[j].reshape(8, 128, 1536)
        for hp in range(4):
            for g in range(3):
                m = 4 * g + hp
                wq[j, hp * 3 + g] = Wk[:, :, 128 * m:128 * m + 128].transpose(1, 0, 2).reshape(128, 1024)
    com['wq'] = wq
    w_o = f(inp['w_o'])
    wo = np.empty((2, 2, 128, 2048), np.float32)
    for j in range(2):
        Wk = w_o[j].reshape(4, 128, 1024)
        for i in range(2):
            wo[j, i] = Wk[:, :, 512 * i:512 * i + 512].transpose(1, 0, 2).reshape(128, 2048)
    com['wo'] = wo
    xp = f(inp['x_prompt'])
    xs = f(inp['x_sample'])
    ck_, cv_ = f(inp['cache_k']), f(inp['cache_v'])
    rows = [1920 + np.arange(128)]
    for s in range(4):
        rows.append(1536 + s + 4 * np.arange(128))
    for s in range(4):
        rows.append(s + 16 * np.arange(128))
    rows = np.stack(rows, 0)
    sh, sc, sf = f(inp['state_rglru_h']), f(inp['state_rglru_conv']), f(inp['state_ffn_conv'])
    maps = []
    posv = com.pop('pos')
    for c in range(8):
        m = dict(com)
        bq, hf = c // 2, c % 2
        if hf == 1:
            m['xT'] = np.ascontiguousarray(xp[bq].T)
            m['pos'] = posv
        else:
            xt_ = np.zeros((1024, TOK), np.float32)
            xt_[:, 2 * P:] = xp[bq, 0:2 * P].T
            m['xT'] = xt_
            pz = posv.copy()
            pz[:, 0:2 * P] = 0.0
            pz[:, 2 * P:TOK] = posv[:, 0:2 * P]
            m['pos'] = pz
        pc = par.copy()
        pc[:, _po['vh']] = float(hf)
        m['par'] = pc
        b0 = NB * c
        m['xsT'] = np.ascontiguousarray(xs[b0:b0 + NB].reshape(NS, 1024).T)
        kk = ck_[b0:b0 + NB][:, rows]
        kk = kk.reshape(NB, 9, 128, 4, 2, 64).transpose(0, 1, 4, 5, 3, 2)
        m['ck'] = np.ascontiguousarray(kk.reshape(NB, 9, 128, 512))
        m['cv'] = np.ascontiguousarray(cv_[b0:b0 + NB][:, rows].reshape(NB, 9, 128, 512))
        a = sh[:, b0:b0 + NB].reshape(2, NB, 8, 128)
        m['sh'] = np.ascontiguousarray(a.transpose(0, 3, 2, 1).reshape(2, 128, 128))
        a = sc[:, b0:b0 + NB].reshape(2, NB, 3, 8, 128)
        m['sc'] = np.ascontiguousarray(a.transpose(0, 4, 3, 1, 2).reshape(2, 128, 384))
        a = sf[:, b0:b0 + NB].reshape(4, NB, 2, 24, 128)
        m['sf'] = np.ascontiguousarray(a.transpose(0, 4, 3, 1, 2).reshape(4, 128, 768))
        maps.append(m)
    return maps


_NC_CACHE = {}


def kernel(**inputs):
    maps = _host_layout(inputs)
    if 'nc' not in _NC_CACHE:
        _NC_CACHE['nc'] = build_program()
    nc = _NC_CACHE['nc']
    res = run_bass_kernel_spmd(nc, maps, core_ids=list(range(8)))
    R = res.results
    y = np.stack([np.concatenate([R[2 * b]['yT'].T, R[2 * b + 1]['yT'].T], 0) for b in range(4)], 0)
    ys = np.concatenate([R[c]['ysT'].T.reshape(NB, 4, 1024) for c in range(8)], 0)
    p_h = np.stack([R[2 * b + 1]['oh'].reshape(128, 2, 8).transpose(1, 2, 0).reshape(2, 1024) for b in range(4)], 1)
    p_c = np.stack([R[2 * b + 1]['oc'].reshape(128, 2, 8, 3).transpose(1, 3, 2, 0).reshape(2, 3, 1024) for b in range(4)], 1)
    p_f = np.stack([R[2 * b + 1]['of'].reshape(128, 4, 24, 2).transpose(1, 3, 2, 0).reshape(4, 2, 3072) for b in range(4)], 1)

    def fm2tok(a, n):
        return a.reshape(4, 2, 64, n).transpose(3, 0, 1, 2).reshape(n, 8, 64)
    p_k = np.stack([fm2tok(R[2 * b + 1]['pkT'], 2048) for b in range(4)], 0)
    p_v = np.stack([fm2tok(R[2 * b + 1]['pvT'], 2048) for b in range(4)], 0)
    s_h = np.concatenate([R[c]['soh'].reshape(2, 128, 8, NB).transpose(0, 3, 2, 1).reshape(2, NB, 1024)
                          for c in range(8)], 1)
    s_c = np.concatenate([R[c]['soc'].reshape(2, 128, 8, NB, 3).transpose(0, 3, 4, 2, 1).reshape(2, NB, 3, 1024)
                          for c in range(8)], 1)
    s_f = np.concatenate([R[c]['sof'].reshape(4, 128, 24, NB, 2).transpose(0, 3, 4, 2, 1).reshape(4, NB, 2, 3072)
                          for c in range(8)], 1)
    s_k = np.concatenate([fm2tok(R[c]['skT'], NS).reshape(NB, 4, 8, 64) for c in range(8)], 0)
    s_v = np.concatenate([R[c]['svtok'].reshape(NB, 4, 8, 64) for c in range(8)], 0)
    outs = (y, ys, p_h, p_c, p_f, p_k, p_v, s_h, s_c, s_f, s_k, s_v)
    return tuple(np.ascontiguousarray(o, dtype=np.float32) for o in outs)
```

```python
import math
from contextlib import ExitStack
import numpy as np
import concourse.bass as bass
import concourse.mybir as mybir
from concourse.bass_utils import run_bass_kernel_spmd

F32 = mybir.dt.float32
BF16 = mybir.dt.bfloat16
I32 = mybir.dt.int32
AF = mybir.ActivationFunctionType
ALU = mybir.AluOpType

P = 1024
NT = 4
NS = 64
TOK = 4096
KVW = TOK + NS
HB = 1152
NHB = 33
EPS = 1e-6
NB = 16

_po = {}
_n = 0
for _name, _w in [('nma', 16), ('caw', 64), ('cab', 16), ('grb', 16), ('gib', 16), ('lam', 16), ('nkv', 8),
                  ('nmb', 16), ('nff', 32), ('fcw', 288), ('fcb', 96), ('gk', 1), ('gq', 2), ('inv', 1), ('cl', 16), ('vh', 1)]:
    _po[_name] = _n
    _n += _w
NPAR = _n
C_ID, C_MP, C_MC, C_BD, C_PM, C_ON, C_MS0, C_MN, C_M64 = 0, 128, 256, 384, 512, 640, 768, 772, 784
NCB = 912


class _Space:
    def __init__(self):
        self.segs = []

    def deps(self, lo, hi, is_write, add):
        for a, b, w, r in self.segs:
            if b <= lo or a >= hi:
                continue
            if w is not None:
                add(w)
            if is_write:
                for t in r.values():
                    add(t)

    def apply(self, lo, hi, is_write, tok):
        out = []
        covered = []
        for seg in self.segs:
            a, b, w, r = seg
            if b <= lo or a >= hi:
                out.append(seg)
                continue
            if a < lo:
                out.append([a, lo, w, dict(r)])
            if b > hi:
                out.append([hi, b, w, dict(r)])
            ia, ib = max(a, lo), min(b, hi)
            if not is_write:
                r2 = dict(r)
                r2[(tok[0], tok[1])] = tok
                out.append([ia, ib, w, r2])
                covered.append((ia, ib))
        if is_write:
            out.append([lo, hi, tok, {}])
        else:
            covered.sort()
            cur = lo
            for a, b in covered:
                if a > cur:
                    out.append([cur, a, None, {(tok[0], tok[1]): tok}])
                cur = max(cur, b)
            if cur < hi:
                out.append([cur, hi, None, {(tok[0], tok[1]): tok}])
        out.sort(key=lambda s: s[0])
        self.segs = out


_ESZ = {F32: 4, BF16: 2, I32: 4}


def _extent(ap):
    pat = list(ap.ap)
    es = _ESZ[ap.dtype]
    pstride = pat[0][0]
    off = ap.offset % pstride if pstride > 0 else ap.offset
    lo = off
    hi = off
    for st, cnt in pat[1:]:
        if cnt > 1:
            if st >= 0:
                hi += st * (cnt - 1)
            else:
                lo += st * (cnt - 1)
    return ap.tensor.name, lo * es, (hi + 1) * es


class Sched:
    ENGS = ('pe', 'act', 'dve', 'pool', 'sp')

    def __init__(self, nc, stack):
        self.nc = nc
        self.q = {e: [] for e in self.ENGS}
        self.cnt = {e: 0 for e in self.ENGS}
        self.sem = {e: stack.enter_context(nc.semaphore("s_" + e)) for e in self.ENGS}
        self.waited = {e: {} for e in self.ENGS}
        self.spaces = {}
        self.dpool = {}
        for qe in ('sp', 'pool'):
            self.dpool[qe] = [[stack.enter_context(nc.semaphore("d_%s%d" % (qe, i))), 0] for i in range(24)]
        self.dnext = {'sp': 0, 'pool': 0}
        self.dall = []
        self.psums = []
        self.psn = 0
        self.nops = 0

    def psum(self):
        p = self.psums[self.psn % len(self.psums)]
        self.psn += 1
        return p

    def _semh(self, tok):
        if tok[0] == 'e':
            return self.sem[tok[1]]
        return self.dall[tok[1]][0]

    def _wait(self, eng, tok):
        key = (tok[0], tok[1])
        if self.waited[eng].get(key, 0) >= tok[2]:
            return
        self.waited[eng][key] = tok[2]
        semh = self._semh(tok)
        val = tok[2]
        self.q[eng].append(lambda h, semh=semh, val=val: h.wait_ge(semh, val))

    def _collect(self, eng, reads, writes):
        toks = {}

        def add(t):
            k = (t[0], t[1])
            if k not in toks or toks[k][2] < t[2]:
                toks[k] = t
        acc = []
        for ap, isw in [(a, False) for a in reads] + [(a, True) for a in writes]:
            name, lo, hi = _extent(ap)
            sp = self.spaces.get(name)
            if sp is None:
                sp = self.spaces[name] = _Space()
            acc.append((sp, lo, hi, isw))
            sp.deps(lo, hi, isw or name.startswith('ps'), add)
        for t in toks.values():
            if t[0] == 'e' and t[1] == eng and eng == 'pe':
                continue
            self._wait(eng, t)
        return acc

    def _commit(self, acc, tok):
        for sp, lo, hi, isw in acc:
            if not isw:
                sp.apply(lo, hi, False, tok)
        for sp, lo, hi, isw in acc:
            if isw:
                sp.apply(lo, hi, True, tok)

    def do(self, eng, op):
        fn, reads, writes = op
        acc = self._collect(eng, reads, writes)
        self.cnt[eng] += 1
        idx = self.cnt[eng]
        sem = self.sem[eng]
        self.q[eng].append(lambda h, fn=fn, sem=sem: fn(h).then_inc(sem, 1))
        self._commit(acc, ('e', eng, idx))
        self.nops += 1

    def pe(self, ops):
        reads = []
        writes = []
        for fn, r, w in ops:
            reads += r
            writes += w
        acc = self._collect('pe', reads, writes)
        self.cnt['pe'] += 1
        idx = self.cnt['pe']
        sem = self.sem['pe']
        for fn, r, w in ops[:-1]:
            self.q['pe'].append(lambda h, fn=fn: fn(h))
        fn = ops[-1][0]
        self.q['pe'].append(lambda h, fn=fn, sem=sem: fn(h).then_inc(sem, 1))
        self._commit(acc, ('e', 'pe', idx))
        self.nops += len(ops)

    def dma(self, qe, out, in_, r=(), w=()):
        acc = self._collect(qe, list(r), list(w))
        pool = self.dpool[qe]
        k = self.dnext[qe] % len(pool)
        self.dnext[qe] += 1
        ent = pool[k]
        if len(ent) == 2:
            ent.append(len(self.dall))
            self.dall.append(ent)
        gidx = ent[2]
        if ent[1] > 0:
            self._wait(qe, ('d', gidx, ent[1]))
        ent[1] += 16
        semh = ent[0]
        if qe == 'pool':
            self.q[qe].append(lambda h, out=out, in_=in_, semh=semh:
                              h.dma_start(out=out, in_=in_, max_dma_last_dim=4096).then_inc(semh, 16))
        else:
            self.q[qe].append(lambda h, out=out, in_=in_, semh=semh: h.dma_start(out=out, in_=in_).then_inc(semh, 16))
        self._commit(acc, ('d', gidx, ent[1]))
        self.nops += 1

    def finish(self):
        for ent in self.dall:
            if ent[1] > 0:
                self._wait('sp', ('d', ent[2], ent[1]))
        for e in ('pe', 'act', 'dve', 'pool'):
            if self.cnt[e] > 0:
                self._wait('sp', ('e', e, self.cnt[e]))

    def emit(self, block):
        nc = self.nc
        m = {'pe': block.tensor, 'act': block.scalar, 'dve': block.vector, 'pool': block.gpsimd, 'sp': block.sync}
        for e in self.ENGS:
            lst = self.q[e]
            if not lst:
                continue

            def body(h, lst=lst):
                for f in lst:
                    f(h)
            m[e](body)


def _isap(x):
    return hasattr(x, 'ap') and hasattr(x, 'tensor')


def ACT(out, in_, func, bias=None, scale=None):
    kw = {}
    rd = [in_]
    if bias is not None:
        kw['bias'] = bias
        if _isap(bias):
            rd.append(bias)
    if scale is not None:
        kw['scale'] = scale
        if _isap(scale):
            rd.append(scale)
    return (lambda h: h.activation(out=out, in_=in_, func=func, **kw), rd, [out])


def TS(out, in0, s1, s2, op0, op1=None):
    rd = [in0] + [s for s in (s1, s2) if _isap(s)]
    if op1 is None:
        return (lambda h: h.tensor_scalar(out=out, in0=in0, scalar1=s1, scalar2=None, op0=op0), rd, [out])
    return (lambda h: h.tensor_scalar(out=out, in0=in0, scalar1=s1, scalar2=s2, op0=op0, op1=op1), rd, [out])


def TT(out, a, b, op):
    return (lambda h: h.tensor_tensor(out=out, in0=a, in1=b, op=op), [a, b], [out])


def STT(out, in0, sc, in1, op0, op1):
    rd = [in0, in1] + ([sc] if _isap(sc) else [])
    return (lambda h: h.scalar_tensor_tensor(out=out, in0=in0, scalar=sc, in1=in1, op0=op0, op1=op1), rd, [out])


def SCAN(out, d0, d1, init):
    rd = [d0, d1] + ([init] if _isap(init) else [])
    return (lambda h: h.tensor_tensor_scan(out, d0, d1, init, op0=ALU.mult, op1=ALU.add), rd, [out])


def COPY(out, in_):
    return (lambda h: h.tensor_copy(out, in_), [in_], [out])


def RECIP(out, in_):
    return (lambda h: h.reciprocal(out, in_), [in_], [out])


def MEMSET(out, v):
    return (lambda h: h.memset(out, v), [], [out])


def MM(out, lhsT, rhs, start, stop):
    return (lambda h: h.matmul(out, lhsT, rhs, start=start, stop=stop), [lhsT, rhs], [out])


def TR(out, in_, ident):
    return (lambda h: h.transpose(out, in_, ident), [in_, ident], [out])


HALO = 128


class Tile:
    def __init__(self, t, halo=False):
        self.t = t
        self.has_s = (t == NT - 1)
        self.halo = halo
        self.W = P + (NS if self.has_s else 0) + (HALO if halo else 0)
        self.subs = [(0, 512), (512, 512)] + ([(P, NS)] if self.has_s else []) + ([(P, HALO)] if halo else [])


def build_program():
    nc = bass.Bass("TRN2", target_bir_lowering=False)
    d = {}

    def din(name, shape):
        d[name] = nc.dram_tensor(name, list(shape), F32, kind="ExternalInput").ap()

    def dout(name, shape):
        d[name] = nc.dram_tensor(name, list(shape), F32, kind="ExternalOutput").ap()

    din('xT', [1024, TOK]); din('xsT', [1024, NS])
    din('ck', [NB, 9, 128, 512]); din('cv', [NB, 9, 128, 512])
    din('sh', [2, 128, 128]); din('sc', [2, 128, 384]); din('sf', [4, 128, 768])
    din('par', [128, NPAR]); din('cst', [128, NCB]); din('pos', [128, KVW])
    din('win', [2, 8, 128, 2048]); din('wg', [2, 4, 128, 1024]); din('wout', [2, 4, 128, 2048])
    din('wup', [4, 24, 128, 2048]); din('wdn', [4, 3, 128, 8192]); din('wkv', [4, 128, 2048])
    din('wq', [2, 12, 128, 1024]); din('wo', [2, 2, 128, 2048])
    dout('yT', [1024, 2 * P]); dout('ysT', [1024, NS])
    dout('oh', [128, 16]); dout('oc', [128, 48]); dout('of', [128, 192])
    dout('pkT', [512, 2048]); dout('pvT', [512, 2048])
    dout('soh', [2, 128, 128]); dout('soc', [2, 128, 384]); dout('sof', [4, 128, 768])
    dout('skT', [512, NS]); dout('svtok', [NS, 512])

    with ExitStack() as st:
        def sb(name, shape, dt):
            return st.enter_context(nc.sbuf_tensor(name, list(shape), dt))
        X = sb("X", [128, 8, P + HALO], F32)
        KT = sb("KT", [128, 4, KVW], BF16)
        VT = sb("VT", [128, 4, KVW], BF16)
        WS = sb("WS", [128, 4, 2048], BF16)
        AR = sb("AR", [128, NHB * HB], BF16)
        PAR = sb("PAR", [128, NPAR], F32)
        CB = sb("CB", [128, NCB], BF16)
        HST = sb("HST", [128, 2, 8], F32)
        CAH = sb("CAH", [128, 2, 8, 3], F32)
        FH = sb("FH", [128, 4, 24, 2], F32)
        SHL = sb("SHL", [128, 8, NB], F32)
        SCL = sb("SCL", [128, 8, NB, 3], F32)
        SFL = sb("SFL", [128, 24, NB, 2], F32)
        TMP16 = sb("TMP16", [128, NB], F32)
        VSNB = sb("VSNB", [NS, 512], BF16)
        VSNF = sb("VSNF", [NS, 512], F32)
        SMALL = sb("SMALL", [128, 2, 16], F32)
        PTS = sb("PTS", [128, 4, 128], BF16)
        VNB = sb("VNB", [4, 2, 512], BF16)
        S = Sched(nc, st)
        S.psums = [st.enter_context(nc.psum_tensor("ps%d" % i, [128, 512], F32)) for i in range(8)]

        def arb(i, n=HB):
            return AR[:, i * HB:i * HB + n]

        def arf(i, n=HB):
            return AR[:, i * HB:(i + 2) * HB].bitcast(F32)[:, 0:n]

        def XN(k):
            return arb(k)

        IDENT = CB[:, C_ID:C_ID + 128]
        MASKP = CB[:, C_MP:C_MP + 128]
        MASKC = CB[:, C_MC:C_MC + 128]
        BD = CB[:, C_BD:C_BD + 128]
        PM = CB[:, C_PM:C_PM + 128]
        ONES = CB[:, C_ON:C_ON + 128]

        def par(name, i=0):
            o = _po[name] + i
            return PAR[:, o:o + 1]

        wsn = [0]

        def wload(src, n=2048, parts=128):
            k = wsn[0] % 4
            wsn[0] += 1
            dst = WS[0:parts, k, 0:n]
            S.dma('pool', dst, src, w=[dst])
            return WS[:, k, :]

        S.dma('sp', PAR[:, :], d['par'], w=[PAR[:, :]])
        S.dma('pool', CB[:, :], d['cst'], w=[CB[:, :]])
        S.do('dve', MEMSET(HST[:, :, :], 0.0))
        S.do('dve', MEMSET(CAH[:, :, :, :], 0.0))
        S.do('dve', MEMSET(FH[:, :, :, :], 0.0))
        HV = PTS[:, 0, 64:128]
        HV2 = PTS[:, 1, 64:128]
        vhp = PAR[:, _po['vh']:_po['vh'] + 1]
        S.do('dve', TS(HV, ONES[:, 0:64], vhp, None, ALU.mult))
        S.do('dve', TS(HV2[0:64, :], ONES[0:64, 0:64], PAR[0:64, _po['vh']:_po['vh'] + 1], None, ALU.mult))
        S.do('dve', COPY(HV2[64:128, :], ONES[64:128, 0:64]))
        lam = PAR[:, _po['lam']:_po['lam'] + 16]
        clv = PAR[:, _po['cl']:_po['cl'] + 16]
        S.do('act', ACT(clv, lam, AF.Exp, scale=-1.0))
        S.do('act', ACT(clv, clv, AF.Ln, bias=1.0))
        S.do('dve', TS(clv, clv, -8.0, None, ALU.mult))
        S.do('dve', TS(lam, clv, 2.0, None, ALU.mult))

        def load_x(T):
            src = d['xT'].rearrange("(c p) t -> p c t", p=128)[:, :, P * T.t:P * T.t + P]
            S.dma('sp', X[:, :, 0:P], src, w=[X[:, :, 0:P]])
            if T.has_s:
                src = d['xsT'].rearrange("(c p) t -> p c t", p=128)
                S.dma('sp', X[:, :, P:P + NS], src, w=[X[:, :, P:P + NS]])

        def store_y(T):
            dst = d['yT'].rearrange("(c p) t -> p c t", p=128)[:, :, P * (T.t - 2):P * (T.t - 2) + P]
            S.dma('sp', dst, X[:, :, 0:P], r=[X[:, :, 0:P]])
            if T.has_s:
                dst = d['ysT'].rearrange("(c p) t -> p c t", p=128)
                S.dma('sp', dst, X[:, :, P:P + NS], r=[X[:, :, P:P + NS]])

        def rmsnorm(T, gname, gi):
            SQ = arb(32)
            RS = arf(30)
            W = T.W
            for (c0, n) in T.subs:
                ps = S.psum()
                for c in range(8):
                    sq = SQ[:, (c % 2) * 512:(c % 2) * 512 + n]
                    S.do('act', ACT(sq, X[:, c, c0:c0 + n], AF.Square))
                    S.pe([MM(ps[:, 0:n], ONES, sq, c == 0, c == 7)])
                S.do('act', ACT(RS[:, c0:c0 + n], ps[:, 0:n], AF.Ln, bias=EPS, scale=1.0 / 1024))
                S.do('act', ACT(RS[:, c0:c0 + n], RS[:, c0:c0 + n], AF.Exp, scale=-0.5))
            for c in range(8):
                S.do('dve', STT(XN(c)[:, 0:W], X[:, c, 0:W], par(gname, gi * 8 + c), RS[:, 0:W], ALU.mult, ALU.mult))

        def sview(ap64, s=4):
            return ap64.rearrange("p (b s) -> p b s", s=s)

        def a_mix(T, L):
            W = T.W
            subs = T.subs
            XBH = arf(16)
            XBHs = XBH[:, 1028:1028 + NB * 7].rearrange("p (b j) -> p b j", j=7)
            XCs = [arf(18), arf(20)]
            XCBs = [arb(22), arb(23)]
            RAs = [arf(24), arf(30)]
            IU = arf(26)
            TH = arf(28)

            def GG(c):
                return arb(8 + c)
            if T.has_s:
                S.dma('sp', SHL[:, :, :], d['sh'][L].rearrange("p (c b) -> p c b", c=8), w=[SHL[:, :, :]])
                S.dma('sp', SCL[:, :, :, :], d['sc'][L].rearrange("p (c b j) -> p c b j", c=8, b=NB),
                      w=[SCL[:, :, :, :]])
            for n in range(4):
                wxb = wload(d['win'][L, 2 * n + 1]).rearrange("p (c k q) -> p c k q", c=2, k=8)
                wrg = wload(d['wg'][L, n], n=1024)[:, 0:1024].rearrange("p (g k o) -> p g k o", g=2, k=2)
                for cc in range(2):
                    c = 2 * n + cc
                    XC = XCs[cc]
                    S.do('dve', COPY(XBH[:, 0:3], CAH[:, L, c, :]))
                    if T.has_s:
                        S.do('dve', COPY(XBHs[:, :, 0:3], SCL[:, c, :, :]))
                    for (c0, nn) in subs:
                        ps = S.psum()
                        S.pe([MM(ps[:, 0:nn], wxb[:, cc, k, :], XN(k)[:, c0:c0 + nn], k == 0, k == 7) for k in range(8)])
                        if c0 < P:
                            S.do('act', ACT(XBH[:, 3 + c0:3 + c0 + nn], ps[:, 0:nn], AF.Copy))
                        else:
                            S.do('act', ACT(XBHs[:, :, 3:7], sview(ps[:, 0:NS]), AF.Copy))
                    S.do('dve', COPY(CAH[:, L, c, :], XBH[:, P:P + 3]))
                    if T.has_s:
                        S.do('dve', COPY(SCL[:, c, :, :], XBHs[:, :, 4:7]))
                    cw = [par('caw', L * 32 + j * 8 + c) for j in range(4)]
                    cbias = par('cab', L * 8 + c)
                    S.do('dve', TS(XC[:, 0:P], XBH[:, 3:3 + P], cw[3], cbias, ALU.mult, ALU.add))
                    for j in (2, 1, 0):
                        S.do('dve', STT(XC[:, 0:P], XBH[:, j:j + P], cw[j], XC[:, 0:P], ALU.mult, ALU.add))
                    if T.has_s:
                        XCv = sview(XC[:, P:P + NS])
                        S.do('dve', TS(XCv, XBHs[:, :, 3:7], cw[3], cbias, ALU.mult, ALU.add))
                        for j in (2, 1, 0):
                            S.do('dve', STT(XCv, XBHs[:, :, j:j + 4], cw[j], XCv, ALU.mult, ALU.add))
                    S.do('act', ACT(XCBs[cc][:, 0:W], XC[:, 0:W], AF.Copy))
                wgt = wload(d['win'][L, 2 * n]).rearrange("p (c k q) -> p c k q", c=2, k=8)
                for cc in range(2):
                    c = 2 * n + cc
                    for (c0, nn) in subs:
                        ps = S.psum()
                        S.pe([MM(ps[:, 0:nn], wgt[:, cc, k, :], XN(k)[:, c0:c0 + nn], k == 0, k == 7) for k in range(8)])
                        S.do('act', ACT(GG(c)[:, c0:c0 + nn], ps[:, 0:nn], AF.Gelu_apprx_tanh))
                for cc in range(2):
                    c = 2 * n + cc
                    XC = XCs[cc]
                    RA = RAs[cc]
                    for gate in range(2):
                        dst = RA if gate == 0 else IU
                        gb = par('grb' if gate == 0 else 'gib', L * 8 + c)
                        for (c0, nn) in subs:
                            ps = S.psum()
                            S.pe([MM(ps[:, 0:nn], wrg[:, gate, k, cc * 128:(cc + 1) * 128], XCBs[k][:, c0:c0 + nn],
                                     k == 0, k == 1) for k in range(2)])
                            S.do('act', ACT(dst[:, c0:c0 + nn], ps[:, 0:nn], AF.Sigmoid, bias=gb))
                    S.do('act', ACT(TH[:, 0:W], RA[:, 0:W], AF.Exp, scale=par('lam', L * 8 + c)))
                    S.do('act', ACT(RA[:, 0:W], RA[:, 0:W], AF.Exp, scale=par('cl', L * 8 + c)))
                    S.do('dve', TT(IU[:, 0:W], IU[:, 0:W], XC[:, 0:W], ALU.mult))
                    S.do('act', ACT(TH[:, 0:W], TH[:, 0:W], AF.Ln, bias=1.0, scale=-1.0))
                    S.do('act', ACT(TH[:, 0:W], TH[:, 0:W], AF.Exp, scale=0.5))
                    S.do('dve', TT(IU[:, 0:W], IU[:, 0:W], TH[:, 0:W], ALU.mult))
                    if T.t < 2:
                        S.do('dve', TS(IU[:, 0:W], IU[:, 0:W], par('vh'), None, ALU.mult))
                    S.do('dve', SCAN(TH[:, 0:P], RA[:, 0:P], IU[:, 0:P], HST[:, L, c:c + 1]))
                    S.do('dve', COPY(HST[:, L, c:c + 1], TH[:, P - 1:P]))
                    if T.has_s:
                        As = sview(RA[:, P:P + NS])
                        Us = sview(IU[:, P:P + NS])
                        S.do('dve', TT(TMP16[:, :], As[:, :, 0], SHL[:, c, :], ALU.mult))
                        S.do('dve', TT(Us[:, :, 0], Us[:, :, 0], TMP16[:, :], ALU.add))
                        S.do('dve', MEMSET(As[:, :, 0], 0.0))
                        S.do('dve', SCAN(TH[:, P:P + NS], RA[:, P:P + NS], IU[:, P:P + NS], 0.0))
                        S.do('dve', COPY(SHL[:, c, :], sview(TH[:, P:P + NS])[:, :, 3]))
                    S.do('dve', TT(GG(c)[:, 0:W], TH[:, 0:W], GG(c)[:, 0:W], ALU.mult))
            if T.has_s:
                S.dma('sp', d['soh'][L].rearrange("p (c b) -> p c b", c=8), SHL[:, :, :], r=[SHL[:, :, :]])
                S.dma('sp', d['soc'][L].rearrange("p (c b j) -> p c b j", c=8, b=NB), SCL[:, :, :, :],
                      r=[SCL[:, :, :, :]])
            for i in range(4):
                wsl = wload(d['wout'][L, i]).rearrange("p (c k q) -> p c k q", c=2, k=8)
                for cc in range(2):
                    m = 2 * i + cc
                    for (c0, nn) in subs:
                        ps = S.psum()
                        S.pe([MM(ps[:, 0:nn], wsl[:, cc, k, :], GG(k)[:, c0:c0 + nn], k == 0, k == 7) for k in range(8)])
                        S.do('dve', TT(X[:, m, c0:c0 + nn], X[:, m, c0:c0 + nn], ps[:, 0:nn], ALU.add))

        def ffn(T, L):
            W = T.W
            subs = T.subs

            def ACTB(jj):
                return arb(8 + jj)
            WD = AR[:, 16 * HB:16 * HB + 8192].rearrange("p (j m) -> p j m", j=8)
            GH = arf(24)
            GHs = GH[:, 1028:1028 + NB * 6].rearrange("p (b j) -> p b j", j=6)
            GC = arf(26)
            VB = arb(28)
            if T.has_s:
                S.dma('sp', SFL[:, :, :, :], d['sf'][L].rearrange("p (j b s) -> p j b s", j=24, b=NB),
                      w=[SFL[:, :, :, :]])
            for G in range(3):
                for jj in range(8):
                    j = 8 * G + jj
                    wsl = wload(d['wup'][L, j]).rearrange("p (k q) -> p k q", k=8)
                    if jj == 2:
                        wdst = AR[:, 16 * HB:16 * HB + 8192].rearrange("p (a b) -> p a b", a=4)
                        S.dma('pool', wdst, d['wdn'][L, G].rearrange("p (a b) -> p a b", a=4), w=[wdst])
                    if not T.halo:
                        S.do('dve', COPY(GH[:, 0:2], FH[:, L, j, :]))
                    if T.has_s:
                        S.do('dve', COPY(GHs[:, :, 0:2], SFL[:, j, :, :]))
                    for (c0, nn) in subs:
                        psg = S.psum()
                        S.pe([MM(psg[:, 0:nn], wsl[:, k, 0:128], XN(k)[:, c0:c0 + nn], k == 0, k == 7) for k in range(8)])
                        psv = S.psum()
                        S.pe([MM(psv[:, 0:nn], wsl[:, k, 128:256], XN(k)[:, c0:c0 + nn], k == 0, k == 7) for k in range(8)])
                        if T.halo:
                            if c0 < P:
                                S.do('act', ACT(GH[:, HALO + c0:HALO + c0 + nn], psg[:, 0:nn], AF.Copy))
                            else:
                                S.do('act', ACT(GH[:, 0:HALO], psg[:, 0:nn], AF.Copy))
                        elif c0 < P:
                            S.do('act', ACT(GH[:, 2 + c0:2 + c0 + nn], psg[:, 0:nn], AF.Copy))
                        else:
                            S.do('act', ACT(GHs[:, :, 2:6], sview(psg[:, 0:NS]), AF.Copy))
                        S.do('act', ACT(VB[:, c0:c0 + nn], psv[:, 0:nn], AF.Copy))
                    if T.halo:
                        S.do('dve', COPY(FH[:, L, j, :], GH[:, HALO + P - 2:HALO + P]))
                    else:
                        S.do('dve', COPY(FH[:, L, j, :], GH[:, P:P + 2]))
                    if T.has_s:
                        S.do('dve', COPY(SFL[:, j, :, :], GHs[:, :, 4:6]))
                    fw = [par('fcw', L * 72 + tap * 24 + j) for tap in range(3)]
                    fb = par('fcb', L * 24 + j)
                    if T.halo:
                        S.do('dve', TS(GC[:, 0:P], GH[:, HALO:HALO + P], fw[2], fb, ALU.mult, ALU.add))
                        S.do('dve', STT(GC[:, 0:P], GH[:, HALO - 1:HALO - 1 + P], fw[1], GC[:, 0:P], ALU.mult, ALU.add))
                        S.do('dve', STT(GC[:, 0:P], GH[:, HALO - 2:HALO - 2 + P], fw[0], GC[:, 0:P], ALU.mult, ALU.add))
                        S.do('dve', MEMSET(GC[:, P:P + 2], 0.0))
                        gch = GC[:, P + 2:P + HALO]
                        S.do('dve', TS(gch, GH[:, 2:HALO], fw[2], fb, ALU.mult, ALU.add))
                        S.do('dve', STT(gch, GH[:, 1:HALO - 1], fw[1], gch, ALU.mult, ALU.add))
                        S.do('dve', STT(gch, GH[:, 0:HALO - 2], fw[0], gch, ALU.mult, ALU.add))
                    else:
                        S.do('dve', TS(GC[:, 0:P], GH[:, 2:2 + P], fw[2], fb, ALU.mult, ALU.add))
                        S.do('dve', STT(GC[:, 0:P], GH[:, 1:1 + P], fw[1], GC[:, 0:P], ALU.mult, ALU.add))
                        S.do('dve', STT(GC[:, 0:P], GH[:, 0:P], fw[0], GC[:, 0:P], ALU.mult, ALU.add))
                    if T.has_s:
                        GCv = sview(GC[:, P:P + NS])
                        S.do('dve', TS(GCv, GHs[:, :, 2:6], fw[2], fb, ALU.mult, ALU.add))
                        S.do('dve', STT(GCv, GHs[:, :, 1:5], fw[1], GCv, ALU.mult, ALU.add))
                        S.do('dve', STT(GCv, GHs[:, :, 0:4], fw[0], GCv, ALU.mult, ALU.add))
                    S.do('act', ACT(ACTB(jj)[:, 0:W], GC[:, 0:W], AF.Gelu_apprx_tanh))
                    S.do('dve', TT(ACTB(jj)[:, 0:W], ACTB(jj)[:, 0:W], VB[:, 0:W], ALU.mult))
                for m in range(8):
                    for (c0, nn) in subs:
                        ps = S.psum()
                        S.pe([MM(ps[:, 0:nn], WD[:, jj, 128 * m:128 * m + 128], ACTB(jj)[:, c0:c0 + nn], jj == 0, jj == 7)
                              for jj in range(8)])
                        S.do('dve', TT(X[:, m, c0:c0 + nn], X[:, m, c0:c0 + nn], ps[:, 0:nn], ALU.add))
            if T.has_s:
                S.dma('sp', d['sof'][L].rearrange("p (j b s) -> p j b s", j=24, b=NB), SFL[:, :, :, :],
                      r=[SFL[:, :, :, :]])

        def rope_tables(T):
            W = T.W
            ANG = arf(26)
            KI = AR[:, 28 * HB:30 * HB].bitcast(I32)[:, 0:HB]
            KF = arf(30)
            C = arf(22)
            Sn = arf(24)
            S.dma('sp', ANG[:, 0:P], d['pos'][:, P * T.t:P * T.t + P], w=[ANG[:, 0:P]])
            if T.has_s:
                S.dma('sp', ANG[:, P:P + NS], d['pos'][:, TOK:TOK + NS], w=[ANG[:, P:P + NS]])
            if T.halo:
                S.dma('sp', ANG[:, P:P + HALO], d['pos'][:, P * T.t - HALO:P * T.t], w=[ANG[:, P:P + HALO]])
            S.do('dve', TS(ANG[:, 0:W], ANG[:, 0:W], par('inv'), None, ALU.mult))
            S.do('dve', TS(KI[:, 0:W], ANG[:, 0:W], 1.0 / (2 * math.pi), None, ALU.mult))
            S.do('dve', COPY(KF[:, 0:W], KI[:, 0:W]))
            S.do('dve', STT(ANG[:, 0:W], KF[:, 0:W], -2.0 * math.pi, ANG[:, 0:W], ALU.mult, ALU.add))
            S2 = arf(28)
            S4 = arf(30)
            S.do('act', ACT(S2[:, 0:W], ANG[:, 0:W], AF.Sin, scale=0.5))
            S.do('act', ACT(S4[:, 0:W], ANG[:, 0:W], AF.Sin, scale=0.25))
            S.do('dve', TT(C[:, 0:W], S2[:, 0:W], S2[:, 0:W], ALU.mult))
            S.do('dve', TS(C[:, 0:W], C[:, 0:W], -2.0, 1.0, ALU.mult, ALU.add))
            S.do('dve', TT(S4[:, 0:W], S4[:, 0:W], S4[:, 0:W], ALU.mult))
            S.do('dve', TS(S4[:, 0:W], S4[:, 0:W], -4.0, 2.0, ALU.mult, ALU.add))
            S.do('dve', TT(Sn[:, 0:W], S2[:, 0:W], S4[:, 0:W], ALU.mult))
            return C, Sn

        def kv(T):
            W = T.W
            subs = T.subs
            t = T.t
            wk = [wload(d['wkv'][0]).rearrange("p (k q) -> p k q", k=4),
                  wload(d['wkv'][1]).rearrange("p (k q) -> p k q", k=4)]
            sets = [dict(KF=arf(16), RS=arf(18), KNb=arb(20), SQ=arb(21)),
                    dict(KF=arf(26), RS=arf(28), KNb=arb(8), SQ=arb(9))]
            C, Sn = None, None
            import os as _os
            _lv = int(_os.environ.get('K_KV', '99'))
            if _lv < 1:
                return
            C, Sn = rope_tables(T)
            if _lv < 2:
                return
            for m in range(4 if _lv >= 3 else 1):
                bs = sets[m % 2]
                KF, RS, KNb, SQ = bs['KF'], bs['RS'], bs['KNb'], bs['SQ']
                for (c0, nn) in subs:
                    ps = S.psum()
                    S.pe([MM(ps[:, 0:nn], wk[k // 4][:, k % 4, 128 * m:128 * m + 128], XN(k)[:, c0:c0 + nn], k == 0, k == 7)
                          for k in range(8)])
                    S.do('act', ACT(SQ[:, c0:c0 + nn], ps[:, 0:nn], AF.Square))
                    S.do('act', ACT(KF[:, c0:c0 + nn], ps[:, 0:nn], AF.Copy))
                    ps2 = S.psum()
                    S.pe([MM(ps2[:, 0:nn], BD, SQ[:, c0:c0 + nn], True, True)])
                    S.do('act', ACT(RS[:, c0:c0 + nn], ps2[:, 0:nn], AF.Ln, bias=EPS, scale=1.0 / 64))
                    S.do('act', ACT(RS[:, c0:c0 + nn], RS[:, c0:c0 + nn], AF.Exp, scale=-0.5))
                S.do('dve', STT(KF[:, 0:W], KF[:, 0:W], par('gk'), RS[:, 0:W], ALU.mult, ALU.mult))
                S.do('act', ACT(KNb[:, 0:W], KF[:, 0:W], AF.Copy))
                for (c0, nn) in subs:
                    ps3 = S.psum()
                    S.pe([MM(ps3[:, 0:nn], PM, KNb[:, c0:c0 + nn], True, True)])
                    S.do('dve', TT(RS[:, c0:c0 + nn], ps3[:, 0:nn], Sn[:, c0:c0 + nn], ALU.mult))
                S.do('dve', TT(KF[:, 0:W], KF[:, 0:W], C[:, 0:W], ALU.mult))
                S.do('dve', TT(KF[:, 0:W], KF[:, 0:W], RS[:, 0:W], ALU.add))
                S.do('act', ACT(KT[:, m, P * t:P * t + P], KF[:, 0:P], AF.Copy))
                if T.has_s:
                    S.do('act', ACT(KT[:, m, TOK:TOK + NS], KF[:, P:P + NS], AF.Copy))
                    S.dma('sp', d['skT'].rearrange("(m p) t -> p m t", p=128)[:, m, :], KF[:, P:P + NS],
                          r=[KF[:, P:P + NS]])
                if t >= 2:
                    dst = d['pkT'].rearrange("(m p) t -> p m t", p=128)[:, m, P * (t - 2):P * (t - 2) + P]
                    S.dma('sp', dst, KF[:, 0:P], r=[KF[:, 0:P]])
            if _lv < 4:
                return
            wv = [wload(d['wkv'][2]).rearrange("p (k q) -> p k q", k=4),
                  wload(d['wkv'][3]).rearrange("p (k q) -> p k q", k=4)]
            for m in range(4):
                VF = sets[m % 2]['KF']
                for (c0, nn) in subs[0:2]:
                    ps = S.psum()
                    S.pe([MM(ps[:, 0:nn], wv[k // 4][:, k % 4, 128 * m:128 * m + 128], XN(k)[:, c0:c0 + nn], k == 0, k == 7)
                          for k in range(8)])
                    _vv = int(_os.environ.get('K_KVV', '3'))
                    if _vv & 1:
                        S.do('act', ACT(VF[:, c0:c0 + nn], ps[:, 0:nn], AF.Copy))
                    if _vv & 2:
                        S.do('dve', COPY(VT[:, m, P * t + c0:P * t + c0 + nn], ps[:, 0:nn]))
                if t >= 2:
                    dst = d['pvT'].rearrange("(m p) t -> p m t", p=128)[:, m, P * (t - 2):P * (t - 2) + P]
                    S.dma('sp', dst, VF[:, 0:P], r=[VF[:, 0:P]])
            if T.has_s:
                ps = S.psum()
                S.pe([MM(ps[0:NS, 0:512], XN(k)[:, P:P + NS], wv[k // 4][:, k % 4, :], k == 0, k == 7) for k in range(8)])
                S.do('act', ACT(VSNF[:, :], ps[0:NS, 0:512], AF.Copy))
                S.do('dve', COPY(VSNB[:, :], ps[0:NS, 0:512]))
                S.dma('sp', d['svtok'], VSNF[:, :], r=[VSNF[:, :]])

        ptn = [0]
        vbn = [0]

        def attention_prompt(T, hp, QT, NUM, DEN):
            t = T.t
            PTbuf = arb(12)
            VBbuf = AR[:, 13 * HB:15 * HB]
            MPC = CB[:, C_MP:C_MP + 256]
            M64 = CB[:, C_M64:C_M64 + 128]
            blocks = []
            hq0 = P * t - HALO
            for bq in range(8):
                blocks.append((0, 1, 128, 128 * bq, P * t + 128 * bq))
            if T.halo:
                blocks.append((0, 1, 128, P, hq0))
            for bb in range(2):
                for r in range(4):
                    blocks.append((1, 4, 128, 512 * bb + r, P * t + 512 * bb + r))
            if T.halo:
                for r in range(4):
                    blocks.append((1, 4, 32, P + r, hq0 + r))
            for r in range(16):
                blocks.append((2, 16, 64, r, P * t + r))
            if T.halo:
                for r in range(16):
                    blocks.append((2, 16, 8, P + r, hq0 + r))
            ND = AR[:, 26 * HB:30 * HB].bitcast(F32).rearrange("p (a c) -> p a c", a=2)
            def stage_a(g, dd, QB, qc, q0):
                kbs = []
                if q0 >= 128 * dd:
                    kbs.append((q0 - 128 * dd, 128, MASKP))
                else:
                    pmin = -((q0 - 128 * dd) // dd)
                    if pmin < 128:
                        assert QB <= pmin
                        kbs.append((q0 - 128 * dd + pmin * dd, 128 - pmin, None))
                kbs.append((q0, QB, MASKC))
                nkb = len(kbs)
                std = (nkb == 2 and kbs[0][1] == 128 and QB in (64, 128))
                pi = ptn[0] % 2
                ptn[0] += 1
                PT = PTbuf[:, pi * 512:pi * 512 + 512].rearrange("p (e c) -> p e c", e=2)
                for e in range(2):
                    pss = S.psum()
                    S.pe([MM(pss[0:nk, i * QB:(i + 1) * QB],
                             KT[64 * e:64 * e + 64, hp, k0:k0 + dd * (nk - 1) + 1:dd],
                             QT[g][64 * e:64 * e + 64, qc:qc + dd * (QB - 1) + 1:dd], True, True)
                          for i, (k0, nk, mask) in enumerate(kbs)])
                    if std:
                        S.do('act', ACT(PT[:, e, 0:2 * QB], pss[:, 0:2 * QB], AF.Exp, scale=0.125))
                    else:
                        for i, (k0, nk, mask) in enumerate(kbs):
                            S.do('act', ACT(PT[0:nk, e, i * QB:(i + 1) * QB], pss[0:nk, i * QB:(i + 1) * QB],
                                            AF.Exp, scale=0.125))
                if std:
                    mc = MPC if QB == 128 else M64
                    S.do('dve', TT(PT[:, :, 0:2 * QB], PT[:, :, 0:2 * QB],
                                   mc[:, None, :].broadcast_to([128, 2, 2 * QB]), ALU.mult))
                else:
                    for i, (k0, nk, mask) in enumerate(kbs):
                        if mask is not None:
                            pv3 = PT[0:nk, :, i * QB:(i + 1) * QB]
                            S.do('dve', TT(pv3, pv3, mask[0:nk, None, 0:QB].broadcast_to([nk, 2, QB]), ALU.mult))
                pst = S.psum()
                pstb = pst[:, :].bitcast(BF16)
                S.pe([TR(pstb[0:nk, i * 128:(i + 1) * 128], VT[:, hp, k0:k0 + dd * (nk - 1) + 1:dd], IDENT)
                      for i, (k0, nk, mask) in enumerate(kbs)])
                vi = vbn[0] % 8
                vbn[0] += 1
                VB = VBbuf[:, vi * 256:vi * 256 + 256]
                if std:
                    S.do('act', ACT(VB[:, 0:256], pstb[:, 0:256], AF.Copy))
                else:
                    for i, (k0, nk, mask) in enumerate(kbs):
                        S.do('dve', COPY(VB[0:nk, i * 128:(i + 1) * 128], pstb[0:nk, i * 128:(i + 1) * 128]))
                return (g, dd, QB, qc, kbs, nkb, PT, VB)

            def stage_b(ctx):
                (g, dd, QB, qc, kbs, nkb, PT, VB) = ctx
                psod = S.psum()
                ops = []
                for e in range(2):
                    for i, (k0, nk, mask) in enumerate(kbs):
                        ops.append(MM(psod[64 * e:64 * e + 64, 0:QB], VB[0:nk, i * 128 + 64 * e:i * 128 + 64 * e + 64],
                                      PT[0:nk, e, i * QB:(i + 1) * QB], i == 0, i == nkb - 1))
                for e in range(2):
                    for i, (k0, nk, mask) in enumerate(kbs):
                        nh = min(nk, max(0, -((k0 - 2 * P) // dd)))
                        if nh == 0:
                            dl = ONES[0:nk, 0:64]
                        elif nh == nk:
                            dl = HV[0:nk, :]
                        else:
                            assert nh == 64 and nk == 128
                            dl = HV2[0:nk, :]
                        ops.append(MM(psod[64 * e:64 * e + 64, 128:128 + QB], dl,
                                      PT[0:nk, e, i * QB:(i + 1) * QB], i == 0, i == nkb - 1))
                S.pe(ops)
                ndv = ND[:, :, qc:qc + dd * (QB - 1) + 1:dd]
                src = psod[:, 0:256].rearrange("p (a c) -> p a c", a=2)[:, :, 0:QB]
                if g == 0:
                    S.do('act', ACT(ndv, src, AF.Copy))
                else:
                    S.do('dve', TT(ndv, ndv, src, ALU.add))

            prev = None
            for blk in blocks:
                ctx = stage_a(*blk)
                if prev is not None:
                    stage_b(prev)
                prev = ctx
            stage_b(prev)

        def attention_sample(T, QS, ATBs):
            KSb = AR[:, 16 * HB:16 * HB + 4 * 512].rearrange("p (r q) -> p r q", r=4)
            VSb = AR[:, 18 * HB:18 * HB + 8 * 512].rearrange("p (r q) -> p r q", r=8)
            MS0 = CB[:, C_MS0:C_MS0 + 4]
            MN = CB[0:4, C_MN:C_MN + 12]
            kn = [0]
            vn = [0]
            NDs = SMALL[:, :, :]
            NUMs = SMALL[:, 0, 0:16]
            DENs = SMALL[:, 1, 0:16]
            bstate = {}

            def stage_a(b, g):
                if g == 0:
                    vslot = b % 2
                    S.dma('sp', VNB[0:4, vslot, :], VSNB[4 * b:4 * b + 4, :], r=[VSNB[4 * b:4 * b + 4, :]],
                          w=[VNB[0:4, vslot, :]])
                    VN = VNB[0:4, vslot, :]
                    PTN = PTS[0:4, 3, 0:96]
                    for e in range(2):
                        psn_ = S.psum()
                        ops = []
                        for gg in range(3):
                            for hp in range(4):
                                ops.append(MM(psn_[0:4, gg * 16 + hp * 4:gg * 16 + hp * 4 + 4],
                                              KT[64 * e:64 * e + 64, hp, TOK + 4 * b:TOK + 4 * b + 4],
                                              QS[64 * e:64 * e + 64, gg * 4 + hp, 4 * b:4 * b + 4], True, True))
                        S.pe(ops)
                        S.do('act', ACT(PTN[:, e * 48:(e + 1) * 48], psn_[0:4, 0:48], AF.Exp, scale=0.125))
                        pn4 = PTN[:, e * 48:(e + 1) * 48].rearrange("p (g h s) -> p g h s", g=3, h=4)
                        mn4 = MN.rearrange("p (g s) -> p g s", g=3)[:, :, None, :].broadcast_to([4, 3, 4, 4])
                        S.do('dve', TT(pn4, pn4, mn4, ALU.mult))
                    pn = PTN.rearrange("p (e g c) -> p e g c", e=2, g=3)
                    PTNS = PTS[0:4, 3 - (b % 2), 96:128]
                    PTNS3 = PTNS.rearrange("p (e c) -> p e c", e=2)
                    S.do('dve', TT(PTNS3, pn[:, :, 0, :], pn[:, :, 1, :], ALU.add))
                    S.do('dve', TT(PTNS3, PTNS3, pn[:, :, 2, :], ALU.add))
                    bstate[b] = (VN, PTNS)
                nblk = 1 if g == 0 else 4
                ks = []
                vs = []
                for i in range(nblk):
                    blk = 0 if g == 0 else 1 + 4 * (g - 1) + i
                    ki = kn[0] % 4
                    kn[0] += 1
                    vi = vn[0] % 8
                    vn[0] += 1
                    S.dma('pool', KSb[:, ki, :], d['ck'][b, blk], w=[KSb[:, ki, :]])
                    S.dma('pool', VSb[:, vi, :], d['cv'][b, blk], w=[VSb[:, vi, :]])
                    ks.append(KSb[:, ki, :].rearrange("p (a k) -> p a k", a=4))
                    vs.append(VSb[:, vi, :])
                PT = PTS[:, g, 0:32]
                for e in range(2):
                    pss = S.psum()
                    ops = []
                    for hp in range(4):
                        if g == 0:
                            ops.append(MM(pss[:, hp * 4:hp * 4 + 4], ks[0][64 * e:64 * e + 64, hp, :],
                                          QS[64 * e:64 * e + 64, hp, 4 * b:4 * b + 4], True, True))
                        else:
                            for s_ in range(4):
                                ops.append(MM(pss[:, hp * 4 + s_:hp * 4 + s_ + 1], ks[s_][64 * e:64 * e + 64, hp, :],
                                              QS[64 * e:64 * e + 64, g * 4 + hp, 4 * b + s_:4 * b + s_ + 1], True, True))
                    S.pe(ops)
                    S.do('act', ACT(PT[:, e * 16:(e + 1) * 16], pss[:, 0:16], AF.Exp, scale=0.125))
                if g == 0:
                    p3 = PT.rearrange("p (h s) -> p h s", h=8)
                    S.do('dve', TT(p3, p3, MS0[:, None, :].broadcast_to([128, 8, 4]), ALU.mult))
                return (b, g, vs, PT)

            def stage_b(ctx):
                (b, g, vs, PT) = ctx
                (VN, PTNS) = bstate[b]
                last = (g == 2)
                psod = S.psum()
                ops = []
                for e in range(2):
                    for hp in range(4):
                        h = 2 * hp + e
                        for s_ in range(4):
                            col = hp * 4 + s_
                            vblk = vs[0] if g == 0 else vs[s_]
                            ops.append(MM(psod[64 * e:64 * e + 64, col:col + 1], vblk[:, h * 64:h * 64 + 64],
                                          PT[:, e * 16 + col:e * 16 + col + 1], True, not last))
                            if last:
                                ops.append(MM(psod[64 * e:64 * e + 64, col:col + 1], VN[:, h * 64:h * 64 + 64],
                                              PTNS[:, e * 16 + col:e * 16 + col + 1], False, True))
                for e in range(2):
                    ops.append(MM(psod[64 * e:64 * e + 64, 16:32], ONES[:, 0:64], PT[:, e * 16:(e + 1) * 16], True, not last))
                    if last:
                        ops.append(MM(psod[64 * e:64 * e + 64, 16:32], ONES[0:4, 0:64],
                                      PTNS[:, e * 16:(e + 1) * 16], False, True))
                S.pe(ops)
                src = psod[:, 0:32].rearrange("p (a c) -> p a c", a=2)
                if g == 0:
                    S.do('act', ACT(NDs, src, AF.Copy))
                else:
                    S.do('dve', TT(NDs, NDs, src, ALU.add))
                if last:
                    S.do('dve', RECIP(DENs, DENs))
                    S.do('dve', TT(ATBs[:, :, 4 * b:4 * b + 4], NUMs.rearrange("p (a c) -> p a c", a=4),
                                   DENs.rearrange("p (a c) -> p a c", a=4), ALU.mult))

            prev = None
            for b in range(NB):
                for g in range(3):
                    ctx = stage_a(b, g)
                    if prev is not None:
                        stage_b(prev)
                    prev = ctx
            stage_b(prev)

        def b_mix(T, jB):
            W = T.W
            subs = T.subs
            C, Sn = rope_tables(T)
            QN = arf(16)
            QNb = arb(18)
            QT = [arb(19), arb(20), arb(21)]
            RSq = arf(30)
            SQq = arb(32)
            NUM = arf(26)
            DEN = arf(28)
            QS = arb(15)[:, 0:12 * NS].rearrange("p (m c) -> p m c", m=12)

            def ATB(hp):
                return arb(8 + hp)
            for hp in range(4):
                for g in range(3):
                    wsl = wload(d['wq'][jB, hp * 3 + g], n=1024)[:, 0:1024].rearrange("p (k q) -> p k q", k=8)
                    for (c0, nn) in subs:
                        psq = S.psum()
                        S.pe([MM(psq[:, 0:nn], wsl[:, k, :], XN(k)[:, c0:c0 + nn], k == 0, k == 7) for k in range(8)])
                        S.do('act', ACT(SQq[:, c0:c0 + nn], psq[:, 0:nn], AF.Square))
                        ps2 = S.psum()
                        S.pe([MM(ps2[:, 0:nn], BD, SQq[:, c0:c0 + nn], True, True)])
                        S.do('act', ACT(RSq[:, c0:c0 + nn], ps2[:, 0:nn], AF.Ln, bias=EPS, scale=1.0 / 64))
                        S.do('act', ACT(RSq[:, c0:c0 + nn], RSq[:, c0:c0 + nn], AF.Exp, scale=-0.5))
                        S.do('dve', STT(QN[:, c0:c0 + nn], psq[:, 0:nn], par('gq', jB), RSq[:, c0:c0 + nn],
                                        ALU.mult, ALU.mult))
                    S.do('act', ACT(QNb[:, 0:W], QN[:, 0:W], AF.Copy))
                    for (c0, nn) in subs:
                        ps3 = S.psum()
                        S.pe([MM(ps3[:, 0:nn], PM, QNb[:, c0:c0 + nn], True, True)])
                        S.do('dve', TT(RSq[:, c0:c0 + nn], ps3[:, 0:nn], Sn[:, c0:c0 + nn], ALU.mult))
                    S.do('dve', TT(QN[:, 0:W], QN[:, 0:W], C[:, 0:W], ALU.mult))
                    S.do('dve', TT(QT[g][:, 0:W], QN[:, 0:W], RSq[:, 0:W], ALU.add))
                    if T.has_s:
                        S.do('act', ACT(QS[:, g * 4 + hp, :], QT[g][:, P:P + NS], AF.Copy))
                attention_prompt(T, hp, QT, NUM, DEN)
                Wa = P + (HALO if T.halo else 0)
                S.do('dve', TS(DEN[:, 0:Wa], DEN[:, 0:Wa], 1e-18, None, ALU.max))
                S.do('act', ACT(DEN[:, 0:Wa], DEN[:, 0:Wa], AF.Ln))
                S.do('act', ACT(DEN[:, 0:Wa], DEN[:, 0:Wa], AF.Exp, scale=-1.0))
                S.do('dve', TT(ATB(hp)[:, 0:Wa], NUM[:, 0:Wa], DEN[:, 0:Wa], ALU.mult))
            if T.has_s:
                ATBs = AR[:, 8 * HB:12 * HB].rearrange("p (h w) -> p h w", h=4)[:, :, P:P + NS]
                attention_sample(T, QS, ATBs)
            wos = [wload(d['wo'][jB, i]).rearrange("p (h q) -> p h q", h=4) for i in range(2)]
            for m in range(8):
                for (c0, nn) in subs:
                    ps = S.psum()
                    S.pe([MM(ps[:, 0:nn], wos[m // 4][:, hp, (m % 4) * 128:(m % 4) * 128 + 128], ATB(hp)[:, c0:c0 + nn],
                             hp == 0, hp == 3) for hp in range(4)])
                    S.do('dve', TT(X[:, m, c0:c0 + nn], X[:, m, c0:c0 + nn], ps[:, 0:nn], ALU.add))

        for t in range(NT):
            T = Tile(t)
            TB = Tile(t, halo=(t == 2))
            load_x(T)
            for L in range(2):
                rmsnorm(T, 'nma', L)
                a_mix(T, L)
                rmsnorm(T, 'nff', L)
                ffn(T, L)
            rmsnorm(T, 'nkv', 0)
            kv(T)
            if t == 1:
                S.do('act', ACT(X[:, :, P:P + HALO], X[:, :, P - HALO:P], AF.Copy))
            if t >= 2:
                for jB in range(2):
                    rmsnorm(TB, 'nmb', jB)
                    b_mix(TB, jB)
                    rmsnorm(TB, 'nff', 2 + jB)
                    ffn(TB, 2 + jB)
                store_y(T)
        S.dma('sp', d['oh'], HST[:, :, :].rearrange("p l c -> p (l c)"), r=[HST[:, :, :]])
        S.dma('sp', d['oc'], CAH[:, :, :, :].rearrange("p l c j -> p (l c j)"), r=[CAH[:, :, :, :]])
        S.dma('sp', d['of'], FH[:, :, :, :].rearrange("p l c j -> p (l c j)"), r=[FH[:, :, :, :]])
        S.finish()
        with nc.Block() as block:
            S.emit(block)
    return nc


def _chunk(v, nchunk):
    return np.ascontiguousarray(np.asarray(v, np.float32).reshape(nchunk, 128).T)


def _consts():
    cst = np.zeros((128, NCB), np.float32)
    p = np.arange(128)[:, None]
    f = np.arange(128)[None, :]
    cst[:, C_ID:C_ID + 128] = np.eye(128)
    cst[:, C_MP:C_MP + 128] = (f <= p)
    cst[:, C_MC:C_MC + 128] = (f >= p)
    cst[:, C_BD:C_BD + 128] = ((p // 64) == (f // 64))
    pm = np.zeros((128, 128), np.float32)
    for m in range(128):
        i = m % 64
        if i < 8:
            pm[m + 8, m] = -1.0
        elif i < 16:
            pm[m - 8, m] = 1.0
    cst[:, C_PM:C_PM + 128] = pm
    cst[:, C_ON:C_ON + 128] = 1.0
    s = np.arange(4)[None, :]
    cst[:, C_MS0:C_MS0 + 4] = (np.arange(128)[:, None] >= s)
    mn = np.zeros((4, 3, 4), np.float32)
    sp = np.arange(4)[:, None]
    mn[:, 0, :] = (sp <= s)
    mn[:, 1, :] = (sp == s)
    mn[:, 2, :] = (sp == s)
    cst[0:4, C_MN:C_MN + 12] = mn.reshape(4, 12)
    cst[:, C_M64:C_M64 + 64] = (f <= p)[:, 0:64]
    cst[:, C_M64 + 64:C_M64 + 128] = (f >= p)[:, 0:64]
    return cst


def _host_layout(inp):
    f = lambda a: np.asarray(a, np.float32)
    com = {}
    par = np.zeros((128, NPAR), np.float32)

    def put(name, off, arr):
        o = _po[name] + off
        par[:, o:o + arr.shape[1]] = arr
    for L in range(2):
        put('nma', 8 * L, _chunk(f(inp['norm_mix_a'])[L], 8))
        for j in range(4):
            put('caw', 32 * L + 8 * j, _chunk(f(inp['conv_a_w'])[L, j], 8))
        put('cab', 8 * L, _chunk(f(inp['conv_a_b'])[L], 8))
        put('grb', 8 * L, _chunk(f(inp['gate_r_b'])[L].reshape(-1), 8))
        put('gib', 8 * L, _chunk(f(inp['gate_i_b'])[L].reshape(-1), 8))
        put('lam', 8 * L, _chunk(f(inp['lru_lambda'])[L], 8))
        put('nmb', 8 * L, _chunk(f(inp['norm_mix_b'])[L], 8))
        par[:, _po['gq'] + L] = np.tile(f(inp['q_norm'])[L], 2)
    put('nkv', 0, _chunk(f(inp['norm_kv']), 8))
    for L in range(4):
        put('nff', 8 * L, _chunk(f(inp['norm_ffn'])[L], 8))
        for tap in range(3):
            put('fcw', 72 * L + 24 * tap, _chunk(f(inp['ffn_conv_w'])[L, tap], 24))
        put('fcb', 24 * L, _chunk(f(inp['ffn_conv_b'])[L], 24))
    par[:, _po['gk']] = np.tile(f(inp['k_norm']), 2)
    half = 8
    inv = (500000.0 ** (-np.arange(half, dtype=np.float32) * np.float32(2.0 / 16))).astype(np.float32)
    invp = np.zeros(128, np.float32)
    for pp in range(128):
        i = pp % 64
        if i < 16:
            invp[pp] = inv[i % 8]
    par[:, _po['inv']] = invp
    com['par'] = par
    com['cst'] = _consts()
    pos = np.zeros((KVW,), np.float32)
    pos[:TOK] = np.arange(TOK)
    pos[TOK:] = np.tile(2048 + np.arange(4), NB)
    com['pos'] = np.ascontiguousarray(np.broadcast_to(pos[None, :], (128, KVW)))
    w_in = f(inp['w_in_a'])
    win = np.empty((2, 8, 128, 2048), np.float32)
    for L in range(2):
        Wk = w_in[L].reshape(8, 128, 2048)
        for i in range(8):
            n, kind = i // 2, i % 2
            c0 = kind * 1024 + 256 * n
            blk = Wk[:, :, c0:c0 + 256].reshape(8, 128, 2, 128)
            win[L, i] = blk.transpose(1, 2, 0, 3).reshape(128, 2048)
    com['win'] = win
    wg = np.empty((2, 4, 128, 1024), np.float32)
    rw, iw = f(inp['gate_r_w']), f(inp['gate_i_w'])
    for L in range(2):
        for n in range(4):
            a = np.stack([rw[L, n], iw[L, n]], 0).reshape(2, 2, 128, 256)
            wg[L, n] = a.transpose(2, 0, 1, 3).reshape(128, 1024)
    com['wg'] = wg
    w_out = f(inp['w_out_a'])
    wout = np.empty((2, 4, 128, 2048), np.float32)
    for L in range(2):
        Wk = w_out[L].reshape(8, 128, 1024)
        for i in range(4):
            blk = Wk[:, :, 256 * i:256 * i + 256].reshape(8, 128, 2, 128)
            wout[L, i] = blk.transpose(1, 2, 0, 3).reshape(128, 2048)
    com['wout'] = wout
    w_up = f(inp['w_ffn_up'])
    wup = np.empty((4, 24, 128, 2048), np.float32)
    for L in range(4):
        Wk = w_up[L].reshape(8, 128, 6144)
        g = Wk[:, :, 0:3072].reshape(8, 128, 24, 128)
        v = Wk[:, :, 3072:6144].reshape(8, 128, 24, 128)
        gv = np.stack([g, v], 3)
        wup[L] = gv.transpose(2, 1, 0, 3, 4).reshape(24, 128, 2048)
    com['wup'] = wup
    w_dn = f(inp['w_ffn_down'])
    wdn = np.empty((4, 3, 128, 8192), np.float32)
    for L in range(4):
        a = w_dn[L].reshape(3, 8, 128, 1024)
        wdn[L] = a.transpose(0, 2, 1, 3).reshape(3, 128, 8192)
    com['wdn'] = wdn
    w_kv = f(inp['w_kv']).reshape(2, 4, 128, 1024)
    wkv = np.empty((4, 128, 2048), np.float32)
    for half_ in range(2):
        for kh in range(2):
            a = w_kv[kh][:, :, 512 * half_:512 * half_ + 512]
            wkv[2 * half_ + kh] = a.transpose(1, 0, 2).reshape(128, 2048)
    com['wkv'] = wkv
    w_q = f(inp['w_q'])
    wq = np.empty((2, 12, 128, 1024), np.float32)
    for j in range(2):
        Wk = w_q[j].reshape(8, 128, 1536)
        for hp in range(4):
            for g in range(3):
                m = 4 * g + hp
                wq[j, hp * 3 + g] = Wk[:, :, 128 * m:128 * m + 128].transpose(1, 0, 2).reshape(128, 1024)
    com['wq'] = wq
    w_o = f(inp['w_o'])
    wo = np.empty((2, 2, 128, 2048), np.float32)
    for j in range(2):
        Wk = w_o[j].reshape(4, 128, 1024)
        for i in range(2):
            wo[j, i] = Wk[:, :, 512 * i:512 * i + 512].transpose(1, 0, 2).reshape(128, 2048)
    com['wo'] = wo
    xp = f(inp['x_prompt'])
    xs = f(inp['x_sample'])
    ck_, cv_ = f(inp['cache_k']), f(inp['cache_v'])
    rows = [1920 + np.arange(128)]
    for s in range(4):
        rows.append(1536 + s + 4 * np.arange(128))
    for s in range(4):
        rows.append(s + 16 * np.arange(128))
    rows = np.stack(rows, 0)
    sh, sc, sf = f(inp['state_rglru_h']), f(inp['state_rglru_conv']), f(inp['state_ffn_conv'])
    maps = []
    posv = com.pop('pos')
    for c in range(8):
        m = dict(com)
        bq, hf = c // 2, c % 2
        if hf == 1:
            m['xT'] = np.ascontiguousarray(xp[bq].T)
            m['pos'] = posv
        else:
            xt_ = np.zeros((1024, TOK), np.float32)
            xt_[:, 2 * P:] = xp[bq, 0:2 * P].T
            m['xT'] = xt_
            pz = posv.copy()
            pz[:, 0:2 * P] = 0.0
            pz[:, 2 * P:TOK] = posv[:, 0:2 * P]
            m['pos'] = pz
        pc = par.copy()
        pc[:, _po['vh']] = float(hf)
        m['par'] = pc
        b0 = NB * c
        m['xsT'] = np.ascontiguousarray(xs[b0:b0 + NB].reshape(NS, 1024).T)
        kk = ck_[b0:b0 + NB][:, rows]
        kk = kk.reshape(NB, 9, 128, 4, 2, 64).transpose(0, 1, 4, 5, 3, 2)
        m['ck'] = np.ascontiguousarray(kk.reshape(NB, 9, 128, 512))
        m['cv'] = np.ascontiguousarray(cv_[b0:b0 + NB][:, rows].reshape(NB, 9, 128, 512))
        a = sh[:, b0:b0 + NB].reshape(2, NB, 8, 128)
        m['sh'] = np.ascontiguousarray(a.transpose(0, 3, 2, 1).reshape(2, 128, 128))
        a = sc[:, b0:b0 + NB].reshape(2, NB, 3, 8, 128)
        m['sc'] = np.ascontiguousarray(a.transpose(0, 4, 3, 1, 2).reshape(2, 128, 384))
        a = sf[:, b0:b0 + NB].reshape(4, NB, 2, 24, 128)
        m['sf'] = np.ascontiguousarray(a.transpose(0, 4, 3, 1, 2).reshape(4, 128, 768))
        maps.append(m)
    return maps


_NC_CACHE = {}


def kernel(**inputs):
    maps = _host_layout(inputs)
    if 'nc' not in _NC_CACHE:
        _NC_CACHE['nc'] = build_program()
    nc = _NC_CACHE['nc']
    res = run_bass_kernel_spmd(nc, maps, core_ids=list(range(8)))
    R = res.results
    y = np.stack([np.concatenate([R[2 * b]['yT'].T, R[2 * b + 1]['yT'].T], 0) for b in range(4)], 0)
    ys = np.concatenate([R[c]['ysT'].T.reshape(NB, 4, 1024) for c in range(8)], 0)
    p_h = np.stack([R[2 * b + 1]['oh'].reshape(128, 2, 8).transpose(1, 2, 0).reshape(2, 1024) for b in range(4)], 1)
    p_c = np.stack([R[2 * b + 1]['oc'].reshape(128, 2, 8, 3).transpose(1, 3, 2, 0).reshape(2, 3, 1024) for b in range(4)], 1)
    p_f = np.stack([R[2 * b + 1]['of'].reshape(128, 4, 24, 2).transpose(1, 3, 2, 0).reshape(4, 2, 3072) for b in range(4)], 1)

    def fm2tok(a, n):
        return a.reshape(4, 2, 64, n).transpose(3, 0, 1, 2).reshape(n, 8, 64)
    p_k = np.stack([fm2tok(R[2 * b + 1]['pkT'], 2048) for b in range(4)], 0)
    p_v = np.stack([fm2tok(R[2 * b + 1]['pvT'], 2048) for b in range(4)], 0)
    s_h = np.concatenate([R[c]['soh'].reshape(2, 128, 8, NB).transpose(0, 3, 2, 1).reshape(2, NB, 1024)
                          for c in range(8)], 1)
    s_c = np.concatenate([R[c]['soc'].reshape(2, 128, 8, NB, 3).transpose(0, 3, 4, 2, 1).reshape(2, NB, 3, 1024)
                          for c in range(8)], 1)
    s_f = np.concatenate([R[c]['sof'].reshape(4, 128, 24, NB, 2).transpose(0, 3, 4, 2, 1).reshape(4, NB, 2, 3072)
                          for c in range(8)], 1)
    s_k = np.concatenate([fm2tok(R[c]['skT'], NS).reshape(NB, 4, 8, 64) for c in range(8)], 0)
    s_v = np.concatenate([R[c]['svtok'].reshape(NB, 4, 8, 64) for c in range(8)], 0)
    outs = (y, ys, p_h, p_c, p_f, p_k, p_v, s_h, s_c, s_f, s_k, s_v)
    return tuple(np.ascontiguousarray(o, dtype=np.float32) for o in outs)
```

```python
import math
from contextlib import ExitStack
import numpy as np
import concourse.bass as bass
import concourse.mybir as mybir
from concourse.bass_utils import run_bass_kernel_spmd

F32 = mybir.dt.float32
BF16 = mybir.dt.bfloat16
I32 = mybir.dt.int32
AF = mybir.ActivationFunctionType
ALU = mybir.AluOpType

P = 1024
NT = 4
NS = 64
TOK = 4096
KVW = TOK + NS
HB = 1152
NHB = 33
EPS = 1e-6
NB = 16

_po = {}
_n = 0
for _name, _w in [('nma', 16), ('caw', 64), ('cab', 16), ('grb', 16), ('gib', 16), ('lam', 16), ('nkv', 8),
                  ('nmb', 16), ('nff', 32), ('fcw', 288), ('fcb', 96), ('gk', 1), ('gq', 2), ('inv', 1), ('cl', 16), ('vh', 1)]:
    _po[_name] = _n
    _n += _w
NPAR = _n
C_ID, C_MP, C_MC, C_BD, C_PM, C_ON, C_MS0, C_MN, C_M64 = 0, 128, 256, 384, 512, 640, 768, 772, 784
NCB = 912


class _Space:
    def __init__(self):
        self.segs = []

    def deps(self, lo, hi, is_write, add):
        for a, b, w, r in self.segs:
            if b <= lo or a >= hi:
                continue
            if w is not None:
                add(w)
            if is_write:
                for t in r.values():
                    add(t)

    def apply(self, lo, hi, is_write, tok):
        out = []
        covered = []
        for seg in self.segs:
            a, b, w, r = seg
            if b <= lo or a >= hi:
                out.append(seg)
                continue
            if a < lo:
                out.append([a, lo, w, dict(r)])
            if b > hi:
                out.append([hi, b, w, dict(r)])
            ia, ib = max(a, lo), min(b, hi)
            if not is_write:
                r2 = dict(r)
                r2[(tok[0], tok[1])] = tok
                out.append([ia, ib, w, r2])
                covered.append((ia, ib))
        if is_write:
            out.append([lo, hi, tok, {}])
        else:
            covered.sort()
            cur = lo
            for a, b in covered:
                if a > cur:
                    out.append([cur, a, None, {(tok[0], tok[1]): tok}])
                cur = max(cur, b)
            if cur < hi:
                out.append([cur, hi, None, {(tok[0], tok[1]): tok}])
        out.sort(key=lambda s: s[0])
        self.segs = out


_ESZ = {F32: 4, BF16: 2, I32: 4}


def _extent(ap):
    pat = list(ap.ap)
    es = _ESZ[ap.dtype]
    pstride = pat[0][0]
    off = ap.offset % pstride if pstride > 0 else ap.offset
    lo = off
    hi = off
    for st, cnt in pat[1:]:
        if cnt > 1:
            if st >= 0:
                hi += st * (cnt - 1)
            else:
                lo += st * (cnt - 1)
    return ap.tensor.name, lo * es, (hi + 1) * es


class Sched:
    ENGS = ('pe', 'act', 'dve', 'pool', 'sp')

    def __init__(self, nc, stack):
        self.nc = nc
        self.q = {e: [] for e in self.ENGS}
        self.cnt = {e: 0 for e in self.ENGS}
        self.sem = {e: stack.enter_context(nc.semaphore("s_" + e)) for e in self.ENGS}
        self.waited = {e: {} for e in self.ENGS}
        self.spaces = {}
        self.dpool = {}
        for qe in ('sp', 'pool'):
            self.dpool[qe] = [[stack.enter_context(nc.semaphore("d_%s%d" % (qe, i))), 0] for i in range(24)]
        self.dnext = {'sp': 0, 'pool': 0}
        self.dall = []
        self.psums = []
        self.psn = 0
        self.nops = 0

    def psum(self):
        p = self.psums[self.psn % len(self.psums)]
        self.psn += 1
        return p

    def _semh(self, tok):
        if tok[0] == 'e':
            return self.sem[tok[1]]
        return self.dall[tok[1]][0]

    def _wait(self, eng, tok):
        key = (tok[0], tok[1])
        if self.waited[eng].get(key, 0) >= tok[2]:
            return
        self.waited[eng][key] = tok[2]
        semh = self._semh(tok)
        val = tok[2]
        self.q[eng].append(lambda h, semh=semh, val=val: h.wait_ge(semh, val))

    def _collect(self, eng, reads, writes):
        toks = {}

        def add(t):
            k = (t[0], t[1])
            if k not in toks or toks[k][2] < t[2]:
                toks[k] = t
        acc = []
        for ap, isw in [(a, False) for a in reads] + [(a, True) for a in writes]:
            name, lo, hi = _extent(ap)
            sp = self.spaces.get(name)
            if sp is None:
                sp = self.spaces[name] = _Space()
            acc.append((sp, lo, hi, isw))
            sp.deps(lo, hi, isw or name.startswith('ps'), add)
        for t in toks.values():
            if t[0] == 'e' and t[1] == eng and eng == 'pe':
                continue
            self._wait(eng, t)
        return acc

    def _commit(self, acc, tok):
        for sp, lo, hi, isw in acc:
            if not isw:
                sp.apply(lo, hi, False, tok)
        for sp, lo, hi, isw in acc:
            if isw:
                sp.apply(lo, hi, True, tok)

    def do(self, eng, op):
        fn, reads, writes = op
        acc = self._collect(eng, reads, writes)
        self.cnt[eng] += 1
        idx = self.cnt[eng]
        sem = self.sem[eng]
        self.q[eng].append(lambda h, fn=fn, sem=sem: fn(h).then_inc(sem, 1))
        self._commit(acc, ('e', eng, idx))
        self.nops += 1

    def pe(self, ops):
        reads = []
        writes = []
        for fn, r, w in ops:
            reads += r
            writes += w
        acc = self._collect('pe', reads, writes)
        self.cnt['pe'] += 1
        idx = self.cnt['pe']
        sem = self.sem['pe']
        for fn, r, w in ops[:-1]:
            self.q['pe'].append(lambda h, fn=fn: fn(h))
        fn = ops[-1][0]
        self.q['pe'].append(lambda h, fn=fn, sem=sem: fn(h).then_inc(sem, 1))
        self._commit(acc, ('e', 'pe', idx))
        self.nops += len(ops)

    def dma(self, qe, out, in_, r=(), w=()):
        acc = self._collect(qe, list(r), list(w))
        pool = self.dpool[qe]
        k = self.dnext[qe] % len(pool)
        self.dnext[qe] += 1
        ent = pool[k]
        if len(ent) == 2:
            ent.append(len(self.dall))
            self.dall.append(ent)
        gidx = ent[2]
        if ent[1] > 0:
            self._wait(qe, ('d', gidx, ent[1]))
        ent[1] += 16
        semh = ent[0]
        if qe == 'pool':
            self.q[qe].append(lambda h, out=out, in_=in_, semh=semh:
                              h.dma_start(out=out, in_=in_, max_dma_last_dim=4096).then_inc(semh, 16))
        else:
            self.q[qe].append(lambda h, out=out, in_=in_, semh=semh: h.dma_start(out=out, in_=in_).then_inc(semh, 16))
        self._commit(acc, ('d', gidx, ent[1]))
        self.nops += 1

    def finish(self):
        for ent in self.dall:
            if ent[1] > 0:
                self._wait('sp', ('d', ent[2], ent[1]))
        for e in ('pe', 'act', 'dve', 'pool'):
            if self.cnt[e] > 0:
                self._wait('sp', ('e', e, self.cnt[e]))

    def emit(self, block):
        nc = self.nc
        m = {'pe': block.tensor, 'act': block.scalar, 'dve': block.vector, 'pool': block.gpsimd, 'sp': block.sync}
        for e in self.ENGS:
            lst = self.q[e]
            if not lst:
                continue

            def body(h, lst=lst):
                for f in lst:
                    f(h)
            m[e](body)


def _isap(x):
    return hasattr(x, 'ap') and hasattr(x, 'tensor')


def ACT(out, in_, func, bias=None, scale=None):
    kw = {}
    rd = [in_]
    if bias is not None:
        kw['bias'] = bias
        if _isap(bias):
            rd.append(bias)
    if scale is not None:
        kw['scale'] = scale
        if _isap(scale):
            rd.append(scale)
    return (lambda h: h.activation(out=out, in_=in_, func=func, **kw), rd, [out])


def TS(out, in0, s1, s2, op0, op1=None):
    rd = [in0] + [s for s in (s1, s2) if _isap(s)]
    if op1 is None:
        return (lambda h: h.tensor_scalar(out=out, in0=in0, scalar1=s1, scalar2=None, op0=op0), rd, [out])
    return (lambda h: h.tensor_scalar(out=out, in0=in0, scalar1=s1, scalar2=s2, op0=op0, op1=op1), rd, [out])


def TT(out, a, b, op):
    return (lambda h: h.tensor_tensor(out=out, in0=a, in1=b, op=op), [a, b], [out])


def STT(out, in0, sc, in1, op0, op1):
    rd = [in0, in1] + ([sc] if _isap(sc) else [])
    return (lambda h: h.scalar_tensor_tensor(out=out, in0=in0, scalar=sc, in1=in1, op0=op0, op1=op1), rd, [out])


def SCAN(out, d0, d1, init):
    rd = [d0, d1] + ([init] if _isap(init) else [])
    return (lambda h: h.tensor_tensor_scan(out, d0, d1, init, op0=ALU.mult, op1=ALU.add), rd, [out])


def COPY(out, in_):
    return (lambda h: h.tensor_copy(out, in_), [in_], [out])


def RECIP(out, in_):
    return (lambda h: h.reciprocal(out, in_), [in_], [out])


def MEMSET(out, v):
    return (lambda h: h.memset(out, v), [], [out])


def MM(out, lhsT, rhs, start, stop):
    return (lambda h: h.matmul(out, lhsT, rhs, start=start, stop=stop), [lhsT, rhs], [out])


def TR(out, in_, ident):
    return (lambda h: h.transpose(out, in_, ident), [in_, ident], [out])


HALO = 128


class Tile:
    def __init__(self, t, halo=False):
        self.t = t
        self.has_s = (t == NT - 1)
        self.halo = halo
        self.W = P + (NS if self.has_s else 0) + (HALO if halo else 0)
        self.subs = [(0, 512), (512, 512)] + ([(P, NS)] if self.has_s else []) + ([(P, HALO)] if halo else [])


def build_program():
    nc = bass.Bass("TRN2", target_bir_lowering=False)
    d = {}

    def din(name, shape):
        d[name] = nc.dram_tensor(name, list(shape), F32, kind="ExternalInput").ap()

    def dout(name, shape):
        d[name] = nc.dram_tensor(name, list(shape), F32, kind="ExternalOutput").ap()

    din('xT', [1024, TOK]); din('xsT', [1024, NS])
    din('ck', [NB, 9, 128, 512]); din('cv', [NB, 9, 128, 512])
    din('sh', [2, 128, 128]); din('sc', [2, 128, 384]); din('sf', [4, 128, 768])
    din('par', [128, NPAR]); din('cst', [128, NCB]); din('pos', [128, KVW])
    din('win', [2, 8, 128, 2048]); din('wg', [2, 4, 128, 1024]); din('wout', [2, 4, 128, 2048])
    din('wup', [4, 24, 128, 2048]); din('wdn', [4, 3, 128, 8192]); din('wkv', [4, 128, 2048])
    din('wq', [2, 12, 128, 1024]); din('wo', [2, 2, 128, 2048])
    dout('yT', [1024, 2 * P]); dout('ysT', [1024, NS])
    dout('oh', [128, 16]); dout('oc', [128, 48]); dout('of', [128, 192])
    dout('pkT', [512, 2048]); dout('pvT', [512, 2048])
    dout('soh', [2, 128, 128]); dout('soc', [2, 128, 384]); dout('sof', [4, 128, 768])
    dout('skT', [512, NS]); dout('svtok', [NS, 512])

    with ExitStack() as st:
        def sb(name, shape, dt):
            return st.enter_context(nc.sbuf_tensor(name, list(shape), dt))
        X = sb("X", [128, 8, P + HALO], F32)
        KT = sb("KT", [128, 4, KVW], BF16)
        VT = sb("VT", [128, 4, KVW], BF16)
        WS = sb("WS", [128, 4, 2048], BF16)
        AR = sb("AR", [128, NHB * HB], BF16)
        PAR = sb("PAR", [128, NPAR], F32)
        CB = sb("CB", [128, NCB], BF16)
        HST = sb("HST", [128, 2, 8], F32)
        CAH = sb("CAH", [128, 2, 8, 3], F32)
        FH = sb("FH", [128, 4, 24, 2], F32)
        SHL = sb("SHL", [128, 8, NB], F32)
        SCL = sb("SCL", [128, 8, NB, 3], F32)
        SFL = sb("SFL", [128, 24, NB, 2], F32)
        TMP16 = sb("TMP16", [128, NB], F32)
        VSNB = sb("VSNB", [NS, 512], BF16)
        VSNF = sb("VSNF", [NS, 512], F32)
        SMALL = sb("SMALL", [128, 2, 16], F32)
        PTS = sb("PTS", [128, 4, 128], BF16)
        VNB = sb("VNB", [4, 2, 512], BF16)
        S = Sched(nc, st)
        S.psums = [st.enter_context(nc.psum_tensor("ps%d" % i, [128, 512], F32)) for i in range(8)]

        def arb(i, n=HB):
            return AR[:, i * HB:i * HB + n]

        def arf(i, n=HB):
            return AR[:, i * HB:(i + 2) * HB].bitcast(F32)[:, 0:n]

        def XN(k):
            return arb(k)

        IDENT = CB[:, C_ID:C_ID + 128]
        MASKP = CB[:, C_MP:C_MP + 128]
        MASKC = CB[:, C_MC:C_MC + 128]
        BD = CB[:, C_BD:C_BD + 128]
        PM = CB[:, C_PM:C_PM + 128]
        ONES = CB[:, C_ON:C_ON + 128]

        def par(name, i=0):
            o = _po[name] + i
            return PAR[:, o:o + 1]

        wsn = [0]

        def wload(src, n=2048, parts=128):
            k = wsn[0] % 4
            wsn[0] += 1
            dst = WS[0:parts, k, 0:n]
            S.dma('pool', dst, src, w=[dst])
            return WS[:, k, :]

        S.dma('sp', PAR[:, :], d['par'], w=[PAR[:, :]])
        S.dma('pool', CB[:, :], d['cst'], w=[CB[:, :]])
        S.do('dve', MEMSET(HST[:, :, :], 0.0))
        S.do('dve', MEMSET(CAH[:, :, :, :], 0.0))
        S.do('dve', MEMSET(FH[:, :, :, :], 0.0))
        HV = PTS[:, 0, 64:128]
        HV2 = PTS[:, 1, 64:128]
        vhp = PAR[:, _po['vh']:_po['vh'] + 1]
        S.do('dve', TS(HV, ONES[:, 0:64], vhp, None, ALU.mult))
        S.do('dve', TS(HV2[0:64, :], ONES[0:64, 0:64], PAR[0:64, _po['vh']:_po['vh'] + 1], None, ALU.mult))
        S.do('dve', COPY(HV2[64:128, :], ONES[64:128, 0:64]))
        lam = PAR[:, _po['lam']:_po['lam'] + 16]
        clv = PAR[:, _po['cl']:_po['cl'] + 16]
        S.do('act', ACT(clv, lam, AF.Exp, scale=-1.0))
        S.do('act', ACT(clv, clv, AF.Ln, bias=1.0))
        S.do('dve', TS(clv, clv, -8.0, None, ALU.mult))
        S.do('dve', TS(lam, clv, 2.0, None, ALU.mult))

        def load_x(T):
            src = d['xT'].rearrange("(c p) t -> p c t", p=128)[:, :, P * T.t:P * T.t + P]
            S.dma('sp', X[:, :, 0:P], src, w=[X[:, :, 0:P]])
            if T.has_s:
                src = d['xsT'].rearrange("(c p) t -> p c t", p=128)
                S.dma('sp', X[:, :, P:P + NS], src, w=[X[:, :, P:P + NS]])

        def store_y(T):
            dst = d['yT'].rearrange("(c p) t -> p c t", p=128)[:, :, P * (T.t - 2):P * (T.t - 2) + P]
            S.dma('sp', dst, X[:, :, 0:P], r=[X[:, :, 0:P]])
            if T.has_s:
                dst = d['ysT'].rearrange("(c p) t -> p c t", p=128)
                S.dma('sp', dst, X[:, :, P:P + NS], r=[X[:, :, P:P + NS]])

        def rmsnorm(T, gname, gi):
            SQ = arb(32)
            RS = arf(30)
            W = T.W
            for (c0, n) in T.subs:
                ps = S.psum()
                for c in range(8):
                    sq = SQ[:, (c % 2) * 512:(c % 2) * 512 + n]
                    S.do('act', ACT(sq, X[:, c, c0:c0 + n], AF.Square))
                    S.pe([MM(ps[:, 0:n], ONES, sq, c == 0, c == 7)])
                S.do('act', ACT(RS[:, c0:c0 + n], ps[:, 0:n], AF.Ln, bias=EPS, scale=1.0 / 1024))
                S.do('act', ACT(RS[:, c0:c0 + n], RS[:, c0:c0 + n], AF.Exp, scale=-0.5))
            for c in range(8):
                S.do('dve', STT(XN(c)[:, 0:W], X[:, c, 0:W], par(gname, gi * 8 + c), RS[:, 0:W], ALU.mult, ALU.mult))

        def sview(ap64, s=4):
            return ap64.rearrange("p (b s) -> p b s", s=s)

        def a_mix(T, L):
            W = T.W
            subs = T.subs
            XBH = arf(16)
            XBHs = XBH[:, 1028:1028 + NB * 7].rearrange("p (b j) -> p b j", j=7)
            XCs = [arf(18), arf(20)]
            XCBs = [arb(22), arb(23)]
            RAs = [arf(24), arf(30)]
            IU = arf(26)
            TH = arf(28)

            def GG(c):
                return arb(8 + c)
            if T.has_s:
                S.dma('sp', SHL[:, :, :], d['sh'][L].rearrange("p (c b) -> p c b", c=8), w=[SHL[:, :, :]])
                S.dma('sp', SCL[:, :, :, :], d['sc'][L].rearrange("p (c b j) -> p c b j", c=8, b=NB),
                      w=[SCL[:, :, :, :]])
            for n in range(4):
                wxb = wload(d['win'][L, 2 * n + 1]).rearrange("p (c k q) -> p c k q", c=2, k=8)
                wrg = wload(d['wg'][L, n], n=1024)[:, 0:1024].rearrange("p (g k o) -> p g k o", g=2, k=2)
                for cc in range(2):
                    c = 2 * n + cc
                    XC = XCs[cc]
                    S.do('dve', COPY(XBH[:, 0:3], CAH[:, L, c, :]))
                    if T.has_s:
                        S.do('dve', COPY(XBHs[:, :, 0:3], SCL[:, c, :, :]))
                    for (c0, nn) in subs:
                        ps = S.psum()
                        S.pe([MM(ps[:, 0:nn], wxb[:, cc, k, :], XN(k)[:, c0:c0 + nn], k == 0, k == 7) for k in range(8)])
                        if c0 < P:
                            S.do('act', ACT(XBH[:, 3 + c0:3 + c0 + nn], ps[:, 0:nn], AF.Copy))
                        else:
                            S.do('act', ACT(XBHs[:, :, 3:7], sview(ps[:, 0:NS]), AF.Copy))
                    S.do('dve', COPY(CAH[:, L, c, :], XBH[:, P:P + 3]))
                    if T.has_s:
                        S.do('dve', COPY(SCL[:, c, :, :], XBHs[:, :, 4:7]))
                    cw = [par('caw', L * 32 + j * 8 + c) for j in range(4)]
                    cbias = par('cab', L * 8 + c)
                    S.do('dve', TS(XC[:, 0:P], XBH[:, 3:3 + P], cw[3], cbias, ALU.mult, ALU.add))
                    for j in (2, 1, 0):
                        S.do('dve', STT(XC[:, 0:P], XBH[:, j:j + P], cw[j], XC[:, 0:P], ALU.mult, ALU.add))
                    if T.has_s:
                        XCv = sview(XC[:, P:P + NS])
                        S.do('dve', TS(XCv, XBHs[:, :, 3:7], cw[3], cbias, ALU.mult, ALU.add))
                        for j in (2, 1, 0):
                            S.do('dve', STT(XCv, XBHs[:, :, j:j + 4], cw[j], XCv, ALU.mult, ALU.add))
                    S.do('act', ACT(XCBs[cc][:, 0:W], XC[:, 0:W], AF.Copy))
                wgt = wload(d['win'][L, 2 * n]).rearrange("p (c k q) -> p c k q", c=2, k=8)
                for cc in range(2):
                    c = 2 * n + cc
                    for (c0, nn) in subs:
                        ps = S.psum()
                        S.pe([MM(ps[:, 0:nn], wgt[:, cc, k, :], XN(k)[:, c0:c0 + nn], k == 0, k == 7) for k in range(8)])
                        S.do('act', ACT(GG(c)[:, c0:c0 + nn], ps[:, 0:nn], AF.Gelu_apprx_tanh))
                for cc in range(2):
                    c = 2 * n + cc
                    XC = XCs[cc]
                    RA = RAs[cc]
                    for gate in range(2):
                        dst = RA if gate == 0 else IU
                        gb = par('grb' if gate == 0 else 'gib', L * 8 + c)
                        for (c0, nn) in subs:
                            ps = S.psum()
                            S.pe([MM(ps[:, 0:nn], wrg[:, gate, k, cc * 128:(cc + 1) * 128], XCBs[k][:, c0:c0 + nn],
                                     k == 0, k == 1) for k in range(2)])
                            S.do('act', ACT(dst[:, c0:c0 + nn], ps[:, 0:nn], AF.Sigmoid, bias=gb))
                    S.do('act', ACT(TH[:, 0:W], RA[:, 0:W], AF.Exp, scale=par('lam', L * 8 + c)))
                    S.do('act', ACT(RA[:, 0:W], RA[:, 0:W], AF.Exp, scale=par('cl', L * 8 + c)))
                    S.do('dve', TT(IU[:, 0:W], IU[:, 0:W], XC[:, 0:W], ALU.mult))
                    S.do('act', ACT(TH[:, 0:W], TH[:, 0:W], AF.Ln, bias=1.0, scale=-1.0))
                    S.do('act', ACT(TH[:, 0:W], TH[:, 0:W], AF.Exp, scale=0.5))
                    S.do('dve', TT(IU[:, 0:W], IU[:, 0:W], TH[:, 0:W], ALU.mult))
                    if T.t < 2:
                        S.do('dve', TS(IU[:, 0:W], IU[:, 0:W], par('vh'), None, ALU.mult))
                    S.do('dve', SCAN(TH[:, 0:P], RA[:, 0:P], IU[:, 0:P], HST[:, L, c:c + 1]))
                    S.do('dve', COPY(HST[:, L, c:c + 1], TH[:, P - 1:P]))
                    if T.has_s:
                        As = sview(RA[:, P:P + NS])
                        Us = sview(IU[:, P:P + NS])
                        S.do('dve', TT(TMP16[:, :], As[:, :, 0], SHL[:, c, :], ALU.mult))
                        S.do('dve', TT(Us[:, :, 0], Us[:, :, 0], TMP16[:, :], ALU.add))
                        S.do('dve', MEMSET(As[:, :, 0], 0.0))
                        S.do('dve', SCAN(TH[:, P:P + NS], RA[:, P:P + NS], IU[:, P:P + NS], 0.0))
                        S.do('dve', COPY(SHL[:, c, :], sview(TH[:, P:P + NS])[:, :, 3]))
                    S.do('dve', TT(GG(c)[:, 0:W], TH[:, 0:W], GG(c)[:, 0:W], ALU.mult))
            if T.has_s:
                S.dma('sp', d['soh'][L].rearrange("p (c b) -> p c b", c=8), SHL[:, :, :], r=[SHL[:, :, :]])
                S.dma('sp', d['soc'][L].rearrange("p (c b j) -> p c b j", c=8, b=NB), SCL[:, :, :, :],
                      r=[SCL[:, :, :, :]])
            for i in range(4):
                wsl = wload(d['wout'][L, i]).rearrange("p (c k q) -> p c k q", c=2, k=8)
                for cc in range(2):
                    m = 2 * i + cc
                    for (c0, nn) in subs:
                        ps = S.psum()
                        S.pe([MM(ps[:, 0:nn], wsl[:, cc, k, :], GG(k)[:, c0:c0 + nn], k == 0, k == 7) for k in range(8)])
                        S.do('dve', TT(X[:, m, c0:c0 + nn], X[:, m, c0:c0 + nn], ps[:, 0:nn], ALU.add))

        def ffn(T, L):
            W = T.W
            subs = T.subs

            def ACTB(jj):
                return arb(8 + jj)
            WD = AR[:, 16 * HB:16 * HB + 8192].rearrange("p (j m) -> p j m", j=8)
            GH = arf(24)
            GHs = GH[:, 1028:1028 + NB * 6].rearrange("p (b j) -> p b j", j=6)
            GC = arf(26)
            VB = arb(28)
            if T.has_s:
                S.dma('sp', SFL[:, :, :, :], d['sf'][L].rearrange("p (j b s) -> p j b s", j=24, b=NB),
                      w=[SFL[:, :, :, :]])
            for G in range(3):
                for jj in range(8):
                    j = 8 * G + jj
                    wsl = wload(d['wup'][L, j]).rearrange("p (k q) -> p k q", k=8)
                    if jj == 2:
                        wdst = AR[:, 16 * HB:16 * HB + 8192].rearrange("p (a b) -> p a b", a=4)
                        S.dma('pool', wdst, d['wdn'][L, G].rearrange("p (a b) -> p a b", a=4), w=[wdst])
                    if not T.halo:
                        S.do('dve', COPY(GH[:, 0:2], FH[:, L, j, :]))
                    if T.has_s:
                        S.do('dve', COPY(GHs[:, :, 0:2], SFL[:, j, :, :]))
                    for (c0, nn) in subs:
                        psg = S.psum()
                        S.pe([MM(psg[:, 0:nn], wsl[:, k, 0:128], XN(k)[:, c0:c0 + nn], k == 0, k == 7) for k in range(8)])
                        psv = S.psum()
                        S.pe([MM(psv[:, 0:nn], wsl[:, k, 128:256], XN(k)[:, c0:c0 + nn], k == 0, k == 7) for k in range(8)])
                        if T.halo:
                            if c0 < P:
                                S.do('act', ACT(GH[:, HALO + c0:HALO + c0 + nn], psg[:, 0:nn], AF.Copy))
                            else:
                                S.do('act', ACT(GH[:, 0:HALO], psg[:, 0:nn], AF.Copy))
                        elif c0 < P:
                            S.do('act', ACT(GH[:, 2 + c0:2 + c0 + nn], psg[:, 0:nn], AF.Copy))
                        else:
                            S.do('act', ACT(GHs[:, :, 2:6], sview(psg[:, 0:NS]), AF.Copy))
                        S.do('act', ACT(VB[:, c0:c0 + nn], psv[:, 0:nn], AF.Copy))
                    if T.halo:
                        S.do('dve', COPY(FH[:, L, j, :], GH[:, HALO + P - 2:HALO + P]))
                    else:
                        S.do('dve', COPY(FH[:, L, j, :], GH[:, P:P + 2]))
                    if T.has_s:
                        S.do('dve', COPY(SFL[:, j, :, :], GHs[:, :, 4:6]))
                    fw = [par('fcw', L * 72 + tap * 24 + j) for tap in range(3)]
                    fb = par('fcb', L * 24 + j)
                    if T.halo:
                        S.do('dve', TS(GC[:, 0:P], GH[:, HALO:HALO + P], fw[2], fb, ALU.mult, ALU.add))
                        S.do('dve', STT(GC[:, 0:P], GH[:, HALO - 1:HALO - 1 + P], fw[1], GC[:, 0:P], ALU.mult, ALU.add))
                        S.do('dve', STT(GC[:, 0:P], GH[:, HALO - 2:HALO - 2 + P], fw[0], GC[:, 0:P], ALU.mult, ALU.add))
                        S.do('dve', MEMSET(GC[:, P:P + 2], 0.0))
                        gch = GC[:, P + 2:P + HALO]
                        S.do('dve', TS(gch, GH[:, 2:HALO], fw[2], fb, ALU.mult, ALU.add))
                        S.do('dve', STT(gch, GH[:, 1:HALO - 1], fw[1], gch, ALU.mult, ALU.add))
                        S.do('dve', STT(gch, GH[:, 0:HALO - 2], fw[0], gch, ALU.mult, ALU.add))
                    else:
                        S.do('dve', TS(GC[:, 0:P], GH[:, 2:2 + P], fw[2], fb, ALU.mult, ALU.add))
                        S.do('dve', STT(GC[:, 0:P], GH[:, 1:1 + P], fw[1], GC[:, 0:P], ALU.mult, ALU.add))
                        S.do('dve', STT(GC[:, 0:P], GH[:, 0:P], fw[0], GC[:, 0:P], ALU.mult, ALU.add))
                    if T.has_s:
                        GCv = sview(GC[:, P:P + NS])
                        S.do('dve', TS(GCv, GHs[:, :, 2:6], fw[2], fb, ALU.mult, ALU.add))
                        S.do('dve', STT(GCv, GHs[:, :, 1:5], fw[1], GCv, ALU.mult, ALU.add))
                        S.do('dve', STT(GCv, GHs[:, :, 0:4], fw[0], GCv, ALU.mult, ALU.add))
                    S.do('act', ACT(ACTB(jj)[:, 0:W], GC[:, 0:W], AF.Gelu_apprx_tanh))
                    S.do('dve', TT(ACTB(jj)[:, 0:W], ACTB(jj)[:, 0:W], VB[:, 0:W], ALU.mult))
                for m in range(8):
                    for (c0, nn) in subs:
                        ps = S.psum()
                        S.pe([MM(ps[:, 0:nn], WD[:, jj, 128 * m:128 * m + 128], ACTB(jj)[:, c0:c0 + nn], jj == 0, jj == 7)
                              for jj in range(8)])
                        S.do('dve', TT(X[:, m, c0:c0 + nn], X[:, m, c0:c0 + nn], ps[:, 0:nn], ALU.add))
            if T.has_s:
                S.dma('sp', d['sof'][L].rearrange("p (j b s) -> p j b s", j=24, b=NB), SFL[:, :, :, :],
                      r=[SFL[:, :, :, :]])

        def rope_tables(T):
            W = T.W
            ANG = arf(26)
            KI = AR[:, 28 * HB:30 * HB].bitcast(I32)[:, 0:HB]
            KF = arf(30)
            C = arf(22)
            Sn = arf(24)
            S.dma('sp', ANG[:, 0:P], d['pos'][:, P * T.t:P * T.t + P], w=[ANG[:, 0:P]])
            if T.has_s:
                S.dma('sp', ANG[:, P:P + NS], d['pos'][:, TOK:TOK + NS], w=[ANG[:, P:P + NS]])
            if T.halo:
                S.dma('sp', ANG[:, P:P + HALO], d['pos'][:, P * T.t - HALO:P * T.t], w=[ANG[:, P:P + HALO]])
            S.do('dve', TS(ANG[:, 0:W], ANG[:, 0:W], par('inv'), None, ALU.mult))
            S.do('dve', TS(KI[:, 0:W], ANG[:, 0:W], 1.0 / (2 * math.pi), None, ALU.mult))
            S.do('dve', COPY(KF[:, 0:W], KI[:, 0:W]))
            S.do('dve', STT(ANG[:, 0:W], KF[:, 0:W], -2.0 * math.pi, ANG[:, 0:W], ALU.mult, ALU.add))
            S2 = arf(28)
            S4 = arf(30)
            S.do('act', ACT(S2[:, 0:W], ANG[:, 0:W], AF.Sin, scale=0.5))
            S.do('act', ACT(S4[:, 0:W], ANG[:, 0:W], AF.Sin, scale=0.25))
            S.do('dve', TT(C[:, 0:W], S2[:, 0:W], S2[:, 0:W], ALU.mult))
            S.do('dve', TS(C[:, 0:W], C[:, 0:W], -2.0, 1.0, ALU.mult, ALU.add))
            S.do('dve', TT(S4[:, 0:W], S4[:, 0:W], S4[:, 0:W], ALU.mult))
            S.do('dve', TS(S4[:, 0:W], S4[:, 0:W], -4.0, 2.0, ALU.mult, ALU.add))
            S.do('dve', TT(Sn[:, 0:W], S2[:, 0:W], S4[:, 0:W], ALU.mult))
            return C, Sn

        def kv(T):
            W = T.W
            subs = T.subs
            t = T.t
            wk = [wload(d['wkv'][0]).rearrange("p (k q) -> p k q", k=4),
                  wload(d['wkv'][1]).rearrange("p (k q) -> p k q", k=4)]
            sets = [dict(KF=arf(16), RS=arf(18), KNb=arb(20), SQ=arb(21)),
                    dict(KF=arf(26), RS=arf(28), KNb=arb(8), SQ=arb(9))]
            C, Sn = None, None
            import os as _os
            _lv = int(_os.environ.get('K_KV', '99'))
            if _lv < 1:
                return
            C, Sn = rope_tables(T)
            if _lv < 2:
                return
            for m in range(4 if _lv >= 3 else 1):
                bs = sets[m % 2]
                KF, RS, KNb, SQ = bs['KF'], bs['RS'], bs['KNb'], bs['SQ']
                for (c0, nn) in subs:
                    ps = S.psum()
                    S.pe([MM(ps[:, 0:nn], wk[k // 4][:, k % 4, 128 * m:128 * m + 128], XN(k)[:, c0:c0 + nn], k == 0, k == 7)
                          for k in range(8)])
                    S.do('act', ACT(SQ[:, c0:c0 + nn], ps[:, 0:nn], AF.Square))
                    S.do('act', ACT(KF[:, c0:c0 + nn], ps[:, 0:nn], AF.Copy))
                    ps2 = S.psum()
                    S.pe([MM(ps2[:, 0:nn], BD, SQ[:, c0:c0 + nn], True, True)])
                    S.do('act', ACT(RS[:, c0:c0 + nn], ps2[:, 0:nn], AF.Ln, bias=EPS, scale=1.0 / 64))
                    S.do('act', ACT(RS[:, c0:c0 + nn], RS[:, c0:c0 + nn], AF.Exp, scale=-0.5))
                S.do('dve', STT(KF[:, 0:W], KF[:, 0:W], par('gk'), RS[:, 0:W], ALU.mult, ALU.mult))
                S.do('act', ACT(KNb[:, 0:W], KF[:, 0:W], AF.Copy))
                for (c0, nn) in subs:
                    ps3 = S.psum()
                    S.pe([MM(ps3[:, 0:nn], PM, KNb[:, c0:c0 + nn], True, True)])
                    S.do('dve', TT(RS[:, c0:c0 + nn], ps3[:, 0:nn], Sn[:, c0:c0 + nn], ALU.mult))
                S.do('dve', TT(KF[:, 0:W], KF[:, 0:W], C[:, 0:W], ALU.mult))
                S.do('dve', TT(KF[:, 0:W], KF[:, 0:W], RS[:, 0:W], ALU.add))
                S.do('act', ACT(KT[:, m, P * t:P * t + P], KF[:, 0:P], AF.Copy))
                if T.has_s:
                    S.do('act', ACT(KT[:, m, TOK:TOK + NS], KF[:, P:P + NS], AF.Copy))
                    S.dma('sp', d['skT'].rearrange("(m p) t -> p m t", p=128)[:, m, :], KF[:, P:P + NS],
                          r=[KF[:, P:P + NS]])
                if t >= 2:
                    dst = d['pkT'].rearrange("(m p) t -> p m t", p=128)[:, m, P * (t - 2):P * (t - 2) + P]
                    S.dma('sp', dst, KF[:, 0:P], r=[KF[:, 0:P]])
            if _lv < 4:
                return
            wv = [wload(d['wkv'][2]).rearrange("p (k q) -> p k q", k=4),
                  wload(d['wkv'][3]).rearrange("p (k q) -> p k q", k=4)]
            for m in range(4):
                VF = sets[m % 2]['KF']
                for (c0, nn) in subs[0:2]:
                    ps = S.psum()
                    S.pe([MM(ps[:, 0:nn], wv[k // 4][:, k % 4, 128 * m:128 * m + 128], XN(k)[:, c0:c0 + nn], k == 0, k == 7)
                          for k in range(8)])
                    _vv = int(_os.environ.get('K_KVV', '3'))
                    if _vv & 1:
                        S.do('act', ACT(VF[:, c0:c0 + nn], ps[:, 0:nn], AF.Copy))
                    if _vv & 2:
                        S.do('dve', COPY(VT[:, m, P * t + c0:P * t + c0 + nn], ps[:, 0:nn]))
                if t >= 2:
                    dst = d['pvT'].rearrange("(m p) t -> p m t", p=128)[:, m, P * (t - 2):P * (t - 2) + P]
                    S.dma('sp', dst, VF[:, 0:P], r=[VF[:, 0:P]])
            if T.has_s:
                ps = S.psum()
                S.pe([MM(ps[0:NS, 0:512], XN(k)[:, P:P + NS], wv[k // 4][:, k % 4, :], k == 0, k == 7) for k in range(8)])
                S.do('act', ACT(VSNF[:, :], ps[0:NS, 0:512], AF.Copy))
                S.do('dve', COPY(VSNB[:, :], ps[0:NS, 0:512]))
                S.dma('sp', d['svtok'], VSNF[:, :], r=[VSNF[:, :]])

        ptn = [0]
        vbn = [0]

        def attention_prompt(T, hp, QT, NUM, DEN):
            t = T.t
            PTbuf = arb(12)
            VBbuf = AR[:, 13 * HB:15 * HB]
            MPC = CB[:, C_MP:C_MP + 256]
            M64 = CB[:, C_M64:C_M64 + 128]
            blocks = []
            hq0 = P * t - HALO
            for bq in range(8):
                blocks.append((0, 1, 128, 128 * bq, P * t + 128 * bq))
            if T.halo:
                blocks.append((0, 1, 128, P, hq0))
            for bb in range(2):
                for r in range(4):
                    blocks.append((1, 4, 128, 512 * bb + r, P * t + 512 * bb + r))
            if T.halo:
                for r in range(4):
                    blocks.append((1, 4, 32, P + r, hq0 + r))
            for r in range(16):
                blocks.append((2, 16, 64, r, P * t + r))
            if T.halo:
                for r in range(16):
                    blocks.append((2, 16, 8, P + r, hq0 + r))
            ND = AR[:, 26 * HB:30 * HB].bitcast(F32).rearrange("p (a c) -> p a c", a=2)
            def stage_a(g, dd, QB, qc, q0):
                kbs = []
                if q0 >= 128 * dd:
                    kbs.append((q0 - 128 * dd, 128, MASKP))
                else:
                    pmin = -((q0 - 128 * dd) // dd)
                    if pmin < 128:
                        assert QB <= pmin
                        kbs.append((q0 - 128 * dd + pmin * dd, 128 - pmin, None))
                kbs.append((q0, QB, MASKC))
                nkb = len(kbs)
                std = (nkb == 2 and kbs[0][1] == 128 and QB in (64, 128))
                pi = ptn[0] % 2
                ptn[0] += 1
                PT = PTbuf[:, pi * 512:pi * 512 + 512].rearrange("p (e c) -> p e c", e=2)
                pss2 = [S.psum(), S.psum()]
                S.pe([MM(pss2[e][0:nk, i * QB:(i + 1) * QB],
                         KT[64 * e:64 * e + 64, hp, k0:k0 + dd * (nk - 1) + 1:dd],
                         QT[g][64 * e:64 * e + 64, qc:qc + dd * (QB - 1) + 1:dd], True, True)
                      for i, (k0, nk, mask) in enumerate(kbs) for e in range(2)])
                for e in range(2):
                    pss = pss2[e]
                    if std:
                        S.do('act', ACT(PT[:, e, 0:2 * QB], pss[:, 0:2 * QB], AF.Exp, scale=0.125))
                    else:
                        for i, (k0, nk, mask) in enumerate(kbs):
                            S.do('act', ACT(PT[0:nk, e, i * QB:(i + 1) * QB], pss[0:nk, i * QB:(i + 1) * QB],
                                            AF.Exp, scale=0.125))
                if std:
                    mc = MPC if QB == 128 else M64
                    S.do('dve', TT(PT[:, :, 0:2 * QB], PT[:, :, 0:2 * QB],
                                   mc[:, None, :].broadcast_to([128, 2, 2 * QB]), ALU.mult))
                else:
                    for i, (k0, nk, mask) in enumerate(kbs):
                        if mask is not None:
                            pv3 = PT[0:nk, :, i * QB:(i + 1) * QB]
                            S.do('dve', TT(pv3, pv3, mask[0:nk, None, 0:QB].broadcast_to([nk, 2, QB]), ALU.mult))
                pst = S.psum()
                pstb = pst[:, :].bitcast(BF16)
                S.pe([TR(pstb[0:nk, i * 128:(i + 1) * 128], VT[:, hp, k0:k0 + dd * (nk - 1) + 1:dd], IDENT)
                      for i, (k0, nk, mask) in enumerate(kbs)])
                vi = vbn[0] % 8
                vbn[0] += 1
                VB = VBbuf[:, vi * 256:vi * 256 + 256]
                if std:
                    S.do('act', ACT(VB[:, 0:256], pstb[:, 0:256], AF.Copy))
                else:
                    for i, (k0, nk, mask) in enumerate(kbs):
                        S.do('dve', COPY(VB[0:nk, i * 128:(i + 1) * 128], pstb[0:nk, i * 128:(i + 1) * 128]))
                return (g, dd, QB, qc, kbs, nkb, PT, VB)

            def stage_b(ctx):
                (g, dd, QB, qc, kbs, nkb, PT, VB) = ctx
                psod = S.psum()
                ops = []
                for i, (k0, nk, mask) in enumerate(kbs):
                    for e in range(2):
                        ops.append(MM(psod[64 * e:64 * e + 64, 0:QB], VB[0:nk, i * 128 + 64 * e:i * 128 + 64 * e + 64],
                                      PT[0:nk, e, i * QB:(i + 1) * QB], i == 0, i == nkb - 1))
                for i, (k0, nk, mask) in enumerate(kbs):
                    for e in range(2):
                        nh = min(nk, max(0, -((k0 - 2 * P) // dd)))
                        if nh == 0:
                            dl = ONES[0:nk, 0:64]
                        elif nh == nk:
                            dl = HV[0:nk, :]
                        else:
                            assert nh == 64 and nk == 128
                            dl = HV2[0:nk, :]
                        ops.append(MM(psod[64 * e:64 * e + 64, 128:128 + QB], dl,
                                      PT[0:nk, e, i * QB:(i + 1) * QB], i == 0, i == nkb - 1))
                S.pe(ops)
                ndv = ND[:, :, qc:qc + dd * (QB - 1) + 1:dd]
                src = psod[:, 0:256].rearrange("p (a c) -> p a c", a=2)[:, :, 0:QB]
                if g == 0:
                    S.do('act', ACT(ndv, src, AF.Copy))
                else:
                    S.do('dve', TT(ndv, ndv, src, ALU.add))

            prev = None
            for blk in blocks:
                ctx = stage_a(*blk)
                if prev is not None:
                    stage_b(prev)
                prev = ctx
            stage_b(prev)

        def attention_sample(T, QS, ATBs):
            KSb = AR[:, 16 * HB:16 * HB + 4 * 512].rearrange("p (r q) -> p r q", r=4)
            VSb = AR[:, 18 * HB:18 * HB + 8 * 512].rearrange("p (r q) -> p r q", r=8)
            MS0 = CB[:, C_MS0:C_MS0 + 4]
            MN = CB[0:4, C_MN:C_MN + 12]
            kn = [0]
            vn = [0]
            NDs = SMALL[:, :, :]
            NUMs = SMALL[:, 0, 0:16]
            DENs = SMALL[:, 1, 0:16]
            bstate = {}

            def stage_a(b, g):
                if g == 0:
                    vslot = b % 2
                    S.dma('sp', VNB[0:4, vslot, :], VSNB[4 * b:4 * b + 4, :], r=[VSNB[4 * b:4 * b + 4, :]],
                          w=[VNB[0:4, vslot, :]])
                    VN = VNB[0:4, vslot, :]
                    PTN = PTS[0:4, 3, 0:96]
                    psn2 = [S.psum(), S.psum()]
                    ops = []
                    for gg in range(3):
                        for hp in range(4):
                            for e in range(2):
                                ops.append(MM(psn2[e][0:4, gg * 16 + hp * 4:gg * 16 + hp * 4 + 4],
                                              KT[64 * e:64 * e + 64, hp, TOK + 4 * b:TOK + 4 * b + 4],
                                              QS[64 * e:64 * e + 64, gg * 4 + hp, 4 * b:4 * b + 4], True, True))
                    S.pe(ops)
                    for e in range(2):
                        psn_ = psn2[e]
                        S.do('act', ACT(PTN[:, e * 48:(e + 1) * 48], psn_[0:4, 0:48], AF.Exp, scale=0.125))
                        pn4 = PTN[:, e * 48:(e + 1) * 48].rearrange("p (g h s) -> p g h s", g=3, h=4)
                        mn4 = MN.rearrange("p (g s) -> p g s", g=3)[:, :, None, :].broadcast_to([4, 3, 4, 4])
                        S.do('dve', TT(pn4, pn4, mn4, ALU.mult))
                    pn = PTN.rearrange("p (e g c) -> p e g c", e=2, g=3)
                    PTNS = PTS[0:4, 3 - (b % 2), 96:128]
                    PTNS3 = PTNS.rearrange("p (e c) -> p e c", e=2)
                    S.do('dve', TT(PTNS3, pn[:, :, 0, :], pn[:, :, 1, :], ALU.add))
                    S.do('dve', TT(PTNS3, PTNS3, pn[:, :, 2, :], ALU.add))
                    bstate[b] = (VN, PTNS)
                nblk = 1 if g == 0 else 4
                ks = []
                vs = []
                for i in range(nblk):
                    blk = 0 if g == 0 else 1 + 4 * (g - 1) + i
                    ki = kn[0] % 4
                    kn[0] += 1
                    vi = vn[0] % 8
                    vn[0] += 1
                    S.dma('pool', KSb[:, ki, :], d['ck'][b, blk], w=[KSb[:, ki, :]])
                    S.dma('pool', VSb[:, vi, :], d['cv'][b, blk], w=[VSb[:, vi, :]])
                    ks.append(KSb[:, ki, :].rearrange("p (a k) -> p a k", a=4))
                    vs.append(VSb[:, vi, :])
                PT = PTS[:, g, 0:32]
                pss2 = [S.psum(), S.psum()]
                ops = []
                for hp in range(4):
                    if g == 0:
                        for e in range(2):
                            ops.append(MM(pss2[e][:, hp * 4:hp * 4 + 4], ks[0][64 * e:64 * e + 64, hp, :],
                                          QS[64 * e:64 * e + 64, hp, 4 * b:4 * b + 4], True, True))
                    else:
                        for s_ in range(4):
                            for e in range(2):
                                ops.append(MM(pss2[e][:, hp * 4 + s_:hp * 4 + s_ + 1], ks[s_][64 * e:64 * e + 64, hp, :],
                                              QS[64 * e:64 * e + 64, g * 4 + hp, 4 * b + s_:4 * b + s_ + 1], True, True))
                S.pe(ops)
                for e in range(2):
                    S.do('act', ACT(PT[:, e * 16:(e + 1) * 16], pss2[e][:, 0:16], AF.Exp, scale=0.125))
                if g == 0:
                    p3 = PT.rearrange("p (h s) -> p h s", h=8)
                    S.do('dve', TT(p3, p3, MS0[:, None, :].broadcast_to([128, 8, 4]), ALU.mult))
                return (b, g, vs, PT)

            def stage_b(ctx):
                (b, g, vs, PT) = ctx
                (VN, PTNS) = bstate[b]
                last = (g == 2)
                psod = S.psum()
                ops = []
                for hp in range(4):
                    for s_ in range(4):
                        for e in range(2):
                            h = 2 * hp + e
                            col = hp * 4 + s_
                            vblk = vs[0] if g == 0 else vs[s_]
                            ops.append(MM(psod[64 * e:64 * e + 64, col:col + 1], vblk[:, h * 64:h * 64 + 64],
                                          PT[:, e * 16 + col:e * 16 + col + 1], True, not last))
                            if last:
                                ops.append(MM(psod[64 * e:64 * e + 64, col:col + 1], VN[:, h * 64:h * 64 + 64],
                                              PTNS[:, e * 16 + col:e * 16 + col + 1], False, True))
                for e in range(2):
                    ops.append(MM(psod[64 * e:64 * e + 64, 16:32], ONES[:, 0:64], PT[:, e * 16:(e + 1) * 16], True, not last))
                    if last:
                        ops.append(MM(psod[64 * e:64 * e + 64, 16:32], ONES[0:4, 0:64],
                                      PTNS[:, e * 16:(e + 1) * 16], False, True))
                S.pe(ops)
                src = psod[:, 0:32].rearrange("p (a c) -> p a c", a=2)
                if g == 0:
                    S.do('act', ACT(NDs, src, AF.Copy))
                else:
                    S.do('dve', TT(NDs, NDs, src, ALU.add))
                if last:
                    S.do('dve', RECIP(DENs, DENs))
                    S.do('dve', TT(ATBs[:, :, 4 * b:4 * b + 4], NUMs.rearrange("p (a c) -> p a c", a=4),
                                   DENs.rearrange("p (a c) -> p a c", a=4), ALU.mult))

            prev = None
            for b in range(NB):
                for g in range(3):
                    ctx = stage_a(b, g)
                    if prev is not None:
                        stage_b(prev)
                    prev = ctx
            stage_b(prev)

        def b_mix(T, jB):
            W = T.W
            subs = T.subs
            C, Sn = rope_tables(T)
            QN = arf(16)
            QNb = arb(18)
            QT = [arb(19), arb(20), arb(21)]
            RSq = arf(30)
            SQq = arb(32)
            NUM = arf(26)
            DEN = arf(28)
            QS = arb(15)[:, 0:12 * NS].rearrange("p (m c) -> p m c", m=12)

            def ATB(hp):
                return arb(8 + hp)
            for hp in range(4):
                for g in range(3):
                    wsl = wload(d['wq'][jB, hp * 3 + g], n=1024)[:, 0:1024].rearrange("p (k q) -> p k q", k=8)
                    for (c0, nn) in subs:
                        psq = S.psum()
                        S.pe([MM(psq[:, 0:nn], wsl[:, k, :], XN(k)[:, c0:c0 + nn], k == 0, k == 7) for k in range(8)])
                        S.do('act', ACT(SQq[:, c0:c0 + nn], psq[:, 0:nn], AF.Square))
                        ps2 = S.psum()
                        S.pe([MM(ps2[:, 0:nn], BD, SQq[:, c0:c0 + nn], True, True)])
                        S.do('act', ACT(RSq[:, c0:c0 + nn], ps2[:, 0:nn], AF.Ln, bias=EPS, scale=1.0 / 64))
                        S.do('act', ACT(RSq[:, c0:c0 + nn], RSq[:, c0:c0 + nn], AF.Exp, scale=-0.5))
                        S.do('dve', STT(QN[:, c0:c0 + nn], psq[:, 0:nn], par('gq', jB), RSq[:, c0:c0 + nn],
                                        ALU.mult, ALU.mult))
                    S.do('act', ACT(QNb[:, 0:W], QN[:, 0:W], AF.Copy))
                    for (c0, nn) in subs:
                        ps3 = S.psum()
                        S.pe([MM(ps3[:, 0:nn], PM, QNb[:, c0:c0 + nn], True, True)])
                        S.do('dve', TT(RSq[:, c0:c0 + nn], ps3[:, 0:nn], Sn[:, c0:c0 + nn], ALU.mult))
                    S.do('dve', TT(QN[:, 0:W], QN[:, 0:W], C[:, 0:W], ALU.mult))
                    S.do('dve', TT(QT[g][:, 0:W], QN[:, 0:W], RSq[:, 0:W], ALU.add))
                    if T.has_s:
                        S.do('act', ACT(QS[:, g * 4 + hp, :], QT[g][:, P:P + NS], AF.Copy))
                attention_prompt(T, hp, QT, NUM, DEN)
                Wa = P + (HALO if T.halo else 0)
                S.do('dve', TS(DEN[:, 0:Wa], DEN[:, 0:Wa], 1e-18, None, ALU.max))
                S.do('act', ACT(DEN[:, 0:Wa], DEN[:, 0:Wa], AF.Ln))
                S.do('act', ACT(DEN[:, 0:Wa], DEN[:, 0:Wa], AF.Exp, scale=-1.0))
                S.do('dve', TT(ATB(hp)[:, 0:Wa], NUM[:, 0:Wa], DEN[:, 0:Wa], ALU.mult))
            if T.has_s:
                ATBs = AR[:, 8 * HB:12 * HB].rearrange("p (h w) -> p h w", h=4)[:, :, P:P + NS]
                attention_sample(T, QS, ATBs)
            wos = [wload(d['wo'][jB, i]).rearrange("p (h q) -> p h q", h=4) for i in range(2)]
            for m in range(8):
                for (c0, nn) in subs:
                    ps = S.psum()
                    S.pe([MM(ps[:, 0:nn], wos[m // 4][:, hp, (m % 4) * 128:(m % 4) * 128 + 128], ATB(hp)[:, c0:c0 + nn],
                             hp == 0, hp == 3) for hp in range(4)])
                    S.do('dve', TT(X[:, m, c0:c0 + nn], X[:, m, c0:c0 + nn], ps[:, 0:nn], ALU.add))

        for t in range(NT):
            T = Tile(t)
            TB = Tile(t, halo=(t == 2))
            load_x(T)
            for L in range(2):
                rmsnorm(T, 'nma', L)
                a_mix(T, L)
                rmsnorm(T, 'nff', L)
                ffn(T, L)
            rmsnorm(T, 'nkv', 0)
            kv(T)
            if t == 1:
                S.do('act', ACT(X[:, :, P:P + HALO], X[:, :, P - HALO:P], AF.Copy))
            if t >= 2:
                for jB in range(2):
                    rmsnorm(TB, 'nmb', jB)
                    b_mix(TB, jB)
                    rmsnorm(TB, 'nff', 2 + jB)
                    ffn(TB, 2 + jB)
                store_y(T)
        S.dma('sp', d['oh'], HST[:, :, :].rearrange("p l c -> p (l c)"), r=[HST[:, :, :]])
        S.dma('sp', d['oc'], CAH[:, :, :, :].rearrange("p l c j -> p (l c j)"), r=[CAH[:, :, :, :]])
        S.dma('sp', d['of'], FH[:, :, :, :].rearrange("p l c j -> p (l c j)"), r=[FH[:, :, :, :]])
        S.finish()
        with nc.Block() as block:
            S.emit(block)
    return nc


def _chunk(v, nchunk):
    return np.ascontiguousarray(np.asarray(v, np.float32).reshape(nchunk, 128).T)


def _consts():
    cst = np.zeros((128, NCB), np.float32)
    p = np.arange(128)[:, None]
    f = np.arange(128)[None, :]
    cst[:, C_ID:C_ID + 128] = np.eye(128)
    cst[:, C_MP:C_MP + 128] = (f <= p)
    cst[:, C_MC:C_MC + 128] = (f >= p)
    cst[:, C_BD:C_BD + 128] = ((p // 64) == (f // 64))
    pm = np.zeros((128, 128), np.float32)
    for m in range(128):
        i = m % 64
        if i < 8:
            pm[m + 8, m] = -1.0
        elif i < 16:
            pm[m - 8, m] = 1.0
    cst[:, C_PM:C_PM + 128] = pm
    cst[:, C_ON:C_ON + 128] = 1.0
    s = np.arange(4)[None, :]
    cst[:, C_MS0:C_MS0 + 4] = (np.arange(128)[:, None] >= s)
    mn = np.zeros((4, 3, 4), np.float32)
    sp = np.arange(4)[:, None]
    mn[:, 0, :] = (sp <= s)
    mn[:, 1, :] = (sp == s)
    mn[:, 2, :] = (sp == s)
    cst[0:4, C_MN:C_MN + 12] = mn.reshape(4, 12)
    cst[:, C_M64:C_M64 + 64] = (f <= p)[:, 0:64]
    cst[:, C_M64 + 64:C_M64 + 128] = (f >= p)[:, 0:64]
    return cst


def _host_layout(inp):
    f = lambda a: np.asarray(a, np.float32)
    com = {}
    par = np.zeros((128, NPAR), np.float32)

    def put(name, off, arr):
        o = _po[name] + off
        par[:, o:o + arr.shape[1]] = arr
    for L in range(2):
        put('nma', 8 * L, _chunk(f(inp['norm_mix_a'])[L], 8))
        for j in range(4):
            put('caw', 32 * L + 8 * j, _chunk(f(inp['conv_a_w'])[L, j], 8))
        put('cab', 8 * L, _chunk(f(inp['conv_a_b'])[L], 8))
        put('grb', 8 * L, _chunk(f(inp['gate_r_b'])[L].reshape(-1), 8))
        put('gib', 8 * L, _chunk(f(inp['gate_i_b'])[L].reshape(-1), 8))
        put('lam', 8 * L, _chunk(f(inp['lru_lambda'])[L], 8))
        put('nmb', 8 * L, _chunk(f(inp['norm_mix_b'])[L], 8))
        par[:, _po['gq'] + L] = np.tile(f(inp['q_norm'])[L], 2)
    put('nkv', 0, _chunk(f(inp['norm_kv']), 8))
    for L in range(4):
        put('nff', 8 * L, _chunk(f(inp['norm_ffn'])[L], 8))
        for tap in range(3):
            put('fcw', 72 * L + 24 * tap, _chunk(f(inp['ffn_conv_w'])[L, tap], 24))
        put('fcb', 24 * L, _chunk(f(inp['ffn_conv_b'])[L], 24))
    par[:, _po['gk']] = np.tile(f(inp['k_norm']), 2)
    half = 8
    inv = (500000.0 ** (-np.arange(half, dtype=np.float32) * np.float32(2.0 / 16))).astype(np.float32)
    invp = np.zeros(128, np.float32)
    for pp in range(128):
        i = pp % 64
        if i < 16:
            invp[pp] = inv[i % 8]
    par[:, _po['inv']] = invp
    com['par'] = par
    com['cst'] = _consts()
    pos = np.zeros((KVW,), np.float32)
    pos[:TOK] = np.arange(TOK)
    pos[TOK:] = np.tile(2048 + np.arange(4), NB)
    com['pos'] = np.ascontiguousarray(np.broadcast_to(pos[None, :], (128, KVW)))
    w_in = f(inp['w_in_a'])
    win = np.empty((2, 8, 128, 2048), np.float32)
    for L in range(2):
        Wk = w_in[L].reshape(8, 128, 2048)
        for i in range(8):
            n, kind = i // 2, i % 2
            c0 = kind * 1024 + 256 * n
            blk = Wk[:, :, c0:c0 + 256].reshape(8, 128, 2, 128)
            win[L, i] = blk.transpose(1, 2, 0, 3).reshape(128, 2048)
    com['win'] = win
    wg = np.empty((2, 4, 128, 1024), np.float32)
    rw, iw = f(inp['gate_r_w']), f(inp['gate_i_w'])
    for L in range(2):
        for n in range(4):
            a = np.stack([rw[L, n], iw[L, n]], 0).reshape(2, 2, 128, 256)
            wg[L, n] = a.transpose(2, 0, 1, 3).reshape(128, 1024)
    com['wg'] = wg
    w_out = f(inp['w_out_a'])
    wout = np.empty((2, 4, 128, 2048), np.float32)
    for L in range(2):
        Wk = w_out[L].reshape(8, 128, 1024)
        for i in range(4):
            blk = Wk[:, :, 256 * i:256 * i + 256].reshape(8, 128, 2, 128)
            wout[L, i] = blk.transpose(1, 2, 0, 3).reshape(128, 2048)
    com['wout'] = wout
    w_up = f(inp['w_ffn_up'])
    wup = np.empty((4, 24, 128, 2048), np.float32)
    for L in range(4):
        Wk = w_up[L].reshape(8, 128, 6144)
        g = Wk[:, :, 0:3072].reshape(8, 128, 24, 128)
        v = Wk[:, :, 3072:6144].reshape(8, 128, 24, 128)
        gv = np.stack([g, v], 3)
        wup[L] = gv.transpose(2, 1, 0, 3, 4).reshape(24, 128, 2048)
    com['wup'] = wup
    w_dn = f(inp['w_ffn_down'])
    wdn = np.empty((4, 3, 128, 8192), np.float32)
    for L in range(4):
        a = w_dn[L].reshape(3, 8, 128, 1024)
        wdn[L] = a.transpose(0, 2, 1, 3).reshape(3, 128, 8192)
    com['wdn'] = wdn
    w_kv = f(inp['w_kv']).reshape(2, 4, 128, 1024)
    wkv = np.empty((4, 128, 2048), np.float32)
    for half_ in range(2):
        for kh in range(2):
            a = w_kv[kh][:, :, 512 * half_:512 * half_ + 512]
            wkv[2 * half_ + kh] = a.transpose(1, 0, 2).reshape(128, 2048)
    com['wkv'] = wkv
    w_q = f(inp['w_q'])
    wq = np.empty((2, 12, 128, 1024), np.float32)
    for j in range(2):
        Wk = w_q[j].reshape(8, 128, 1536)
        for hp in range(4):
            for g in range(3):
                m = 4 * g + hp
                wq[j, hp * 3 + g] = Wk[:, :, 128 * m:128 * m + 128].transpose(1, 0, 2).reshape(128, 1024)
    com['wq'] = wq
    w_o = f(inp['w_o'])
    wo = np.empty((2, 2, 128, 2048), np.float32)
    for j in range(2):
        Wk = w_o[j].reshape(4, 128, 1024)
        for i in range(2):
            wo[j, i] = Wk[:, :, 512 * i:512 * i + 512].transpose(1, 0, 2).reshape(128, 2048)
    com['wo'] = wo
    xp = f(inp['x_prompt'])
    xs = f(inp['x_sample'])
    ck_, cv_ = f(inp['cache_k']), f(inp['cache_v'])
    rows = [1920 + np.arange(128)]
    for s in range(4):
        rows.append(1536 + s + 4 * np.arange(128))
    for s in range(4):
        rows.append(s + 16 * np.arange(128))
    rows = np.stack(rows, 0)
    sh, sc, sf = f(inp['state_rglru_h']), f(inp['state_rglru_conv']), f(inp['state_ffn_conv'])
    maps = []
    posv = com.pop('pos')
    for c in range(8):
        m = dict(com)
        bq, hf = c // 2, c % 2
        if hf == 1:
            m['xT'] = np.ascontiguousarray(xp[bq].T)
            m['pos'] = posv
        else:
            xt_ = np.zeros((1024, TOK), np.float32)
            xt_[:, 2 * P:] = xp[bq, 0:2 * P].T
            m['xT'] = xt_
            pz = posv.copy()
            pz[:, 0:2 * P] = 0.0
            pz[:, 2 * P:TOK] = posv[:, 0:2 * P]
            m['pos'] = pz
        pc = par.copy()
        pc[:, _po['vh']] = float(hf)
        m['par'] = pc
        b0 = NB * c
        m['xsT'] = np.ascontiguousarray(xs[b0:b0 + NB].reshape(NS, 1024).T)
        kk = ck_[b0:b0 + NB][:, rows]
        kk = kk.reshape(NB, 9, 128, 4, 2, 64).transpose(0, 1, 4, 5, 3, 2)
        m['ck'] = np.ascontiguousarray(kk.reshape(NB, 9, 128, 512))
        m['cv'] = np.ascontiguousarray(cv_[b0:b0 + NB][:, rows].reshape(NB, 9, 128, 512))
        a = sh[:, b0:b0 + NB].reshape(2, NB, 8, 128)
        m['sh'] = np.ascontiguousarray(a.transpose(0, 3, 2, 1).reshape(2, 128, 128))
        a = sc[:, b0:b0 + NB].reshape(2, NB, 3, 8, 128)
        m['sc'] = np.ascontiguousarray(a.transpose(0, 4, 3, 1, 2).reshape(2, 128, 384))
        a = sf[:, b0:b0 + NB].reshape(4, NB, 2, 24, 128)
        m['sf'] = np.ascontiguousarray(a.transpose(0, 4, 3, 1, 2).reshape(4, 128, 768))
        maps.append(m)
    return maps


_NC_CACHE = {}


def kernel(**inputs):
    maps = _host_layout(inputs)
    if 'nc' not in _NC_CACHE:
        _NC_CACHE['nc'] = build_program()
    nc = _NC_CACHE['nc']
    res = run_bass_kernel_spmd(nc, maps, core_ids=list(range(8)))
    R = res.results
    y = np.stack([np.concatenate([R[2 * b]['yT'].T, R[2 * b + 1]['yT'].T], 0) for b in range(4)], 0)
    ys = np.concatenate([R[c]['ysT'].T.reshape(NB, 4, 1024) for c in range(8)], 0)
    p_h = np.stack([R[2 * b + 1]['oh'].reshape(128, 2, 8).transpose(1, 2, 0).reshape(2, 1024) for b in range(4)], 1)
    p_c = np.stack([R[2 * b + 1]['oc'].reshape(128, 2, 8, 3).transpose(1, 3, 2, 0).reshape(2, 3, 1024) for b in range(4)], 1)
    p_f = np.stack([R[2 * b + 1]['of'].reshape(128, 4, 24, 2).transpose(1, 3, 2, 0).reshape(4, 2, 3072) for b in range(4)], 1)

    def fm2tok(a, n):
        return a.reshape(4, 2, 64, n).transpose(3, 0, 1, 2).reshape(n, 8, 64)
    p_k = np.stack([fm2tok(R[2 * b + 1]['pkT'], 2048) for b in range(4)], 0)
    p_v = np.stack([fm2tok(R[2 * b + 1]['pvT'], 2048) for b in range(4)], 0)
    s_h = np.concatenate([R[c]['soh'].reshape(2, 128, 8, NB).transpose(0, 3, 2, 1).reshape(2, NB, 1024)
                          for c in range(8)], 1)
    s_c = np.concatenate([R[c]['soc'].reshape(2, 128, 8, NB, 3).transpose(0, 3, 4, 2, 1).reshape(2, NB, 3, 1024)
                          for c in range(8)], 1)
    s_f = np.concatenate([R[c]['sof'].reshape(4, 128, 24, NB, 2).transpose(0, 3, 4, 2, 1).reshape(4, NB, 2, 3072)
                          for c in range(8)], 1)
    s_k = np.concatenate([fm2tok(R[c]['skT'], NS).reshape(NB, 4, 8, 64) for c in range(8)], 0)
    s_v = np.concatenate([R[c]['svtok'].reshape(NB, 4, 8, 64) for c in range(8)], 0)
    outs = (y, ys, p_h, p_c, p_f, p_k, p_v, s_h, s_c, s_f, s_k, s_v)
    return tuple(np.ascontiguousarray(o, dtype=np.float32) for o in outs)
```

```python
import math
from contextlib import ExitStack
import numpy as np
import concourse.bass as bass
import concourse.mybir as mybir
from concourse.bass_utils import run_bass_kernel_spmd

F32 = mybir.dt.float32
BF16 = mybir.dt.bfloat16
I32 = mybir.dt.int32
AF = mybir.ActivationFunctionType
ALU = mybir.AluOpType

P = 1024
NT = 4
NS = 64
TOK = 4096
KVW = TOK + NS
HB = 1152
NHB = 33
EPS = 1e-6
NB = 16

_po = {}
_n = 0
for _name, _w in [('nma', 16), ('caw', 64), ('cab', 16), ('grb', 16), ('gib', 16), ('lam', 16), ('nkv', 8),
                  ('nmb', 16), ('nff', 32), ('fcw', 288), ('fcb', 96), ('gk', 1), ('gq', 2), ('inv', 1), ('cl', 16), ('vh', 1)]:
    _po[_name] = _n
    _n += _w
NPAR = _n
C_ID, C_MP, C_MC, C_BD, C_PM, C_ON, C_MS0, C_MN, C_M64 = 0, 128, 256, 384, 512, 640, 768, 772, 784
NCB = 912


class _Space:
    def __init__(self):
        self.segs = []

    def deps(self, lo, hi, is_write, add):
        for a, b, w, r in self.segs:
            if b <= lo or a >= hi:
                continue
            if w is not None:
                add(w)
            if is_write:
                for t in r.values():
                    add(t)

    def apply(self, lo, hi, is_write, tok):
        out = []
        covered = []
        for seg in self.segs:
            a, b, w, r = seg
            if b <= lo or a >= hi:
                out.append(seg)
                continue
            if a < lo:
                out.append([a, lo, w, dict(r)])
            if b > hi:
                out.append([hi, b, w, dict(r)])
            ia, ib = max(a, lo), min(b, hi)
            if not is_write:
                r2 = dict(r)
                r2[(tok[0], tok[1])] = tok
                out.append([ia, ib, w, r2])
                covered.append((ia, ib))
        if is_write:
            out.append([lo, hi, tok, {}])
        else:
            covered.sort()
            cur = lo
            for a, b in covered:
                if a > cur:
                    out.append([cur, a, None, {(tok[0], tok[1]): tok}])
                cur = max(cur, b)
            if cur < hi:
                out.append([cur, hi, None, {(tok[0], tok[1]): tok}])
        out.sort(key=lambda s: s[0])
        self.segs = out


_ESZ = {F32: 4, BF16: 2, I32: 4}


def _extent(ap):
    pat = list(ap.ap)
    es = _ESZ[ap.dtype]
    pstride = pat[0][0]
    off = ap.offset % pstride if pstride > 0 else ap.offset
    lo = off
    hi = off
    for st, cnt in pat[1:]:
        if cnt > 1:
            if st >= 0:
                hi += st * (cnt - 1)
            else:
                lo += st * (cnt - 1)
    return ap.tensor.name, lo * es, (hi + 1) * es


class Sched:
    ENGS = ('pe', 'act', 'dve', 'pool', 'sp')

    def __init__(self, nc, stack):
        self.nc = nc
        self.q = {e: [] for e in self.ENGS}
        self.cnt = {e: 0 for e in self.ENGS}
        self.sem = {e: stack.enter_context(nc.semaphore("s_" + e)) for e in self.ENGS}
        self.waited = {e: {} for e in self.ENGS}
        self.spaces = {}
        self.dpool = {}
        for qe in ('sp', 'pool'):
            self.dpool[qe] = [[stack.enter_context(nc.semaphore("d_%s%d" % (qe, i))), 0] for i in range(24)]
        self.dnext = {'sp': 0, 'pool': 0}
        self.dall = []
        self.psums = []
        self.psn = 0
        self.nops = 0

    def psum(self):
        p = self.psums[self.psn % len(self.psums)]
        self.psn += 1
        return p

    def _semh(self, tok):
        if tok[0] == 'e':
            return self.sem[tok[1]]
        return self.dall[tok[1]][0]

    def _wait(self, eng, tok):
        key = (tok[0], tok[1])
        if self.waited[eng].get(key, 0) >= tok[2]:
            return
        self.waited[eng][key] = tok[2]
        semh = self._semh(tok)
        val = tok[2]
        self.q[eng].append(lambda h, semh=semh, val=val: h.wait_ge(semh, val))

    def _collect(self, eng, reads, writes):
        toks = {}

        def add(t):
            k = (t[0], t[1])
            if k not in toks or toks[k][2] < t[2]:
                toks[k] = t
        acc = []
        for ap, isw in [(a, False) for a in reads] + [(a, True) for a in writes]:
            name, lo, hi = _extent(ap)
            sp = self.spaces.get(name)
            if sp is None:
                sp = self.spaces[name] = _Space()
            acc.append((sp, lo, hi, isw))
            sp.deps(lo, hi, isw or name.startswith('ps'), add)
        for t in toks.values():
            if t[0] == 'e' and t[1] == eng and eng == 'pe':
                continue
            self._wait(eng, t)
        return acc

    def _commit(self, acc, tok):
        for sp, lo, hi, isw in acc:
            if not isw:
                sp.apply(lo, hi, False, tok)
        for sp, lo, hi, isw in acc:
            if isw:
                sp.apply(lo, hi, True, tok)

    def do(self, eng, op):
        fn, reads, writes = op
        acc = self._collect(eng, reads, writes)
        self.cnt[eng] += 1
        idx = self.cnt[eng]
        sem = self.sem[eng]
        self.q[eng].append(lambda h, fn=fn, sem=sem: fn(h).then_inc(sem, 1))
        self._commit(acc, ('e', eng, idx))
        self.nops += 1

    def pe(self, ops):
        reads = []
        writes = []
        for fn, r, w in ops:
            reads += r
            writes += w
        acc = self._collect('pe', reads, writes)
        self.cnt['pe'] += 1
        idx = self.cnt['pe']
        sem = self.sem['pe']
        for fn, r, w in ops[:-1]:
            self.q['pe'].append(lambda h, fn=fn: fn(h))
        fn = ops[-1][0]
        self.q['pe'].append(lambda h, fn=fn, sem=sem: fn(h).then_inc(sem, 1))
        self._commit(acc, ('e', 'pe', idx))
        self.nops += len(ops)

    def dma(self, qe, out, in_, r=(), w=()):
        acc = self._collect(qe, list(r), list(w))
        pool = self.dpool[qe]
        k = self.dnext[qe] % len(pool)
        self.dnext[qe] += 1
        ent = pool[k]
        if len(ent) == 2:
            ent.append(len(self.dall))
            self.dall.append(ent)
        gidx = ent[2]
        if ent[1] > 0:
            self._wait(qe, ('d', gidx, ent[1]))
        ent[1] += 16
        semh = ent[0]
        if qe == 'pool':
            self.q[qe].append(lambda h, out=out, in_=in_, semh=semh:
                              h.dma_start(out=out, in_=in_, max_dma_last_dim=4096).then_inc(semh, 16))
        else:
            self.q[qe].append(lambda h, out=out, in_=in_, semh=semh: h.dma_start(out=out, in_=in_).then_inc(semh, 16))
        self._commit(acc, ('d', gidx, ent[1]))
        self.nops += 1

    def finish(self):
        for ent in self.dall:
            if ent[1] > 0:
                self._wait('sp', ('d', ent[2], ent[1]))
        for e in ('pe', 'act', 'dve', 'pool'):
            if self.cnt[e] > 0:
                self._wait('sp', ('e', e, self.cnt[e]))

    def emit(self, block):
        nc = self.nc
        m = {'pe': block.tensor, 'act': block.scalar, 'dve': block.vector, 'pool': block.gpsimd, 'sp': block.sync}
        for e in self.ENGS:
            lst = self.q[e]
            if not lst:
                continue

            def body(h, lst=lst):
                for f in lst:
                    f(h)
            m[e](body)


def _isap(x):
    return hasattr(x, 'ap') and hasattr(x, 'tensor')


def ACT(out, in_, func, bias=None, scale=None):
    kw = {}
    rd = [in_]
    if bias is not None:
        kw['bias'] = bias
        if _isap(bias):
            rd.append(bias)
    if scale is not None:
        kw['scale'] = scale
        if _isap(scale):
            rd.append(scale)
    return (lambda h: h.activation(out=out, in_=in_, func=func, **kw), rd, [out])


def TS(out, in0, s1, s2, op0, op1=None):
    rd = [in0] + [s for s in (s1, s2) if _isap(s)]
    if op1 is None:
        return (lambda h: h.tensor_scalar(out=out, in0=in0, scalar1=s1, scalar2=None, op0=op0), rd, [out])
    return (lambda h: h.tensor_scalar(out=out, in0=in0, scalar1=s1, scalar2=s2, op0=op0, op1=op1), rd, [out])


def TT(out, a, b, op):
    return (lambda h: h.tensor_tensor(out=out, in0=a, in1=b, op=op), [a, b], [out])


def STT(out, in0, sc, in1, op0, op1):
    rd = [in0, in1] + ([sc] if _isap(sc) else [])
    return (lambda h: h.scalar_tensor_tensor(out=out, in0=in0, scalar=sc, in1=in1, op0=op0, op1=op1), rd, [out])


def SCAN(out, d0, d1, init):
    rd = [d0, d1] + ([init] if _isap(init) else [])
    return (lambda h: h.tensor_tensor_scan(out, d0, d1, init, op0=ALU.mult, op1=ALU.add), rd, [out])


def COPY(out, in_):
    return (lambda h: h.tensor_copy(out, in_), [in_], [out])


def RECIP(out, in_):
    return (lambda h: h.reciprocal(out, in_), [in_], [out])


def MEMSET(out, v):
    return (lambda h: h.memset(out, v), [], [out])


def MM(out, lhsT, rhs, start, stop):
    return (lambda h: h.matmul(out, lhsT, rhs, start=start, stop=stop), [lhsT, rhs], [out])


def TR(out, in_, ident):
    return (lambda h: h.transpose(out, in_, ident), [in_, ident], [out])


HALO = 128


class Tile:
    def __init__(self, t, halo=False):
        self.t = t
        self.has_s = (t == NT - 1)
        self.halo = halo
        self.W = P + (NS if self.has_s else 0) + (HALO if halo else 0)
        self.subs = [(0, 512), (512, 512)] + ([(P, NS)] if self.has_s else []) + ([(P, HALO)] if halo else [])


def build_program():
    nc = bass.Bass("TRN2", target_bir_lowering=False)
    d = {}

    def din(name, shape):
        d[name] = nc.dram_tensor(name, list(shape), F32, kind="ExternalInput").ap()

    def dout(name, shape):
        d[name] = nc.dram_tensor(name, list(shape), F32, kind="ExternalOutput").ap()

    din('xT', [1024, TOK]); din('xsT', [1024, NS])
    din('ck', [NB, 9, 128, 512]); din('cv', [NB, 9, 128, 512])
    din('sh', [2, 128, 128]); din('sc', [2, 128, 384]); din('sf', [4, 128, 768])
    din('par', [128, NPAR]); din('cst', [128, NCB]); din('pos', [128, KVW])
    din('win', [2, 8, 128, 2048]); din('wg', [2, 4, 128, 1024]); din('wout', [2, 4, 128, 2048])
    din('wup', [4, 24, 128, 2048]); din('wdn', [4, 3, 128, 8192]); din('wkv', [4, 128, 2048])
    din('wq', [2, 12, 128, 1024]); din('wo', [2, 2, 128, 2048])
    dout('yT', [1024, 2 * P]); dout('ysT', [1024, NS])
    dout('oh', [128, 16]); dout('oc', [128, 48]); dout('of', [128, 192])
    dout('pkT', [512, 2048]); dout('pvT', [512, 2048])
    dout('soh', [2, 128, 128]); dout('soc', [2, 128, 384]); dout('sof', [4, 128, 768])
    dout('skT', [512, NS]); dout('svtok', [NS, 512])

    with ExitStack() as st:
        def sb(name, shape, dt):
            return st.enter_context(nc.sbuf_tensor(name, list(shape), dt))
        X = sb("X", [128, 8, P + HALO], F32)
        KT = sb("KT", [128, 4, KVW], BF16)
        VT = sb("VT", [128, 4, KVW], BF16)
        WS = sb("WS", [128, 4, 2048], BF16)
        AR = sb("AR", [128, NHB * HB], BF16)
        PAR = sb("PAR", [128, NPAR], F32)
        CB = sb("CB", [128, NCB], BF16)
        HST = sb("HST", [128, 2, 8], F32)
        CAH = sb("CAH", [128, 2, 8, 3], F32)
        FH = sb("FH", [128, 4, 24, 2], F32)
        SHL = sb("SHL", [128, 8, NB], F32)
        SCL = sb("SCL", [128, 8, NB, 3], F32)
        SFL = sb("SFL", [128, 24, NB, 2], F32)
        TMP16 = sb("TMP16", [128, NB], F32)
        VSNB = sb("VSNB", [NS, 512], BF16)
        VSNF = sb("VSNF", [NS, 512], F32)
        SMALL = sb("SMALL", [128, 2, 16], F32)
        PTS = sb("PTS", [128, 4, 128], BF16)
        VNB = sb("VNB", [4, 2, 512], BF16)
        S = Sched(nc, st)
        S.psums = [st.enter_context(nc.psum_tensor("ps%d" % i, [128, 512], F32)) for i in range(8)]

        def arb(i, n=HB):
            return AR[:, i * HB:i * HB + n]

        def arf(i, n=HB):
            return AR[:, i * HB:(i + 2) * HB].bitcast(F32)[:, 0:n]

        def XN(k):
            return arb(k)

        IDENT = CB[:, C_ID:C_ID + 128]
        MASKP = CB[:, C_MP:C_MP + 128]
        MASKC = CB[:, C_MC:C_MC + 128]
        BD = CB[:, C_BD:C_BD + 128]
        PM = CB[:, C_PM:C_PM + 128]
        ONES = CB[:, C_ON:C_ON + 128]

        def par(name, i=0):
            o = _po[name] + i
            return PAR[:, o:o + 1]

        wsn = [0]

        def wload(src, n=2048, parts=128):
            k = wsn[0] % 4
            wsn[0] += 1
            dst = WS[0:parts, k, 0:n]
            S.dma('pool', dst, src, w=[dst])
            return WS[:, k, :]

        S.dma('sp', PAR[:, :], d['par'], w=[PAR[:, :]])
        S.dma('pool', CB[:, :], d['cst'], w=[CB[:, :]])
        S.do('dve', MEMSET(HST[:, :, :], 0.0))
        S.do('dve', MEMSET(CAH[:, :, :, :], 0.0))
        S.do('dve', MEMSET(FH[:, :, :, :], 0.0))
        HV = PTS[:, 0, 64:128]
        HV2 = PTS[:, 1, 64:128]
        vhp = PAR[:, _po['vh']:_po['vh'] + 1]
        S.do('dve', TS(HV, ONES[:, 0:64], vhp, None, ALU.mult))
        S.do('dve', TS(HV2[0:64, :], ONES[0:64, 0:64], PAR[0:64, _po['vh']:_po['vh'] + 1], None, ALU.mult))
        S.do('dve', COPY(HV2[64:128, :], ONES[64:128, 0:64]))
        lam = PAR[:, _po['lam']:_po['lam'] + 16]
        clv = PAR[:, _po['cl']:_po['cl'] + 16]
        S.do('act', ACT(clv, lam, AF.Exp, scale=-1.0))
        S.do('act', ACT(clv, clv, AF.Ln, bias=1.0))
        S.do('dve', TS(clv, clv, -8.0, None, ALU.mult))
        S.do('dve', TS(lam, clv, 2.0, None, ALU.mult))

        def load_x(T):
            src = d['xT'].rearrange("(c p) t -> p c t", p=128)[:, :, P * T.t:P * T.t + P]
            S.dma('sp', X[:, :, 0:P], src, w=[X[:, :, 0:P]])
            if T.has_s:
                src = d['xsT'].rearrange("(c p) t -> p c t", p=128)
                S.dma('sp', X[:, :, P:P + NS], src, w=[X[:, :, P:P + NS]])

        def store_y(T):
            dst = d['yT'].rearrange("(c p) t -> p c t", p=128)[:, :, P * (T.t - 2):P * (T.t - 2) + P]
            S.dma('sp', dst, X[:, :, 0:P], r=[X[:, :, 0:P]])
            if T.has_s:
                dst = d['ysT'].rearrange("(c p) t -> p c t", p=128)
                S.dma('sp', dst, X[:, :, P:P + NS], r=[X[:, :, P:P + NS]])

        def rmsnorm(T, gname, gi):
            SQ = arb(32)
            RS = arf(30)
            W = T.W
            for (c0, n) in T.subs:
                ps = S.psum()
                for c in range(8):
                    sq = SQ[:, (c % 2) * 512:(c % 2) * 512 + n]
                    if c % 2 == 0:
                        S.do('act', ACT(sq, X[:, c, c0:c0 + n], AF.Square))
                    else:
                        S.do('dve', TT(sq, X[:, c, c0:c0 + n], X[:, c, c0:c0 + n], ALU.mult))
                    S.pe([MM(ps[:, 0:n], ONES, sq, c == 0, c == 7)])
                S.do('act', ACT(RS[:, c0:c0 + n], ps[:, 0:n], AF.Ln, bias=EPS, scale=1.0 / 1024))
                S.do('act', ACT(RS[:, c0:c0 + n], RS[:, c0:c0 + n], AF.Exp, scale=-0.5))
            for c in range(8):
                S.do('dve', STT(XN(c)[:, 0:W], X[:, c, 0:W], par(gname, gi * 8 + c), RS[:, 0:W], ALU.mult, ALU.mult))

        def sview(ap64, s=4):
            return ap64.rearrange("p (b s) -> p b s", s=s)

        def a_mix(T, L):
            W = T.W
            subs = T.subs
            XBH = arf(16)
            XBHs = XBH[:, 1028:1028 + NB * 7].rearrange("p (b j) -> p b j", j=7)
            XCs = [arf(18), arf(20)]
            XCBs = [arb(22), arb(23)]
            RAs = [arf(24), arf(30)]
            IU = arf(26)
            TH = arf(28)

            def GG(c):
                return arb(8 + c)
            if T.has_s:
                S.dma('sp', SHL[:, :, :], d['sh'][L].rearrange("p (c b) -> p c b", c=8), w=[SHL[:, :, :]])
                S.dma('sp', SCL[:, :, :, :], d['sc'][L].rearrange("p (c b j) -> p c b j", c=8, b=NB),
                      w=[SCL[:, :, :, :]])
            for n in range(4):
                wxb = wload(d['win'][L, 2 * n + 1]).rearrange("p (c k q) -> p c k q", c=2, k=8)
                wrg = wload(d['wg'][L, n], n=1024)[:, 0:1024].rearrange("p (g k o) -> p g k o", g=2, k=2)
                for cc in range(2):
                    c = 2 * n + cc
                    XC = XCs[cc]
                    S.do('dve', COPY(XBH[:, 0:3], CAH[:, L, c, :]))
                    if T.has_s:
                        S.do('dve', COPY(XBHs[:, :, 0:3], SCL[:, c, :, :]))
                    for (c0, nn) in subs:
                        ps = S.psum()
                        S.pe([MM(ps[:, 0:nn], wxb[:, cc, k, :], XN(k)[:, c0:c0 + nn], k == 0, k == 7) for k in range(8)])
                        if c0 < P:
                            S.do('act', ACT(XBH[:, 3 + c0:3 + c0 + nn], ps[:, 0:nn], AF.Copy))
                        else:
                            S.do('act', ACT(XBHs[:, :, 3:7], sview(ps[:, 0:NS]), AF.Copy))
                    S.do('dve', COPY(CAH[:, L, c, :], XBH[:, P:P + 3]))
                    if T.has_s:
                        S.do('dve', COPY(SCL[:, c, :, :], XBHs[:, :, 4:7]))
                    cw = [par('caw', L * 32 + j * 8 + c) for j in range(4)]
                    cbias = par('cab', L * 8 + c)
                    S.do('dve', TS(XC[:, 0:P], XBH[:, 3:3 + P], cw[3], cbias, ALU.mult, ALU.add))
                    for j in (2, 1, 0):
                        S.do('dve', STT(XC[:, 0:P], XBH[:, j:j + P], cw[j], XC[:, 0:P], ALU.mult, ALU.add))
                    if T.has_s:
                        XCv = sview(XC[:, P:P + NS])
                        S.do('dve', TS(XCv, XBHs[:, :, 3:7], cw[3], cbias, ALU.mult, ALU.add))
                        for j in (2, 1, 0):
                            S.do('dve', STT(XCv, XBHs[:, :, j:j + 4], cw[j], XCv, ALU.mult, ALU.add))
                    S.do('act', ACT(XCBs[cc][:, 0:W], XC[:, 0:W], AF.Copy))
                wgt = wload(d['win'][L, 2 * n]).rearrange("p (c k q) -> p c k q", c=2, k=8)
                for cc in range(2):
                    c = 2 * n + cc
                    for (c0, nn) in subs:
                        ps = S.psum()
                        S.pe([MM(ps[:, 0:nn], wgt[:, cc, k, :], XN(k)[:, c0:c0 + nn], k == 0, k == 7) for k in range(8)])
                        S.do('act', ACT(GG(c)[:, c0:c0 + nn], ps[:, 0:nn], AF.Gelu_apprx_tanh))
                for cc in range(2):
                    c = 2 * n + cc
                    XC = XCs[cc]
                    RA = RAs[cc]
                    for gate in range(2):
                        dst = RA if gate == 0 else IU
                        gb = par('grb' if gate == 0 else 'gib', L * 8 + c)
                        for (c0, nn) in subs:
                            ps = S.psum()
                            S.pe([MM(ps[:, 0:nn], wrg[:, gate, k, cc * 128:(cc + 1) * 128], XCBs[k][:, c0:c0 + nn],
                                     k == 0, k == 1) for k in range(2)])
                            S.do('act', ACT(dst[:, c0:c0 + nn], ps[:, 0:nn], AF.Sigmoid, bias=gb))
                    S.do('act', ACT(TH[:, 0:W], RA[:, 0:W], AF.Exp, scale=par('lam', L * 8 + c)))
                    S.do('act', ACT(RA[:, 0:W], RA[:, 0:W], AF.Exp, scale=par('cl', L * 8 + c)))
                    S.do('dve', TT(IU[:, 0:W], IU[:, 0:W], XC[:, 0:W], ALU.mult))
                    S.do('act', ACT(TH[:, 0:W], TH[:, 0:W], AF.Ln, bias=1.0, scale=-1.0))
                    S.do('act', ACT(TH[:, 0:W], TH[:, 0:W], AF.Exp, scale=0.5))
                    S.do('dve', TT(IU[:, 0:W], IU[:, 0:W], TH[:, 0:W], ALU.mult))
                    if T.t < 2:
                        S.do('dve', TS(IU[:, 0:W], IU[:, 0:W], par('vh'), None, ALU.mult))
                    S.do('dve', SCAN(TH[:, 0:P], RA[:, 0:P], IU[:, 0:P], HST[:, L, c:c + 1]))
                    S.do('dve', COPY(HST[:, L, c:c + 1], TH[:, P - 1:P]))
                    if T.has_s:
                        As = sview(RA[:, P:P + NS])
                        Us = sview(IU[:, P:P + NS])
                        S.do('dve', TT(TMP16[:, :], As[:, :, 0], SHL[:, c, :], ALU.mult))
                        S.do('dve', TT(Us[:, :, 0], Us[:, :, 0], TMP16[:, :], ALU.add))
                        S.do('dve', MEMSET(As[:, :, 0], 0.0))
                        S.do('dve', SCAN(TH[:, P:P + NS], RA[:, P:P + NS], IU[:, P:P + NS], 0.0))
                        S.do('dve', COPY(SHL[:, c, :], sview(TH[:, P:P + NS])[:, :, 3]))
                    S.do('dve', TT(GG(c)[:, 0:W], TH[:, 0:W], GG(c)[:, 0:W], ALU.mult))
            if T.has_s:
                S.dma('sp', d['soh'][L].rearrange("p (c b) -> p c b", c=8), SHL[:, :, :], r=[SHL[:, :, :]])
                S.dma('sp', d['soc'][L].rearrange("p (c b j) -> p c b j", c=8, b=NB), SCL[:, :, :, :],
                      r=[SCL[:, :, :, :]])
            for i in range(4):
                wsl = wload(d['wout'][L, i]).rearrange("p (c k q) -> p c k q", c=2, k=8)
                for cc in range(2):
                    m = 2 * i + cc
                    for (c0, nn) in subs:
                        ps = S.psum()
                        S.pe([MM(ps[:, 0:nn], wsl[:, cc, k, :], GG(k)[:, c0:c0 + nn], k == 0, k == 7) for k in range(8)])
                        S.do('dve', TT(X[:, m, c0:c0 + nn], X[:, m, c0:c0 + nn], ps[:, 0:nn], ALU.add))

        def ffn(T, L):
            W = T.W
            subs = T.subs

            def ACTB(jj):
                return arb(8 + jj)
            WD = AR[:, 16 * HB:16 * HB + 8192].rearrange("p (j m) -> p j m", j=8)
            GH = arf(24)
            GHs = GH[:, 1028:1028 + NB * 6].rearrange("p (b j) -> p b j", j=6)
            GC = arf(26)
            VB = arb(28)
            if T.has_s:
                S.dma('sp', SFL[:, :, :, :], d['sf'][L].rearrange("p (j b s) -> p j b s", j=24, b=NB),
                      w=[SFL[:, :, :, :]])
            for G in range(3):
                for jj in range(8):
                    j = 8 * G + jj
                    wsl = wload(d['wup'][L, j]).rearrange("p (k q) -> p k q", k=8)
                    if jj == 2:
                        wdst = AR[:, 16 * HB:16 * HB + 8192].rearrange("p (a b) -> p a b", a=4)
                        S.dma('pool', wdst, d['wdn'][L, G].rearrange("p (a b) -> p a b", a=4), w=[wdst])
                    if not T.halo:
                        S.do('dve', COPY(GH[:, 0:2], FH[:, L, j, :]))
                    if T.has_s:
                        S.do('dve', COPY(GHs[:, :, 0:2], SFL[:, j, :, :]))
                    for (c0, nn) in subs:
                        psg = S.psum()
                        S.pe([MM(psg[:, 0:nn], wsl[:, k, 0:128], XN(k)[:, c0:c0 + nn], k == 0, k == 7) for k in range(8)])
                        psv = S.psum()
                        S.pe([MM(psv[:, 0:nn], wsl[:, k, 128:256], XN(k)[:, c0:c0 + nn], k == 0, k == 7) for k in range(8)])
                        if T.halo:
                            if c0 < P:
                                S.do('act', ACT(GH[:, HALO + c0:HALO + c0 + nn], psg[:, 0:nn], AF.Copy))
                            else:
                                S.do('act', ACT(GH[:, 0:HALO], psg[:, 0:nn], AF.Copy))
                        elif c0 < P:
                            S.do('act', ACT(GH[:, 2 + c0:2 + c0 + nn], psg[:, 0:nn], AF.Copy))
                        else:
                            S.do('act', ACT(GHs[:, :, 2:6], sview(psg[:, 0:NS]), AF.Copy))
                        S.do('act', ACT(VB[:, c0:c0 + nn], psv[:, 0:nn], AF.Copy))
                    if T.halo:
                        S.do('dve', COPY(FH[:, L, j, :], GH[:, HALO + P - 2:HALO + P]))
                    else:
                        S.do('dve', COPY(FH[:, L, j, :], GH[:, P:P + 2]))
                    if T.has_s:
                        S.do('dve', COPY(SFL[:, j, :, :], GHs[:, :, 4:6]))
                    fw = [par('fcw', L * 72 + tap * 24 + j) for tap in range(3)]
                    fb = par('fcb', L * 24 + j)
                    if T.halo:
                        S.do('dve', TS(GC[:, 0:P], GH[:, HALO:HALO + P], fw[2], fb, ALU.mult, ALU.add))
                        S.do('dve', STT(GC[:, 0:P], GH[:, HALO - 1:HALO - 1 + P], fw[1], GC[:, 0:P], ALU.mult, ALU.add))
                        S.do('dve', STT(GC[:, 0:P], GH[:, HALO - 2:HALO - 2 + P], fw[0], GC[:, 0:P], ALU.mult, ALU.add))
                        S.do('dve', MEMSET(GC[:, P:P + 2], 0.0))
                        gch = GC[:, P + 2:P + HALO]
                        S.do('dve', TS(gch, GH[:, 2:HALO], fw[2], fb, ALU.mult, ALU.add))
                        S.do('dve', STT(gch, GH[:, 1:HALO - 1], fw[1], gch, ALU.mult, ALU.add))
                        S.do('dve', STT(gch, GH[:, 0:HALO - 2], fw[0], gch, ALU.mult, ALU.add))
                    else:
                        S.do('dve', TS(GC[:, 0:P], GH[:, 2:2 + P], fw[2], fb, ALU.mult, ALU.add))
                        S.do('dve', STT(GC[:, 0:P], GH[:, 1:1 + P], fw[1], GC[:, 0:P], ALU.mult, ALU.add))
                        S.do('dve', STT(GC[:, 0:P], GH[:, 0:P], fw[0], GC[:, 0:P], ALU.mult, ALU.add))
                    if T.has_s:
                        GCv = sview(GC[:, P:P + NS])
                        S.do('dve', TS(GCv, GHs[:, :, 2:6], fw[2], fb, ALU.mult, ALU.add))
                        S.do('dve', STT(GCv, GHs[:, :, 1:5], fw[1], GCv, ALU.mult, ALU.add))
                        S.do('dve', STT(GCv, GHs[:, :, 0:4], fw[0], GCv, ALU.mult, ALU.add))
                    S.do('act', ACT(ACTB(jj)[:, 0:W], GC[:, 0:W], AF.Gelu_apprx_tanh))
                    S.do('dve', TT(ACTB(jj)[:, 0:W], ACTB(jj)[:, 0:W], VB[:, 0:W], ALU.mult))
                for m in range(8):
                    for (c0, nn) in subs:
                        ps = S.psum()
                        S.pe([MM(ps[:, 0:nn], WD[:, jj, 128 * m:128 * m + 128], ACTB(jj)[:, c0:c0 + nn], jj == 0, jj == 7)
                              for jj in range(8)])
                        S.do('dve', TT(X[:, m, c0:c0 + nn], X[:, m, c0:c0 + nn], ps[:, 0:nn], ALU.add))
            if T.has_s:
                S.dma('sp', d['sof'][L].rearrange("p (j b s) -> p j b s", j=24, b=NB), SFL[:, :, :, :],
                      r=[SFL[:, :, :, :]])

        def rope_tables(T):
            W = T.W
            ANG = arf(26)
            KI = AR[:, 28 * HB:30 * HB].bitcast(I32)[:, 0:HB]
            KF = arf(30)
            C = arf(22)
            Sn = arf(24)
            S.dma('sp', ANG[:, 0:P], d['pos'][:, P * T.t:P * T.t + P], w=[ANG[:, 0:P]])
            if T.has_s:
                S.dma('sp', ANG[:, P:P + NS], d['pos'][:, TOK:TOK + NS], w=[ANG[:, P:P + NS]])
            if T.halo:
                S.dma('sp', ANG[:, P:P + HALO], d['pos'][:, P * T.t - HALO:P * T.t], w=[ANG[:, P:P + HALO]])
            S.do('dve', TS(ANG[:, 0:W], ANG[:, 0:W], par('inv'), None, ALU.mult))
            S.do('dve', TS(KI[:, 0:W], ANG[:, 0:W], 1.0 / (2 * math.pi), None, ALU.mult))
            S.do('dve', COPY(KF[:, 0:W], KI[:, 0:W]))
            S.do('dve', STT(ANG[:, 0:W], KF[:, 0:W], -2.0 * math.pi, ANG[:, 0:W], ALU.mult, ALU.add))
            S2 = arf(28)
            S4 = arf(30)
            S.do('act', ACT(S2[:, 0:W], ANG[:, 0:W], AF.Sin, scale=0.5))
            S.do('act', ACT(S4[:, 0:W], ANG[:, 0:W], AF.Sin, scale=0.25))
            S.do('dve', TT(C[:, 0:W], S2[:, 0:W], S2[:, 0:W], ALU.mult))
            S.do('dve', TS(C[:, 0:W], C[:, 0:W], -2.0, 1.0, ALU.mult, ALU.add))
            S.do('dve', TT(S4[:, 0:W], S4[:, 0:W], S4[:, 0:W], ALU.mult))
            S.do('dve', TS(S4[:, 0:W], S4[:, 0:W], -4.0, 2.0, ALU.mult, ALU.add))
            S.do('dve', TT(Sn[:, 0:W], S2[:, 0:W], S4[:, 0:W], ALU.mult))
            return C, Sn

        def kv(T):
            W = T.W
            subs = T.subs
            t = T.t
            wk = [wload(d['wkv'][0]).rearrange("p (k q) -> p k q", k=4),
                  wload(d['wkv'][1]).rearrange("p (k q) -> p k q", k=4)]
            sets = [dict(KF=arf(16), RS=arf(18), KNb=arb(20), SQ=arb(21)),
                    dict(KF=arf(26), RS=arf(28), KNb=arb(8), SQ=arb(9))]
            C, Sn = None, None
            import os as _os
            _lv = int(_os.environ.get('K_KV', '99'))
            if _lv < 1:
                return
            C, Sn = rope_tables(T)
            if _lv < 2:
                return
            for m in range(4 if _lv >= 3 else 1):
                bs = sets[m % 2]
                KF, RS, KNb, SQ = bs['KF'], bs['RS'], bs['KNb'], bs['SQ']
                for (c0, nn) in subs:
                    ps = S.psum()
                    S.pe([MM(ps[:, 0:nn], wk[k // 4][:, k % 4, 128 * m:128 * m + 128], XN(k)[:, c0:c0 + nn], k == 0, k == 7)
                          for k in range(8)])
                    S.do('act', ACT(SQ[:, c0:c0 + nn], ps[:, 0:nn], AF.Square))
                    S.do('act', ACT(KF[:, c0:c0 + nn], ps[:, 0:nn], AF.Copy))
                    ps2 = S.psum()
                    S.pe([MM(ps2[:, 0:nn], BD, SQ[:, c0:c0 + nn], True, True)])
                    S.do('act', ACT(RS[:, c0:c0 + nn], ps2[:, 0:nn], AF.Ln, bias=EPS, scale=1.0 / 64))
                    S.do('act', ACT(RS[:, c0:c0 + nn], RS[:, c0:c0 + nn], AF.Exp, scale=-0.5))
                S.do('dve', STT(KF[:, 0:W], KF[:, 0:W], par('gk'), RS[:, 0:W], ALU.mult, ALU.mult))
                S.do('act', ACT(KNb[:, 0:W], KF[:, 0:W], AF.Copy))
                for (c0, nn) in subs:
                    ps3 = S.psum()
                    S.pe([MM(ps3[:, 0:nn], PM, KNb[:, c0:c0 + nn], True, True)])
                    S.do('dve', TT(RS[:, c0:c0 + nn], ps3[:, 0:nn], Sn[:, c0:c0 + nn], ALU.mult))
                S.do('dve', TT(KF[:, 0:W], KF[:, 0:W], C[:, 0:W], ALU.mult))
                S.do('dve', TT(KF[:, 0:W], KF[:, 0:W], RS[:, 0:W], ALU.add))
                S.do('act', ACT(KT[:, m, P * t:P * t + P], KF[:, 0:P], AF.Copy))
                if T.has_s:
                    S.do('act', ACT(KT[:, m, TOK:TOK + NS], KF[:, P:P + NS], AF.Copy))
                    S.dma('sp', d['skT'].rearrange("(m p) t -> p m t", p=128)[:, m, :], KF[:, P:P + NS],
                          r=[KF[:, P:P + NS]])
                if t >= 2:
                    dst = d['pkT'].rearrange("(m p) t -> p m t", p=128)[:, m, P * (t - 2):P * (t - 2) + P]
                    S.dma('sp', dst, KF[:, 0:P], r=[KF[:, 0:P]])
            if _lv < 4:
                return
            wv = [wload(d['wkv'][2]).rearrange("p (k q) -> p k q", k=4),
                  wload(d['wkv'][3]).rearrange("p (k q) -> p k q", k=4)]
            for m in range(4):
                VF = sets[m % 2]['KF']
                for (c0, nn) in subs[0:2]:
                    ps = S.psum()
                    S.pe([MM(ps[:, 0:nn], wv[k // 4][:, k % 4, 128 * m:128 * m + 128], XN(k)[:, c0:c0 + nn], k == 0, k == 7)
                          for k in range(8)])
                    _vv = int(_os.environ.get('K_KVV', '3'))
                    if _vv & 1:
                        S.do('act', ACT(VF[:, c0:c0 + nn], ps[:, 0:nn], AF.Copy))
                    if _vv & 2:
                        S.do('dve', COPY(VT[:, m, P * t + c0:P * t + c0 + nn], ps[:, 0:nn]))
                if t >= 2:
                    dst = d['pvT'].rearrange("(m p) t -> p m t", p=128)[:, m, P * (t - 2):P * (t - 2) + P]
                    S.dma('sp', dst, VF[:, 0:P], r=[VF[:, 0:P]])
            if T.has_s:
                ps = S.psum()
                S.pe([MM(ps[0:NS, 0:512], XN(k)[:, P:P + NS], wv[k // 4][:, k % 4, :], k == 0, k == 7) for k in range(8)])
                S.do('act', ACT(VSNF[:, :], ps[0:NS, 0:512], AF.Copy))
                S.do('dve', COPY(VSNB[:, :], ps[0:NS, 0:512]))
                S.dma('sp', d['svtok'], VSNF[:, :], r=[VSNF[:, :]])

        ptn = [0]
        vbn = [0]

        def attention_prompt(T, hp, QT, NUM, DEN):
            t = T.t
            PTbuf = arb(12)
            VBbuf = AR[:, 13 * HB:15 * HB]
            MPC = CB[:, C_MP:C_MP + 256]
            M64 = CB[:, C_M64:C_M64 + 128]
            blocks = []
            hq0 = P * t - HALO
            for bq in range(8):
                blocks.append((0, 1, 128, 128 * bq, P * t + 128 * bq))
            if T.halo:
                blocks.append((0, 1, 128, P, hq0))
            for bb in range(2):
                for r in range(4):
                    blocks.append((1, 4, 128, 512 * bb + r, P * t + 512 * bb + r))
            if T.halo:
                for r in range(4):
                    blocks.append((1, 4, 32, P + r, hq0 + r))
            for r in range(16):
                blocks.append((2, 16, 64, r, P * t + r))
            if T.halo:
                for r in range(16):
                    blocks.append((2, 16, 8, P + r, hq0 + r))
            ND = AR[:, 26 * HB:30 * HB].bitcast(F32).rearrange("p (a c) -> p a c", a=2)
            def stage_a(g, dd, QB, qc, q0):
                kbs = []
                if q0 >= 128 * dd:
                    kbs.append((q0 - 128 * dd, 128, MASKP))
                else:
                    pmin = -((q0 - 128 * dd) // dd)
                    if pmin < 128:
                        assert QB <= pmin
                        kbs.append((q0 - 128 * dd + pmin * dd, 128 - pmin, None))
                kbs.append((q0, QB, MASKC))
                nkb = len(kbs)
                std = (nkb == 2 and kbs[0][1] == 128 and QB in (64, 128))
                pi = ptn[0] % 2
                ptn[0] += 1
                PT = PTbuf[:, pi * 512:pi * 512 + 512].rearrange("p (e c) -> p e c", e=2)
                pss2 = [S.psum(), S.psum()]
                S.pe([MM(pss2[e][0:nk, i * QB:(i + 1) * QB],
                         KT[64 * e:64 * e + 64, hp, k0:k0 + dd * (nk - 1) + 1:dd],
                         QT[g][64 * e:64 * e + 64, qc:qc + dd * (QB - 1) + 1:dd], True, True)
                      for i, (k0, nk, mask) in enumerate(kbs) for e in range(2)])
                for e in range(2):
                    pss = pss2[e]
                    if std:
                        S.do('act', ACT(PT[:, e, 0:2 * QB], pss[:, 0:2 * QB], AF.Exp, scale=0.125))
                    else:
                        for i, (k0, nk, mask) in enumerate(kbs):
                            S.do('act', ACT(PT[0:nk, e, i * QB:(i + 1) * QB], pss[0:nk, i * QB:(i + 1) * QB],
                                            AF.Exp, scale=0.125))
                if std:
                    mc = MPC if QB == 128 else M64
                    S.do('dve', TT(PT[:, :, 0:2 * QB], PT[:, :, 0:2 * QB],
                                   mc[:, None, :].broadcast_to([128, 2, 2 * QB]), ALU.mult))
                else:
                    for i, (k0, nk, mask) in enumerate(kbs):
                        if mask is not None:
                            pv3 = PT[0:nk, :, i * QB:(i + 1) * QB]
                            S.do('dve', TT(pv3, pv3, mask[0:nk, None, 0:QB].broadcast_to([nk, 2, QB]), ALU.mult))
                pst = S.psum()
                pstb = pst[:, :].bitcast(BF16)
                S.pe([TR(pstb[0:nk, i * 128:(i + 1) * 128], VT[:, hp, k0:k0 + dd * (nk - 1) + 1:dd], IDENT)
                      for i, (k0, nk, mask) in enumerate(kbs)])
                vi = vbn[0] % 8
                vbn[0] += 1
                VB = VBbuf[:, vi * 256:vi * 256 + 256]
                if std:
                    S.do('act', ACT(VB[:, 0:256], pstb[:, 0:256], AF.Copy))
                else:
                    for i, (k0, nk, mask) in enumerate(kbs):
                        S.do('dve', COPY(VB[0:nk, i * 128:(i + 1) * 128], pstb[0:nk, i * 128:(i + 1) * 128]))
                return (g, dd, QB, qc, kbs, nkb, PT, VB)

            def stage_b(ctx):
                (g, dd, QB, qc, kbs, nkb, PT, VB) = ctx
                psod = S.psum()
                ops = []
                for i, (k0, nk, mask) in enumerate(kbs):
                    for e in range(2):
                        ops.append(MM(psod[64 * e:64 * e + 64, 0:QB], VB[0:nk, i * 128 + 64 * e:i * 128 + 64 * e + 64],
                                      PT[0:nk, e, i * QB:(i + 1) * QB], i == 0, i == nkb - 1))
                for i, (k0, nk, mask) in enumerate(kbs):
                    for e in range(2):
                        nh = min(nk, max(0, -((k0 - 2 * P) // dd)))
                        if nh == 0:
                            dl = ONES[0:nk, 0:64]
                        elif nh == nk:
                            dl = HV[0:nk, :]
                        else:
                            assert nh == 64 and nk == 128
                            dl = HV2[0:nk, :]
                        ops.append(MM(psod[64 * e:64 * e + 64, 128:128 + QB], dl,
                                      PT[0:nk, e, i * QB:(i + 1) * QB], i == 0, i == nkb - 1))
                S.pe(ops)
                ndv = ND[:, :, qc:qc + dd * (QB - 1) + 1:dd]
                src = psod[:, 0:256].rearrange("p (a c) -> p a c", a=2)[:, :, 0:QB]
                if g == 0:
                    S.do('act', ACT(ndv, src, AF.Copy))
                else:
                    S.do('dve', TT(ndv, ndv, src, ALU.add))

            prev = None
            for blk in blocks:
                ctx = stage_a(*blk)
                if prev is not None:
                    stage_b(prev)
                prev = ctx
            stage_b(prev)

        def attention_sample(T, QS, ATBs):
            KSb = AR[:, 16 * HB:16 * HB + 4 * 512].rearrange("p (r q) -> p r q", r=4)
            VSb = AR[:, 18 * HB:18 * HB + 8 * 512].rearrange("p (r q) -> p r q", r=8)
            MS0 = CB[:, C_MS0:C_MS0 + 4]
            MN = CB[0:4, C_MN:C_MN + 12]
            kn = [0]
            vn = [0]
            NDs = SMALL[:, :, :]
            NUMs = SMALL[:, 0, 0:16]
            DENs = SMALL[:, 1, 0:16]
            bstate = {}

            def stage_a(b, g):
                if g == 0:
                    vslot = b % 2
                    S.dma('sp', VNB[0:4, vslot, :], VSNB[4 * b:4 * b + 4, :], r=[VSNB[4 * b:4 * b + 4, :]],
                          w=[VNB[0:4, vslot, :]])
                    VN = VNB[0:4, vslot, :]
                    PTN = PTS[0:4, 3, 0:96]
                    psn2 = [S.psum(), S.psum()]
                    ops = []
                    for gg in range(3):
                        for hp in range(4):
                            for e in range(2):
                                ops.append(MM(psn2[e][0:4, gg * 16 + hp * 4:gg * 16 + hp * 4 + 4],
                                              KT[64 * e:64 * e + 64, hp, TOK + 4 * b:TOK + 4 * b + 4],
                                              QS[64 * e:64 * e + 64, gg * 4 + hp, 4 * b:4 * b + 4], True, True))
                    S.pe(ops)
                    for e in range(2):
                        psn_ = psn2[e]
                        S.do('act', ACT(PTN[:, e * 48:(e + 1) * 48], psn_[0:4, 0:48], AF.Exp, scale=0.125))
                        pn4 = PTN[:, e * 48:(e + 1) * 48].rearrange("p (g h s) -> p g h s", g=3, h=4)
                        mn4 = MN.rearrange("p (g s) -> p g s", g=3)[:, :, None, :].broadcast_to([4, 3, 4, 4])
                        S.do('dve', TT(pn4, pn4, mn4, ALU.mult))
                    pn = PTN.rearrange("p (e g c) -> p e g c", e=2, g=3)
                    PTNS = PTS[0:4, 3 - (b % 2), 96:128]
                    PTNS3 = PTNS.rearrange("p (e c) -> p e c", e=2)
                    S.do('dve', TT(PTNS3, pn[:, :, 0, :], pn[:, :, 1, :], ALU.add))
                    S.do('dve', TT(PTNS3, PTNS3, pn[:, :, 2, :], ALU.add))
                    bstate[b] = (VN, PTNS)
                nblk = 1 if g == 0 else 4
                ks = []
                vs = []
                for i in range(nblk):
                    blk = 0 if g == 0 else 1 + 4 * (g - 1) + i
                    ki = kn[0] % 4
                    kn[0] += 1
                    vi = vn[0] % 8
                    vn[0] += 1
                    S.dma('pool', KSb[:, ki, :], d['ck'][b, blk], w=[KSb[:, ki, :]])
                    S.dma('pool', VSb[:, vi, :], d['cv'][b, blk], w=[VSb[:, vi, :]])
                    ks.append(KSb[:, ki, :].rearrange("p (a k) -> p a k", a=4))
                    vs.append(VSb[:, vi, :])
                PT = PTS[:, g, 0:32]
                pss2 = [S.psum(), S.psum()]
                ops = []
                for hp in range(4):
                    if g == 0:
                        for e in range(2):
                            ops.append(MM(pss2[e][:, hp * 4:hp * 4 + 4], ks[0][64 * e:64 * e + 64, hp, :],
                                          QS[64 * e:64 * e + 64, hp, 4 * b:4 * b + 4], True, True))
                    else:
                        for s_ in range(4):
                            for e in range(2):
                                ops.append(MM(pss2[e][:, hp * 4 + s_:hp * 4 + s_ + 1], ks[s_][64 * e:64 * e + 64, hp, :],
                                              QS[64 * e:64 * e + 64, g * 4 + hp, 4 * b + s_:4 * b + s_ + 1], True, True))
                S.pe(ops)
                for e in range(2):
                    S.do('act', ACT(PT[:, e * 16:(e + 1) * 16], pss2[e][:, 0:16], AF.Exp, scale=0.125))
                if g == 0:
                    p3 = PT.rearrange("p (h s) -> p h s", h=8)
                    S.do('dve', TT(p3, p3, MS0[:, None, :].broadcast_to([128, 8, 4]), ALU.mult))
                return (b, g, vs, PT)

            def stage_b(ctx):
                (b, g, vs, PT) = ctx
                (VN, PTNS) = bstate[b]
                last = (g == 2)
                psod = S.psum()
                ops = []
                for hp in range(4):
                    for s_ in range(4):
                        for e in range(2):
                            h = 2 * hp + e
                            col = hp * 4 + s_
                            vblk = vs[0] if g == 0 else vs[s_]
                            ops.append(MM(psod[64 * e:64 * e + 64, col:col + 1], vblk[:, h * 64:h * 64 + 64],
                                          PT[:, e * 16 + col:e * 16 + col + 1], True, not last))
                            if last:
                                ops.append(MM(psod[64 * e:64 * e + 64, col:col + 1], VN[:, h * 64:h * 64 + 64],
                                              PTNS[:, e * 16 + col:e * 16 + col + 1], False, True))
                for e in range(2):
                    ops.append(MM(psod[64 * e:64 * e + 64, 16:32], ONES[:, 0:64], PT[:, e * 16:(e + 1) * 16], True, not last))
                    if last:
                        ops.append(MM(psod[64 * e:64 * e + 64, 16:32], ONES[0:4, 0:64],
                                      PTNS[:, e * 16:(e + 1) * 16], False, True))
                S.pe(ops)
                src = psod[:, 0:32].rearrange("p (a c) -> p a c", a=2)
                if g == 0:
                    S.do('act', ACT(NDs, src, AF.Copy))
                else:
                    S.do('dve', TT(NDs, NDs, src, ALU.add))
                if last:
                    S.do('dve', RECIP(DENs, DENs))
                    S.do('dve', TT(ATBs[:, :, 4 * b:4 * b + 4], NUMs.rearrange("p (a c) -> p a c", a=4),
                                   DENs.rearrange("p (a c) -> p a c", a=4), ALU.mult))

            prev = None
            for b in range(NB):
                for g in range(3):
                    ctx = stage_a(b, g)
                    if prev is not None:
                        stage_b(prev)
                    prev = ctx
            stage_b(prev)

        def b_mix(T, jB):
            W = T.W
            subs = T.subs
            C, Sn = rope_tables(T)
            QN = arf(16)
            QNb = arb(18)
            QT = [arb(19), arb(20), arb(21)]
            RSq = arf(30)
            SQq = arb(32)
            NUM = arf(26)
            DEN = arf(28)
            QS = arb(15)[:, 0:12 * NS].rearrange("p (m c) -> p m c", m=12)

            def ATB(hp):
                return arb(8 + hp)
            for hp in range(4):
                for g in range(3):
                    wsl = wload(d['wq'][jB, hp * 3 + g], n=1024)[:, 0:1024].rearrange("p (k q) -> p k q", k=8)
                    psqs = []
                    for (c0, nn) in subs:
                        psq = S.psum()
                        S.pe([MM(psq[:, 0:nn], wsl[:, k, :], XN(k)[:, c0:c0 + nn], k == 0, k == 7) for k in range(8)])
                        S.do('act', ACT(SQq[:, c0:c0 + nn], psq[:, 0:nn], AF.Square))
                        psqs.append(psq)
                    for si, (c0, nn) in enumerate(subs):
                        psq = psqs[si]
                        ps2 = S.psum()
                        S.pe([MM(ps2[:, 0:nn], BD, SQq[:, c0:c0 + nn], True, True)])
                        S.do('act', ACT(RSq[:, c0:c0 + nn], ps2[:, 0:nn], AF.Ln, bias=EPS, scale=1.0 / 64))
                        S.do('act', ACT(RSq[:, c0:c0 + nn], RSq[:, c0:c0 + nn], AF.Exp, scale=-0.5))
                        S.do('dve', STT(QN[:, c0:c0 + nn], psq[:, 0:nn], par('gq', jB), RSq[:, c0:c0 + nn],
                                        ALU.mult, ALU.mult))
                    S.do('act', ACT(QNb[:, 0:W], QN[:, 0:W], AF.Copy))
                    for (c0, nn) in subs:
                        ps3 = S.psum()
                        S.pe([MM(ps3[:, 0:nn], PM, QNb[:, c0:c0 + nn], True, True)])
                        S.do('dve', TT(RSq[:, c0:c0 + nn], ps3[:, 0:nn], Sn[:, c0:c0 + nn], ALU.mult))
                    S.do('dve', TT(QN[:, 0:W], QN[:, 0:W], C[:, 0:W], ALU.mult))
                    S.do('dve', TT(QT[g][:, 0:W], QN[:, 0:W], RSq[:, 0:W], ALU.add))
                    if T.has_s:
                        S.do('act', ACT(QS[:, g * 4 + hp, :], QT[g][:, P:P + NS], AF.Copy))
                attention_prompt(T, hp, QT, NUM, DEN)
                Wa = P + (HALO if T.halo else 0)
                S.do('dve', TS(DEN[:, 0:Wa], DEN[:, 0:Wa], 1e-18, None, ALU.max))
                S.do('act', ACT(DEN[:, 0:Wa], DEN[:, 0:Wa], AF.Ln))
                S.do('act', ACT(DEN[:, 0:Wa], DEN[:, 0:Wa], AF.Exp, scale=-1.0))
                S.do('dve', TT(ATB(hp)[:, 0:Wa], NUM[:, 0:Wa], DEN[:, 0:Wa], ALU.mult))
            if T.has_s:
                ATBs = AR[:, 8 * HB:12 * HB].rearrange("p (h w) -> p h w", h=4)[:, :, P:P + NS]
                attention_sample(T, QS, ATBs)
            wos = [wload(d['wo'][jB, i]).rearrange("p (h q) -> p h q", h=4) for i in range(2)]
            for m in range(8):
                for (c0, nn) in subs:
                    ps = S.psum()
                    S.pe([MM(ps[:, 0:nn], wos[m // 4][:, hp, (m % 4) * 128:(m % 4) * 128 + 128], ATB(hp)[:, c0:c0 + nn],
                             hp == 0, hp == 3) for hp in range(4)])
                    S.do('dve', TT(X[:, m, c0:c0 + nn], X[:, m, c0:c0 + nn], ps[:, 0:nn], ALU.add))

        for t in range(NT):
            T = Tile(t)
            TB = Tile(t, halo=(t == 2))
            load_x(T)
            for L in range(2):
                rmsnorm(T, 'nma', L)
                a_mix(T, L)
                rmsnorm(T, 'nff', L)
                ffn(T, L)
            rmsnorm(T, 'nkv', 0)
            kv(T)
            if t == 1:
                S.do('act', ACT(X[:, :, P:P + HALO], X[:, :, P - HALO:P], AF.Copy))
            if t >= 2:
                for jB in range(2):
                    rmsnorm(TB, 'nmb', jB)
                    b_mix(TB, jB)
                    rmsnorm(TB, 'nff', 2 + jB)
                    ffn(TB, 2 + jB)
                store_y(T)
        S.dma('sp', d['oh'], HST[:, :, :].rearrange("p l c -> p (l c)"), r=[HST[:, :, :]])
        S.dma('sp', d['oc'], CAH[:, :, :, :].rearrange("p l c j -> p (l c j)"), r=[CAH[:, :, :, :]])
        S.dma('sp', d['of'], FH[:, :, :, :].rearrange("p l c j -> p (l c j)"), r=[FH[:, :, :, :]])
        S.finish()
        with nc.Block() as block:
            S.emit(block)
    return nc


def _chunk(v, nchunk):
    return np.ascontiguousarray(np.asarray(v, np.float32).reshape(nchunk, 128).T)


def _consts():
    cst = np.zeros((128, NCB), np.float32)
    p = np.arange(128)[:, None]
    f = np.arange(128)[None, :]
    cst[:, C_ID:C_ID + 128] = np.eye(128)
    cst[:, C_MP:C_MP + 128] = (f <= p)
    cst[:, C_MC:C_MC + 128] = (f >= p)
    cst[:, C_BD:C_BD + 128] = ((p // 64) == (f // 64))
    pm = np.zeros((128, 128), np.float32)
    for m in range(128):
        i = m % 64
        if i < 8:
            pm[m + 8, m] = -1.0
        elif i < 16:
            pm[m - 8, m] = 1.0
    cst[:, C_PM:C_PM + 128] = pm
    cst[:, C_ON:C_ON + 128] = 1.0
    s = np.arange(4)[None, :]
    cst[:, C_MS0:C_MS0 + 4] = (np.arange(128)[:, None] >= s)
    mn = np.zeros((4, 3, 4), np.float32)
    sp = np.arange(4)[:, None]
    mn[:, 0, :] = (sp <= s)
    mn[:, 1, :] = (sp == s)
    mn[:, 2, :] = (sp == s)
    cst[0:4, C_MN:C_MN + 12] = mn.reshape(4, 12)
    cst[:, C_M64:C_M64 + 64] = (f <= p)[:, 0:64]
    cst[:, C_M64 + 64:C_M64 + 128] = (f >= p)[:, 0:64]
    return cst


def _host_layout(inp):
    f = lambda a: np.asarray(a, np.float32)
    com = {}
    par = np.zeros((128, NPAR), np.float32)

    def put(name, off, arr):
        o = _po[name] + off
        par[:, o:o + arr.shape[1]] = arr
    for L in range(2):
        put('nma', 8 * L, _chunk(f(inp['norm_mix_a'])[L], 8))
        for j in range(4):
            put('caw', 32 * L + 8 * j, _chunk(f(inp['conv_a_w'])[L, j], 8))
        put('cab', 8 * L, _chunk(f(inp['conv_a_b'])[L], 8))
        put('grb', 8 * L, _chunk(f(inp['gate_r_b'])[L].reshape(-1), 8))
        put('gib', 8 * L, _chunk(f(inp['gate_i_b'])[L].reshape(-1), 8))
        put('lam', 8 * L, _chunk(f(inp['lru_lambda'])[L], 8))
        put('nmb', 8 * L, _chunk(f(inp['norm_mix_b'])[L], 8))
        par[:, _po['gq'] + L] = np.tile(f(inp['q_norm'])[L], 2)
    put('nkv', 0, _chunk(f(inp['norm_kv']), 8))
    for L in range(4):
        put('nff', 8 * L, _chunk(f(inp['norm_ffn'])[L], 8))
        for tap in range(3):
            put('fcw', 72 * L + 24 * tap, _chunk(f(inp['ffn_conv_w'])[L, tap], 24))
        put('fcb', 24 * L, _chunk(f(inp['ffn_conv_b'])[L], 24))
    par[:, _po['gk']] = np.tile(f(inp['k_norm']), 2)
    half = 8
    inv = (500000.0 ** (-np.arange(half, dtype=np.float32) * np.float32(2.0 / 16))).astype(np.float32)
    invp = np.zeros(128, np.float32)
    for pp in range(128):
        i = pp % 64
        if i < 16:
            invp[pp] = inv[i % 8]
    par[:, _po['inv']] = invp
    com['par'] = par
    com['cst'] = _consts()
    pos = np.zeros((KVW,), np.float32)
    pos[:TOK] = np.arange(TOK)
    pos[TOK:] = np.tile(2048 + np.arange(4), NB)
    com['pos'] = np.ascontiguousarray(np.broadcast_to(pos[None, :], (128, KVW)))
    w_in = f(inp['w_in_a'])
    win = np.empty((2, 8, 128, 2048), np.float32)
    for L in range(2):
        Wk = w_in[L].reshape(8, 128, 2048)
        for i in range(8):
            n, kind = i // 2, i % 2
            c0 = kind * 1024 + 256 * n
            blk = Wk[:, :, c0:c0 + 256].reshape(8, 128, 2, 128)
            win[L, i] = blk.transpose(1, 2, 0, 3).reshape(128, 2048)
    com['win'] = win
    wg = np.empty((2, 4, 128, 1024), np.float32)
    rw, iw = f(inp['gate_r_w']), f(inp['gate_i_w'])
    for L in range(2):
        for n in range(4):
            a = np.stack([rw[L, n], iw[L, n]], 0).reshape(2, 2, 128, 256)
            wg[L, n] = a.transpose(2, 0, 1, 3).reshape(128, 1024)
    com['wg'] = wg
    w_out = f(inp['w_out_a'])
    wout = np.empty((2, 4, 128, 2048), np.float32)
    for L in range(2):
        Wk = w_out[L].reshape(8, 128, 1024)
        for i in range(4):
            blk = Wk[:, :, 256 * i:256 * i + 256].reshape(8, 128, 2, 128)
            wout[L, i] = blk.transpose(1, 2, 0, 3).reshape(128, 2048)
    com['wout'] = wout
    w_up = f(inp['w_ffn_up'])
    wup = np.empty((4, 24, 128, 2048), np.float32)
    for L in range(4):
        Wk = w_up[L].reshape(8, 128, 6144)
        g = Wk[:, :, 0:3072].reshape(8, 128, 24, 128)
        v = Wk[:, :, 3072:6144].reshape(8, 128, 24, 128)
        gv = np.stack([g, v], 3)
        wup[L] = gv.transpose(2, 1, 0, 3, 4).reshape(24, 128, 2048)
    com['wup'] = wup
    w_dn = f(inp['w_ffn_down'])
    wdn = np.empty((4, 3, 128, 8192), np.float32)
    for L in range(4):
        a = w_dn[L].reshape(3, 8, 128, 1024)
        wdn[L] = a.transpose(0, 2, 1, 3).reshape(3, 128, 8192)
    com['wdn'] = wdn
    w_kv = f(inp['w_kv']).reshape(2, 4, 128, 1024)
    wkv = np.empty((4, 128, 2048), np.float32)
    for half_ in range(2):
        for kh in range(2):
            a = w_kv[kh][:, :, 512 * half_:512 * half_ + 512]
            wkv[2 * half_ + kh] = a.transpose(1, 0, 2).reshape(128, 2048)
    com['wkv'] = wkv
    w_q = f(inp['w_q'])
    wq = np.empty((2, 12, 128, 1024), np.float32)
    for j in range(2):
        Wk = w_q[j].reshape(8, 128, 1536)
        for hp in range(4):
            for g in range(3):
                m = 4 * g + hp
                wq[j, hp * 3 + g] = Wk[:, :, 128 * m:128 * m + 128].transpose(1, 0, 2).reshape(128, 1024)
    com['wq'] = wq
    w_o = f(inp['w_o'])
    wo = np.empty((2, 2, 128, 2048), np.float32)
    for j in range(2):
        Wk = w_o[j].reshape(4, 128, 1024)
        for i in range(2):
            wo[j, i] = Wk[:, :, 512 * i:512 * i + 512].transpose(1, 0, 2).reshape(128, 2048)
    com['wo'] = wo
    xp = f(inp['x_prompt'])
    xs = f(inp['x_sample'])
    ck_, cv_ = f(inp['cache_k']), f(inp['cache_v'])
    rows = [1920 + np.arange(128)]
    for s in range(4):
        rows.append(1536 + s + 4 * np.arange(128))
    for s in range(4):
        rows.append(s + 16 * np.arange(128))
    rows = np.stack(rows, 0)
    sh, sc, sf = f(inp['state_rglru_h']), f(inp['state_rglru_conv']), f(inp['state_ffn_conv'])
    maps = []
    posv = com.pop('pos')
    for c in range(8):
        m = dict(com)
        bq, hf = c // 2, c % 2
        if hf == 1:
            m['xT'] = np.ascontiguousarray(xp[bq].T)
            m['pos'] = posv
        else:
            xt_ = np.zeros((1024, TOK), np.float32)
            xt_[:, 2 * P:] = xp[bq, 0:2 * P].T
            m['xT'] = xt_
            pz = posv.copy()
            pz[:, 0:2 * P] = 0.0
            pz[:, 2 * P:TOK] = posv[:, 0:2 * P]
            m['pos'] = pz
        pc = par.copy()
        pc[:, _po['vh']] = float(hf)
        m['par'] = pc
        b0 = NB * c
        m['xsT'] = np.ascontiguousarray(xs[b0:b0 + NB].reshape(NS, 1024).T)
        kk = ck_[b0:b0 + NB][:, rows]
        kk = kk.reshape(NB, 9, 128, 4, 2, 64).transpose(0, 1, 4, 5, 3, 2)
        m['ck'] = np.ascontiguousarray(kk.reshape(NB, 9, 128, 512))
        m['cv'] = np.ascontiguousarray(cv_[b0:b0 + NB][:, rows].reshape(NB, 9, 128, 512))
        a = sh[:, b0:b0 + NB].reshape(2, NB, 8, 128)
        m['sh'] = np.ascontiguousarray(a.transpose(0, 3, 2, 1).reshape(2, 128, 128))
        a = sc[:, b0:b0 + NB].reshape(2, NB, 3, 8, 128)
        m['sc'] = np.ascontiguousarray(a.transpose(0, 4, 3, 1, 2).reshape(2, 128, 384))
        a = sf[:, b0:b0 + NB].reshape(4, NB, 2, 24, 128)
        m['sf'] = np.ascontiguousarray(a.transpose(0, 4, 3, 1, 2).reshape(4, 128, 768))
        maps.append(m)
    return maps


_NC_CACHE = {}


def kernel(**inputs):
    maps = _host_layout(inputs)
    if 'nc' not in _NC_CACHE:
        _NC_CACHE['nc'] = build_program()
    nc = _NC_CACHE['nc']
    res = run_bass_kernel_spmd(nc, maps, core_ids=list(range(8)))
    R = res.results
    y = np.stack([np.concatenate([R[2 * b]['yT'].T, R[2 * b + 1]['yT'].T], 0) for b in range(4)], 0)
    ys = np.concatenate([R[c]['ysT'].T.reshape(NB, 4, 1024) for c in range(8)], 0)
    p_h = np.stack([R[2 * b + 1]['oh'].reshape(128, 2, 8).transpose(1, 2, 0).reshape(2, 1024) for b in range(4)], 1)
    p_c = np.stack([R[2 * b + 1]['oc'].reshape(128, 2, 8, 3).transpose(1, 3, 2, 0).reshape(2, 3, 1024) for b in range(4)], 1)
    p_f = np.stack([R[2 * b + 1]['of'].reshape(128, 4, 24, 2).transpose(1, 3, 2, 0).reshape(4, 2, 3072) for b in range(4)], 1)

    def fm2tok(a, n):
        return a.reshape(4, 2, 64, n).transpose(3, 0, 1, 2).reshape(n, 8, 64)
    p_k = np.stack([fm2tok(R[2 * b + 1]['pkT'], 2048) for b in range(4)], 0)
    p_v = np.stack([fm2tok(R[2 * b + 1]['pvT'], 2048) for b in range(4)], 0)
    s_h = np.concatenate([R[c]['soh'].reshape(2, 128, 8, NB).transpose(0, 3, 2, 1).reshape(2, NB, 1024)
                          for c in range(8)], 1)
    s_c = np.concatenate([R[c]['soc'].reshape(2, 128, 8, NB, 3).transpose(0, 3, 4, 2, 1).reshape(2, NB, 3, 1024)
                          for c in range(8)], 1)
    s_f = np.concatenate([R[c]['sof'].reshape(4, 128, 24, NB, 2).transpose(0, 3, 4, 2, 1).reshape(4, NB, 2, 3072)
                          for c in range(8)], 1)
    s_k = np.concatenate([fm2tok(R[c]['skT'], NS).reshape(NB, 4, 8, 64) for c in range(8)], 0)
    s_v = np.concatenate([R[c]['svtok'].reshape(NB, 4, 8, 64) for c in range(8)], 0)
    outs = (y, ys, p_h, p_c, p_f, p_k, p_v, s_h, s_c, s_f, s_k, s_v)
    return tuple(np.ascontiguousarray(o, dtype=np.float32) for o in outs)
```

```python
import math
from contextlib import ExitStack
import numpy as np
import concourse.bass as bass
import concourse.mybir as mybir
from concourse.bass_utils import run_bass_kernel_spmd

F32 = mybir.dt.float32
BF16 = mybir.dt.bfloat16
I32 = mybir.dt.int32
AF = mybir.ActivationFunctionType
ALU = mybir.AluOpType

P = 1024
NT = 4
NS = 64
TOK = 4096
KVW = TOK + NS
HB = 1152
NHB = 33
EPS = 1e-6
NB = 16

_po = {}
_n = 0
for _name, _w in [('nma', 16), ('caw', 64), ('cab', 16), ('grb', 16), ('gib', 16), ('lam', 16), ('nkv', 8),
                  ('nmb', 16), ('nff', 32), ('fcw', 288), ('fcb', 96), ('gk', 1), ('gq', 2), ('inv', 1), ('cl', 16), ('vh', 1)]:
    _po[_name] = _n
    _n += _w
NPAR = _n
C_ID, C_MP, C_MC, C_BD, C_PM, C_ON, C_MS0, C_MN, C_M64 = 0, 128, 256, 384, 512, 640, 768, 772, 784
NCB = 912


class _Space:
    def __init__(self):
        self.segs = []

    def deps(self, lo, hi, is_write, add):
        for a, b, w, r in self.segs:
            if b <= lo or a >= hi:
                continue
            if w is not None:
                add(w)
            if is_write:
                for t in r.values():
                    add(t)

    def apply(self, lo, hi, is_write, tok):
        out = []
        covered = []
        for seg in self.segs:
            a, b, w, r = seg
            if b <= lo or a >= hi:
                out.append(seg)
                continue
            if a < lo:
                out.append([a, lo, w, dict(r)])
            if b > hi:
                out.append([hi, b, w, dict(r)])
            ia, ib = max(a, lo), min(b, hi)
            if not is_write:
                r2 = dict(r)
                r2[(tok[0], tok[1])] = tok
                out.append([ia, ib, w, r2])
                covered.append((ia, ib))
        if is_write:
            out.append([lo, hi, tok, {}])
        else:
            covered.sort()
            cur = lo
            for a, b in covered:
                if a > cur:
                    out.append([cur, a, None, {(tok[0], tok[1]): tok}])
                cur = max(cur, b)
            if cur < hi:
                out.append([cur, hi, None, {(tok[0], tok[1]): tok}])
        out.sort(key=lambda s: s[0])
        self.segs = out


_ESZ = {F32: 4, BF16: 2, I32: 4}


def _extent(ap):
    pat = list(ap.ap)
    es = _ESZ[ap.dtype]
    pstride = pat[0][0]
    off = ap.offset % pstride if pstride > 0 else ap.offset
    lo = off
    hi = off
    for st, cnt in pat[1:]:
        if cnt > 1:
            if st >= 0:
                hi += st * (cnt - 1)
            else:
                lo += st * (cnt - 1)
    return ap.tensor.name, lo * es, (hi + 1) * es


class Sched:
    ENGS = ('pe', 'act', 'dve', 'pool', 'sp')

    def __init__(self, nc, stack):
        self.nc = nc
        self.q = {e: [] for e in self.ENGS}
        self.cnt = {e: 0 for e in self.ENGS}
        self.sem = {e: stack.enter_context(nc.semaphore("s_" + e)) for e in self.ENGS}
        self.waited = {e: {} for e in self.ENGS}
        self.spaces = {}
        self.dpool = {}
        for qe in ('sp', 'pool'):
            self.dpool[qe] = [[stack.enter_context(nc.semaphore("d_%s%d" % (qe, i))), 0] for i in range(24)]
        self.dnext = {'sp': 0, 'pool': 0}
        self.dall = []
        self.psums = []
        self.psn = 0
        self.nops = 0

    def psum(self):
        p = self.psums[self.psn % len(self.psums)]
        self.psn += 1
        return p

    def _semh(self, tok):
        if tok[0] == 'e':
            return self.sem[tok[1]]
        return self.dall[tok[1]][0]

    def _wait(self, eng, tok):
        key = (tok[0], tok[1])
        if self.waited[eng].get(key, 0) >= tok[2]:
            return
        self.waited[eng][key] = tok[2]
        semh = self._semh(tok)
        val = tok[2]
        self.q[eng].append(lambda h, semh=semh, val=val: h.wait_ge(semh, val))

    def _collect(self, eng, reads, writes):
        toks = {}

        def add(t):
            k = (t[0], t[1])
            if k not in toks or toks[k][2] < t[2]:
                toks[k] = t
        acc = []
        for ap, isw in [(a, False) for a in reads] + [(a, True) for a in writes]:
            name, lo, hi = _extent(ap)
            sp = self.spaces.get(name)
            if sp is None:
                sp = self.spaces[name] = _Space()
            acc.append((sp, lo, hi, isw))
            sp.deps(lo, hi, isw or name.startswith('ps'), add)
        for t in toks.values():
            if t[0] == 'e' and t[1] == eng and eng == 'pe':
                continue
            self._wait(eng, t)
        return acc

    def _commit(self, acc, tok):
        for sp, lo, hi, isw in acc:
            if not isw:
                sp.apply(lo, hi, False, tok)
        for sp, lo, hi, isw in acc:
            if isw:
                sp.apply(lo, hi, True, tok)

    def do(self, eng, op):
        fn, reads, writes = op
        acc = self._collect(eng, reads, writes)
        self.cnt[eng] += 1
        idx = self.cnt[eng]
        sem = self.sem[eng]
        self.q[eng].append(lambda h, fn=fn, sem=sem: fn(h).then_inc(sem, 1))
        self._commit(acc, ('e', eng, idx))
        self.nops += 1

    def pe(self, ops):
        reads = []
        writes = []
        for fn, r, w in ops:
            reads += r
            writes += w
        acc = self._collect('pe', reads, writes)
        self.cnt['pe'] += 1
        idx = self.cnt['pe']
        sem = self.sem['pe']
        for fn, r, w in ops[:-1]:
            self.q['pe'].append(lambda h, fn=fn: fn(h))
        fn = ops[-1][0]
        self.q['pe'].append(lambda h, fn=fn, sem=sem: fn(h).then_inc(sem, 1))
        self._commit(acc, ('e', 'pe', idx))
        self.nops += len(ops)

    def dma(self, qe, out, in_, r=(), w=()):
        acc = self._collect(qe, list(r), list(w))
        pool = self.dpool[qe]
        k = self.dnext[qe] % len(pool)
        self.dnext[qe] += 1
        ent = pool[k]
        if len(ent) == 2:
            ent.append(len(self.dall))
            self.dall.append(ent)
        gidx = ent[2]
        if ent[1] > 0:
            self._wait(qe, ('d', gidx, ent[1]))
        ent[1] += 16
        semh = ent[0]
        if qe == 'pool':
            self.q[qe].append(lambda h, out=out, in_=in_, semh=semh:
                              h.dma_start(out=out, in_=in_, max_dma_last_dim=4096).then_inc(semh, 16))
        else:
            self.q[qe].append(lambda h, out=out, in_=in_, semh=semh: h.dma_start(out=out, in_=in_).then_inc(semh, 16))
        self._commit(acc, ('d', gidx, ent[1]))
        self.nops += 1

    def finish(self):
        for ent in self.dall:
            if ent[1] > 0:
                self._wait('sp', ('d', ent[2], ent[1]))
        for e in ('pe', 'act', 'dve', 'pool'):
            if self.cnt[e] > 0:
                self._wait('sp', ('e', e, self.cnt[e]))

    def emit(self, block):
        nc = self.nc
        m = {'pe': block.tensor, 'act': block.scalar, 'dve': block.vector, 'pool': block.gpsimd, 'sp': block.sync}
        for e in self.ENGS:
            lst = self.q[e]
            if not lst:
                continue

            def body(h, lst=lst):
                for f in lst:
                    f(h)
            m[e](body)


def _isap(x):
    return hasattr(x, 'ap') and hasattr(x, 'tensor')


def ACT(out, in_, func, bias=None, scale=None):
    kw = {}
    rd = [in_]
    if bias is not None:
        kw['bias'] = bias
        if _isap(bias):
            rd.append(bias)
    if scale is not None:
        kw['scale'] = scale
        if _isap(scale):
            rd.append(scale)
    return (lambda h: h.activation(out=out, in_=in_, func=func, **kw), rd, [out])


def TS(out, in0, s1, s2, op0, op1=None):
    rd = [in0] + [s for s in (s1, s2) if _isap(s)]
    if op1 is None:
        return (lambda h: h.tensor_scalar(out=out, in0=in0, scalar1=s1, scalar2=None, op0=op0), rd, [out])
    return (lambda h: h.tensor_scalar(out=out, in0=in0, scalar1=s1, scalar2=s2, op0=op0, op1=op1), rd, [out])


def TT(out, a, b, op):
    return (lambda h: h.tensor_tensor(out=out, in0=a, in1=b, op=op), [a, b], [out])


def STT(out, in0, sc, in1, op0, op1):
    rd = [in0, in1] + ([sc] if _isap(sc) else [])
    return (lambda h: h.scalar_tensor_tensor(out=out, in0=in0, scalar=sc, in1=in1, op0=op0, op1=op1), rd, [out])


def SCAN(out, d0, d1, init):
    rd = [d0, d1] + ([init] if _isap(init) else [])
    return (lambda h: h.tensor_tensor_scan(out, d0, d1, init, op0=ALU.mult, op1=ALU.add), rd, [out])


def COPY(out, in_):
    return (lambda h: h.tensor_copy(out, in_), [in_], [out])


def RECIP(out, in_):
    return (lambda h: h.reciprocal(out, in_), [in_], [out])


def MEMSET(out, v):
    return (lambda h: h.memset(out, v), [], [out])


def MM(out, lhsT, rhs, start, stop):
    return (lambda h: h.matmul(out, lhsT, rhs, start=start, stop=stop), [lhsT, rhs], [out])


def TR(out, in_, ident):
    return (lambda h: h.transpose(out, in_, ident), [in_, ident], [out])


HALO = 128


class Tile:
    def __init__(self, t, halo=False):
        self.t = t
        self.has_s = (t == NT - 1)
        self.halo = halo
        self.W = P + (NS if self.has_s else 0) + (HALO if halo else 0)
        self.subs = [(0, 512), (512, 512)] + ([(P, NS)] if self.has_s else []) + ([(P, HALO)] if halo else [])


def build_program():
    nc = bass.Bass("TRN2", target_bir_lowering=False)
    d = {}

    def din(name, shape):
        d[name] = nc.dram_tensor(name, list(shape), F32, kind="ExternalInput").ap()

    def dout(name, shape):
        d[name] = nc.dram_tensor(name, list(shape), F32, kind="ExternalOutput").ap()

    din('xT', [1024, TOK]); din('xsT', [1024, NS])
    din('ck', [NB, 9, 128, 512]); din('cv', [NB, 9, 128, 512])
    din('sh', [2, 128, 128]); din('sc', [2, 128, 384]); din('sf', [4, 128, 768])
    din('par', [128, NPAR]); din('cst', [128, NCB]); din('pos', [128, KVW])
    din('win', [2, 8, 128, 2048]); din('wg', [2, 4, 128, 1024]); din('wout', [2, 4, 128, 2048])
    din('wup', [4, 24, 128, 2048]); din('wdn', [4, 3, 128, 8192]); din('wkv', [4, 128, 2048])
    din('wq', [2, 12, 128, 1024]); din('wo', [2, 2, 128, 2048])
    dout('yT', [1024, 2 * P]); dout('ysT', [1024, NS])
    dout('oh', [128, 16]); dout('oc', [128, 48]); dout('of', [128, 192])
    dout('pkT', [512, 2048]); dout('pvT', [512, 2048])
    dout('soh', [2, 128, 128]); dout('soc', [2, 128, 384]); dout('sof', [4, 128, 768])
    dout('skT', [512, NS]); dout('svtok', [NS, 512])

    with ExitStack() as st:
        def sb(name, shape, dt):
            return st.enter_context(nc.sbuf_tensor(name, list(shape), dt))
        X = sb("X", [128, 8, P + HALO], F32)
        KT = sb("KT", [128, 4, KVW], BF16)
        VT = sb("VT", [128, 4, KVW], BF16)
        WS = sb("WS", [128, 4, 2048], BF16)
        AR = sb("AR", [128, NHB * HB], BF16)
        PAR = sb("PAR", [128, NPAR], F32)
        CB = sb("CB", [128, NCB], BF16)
        HST = sb("HST", [128, 2, 8], F32)
        CAH = sb("CAH", [128, 2, 8, 3], F32)
        FH = sb("FH", [128, 4, 24, 2], F32)
        SHL = sb("SHL", [128, 8, NB], F32)
        SCL = sb("SCL", [128, 8, NB, 3], F32)
        SFL = sb("SFL", [128, 24, NB, 2], F32)
        TMP16 = sb("TMP16", [128, NB], F32)
        VSNB = sb("VSNB", [NS, 512], BF16)
        VSNF = sb("VSNF", [NS, 512], F32)
        SMALL = sb("SMALL", [128, 2, 16], F32)
        PTS = sb("PTS", [128, 4, 128], BF16)
        VNB = sb("VNB", [4, 2, 512], BF16)
        S = Sched(nc, st)
        S.psums = [st.enter_context(nc.psum_tensor("ps%d" % i, [128, 512], F32)) for i in range(8)]

        def arb(i, n=HB):
            return AR[:, i * HB:i * HB + n]

        def arf(i, n=HB):
            return AR[:, i * HB:(i + 2) * HB].bitcast(F32)[:, 0:n]

        def XN(k):
            return arb(k)

        IDENT = CB[:, C_ID:C_ID + 128]
        MASKP = CB[:, C_MP:C_MP + 128]
        MASKC = CB[:, C_MC:C_MC + 128]
        BD = CB[:, C_BD:C_BD + 128]
        PM = CB[:, C_PM:C_PM + 128]
        ONES = CB[:, C_ON:C_ON + 128]

        def par(name, i=0):
            o = _po[name] + i
            return PAR[:, o:o + 1]

        wsn = [0]

        def wload(src, n=2048, parts=128):
            k = wsn[0] % 4
            wsn[0] += 1
            dst = WS[0:parts, k, 0:n]
            S.dma('pool', dst, src, w=[dst])
            return WS[:, k, :]

        S.dma('sp', PAR[:, :], d['par'], w=[PAR[:, :]])
        S.dma('pool', CB[:, :], d['cst'], w=[CB[:, :]])
        S.do('dve', MEMSET(HST[:, :, :], 0.0))
        S.do('dve', MEMSET(CAH[:, :, :, :], 0.0))
        S.do('dve', MEMSET(FH[:, :, :, :], 0.0))
        HV = PTS[:, 0, 64:128]
        HV2 = PTS[:, 1, 64:128]
        vhp = PAR[:, _po['vh']:_po['vh'] + 1]
        S.do('dve', TS(HV, ONES[:, 0:64], vhp, None, ALU.mult))
        S.do('dve', TS(HV2[0:64, :], ONES[0:64, 0:64], PAR[0:64, _po['vh']:_po['vh'] + 1], None, ALU.mult))
        S.do('dve', COPY(HV2[64:128, :], ONES[64:128, 0:64]))
        lam = PAR[:, _po['lam']:_po['lam'] + 16]
        clv = PAR[:, _po['cl']:_po['cl'] + 16]
        S.do('act', ACT(clv, lam, AF.Exp, scale=-1.0))
        S.do('act', ACT(clv, clv, AF.Ln, bias=1.0))
        S.do('dve', TS(clv, clv, -8.0, None, ALU.mult))
        S.do('dve', TS(lam, clv, 2.0, None, ALU.mult))

        def load_x(T):
            src = d['xT'].rearrange("(c p) t -> p c t", p=128)[:, :, P * T.t:P * T.t + P]
            S.dma('sp', X[:, :, 0:P], src, w=[X[:, :, 0:P]])
            if T.has_s:
                src = d['xsT'].rearrange("(c p) t -> p c t", p=128)
                S.dma('sp', X[:, :, P:P + NS], src, w=[X[:, :, P:P + NS]])

        def store_y(T):
            dst = d['yT'].rearrange("(c p) t -> p c t", p=128)[:, :, P * (T.t - 2):P * (T.t - 2) + P]
            S.dma('sp', dst, X[:, :, 0:P], r=[X[:, :, 0:P]])
            if T.has_s:
                dst = d['ysT'].rearrange("(c p) t -> p c t", p=128)
                S.dma('sp', dst, X[:, :, P:P + NS], r=[X[:, :, P:P + NS]])

        def rmsnorm(T, gname, gi):
            SQ = arb(32)
            RS = arf(30)
            W = T.W
            for (c0, n) in T.subs:
                ps = S.psum()
                for c in range(8):
                    sq = SQ[:, (c % 2) * 512:(c % 2) * 512 + n]
                    if c % 2 == 0:
                        S.do('act', ACT(sq, X[:, c, c0:c0 + n], AF.Square))
                    else:
                        S.do('dve', TT(sq, X[:, c, c0:c0 + n], X[:, c, c0:c0 + n], ALU.mult))
                    S.pe([MM(ps[:, 0:n], ONES, sq, c == 0, c == 7)])
                S.do('act', ACT(RS[:, c0:c0 + n], ps[:, 0:n], AF.Ln, bias=EPS, scale=1.0 / 1024))
                S.do('act', ACT(RS[:, c0:c0 + n], RS[:, c0:c0 + n], AF.Exp, scale=-0.5))
            for c in range(8):
                S.do('dve', STT(XN(c)[:, 0:W], X[:, c, 0:W], par(gname, gi * 8 + c), RS[:, 0:W], ALU.mult, ALU.mult))

        def sview(ap64, s=4):
            return ap64.rearrange("p (b s) -> p b s", s=s)

        def a_mix(T, L):
            W = T.W
            subs = T.subs
            XBH = arf(16)
            XBHs = XBH[:, 1028:1028 + NB * 7].rearrange("p (b j) -> p b j", j=7)
            XCs = [arf(18), arf(20)]
            XCBs = [arb(22), arb(23)]
            RAs = [arf(24), arf(30)]
            IU = arf(26)
            TH = arf(28)

            def GG(c):
                return arb(8 + c)
            if T.has_s:
                S.dma('sp', SHL[:, :, :], d['sh'][L].rearrange("p (c b) -> p c b", c=8), w=[SHL[:, :, :]])
                S.dma('sp', SCL[:, :, :, :], d['sc'][L].rearrange("p (c b j) -> p c b j", c=8, b=NB),
                      w=[SCL[:, :, :, :]])
            for n in range(4):
                wxb = wload(d['win'][L, 2 * n + 1]).rearrange("p (c k q) -> p c k q", c=2, k=8)
                wrg = wload(d['wg'][L, n], n=1024)[:, 0:1024].rearrange("p (g k o) -> p g k o", g=2, k=2)
                for cc in range(2):
                    c = 2 * n + cc
                    XC = XCs[cc]
                    S.do('dve', COPY(XBH[:, 0:3], CAH[:, L, c, :]))
                    if T.has_s:
                        S.do('dve', COPY(XBHs[:, :, 0:3], SCL[:, c, :, :]))
                    for (c0, nn) in subs:
                        ps = S.psum()
                        S.pe([MM(ps[:, 0:nn], wxb[:, cc, k, :], XN(k)[:, c0:c0 + nn], k == 0, k == 7) for k in range(8)])
                        if c0 < P:
                            S.do('act', ACT(XBH[:, 3 + c0:3 + c0 + nn], ps[:, 0:nn], AF.Copy))
                        else:
                            S.do('act', ACT(XBHs[:, :, 3:7], sview(ps[:, 0:NS]), AF.Copy))
                    S.do('dve', COPY(CAH[:, L, c, :], XBH[:, P:P + 3]))
                    if T.has_s:
                        S.do('dve', COPY(SCL[:, c, :, :], XBHs[:, :, 4:7]))
                    cw = [par('caw', L * 32 + j * 8 + c) for j in range(4)]
                    cbias = par('cab', L * 8 + c)
                    S.do('dve', TS(XC[:, 0:P], XBH[:, 3:3 + P], cw[3], cbias, ALU.mult, ALU.add))
                    for j in (2, 1, 0):
                        S.do('dve', STT(XC[:, 0:P], XBH[:, j:j + P], cw[j], XC[:, 0:P], ALU.mult, ALU.add))
                    if T.has_s:
                        XCv = sview(XC[:, P:P + NS])
                        S.do('dve', TS(XCv, XBHs[:, :, 3:7], cw[3], cbias, ALU.mult, ALU.add))
                        for j in (2, 1, 0):
                            S.do('dve', STT(XCv, XBHs[:, :, j:j + 4], cw[j], XCv, ALU.mult, ALU.add))
                    S.do('act', ACT(XCBs[cc][:, 0:W], XC[:, 0:W], AF.Copy))
                wgt = wload(d['win'][L, 2 * n]).rearrange("p (c k q) -> p c k q", c=2, k=8)
                for cc in range(2):
                    c = 2 * n + cc
                    for (c0, nn) in subs:
                        ps = S.psum()
                        S.pe([MM(ps[:, 0:nn], wgt[:, cc, k, :], XN(k)[:, c0:c0 + nn], k == 0, k == 7) for k in range(8)])
                        S.do('act', ACT(GG(c)[:, c0:c0 + nn], ps[:, 0:nn], AF.Gelu_apprx_tanh))
                for cc in range(2):
                    c = 2 * n + cc
                    XC = XCs[cc]
                    RA = RAs[cc]
                    for gate in range(2):
                        dst = RA if gate == 0 else IU
                        gb = par('grb' if gate == 0 else 'gib', L * 8 + c)
                        for (c0, nn) in subs:
                            ps = S.psum()
                            S.pe([MM(ps[:, 0:nn], wrg[:, gate, k, cc * 128:(cc + 1) * 128], XCBs[k][:, c0:c0 + nn],
                                     k == 0, k == 1) for k in range(2)])
                            S.do('act', ACT(dst[:, c0:c0 + nn], ps[:, 0:nn], AF.Sigmoid, bias=gb))
                    S.do('act', ACT(TH[:, 0:W], RA[:, 0:W], AF.Exp, scale=par('lam', L * 8 + c)))
                    S.do('act', ACT(RA[:, 0:W], RA[:, 0:W], AF.Exp, scale=par('cl', L * 8 + c)))
                    S.do('dve', TT(IU[:, 0:W], IU[:, 0:W], XC[:, 0:W], ALU.mult))
                    S.do('act', ACT(TH[:, 0:W], TH[:, 0:W], AF.Ln, bias=1.0, scale=-1.0))
                    S.do('act', ACT(TH[:, 0:W], TH[:, 0:W], AF.Exp, scale=0.5))
                    S.do('dve', TT(IU[:, 0:W], IU[:, 0:W], TH[:, 0:W], ALU.mult))
                    if T.t < 2:
                        S.do('dve', TS(IU[:, 0:W], IU[:, 0:W], par('vh'), None, ALU.mult))
                    S.do('dve', SCAN(TH[:, 0:P], RA[:, 0:P], IU[:, 0:P], HST[:, L, c:c + 1]))
                    S.do('dve', COPY(HST[:, L, c:c + 1], TH[:, P - 1:P]))
                    if T.has_s:
                        As = sview(RA[:, P:P + NS])
                        Us = sview(IU[:, P:P + NS])
                        S.do('dve', TT(TMP16[:, :], As[:, :, 0], SHL[:, c, :], ALU.mult))
                        S.do('dve', TT(Us[:, :, 0], Us[:, :, 0], TMP16[:, :], ALU.add))
                        S.do('dve', MEMSET(As[:, :, 0], 0.0))
                        S.do('dve', SCAN(TH[:, P:P + NS], RA[:, P:P + NS], IU[:, P:P + NS], 0.0))
                        S.do('dve', COPY(SHL[:, c, :], sview(TH[:, P:P + NS])[:, :, 3]))
                    S.do('dve', TT(GG(c)[:, 0:W], TH[:, 0:W], GG(c)[:, 0:W], ALU.mult))
            if T.has_s:
                S.dma('sp', d['soh'][L].rearrange("p (c b) -> p c b", c=8), SHL[:, :, :], r=[SHL[:, :, :]])
                S.dma('sp', d['soc'][L].rearrange("p (c b j) -> p c b j", c=8, b=NB), SCL[:, :, :, :],
                      r=[SCL[:, :, :, :]])
            for i in range(4):
                wsl = wload(d['wout'][L, i]).rearrange("p (c k q) -> p c k q", c=2, k=8)
                for cc in range(2):
                    m = 2 * i + cc
                    for (c0, nn) in subs:
                        ps = S.psum()
                        S.pe([MM(ps[:, 0:nn], wsl[:, cc, k, :], GG(k)[:, c0:c0 + nn], k == 0, k == 7) for k in range(8)])
                        S.do('dve', TT(X[:, m, c0:c0 + nn], X[:, m, c0:c0 + nn], ps[:, 0:nn], ALU.add))

        def ffn(T, L):
            W = T.W
            subs = T.subs

            def ACTB(jj):
                return arb(8 + jj)
            WD = AR[:, 16 * HB:16 * HB + 8192].rearrange("p (j m) -> p j m", j=8)
            GH = arf(24)
            GHs = GH[:, 1028:1028 + NB * 6].rearrange("p (b j) -> p b j", j=6)
            GC = arf(26)
            VB = arb(28)
            if T.has_s:
                S.dma('sp', SFL[:, :, :, :], d['sf'][L].rearrange("p (j b s) -> p j b s", j=24, b=NB),
                      w=[SFL[:, :, :, :]])
            for G in range(3):
                for jj in range(8):
                    j = 8 * G + jj
                    wsl = wload(d['wup'][L, j]).rearrange("p (k q) -> p k q", k=8)
                    if jj == 2:
                        wdst = AR[:, 16 * HB:16 * HB + 8192].rearrange("p (a b) -> p a b", a=4)
                        S.dma('pool', wdst, d['wdn'][L, G].rearrange("p (a b) -> p a b", a=4), w=[wdst])
                    if not T.halo:
                        S.do('dve', COPY(GH[:, 0:2], FH[:, L, j, :]))
                    if T.has_s:
                        S.do('dve', COPY(GHs[:, :, 0:2], SFL[:, j, :, :]))
                    for (c0, nn) in subs:
                        psg = S.psum()
                        S.pe([MM(psg[:, 0:nn], wsl[:, k, 0:128], XN(k)[:, c0:c0 + nn], k == 0, k == 7) for k in range(8)])
                        psv = S.psum()
                        S.pe([MM(psv[:, 0:nn], wsl[:, k, 128:256], XN(k)[:, c0:c0 + nn], k == 0, k == 7) for k in range(8)])
                        if T.halo:
                            if c0 < P:
                                S.do('act', ACT(GH[:, HALO + c0:HALO + c0 + nn], psg[:, 0:nn], AF.Copy))
                            else:
                                S.do('act', ACT(GH[:, 0:HALO], psg[:, 0:nn], AF.Copy))
                        elif c0 < P:
                            S.do('act', ACT(GH[:, 2 + c0:2 + c0 + nn], psg[:, 0:nn], AF.Copy))
                        else:
                            S.do('act', ACT(GHs[:, :, 2:6], sview(psg[:, 0:NS]), AF.Copy))
                        S.do('act', ACT(VB[:, c0:c0 + nn], psv[:, 0:nn], AF.Copy))
                    if T.halo:
                        S.do('dve', COPY(FH[:, L, j, :], GH[:, HALO + P - 2:HALO + P]))
                    else:
                        S.do('dve', COPY(FH[:, L, j, :], GH[:, P:P + 2]))
                    if T.has_s:
                        S.do('dve', COPY(SFL[:, j, :, :], GHs[:, :, 4:6]))
                    fw = [par('fcw', L * 72 + tap * 24 + j) for tap in range(3)]
                    fb = par('fcb', L * 24 + j)
                    if T.halo:
                        S.do('dve', TS(GC[:, 0:P], GH[:, HALO:HALO + P], fw[2], fb, ALU.mult, ALU.add))
                        S.do('dve', STT(GC[:, 0:P], GH[:, HALO - 1:HALO - 1 + P], fw[1], GC[:, 0:P], ALU.mult, ALU.add))
                        S.do('dve', STT(GC[:, 0:P], GH[:, HALO - 2:HALO - 2 + P], fw[0], GC[:, 0:P], ALU.mult, ALU.add))
                        S.do('dve', MEMSET(GC[:, P:P + 2], 0.0))
                        gch = GC[:, P + 2:P + HALO]
                        S.do('dve', TS(gch, GH[:, 2:HALO], fw[2], fb, ALU.mult, ALU.add))
                        S.do('dve', STT(gch, GH[:, 1:HALO - 1], fw[1], gch, ALU.mult, ALU.add))
                        S.do('dve', STT(gch, GH[:, 0:HALO - 2], fw[0], gch, ALU.mult, ALU.add))
                    else:
                        S.do('dve', TS(GC[:, 0:P], GH[:, 2:2 + P], fw[2], fb, ALU.mult, ALU.add))
                        S.do('dve', STT(GC[:, 0:P], GH[:, 1:1 + P], fw[1], GC[:, 0:P], ALU.mult, ALU.add))
                        S.do('dve', STT(GC[:, 0:P], GH[:, 0:P], fw[0], GC[:, 0:P], ALU.mult, ALU.add))
                    if T.has_s:
                        GCv = sview(GC[:, P:P + NS])
                        S.do('dve', TS(GCv, GHs[:, :, 2:6], fw[2], fb, ALU.mult, ALU.add))
                        S.do('dve', STT(GCv, GHs[:, :, 1:5], fw[1], GCv, ALU.mult, ALU.add))
                        S.do('dve', STT(GCv, GHs[:, :, 0:4], fw[0], GCv, ALU.mult, ALU.add))
                    S.do('act', ACT(ACTB(jj)[:, 0:W], GC[:, 0:W], AF.Gelu_apprx_tanh))
                    S.do('dve', TT(ACTB(jj)[:, 0:W], ACTB(jj)[:, 0:W], VB[:, 0:W], ALU.mult))
                for m in range(8):
                    for (c0, nn) in subs:
                        ps = S.psum()
                        S.pe([MM(ps[:, 0:nn], WD[:, jj, 128 * m:128 * m + 128], ACTB(jj)[:, c0:c0 + nn], jj == 0, jj == 7)
                              for jj in range(8)])
                        S.do('dve', TT(X[:, m, c0:c0 + nn], X[:, m, c0:c0 + nn], ps[:, 0:nn], ALU.add))
            if T.has_s:
                S.dma('sp', d['sof'][L].rearrange("p (j b s) -> p j b s", j=24, b=NB), SFL[:, :, :, :],
                      r=[SFL[:, :, :, :]])

        def rope_tables(T):
            W = T.W
            ANG = arf(26)
            KI = AR[:, 28 * HB:30 * HB].bitcast(I32)[:, 0:HB]
            KF = arf(30)
            C = arf(22)
            Sn = arf(24)
            S.dma('sp', ANG[:, 0:P], d['pos'][:, P * T.t:P * T.t + P], w=[ANG[:, 0:P]])
            if T.has_s:
                S.dma('sp', ANG[:, P:P + NS], d['pos'][:, TOK:TOK + NS], w=[ANG[:, P:P + NS]])
            if T.halo:
                S.dma('sp', ANG[:, P:P + HALO], d['pos'][:, P * T.t - HALO:P * T.t], w=[ANG[:, P:P + HALO]])
            S.do('dve', TS(ANG[:, 0:W], ANG[:, 0:W], par('inv'), None, ALU.mult))
            S.do('dve', TS(KI[:, 0:W], ANG[:, 0:W], 1.0 / (2 * math.pi), None, ALU.mult))
            S.do('dve', COPY(KF[:, 0:W], KI[:, 0:W]))
            S.do('dve', STT(ANG[:, 0:W], KF[:, 0:W], -2.0 * math.pi, ANG[:, 0:W], ALU.mult, ALU.add))
            S2 = arf(28)
            S4 = arf(30)
            S.do('act', ACT(S2[:, 0:W], ANG[:, 0:W], AF.Sin, scale=0.5))
            S.do('act', ACT(S4[:, 0:W], ANG[:, 0:W], AF.Sin, scale=0.25))
            S.do('dve', TT(C[:, 0:W], S2[:, 0:W], S2[:, 0:W], ALU.mult))
            S.do('dve', TS(C[:, 0:W], C[:, 0:W], -2.0, 1.0, ALU.mult, ALU.add))
            S.do('dve', TT(S4[:, 0:W], S4[:, 0:W], S4[:, 0:W], ALU.mult))
            S.do('dve', TS(S4[:, 0:W], S4[:, 0:W], -4.0, 2.0, ALU.mult, ALU.add))
            S.do('dve', TT(Sn[:, 0:W], S2[:, 0:W], S4[:, 0:W], ALU.mult))
            return C, Sn

        def kv(T):
            W = T.W
            subs = T.subs
            t = T.t
            wk = [wload(d['wkv'][0]).rearrange("p (k q) -> p k q", k=4),
                  wload(d['wkv'][1]).rearrange("p (k q) -> p k q", k=4)]
            sets = [dict(KF=arf(16), RS=arf(18), KNb=arb(20), SQ=arb(21)),
                    dict(KF=arf(26), RS=arf(28), KNb=arb(8), SQ=arb(9))]
            C, Sn = None, None
            import os as _os
            _lv = int(_os.environ.get('K_KV', '99'))
            if _lv < 1:
                return
            C, Sn = rope_tables(T)
            if _lv < 2:
                return
            for m in range(4 if _lv >= 3 else 1):
                bs = sets[m % 2]
                KF, RS, KNb, SQ = bs['KF'], bs['RS'], bs['KNb'], bs['SQ']
                pks = []
                for (c0, nn) in subs:
                    ps = S.psum()
                    S.pe([MM(ps[:, 0:nn], wk[k // 4][:, k % 4, 128 * m:128 * m + 128], XN(k)[:, c0:c0 + nn], k == 0, k == 7)
                          for k in range(8)])
                    S.do('act', ACT(SQ[:, c0:c0 + nn], ps[:, 0:nn], AF.Square))
                    S.do('act', ACT(KF[:, c0:c0 + nn], ps[:, 0:nn], AF.Copy))
                    pks.append(ps)
                for si, (c0, nn) in enumerate(subs):
                    ps2 = S.psum()
                    S.pe([MM(ps2[:, 0:nn], BD, SQ[:, c0:c0 + nn], True, True)])
                    S.do('act', ACT(RS[:, c0:c0 + nn], ps2[:, 0:nn], AF.Ln, bias=EPS, scale=1.0 / 64))
                    S.do('act', ACT(RS[:, c0:c0 + nn], RS[:, c0:c0 + nn], AF.Exp, scale=-0.5))
                S.do('dve', STT(KF[:, 0:W], KF[:, 0:W], par('gk'), RS[:, 0:W], ALU.mult, ALU.mult))
                S.do('act', ACT(KNb[:, 0:W], KF[:, 0:W], AF.Copy))
                for (c0, nn) in subs:
                    ps3 = S.psum()
                    S.pe([MM(ps3[:, 0:nn], PM, KNb[:, c0:c0 + nn], True, True)])
                    S.do('dve', TT(RS[:, c0:c0 + nn], ps3[:, 0:nn], Sn[:, c0:c0 + nn], ALU.mult))
                S.do('dve', TT(KF[:, 0:W], KF[:, 0:W], C[:, 0:W], ALU.mult))
                S.do('dve', TT(KF[:, 0:W], KF[:, 0:W], RS[:, 0:W], ALU.add))
                S.do('act', ACT(KT[:, m, P * t:P * t + P], KF[:, 0:P], AF.Copy))
                if T.has_s:
                    S.do('act', ACT(KT[:, m, TOK:TOK + NS], KF[:, P:P + NS], AF.Copy))
                    S.dma('sp', d['skT'].rearrange("(m p) t -> p m t", p=128)[:, m, :], KF[:, P:P + NS],
                          r=[KF[:, P:P + NS]])
                if t >= 2:
                    dst = d['pkT'].rearrange("(m p) t -> p m t", p=128)[:, m, P * (t - 2):P * (t - 2) + P]
                    S.dma('sp', dst, KF[:, 0:P], r=[KF[:, 0:P]])
            if _lv < 4:
                return
            wv = [wload(d['wkv'][2]).rearrange("p (k q) -> p k q", k=4),
                  wload(d['wkv'][3]).rearrange("p (k q) -> p k q", k=4)]
            for m in range(4):
                VF = sets[m % 2]['KF']
                for (c0, nn) in subs[0:2]:
                    ps = S.psum()
                    S.pe([MM(ps[:, 0:nn], wv[k // 4][:, k % 4, 128 * m:128 * m + 128], XN(k)[:, c0:c0 + nn], k == 0, k == 7)
                          for k in range(8)])
                    _vv = int(_os.environ.get('K_KVV', '3'))
                    if _vv & 1:
                        S.do('act', ACT(VF[:, c0:c0 + nn], ps[:, 0:nn], AF.Copy))
                    if _vv & 2:
                        S.do('dve', COPY(VT[:, m, P * t + c0:P * t + c0 + nn], ps[:, 0:nn]))
                if t >= 2:
                    dst = d['pvT'].rearrange("(m p) t -> p m t", p=128)[:, m, P * (t - 2):P * (t - 2) + P]
                    S.dma('sp', dst, VF[:, 0:P], r=[VF[:, 0:P]])
            if T.has_s:
                ps = S.psum()
                S.pe([MM(ps[0:NS, 0:512], XN(k)[:, P:P + NS], wv[k // 4][:, k % 4, :], k == 0, k == 7) for k in range(8)])
                S.do('act', ACT(VSNF[:, :], ps[0:NS, 0:512], AF.Copy))
                S.do('dve', COPY(VSNB[:, :], ps[0:NS, 0:512]))
                S.dma('sp', d['svtok'], VSNF[:, :], r=[VSNF[:, :]])

        ptn = [0]
        vbn = [0]

        def attention_prompt(T, hp, QT, NUM, DEN):
            t = T.t
            PTbuf = arb(12)
            VBbuf = AR[:, 13 * HB:15 * HB]
            MPC = CB[:, C_MP:C_MP + 256]
            M64 = CB[:, C_M64:C_M64 + 128]
            blocks = []
            hq0 = P * t - HALO
            for bq in range(8):
                blocks.append((0, 1, 128, 128 * bq, P * t + 128 * bq))
            if T.halo:
                blocks.append((0, 1, 128, P, hq0))
            for bb in range(2):
                for r in range(4):
                    blocks.append((1, 4, 128, 512 * bb + r, P * t + 512 * bb + r))
            if T.halo:
                for r in range(4):
                    blocks.append((1, 4, 32, P + r, hq0 + r))
            for r in range(16):
                blocks.append((2, 16, 64, r, P * t + r))
            if T.halo:
                for r in range(16):
                    blocks.append((2, 16, 8, P + r, hq0 + r))
            ND = AR[:, 26 * HB:30 * HB].bitcast(F32).rearrange("p (a c) -> p a c", a=2)
            def stage_a(g, dd, QB, qc, q0):
                kbs = []
                if q0 >= 128 * dd:
                    kbs.append((q0 - 128 * dd, 128, MASKP))
                else:
                    pmin = -((q0 - 128 * dd) // dd)
                    if pmin < 128:
                        assert QB <= pmin
                        kbs.append((q0 - 128 * dd + pmin * dd, 128 - pmin, None))
                kbs.append((q0, QB, MASKC))
                nkb = len(kbs)
                std = (nkb == 2 and kbs[0][1] == 128 and QB in (64, 128))
                pi = ptn[0] % 2
                ptn[0] += 1
                PT = PTbuf[:, pi * 512:pi * 512 + 512].rearrange("p (e c) -> p e c", e=2)
                pss2 = [S.psum(), S.psum()]
                S.pe([MM(pss2[e][0:nk, i * QB:(i + 1) * QB],
                         KT[64 * e:64 * e + 64, hp, k0:k0 + dd * (nk - 1) + 1:dd],
                         QT[g][64 * e:64 * e + 64, qc:qc + dd * (QB - 1) + 1:dd], True, True)
                      for i, (k0, nk, mask) in enumerate(kbs) for e in range(2)])
                for e in range(2):
                    pss = pss2[e]
                    if std:
                        S.do('act', ACT(PT[:, e, 0:2 * QB], pss[:, 0:2 * QB], AF.Exp, scale=0.125))
                    else:
                        for i, (k0, nk, mask) in enumerate(kbs):
                            S.do('act', ACT(PT[0:nk, e, i * QB:(i + 1) * QB], pss[0:nk, i * QB:(i + 1) * QB],
                                            AF.Exp, scale=0.125))
                if std:
                    mc = MPC if QB == 128 else M64
                    S.do('dve', TT(PT[:, :, 0:2 * QB], PT[:, :, 0:2 * QB],
                                   mc[:, None, :].broadcast_to([128, 2, 2 * QB]), ALU.mult))
                else:
                    for i, (k0, nk, mask) in enumerate(kbs):
                        if mask is not None:
                            pv3 = PT[0:nk, :, i * QB:(i + 1) * QB]
                            S.do('dve', TT(pv3, pv3, mask[0:nk, None, 0:QB].broadcast_to([nk, 2, QB]), ALU.mult))
                pst = S.psum()
                pstb = pst[:, :].bitcast(BF16)
                S.pe([TR(pstb[0:nk, i * 128:(i + 1) * 128], VT[:, hp, k0:k0 + dd * (nk - 1) + 1:dd], IDENT)
                      for i, (k0, nk, mask) in enumerate(kbs)])
                vi = vbn[0] % 8
                vbn[0] += 1
                VB = VBbuf[:, vi * 256:vi * 256 + 256]
                if std:
                    S.do('act', ACT(VB[:, 0:256], pstb[:, 0:256], AF.Copy))
                else:
                    for i, (k0, nk, mask) in enumerate(kbs):
                        S.do('dve', COPY(VB[0:nk, i * 128:(i + 1) * 128], pstb[0:nk, i * 128:(i + 1) * 128]))
                return (g, dd, QB, qc, kbs, nkb, PT, VB)

            def stage_b(ctx):
                (g, dd, QB, qc, kbs, nkb, PT, VB) = ctx
                psod = S.psum()
                ops = []
                for i, (k0, nk, mask) in enumerate(kbs):
                    for e in range(2):
                        ops.append(MM(psod[64 * e:64 * e + 64, 0:QB], VB[0:nk, i * 128 + 64 * e:i * 128 + 64 * e + 64],
                                      PT[0:nk, e, i * QB:(i + 1) * QB], i == 0, i == nkb - 1))
                for i, (k0, nk, mask) in enumerate(kbs):
                    for e in range(2):
                        nh = min(nk, max(0, -((k0 - 2 * P) // dd)))
                        if nh == 0:
                            dl = ONES[0:nk, 0:64]
                        elif nh == nk:
                            dl = HV[0:nk, :]
                        else:
                            assert nh == 64 and nk == 128
                            dl = HV2[0:nk, :]
                        ops.append(MM(psod[64 * e:64 * e + 64, 128:128 + QB], dl,
                                      PT[0:nk, e, i * QB:(i + 1) * QB], i == 0, i == nkb - 1))
                S.pe(ops)
                ndv = ND[:, :, qc:qc + dd * (QB - 1) + 1:dd]
                src = psod[:, 0:256].rearrange("p (a c) -> p a c", a=2)[:, :, 0:QB]
                if g == 0:
                    S.do('act', ACT(ndv, src, AF.Copy))
                else:
                    S.do('dve', TT(ndv, ndv, src, ALU.add))

            prev = None
            for blk in blocks:
                ctx = stage_a(*blk)
                if prev is not None:
                    stage_b(prev)
                prev = ctx
            stage_b(prev)

        def attention_sample(T, QS, ATBs):
            KSb = AR[:, 16 * HB:16 * HB + 4 * 512].rearrange("p (r q) -> p r q", r=4)
            VSb = AR[:, 18 * HB:18 * HB + 8 * 512].rearrange("p (r q) -> p r q", r=8)
            MS0 = CB[:, C_MS0:C_MS0 + 4]
            MN = CB[0:4, C_MN:C_MN + 12]
            kn = [0]
            vn = [0]
            NDs = SMALL[:, :, :]
            NUMs = SMALL[:, 0, 0:16]
            DENs = SMALL[:, 1, 0:16]
            bstate = {}

            def stage_a(b, g):
                if g == 0:
                    vslot = b % 2
                    S.dma('sp', VNB[0:4, vslot, :], VSNB[4 * b:4 * b + 4, :], r=[VSNB[4 * b:4 * b + 4, :]],
                          w=[VNB[0:4, vslot, :]])
                    VN = VNB[0:4, vslot, :]
                    PTN = PTS[0:4, 3, 0:96]
                    psn2 = [S.psum(), S.psum()]
                    ops = []
                    for gg in range(3):
                        for hp in range(4):
                            for e in range(2):
                                ops.append(MM(psn2[e][0:4, gg * 16 + hp * 4:gg * 16 + hp * 4 + 4],
                                              KT[64 * e:64 * e + 64, hp, TOK + 4 * b:TOK + 4 * b + 4],
                                              QS[64 * e:64 * e + 64, gg * 4 + hp, 4 * b:4 * b + 4], True, True))
                    S.pe(ops)
                    for e in range(2):
                        psn_ = psn2[e]
                        S.do('act', ACT(PTN[:, e * 48:(e + 1) * 48], psn_[0:4, 0:48], AF.Exp, scale=0.125))
                        pn4 = PTN[:, e * 48:(e + 1) * 48].rearrange("p (g h s) -> p g h s", g=3, h=4)
                        mn4 = MN.rearrange("p (g s) -> p g s", g=3)[:, :, None, :].broadcast_to([4, 3, 4, 4])
                        S.do('dve', TT(pn4, pn4, mn4, ALU.mult))
                    pn = PTN.rearrange("p (e g c) -> p e g c", e=2, g=3)
                    PTNS = PTS[0:4, 3 - (b % 2), 96:128]
                    PTNS3 = PTNS.rearrange("p (e c) -> p e c", e=2)
                    S.do('dve', TT(PTNS3, pn[:, :, 0, :], pn[:, :, 1, :], ALU.add))
                    S.do('dve', TT(PTNS3, PTNS3, pn[:, :, 2, :], ALU.add))
                    bstate[b] = (VN, PTNS)
                nblk = 1 if g == 0 else 4
                ks = []
                vs = []
                for i in range(nblk):
                    blk = 0 if g == 0 else 1 + 4 * (g - 1) + i
                    ki = kn[0] % 4
                    kn[0] += 1
                    vi = vn[0] % 8
                    vn[0] += 1
                    S.dma('pool', KSb[:, ki, :], d['ck'][b, blk], w=[KSb[:, ki, :]])
                    S.dma('pool', VSb[:, vi, :], d['cv'][b, blk], w=[VSb[:, vi, :]])
                    ks.append(KSb[:, ki, :].rearrange("p (a k) -> p a k", a=4))
                    vs.append(VSb[:, vi, :])
                PT = PTS[:, g, 0:32]
                pss2 = [S.psum(), S.psum()]
                ops = []
                for hp in range(4):
                    if g == 0:
                        for e in range(2):
                            ops.append(MM(pss2[e][:, hp * 4:hp * 4 + 4], ks[0][64 * e:64 * e + 64, hp, :],
                                          QS[64 * e:64 * e + 64, hp, 4 * b:4 * b + 4], True, True))
                    else:
                        for s_ in range(4):
                            for e in range(2):
                                ops.append(MM(pss2[e][:, hp * 4 + s_:hp * 4 + s_ + 1], ks[s_][64 * e:64 * e + 64, hp, :],
                                              QS[64 * e:64 * e + 64, g * 4 + hp, 4 * b + s_:4 * b + s_ + 1], True, True))
                S.pe(ops)
                for e in range(2):
                    S.do('act', ACT(PT[:, e * 16:(e + 1) * 16], pss2[e][:, 0:16], AF.Exp, scale=0.125))
                if g == 0:
                    p3 = PT.rearrange("p (h s) -> p h s", h=8)
                    S.do('dve', TT(p3, p3, MS0[:, None, :].broadcast_to([128, 8, 4]), ALU.mult))
                return (b, g, vs, PT)

            def stage_b(ctx):
                (b, g, vs, PT) = ctx
                (VN, PTNS) = bstate[b]
                last = (g == 2)
                psod = S.psum()
                ops = []
                for hp in range(4):
                    for s_ in range(4):
                        for e in range(2):
                            h = 2 * hp + e
                            col = hp * 4 + s_
                            vblk = vs[0] if g == 0 else vs[s_]
                            ops.append(MM(psod[64 * e:64 * e + 64, col:col + 1], vblk[:, h * 64:h * 64 + 64],
                                          PT[:, e * 16 + col:e * 16 + col + 1], True, not last))
                            if last:
                                ops.append(MM(psod[64 * e:64 * e + 64, col:col + 1], VN[:, h * 64:h * 64 + 64],
                                              PTNS[:, e * 16 + col:e * 16 + col + 1], False, True))
                for e in range(2):
                    ops.append(MM(psod[64 * e:64 * e + 64, 16:32], ONES[:, 0:64], PT[:, e * 16:(e + 1) * 16], True, not last))
                    if last:
                        ops.append(MM(psod[64 * e:64 * e + 64, 16:32], ONES[0:4, 0:64],
                                      PTNS[:, e * 16:(e + 1) * 16], False, True))
                S.pe(ops)
                src = psod[:, 0:32].rearrange("p (a c) -> p a c", a=2)
                if g == 0:
                    S.do('act', ACT(NDs, src, AF.Copy))
                else:
                    S.do('dve', TT(NDs, NDs, src, ALU.add))
                if last:
                    S.do('dve', RECIP(DENs, DENs))
                    S.do('dve', TT(ATBs[:, :, 4 * b:4 * b + 4], NUMs.rearrange("p (a c) -> p a c", a=4),
                                   DENs.rearrange("p (a c) -> p a c", a=4), ALU.mult))

            prev = None
            for b in range(NB):
                for g in range(3):
                    ctx = stage_a(b, g)
                    if prev is not None:
                        stage_b(prev)
                    prev = ctx
            stage_b(prev)

        def b_mix(T, jB):
            W = T.W
            subs = T.subs
            C, Sn = rope_tables(T)
            QN = arf(16)
            QNb = arb(18)
            QT = [arb(19), arb(20), arb(21)]
            RSq = arf(30)
            SQq = arb(32)
            NUM = arf(26)
            DEN = arf(28)
            QS = arb(15)[:, 0:12 * NS].rearrange("p (m c) -> p m c", m=12)

            def ATB(hp):
                return arb(8 + hp)
            for hp in range(4):
                for g in range(3):
                    wsl = wload(d['wq'][jB, hp * 3 + g], n=1024)[:, 0:1024].rearrange("p (k q) -> p k q", k=8)
                    psqs = []
                    for (c0, nn) in subs:
                        psq = S.psum()
                        S.pe([MM(psq[:, 0:nn], wsl[:, k, :], XN(k)[:, c0:c0 + nn], k == 0, k == 7) for k in range(8)])
                        S.do('act', ACT(SQq[:, c0:c0 + nn], psq[:, 0:nn], AF.Square))
                        psqs.append(psq)
                    for si, (c0, nn) in enumerate(subs):
                        psq = psqs[si]
                        ps2 = S.psum()
                        S.pe([MM(ps2[:, 0:nn], BD, SQq[:, c0:c0 + nn], True, True)])
                        S.do('act', ACT(RSq[:, c0:c0 + nn], ps2[:, 0:nn], AF.Ln, bias=EPS, scale=1.0 / 64))
                        S.do('act', ACT(RSq[:, c0:c0 + nn], RSq[:, c0:c0 + nn], AF.Exp, scale=-0.5))
                        S.do('dve', STT(QN[:, c0:c0 + nn], psq[:, 0:nn], par('gq', jB), RSq[:, c0:c0 + nn],
                                        ALU.mult, ALU.mult))
                    S.do('act', ACT(QNb[:, 0:W], QN[:, 0:W], AF.Copy))
                    for (c0, nn) in subs:
                        ps3 = S.psum()
                        S.pe([MM(ps3[:, 0:nn], PM, QNb[:, c0:c0 + nn], True, True)])
                        S.do('dve', TT(RSq[:, c0:c0 + nn], ps3[:, 0:nn], Sn[:, c0:c0 + nn], ALU.mult))
                    S.do('dve', TT(QN[:, 0:W], QN[:, 0:W], C[:, 0:W], ALU.mult))
                    S.do('dve', TT(QT[g][:, 0:W], QN[:, 0:W], RSq[:, 0:W], ALU.add))
                    if T.has_s:
                        S.do('act', ACT(QS[:, g * 4 + hp, :], QT[g][:, P:P + NS], AF.Copy))
                attention_prompt(T, hp, QT, NUM, DEN)
                Wa = P + (HALO if T.halo else 0)
                S.do('dve', TS(DEN[:, 0:Wa], DEN[:, 0:Wa], 1e-18, None, ALU.max))
                S.do('act', ACT(DEN[:, 0:Wa], DEN[:, 0:Wa], AF.Ln))
                S.do('act', ACT(DEN[:, 0:Wa], DEN[:, 0:Wa], AF.Exp, scale=-1.0))
                S.do('dve', TT(ATB(hp)[:, 0:Wa], NUM[:, 0:Wa], DEN[:, 0:Wa], ALU.mult))
            if T.has_s:
                ATBs = AR[:, 8 * HB:12 * HB].rearrange("p (h w) -> p h w", h=4)[:, :, P:P + NS]
                attention_sample(T, QS, ATBs)
            wos = [wload(d['wo'][jB, i]).rearrange("p (h q) -> p h q", h=4) for i in range(2)]
            for m in range(8):
                for (c0, nn) in subs:
                    ps = S.psum()
                    S.pe([MM(ps[:, 0:nn], wos[m // 4][:, hp, (m % 4) * 128:(m % 4) * 128 + 128], ATB(hp)[:, c0:c0 + nn],
                             hp == 0, hp == 3) for hp in range(4)])
                    S.do('dve', TT(X[:, m, c0:c0 + nn], X[:, m, c0:c0 + nn], ps[:, 0:nn], ALU.add))

        for t in range(NT):
            T = Tile(t)
            TB = Tile(t, halo=(t == 2))
            load_x(T)
            for L in range(2):
                rmsnorm(T, 'nma', L)
                a_mix(T, L)
                rmsnorm(T, 'nff', L)
                ffn(T, L)
            rmsnorm(T, 'nkv', 0)
            kv(T)
            if t == 1:
                S.do('act', ACT(X[:, :, P:P + HALO], X[:, :, P - HALO:P], AF.Copy))
            if t >= 2:
                for jB in range(2):
                    rmsnorm(TB, 'nmb', jB)
                    b_mix(TB, jB)
                    rmsnorm(TB, 'nff', 2 + jB)
                    ffn(TB, 2 + jB)
                store_y(T)
        S.dma('sp', d['oh'], HST[:, :, :].rearrange("p l c -> p (l c)"), r=[HST[:, :, :]])
        S.dma('sp', d['oc'], CAH[:, :, :, :].rearrange("p l c j -> p (l c j)"), r=[CAH[:, :, :, :]])
        S.dma('sp', d['of'], FH[:, :, :, :].rearrange("p l c j -> p (l c j)"), r=[FH[:, :, :, :]])
        S.finish()
        with nc.Block() as block:
            S.emit(block)
    return nc


def _chunk(v, nchunk):
    return np.ascontiguousarray(np.asarray(v, np.float32).reshape(nchunk, 128).T)


def _consts():
    cst = np.zeros((128, NCB), np.float32)
    p = np.arange(128)[:, None]
    f = np.arange(128)[None, :]
    cst[:, C_ID:C_ID + 128] = np.eye(128)
    cst[:, C_MP:C_MP + 128] = (f <= p)
    cst[:, C_MC:C_MC + 128] = (f >= p)
    cst[:, C_BD:C_BD + 128] = ((p // 64) == (f // 64))
    pm = np.zeros((128, 128), np.float32)
    for m in range(128):
        i = m % 64
        if i < 8:
            pm[m + 8, m] = -1.0
        elif i < 16:
            pm[m - 8, m] = 1.0
    cst[:, C_PM:C_PM + 128] = pm
    cst[:, C_ON:C_ON + 128] = 1.0
    s = np.arange(4)[None, :]
    cst[:, C_MS0:C_MS0 + 4] = (np.arange(128)[:, None] >= s)
    mn = np.zeros((4, 3, 4), np.float32)
    sp = np.arange(4)[:, None]
    mn[:, 0, :] = (sp <= s)
    mn[:, 1, :] = (sp == s)
    mn[:, 2, :] = (sp == s)
    cst[0:4, C_MN:C_MN + 12] = mn.reshape(4, 12)
    cst[:, C_M64:C_M64 + 64] = (f <= p)[:, 0:64]
    cst[:, C_M64 + 64:C_M64 + 128] = (f >= p)[:, 0:64]
    return cst


def _host_layout(inp):
    f = lambda a: np.asarray(a, np.float32)
    com = {}
    par = np.zeros((128, NPAR), np.float32)

    def put(name, off, arr):
        o = _po[name] + off
        par[:, o:o + arr.shape[1]] = arr
    for L in range(2):
        put('nma', 8 * L, _chunk(f(inp['norm_mix_a'])[L], 8))
        for j in range(4):
            put('caw', 32 * L + 8 * j, _chunk(f(inp['conv_a_w'])[L, j], 8))
        put('cab', 8 * L, _chunk(f(inp['conv_a_b'])[L], 8))
        put('grb', 8 * L, _chunk(f(inp['gate_r_b'])[L].reshape(-1), 8))
        put('gib', 8 * L, _chunk(f(inp['gate_i_b'])[L].reshape(-1), 8))
        put('lam', 8 * L, _chunk(f(inp['lru_lambda'])[L], 8))
        put('nmb', 8 * L, _chunk(f(inp['norm_mix_b'])[L], 8))
        par[:, _po['gq'] + L] = np.tile(f(inp['q_norm'])[L], 2)
    put('nkv', 0, _chunk(f(inp['norm_kv']), 8))
    for L in range(4):
        put('nff', 8 * L, _chunk(f(inp['norm_ffn'])[L], 8))
        for tap in range(3):
            put('fcw', 72 * L + 24 * tap, _chunk(f(inp['ffn_conv_w'])[L, tap], 24))
        put('fcb', 24 * L, _chunk(f(inp['ffn_conv_b'])[L], 24))
    par[:, _po['gk']] = np.tile(f(inp['k_norm']), 2)
    half = 8
    inv = (500000.0 ** (-np.arange(half, dtype=np.float32) * np.float32(2.0 / 16))).astype(np.float32)
    invp = np.zeros(128, np.float32)
    for pp in range(128):
        i = pp % 64
        if i < 16:
            invp[pp] = inv[i % 8]
    par[:, _po['inv']] = invp
    com['par'] = par
    com['cst'] = _consts()
    pos = np.zeros((KVW,), np.float32)
    pos[:TOK] = np.arange(TOK)
    pos[TOK:] = np.tile(2048 + np.arange(4), NB)
    com['pos'] = np.ascontiguousarray(np.broadcast_to(pos[None, :], (128, KVW)))
    w_in = f(inp['w_in_a'])
    win = np.empty((2, 8, 128, 2048), np.float32)
    for L in range(2):
        Wk = w_in[L].reshape(8, 128, 2048)
        for i in range(8):
            n, kind = i // 2, i % 2
            c0 = kind * 1024 + 256 * n
            blk = Wk[:, :, c0:c0 + 256].reshape(8, 128, 2, 128)
            win[L, i] = blk.transpose(1, 2, 0, 3).reshape(128, 2048)
    com['win'] = win
    wg = np.empty((2, 4, 128, 1024), np.float32)
    rw, iw = f(inp['gate_r_w']), f(inp['gate_i_w'])
    for L in range(2):
        for n in range(4):
            a = np.stack([rw[L, n], iw[L, n]], 0).reshape(2, 2, 128, 256)
            wg[L, n] = a.transpose(2, 0, 1, 3).reshape(128, 1024)
    com['wg'] = wg
    w_out = f(inp['w_out_a'])
    wout = np.empty((2, 4, 128, 2048), np.float32)
    for L in range(2):
        Wk = w_out[L].reshape(8, 128, 1024)
        for i in range(4):
            blk = Wk[:, :, 256 * i:256 * i + 256].reshape(8, 128, 2, 128)
            wout[L, i] = blk.transpose(1, 2, 0, 3).reshape(128, 2048)
    com['wout'] = wout
    w_up = f(inp['w_ffn_up'])
    wup = np.empty((4, 24, 128, 2048), np.float32)
    for L in range(4):
        Wk = w_up[L].reshape(8, 128, 6144)
        g = Wk[:, :, 0:3072].reshape(8, 128, 24, 128)
        v = Wk[:, :, 3072:6144].reshape(8, 128, 24, 128)
        gv = np.stack([g, v], 3)
        wup[L] = gv.transpose(2, 1, 0, 3, 4).reshape(24, 128, 2048)
    com['wup'] = wup
    w_dn = f(inp['w_ffn_down'])
    wdn = np.empty((4, 3, 128, 8192), np.float32)
    for L in range(4):
        a = w_dn[L].reshape(3, 8, 128, 1024)
        wdn[L] = a.transpose(0, 2, 1, 3).reshape(3, 128, 8192)
    com['wdn'] = wdn
    w_kv = f(inp['w_kv']).reshape(2, 4, 128, 1024)
    wkv = np.empty((4, 128, 2048), np.float32)
    for half_ in range(2):
        for kh in range(2):
            a = w_kv[kh][:, :, 512 * half_:512 * half_ + 512]
            wkv[2 * half_ + kh] = a.transpose(1, 0, 2).reshape(128, 2048)
    com['wkv'] = wkv
    w_q = f(inp['w_q'])
    wq = np.empty((2, 12, 128, 1024), np.float32)
    for j in range(2):
        Wk = w_q[j].reshape(8, 128, 1536)
        for hp in range(4):
            for g in range(3):
                m = 4 * g + hp
                wq[j, hp * 3 + g] = Wk[:, :, 128 * m:128 * m + 128].transpose(1, 0, 2).reshape(128, 1024)
    com['wq'] = wq
    w_o = f(inp['w_o'])
    wo = np.empty((2, 2, 128, 2048), np.float32)
    for j in range(2):
        Wk = w_o[j].reshape(4, 128, 1024)
        for i in range(2):
            wo[j, i] = Wk[:, :, 512 * i:512 * i + 512].transpose(1, 0, 2).reshape(128, 2048)
    com['wo'] = wo
    xp = f(inp['x_prompt'])
    xs = f(inp['x_sample'])
    ck_, cv_ = f(inp['cache_k']), f(inp['cache_v'])
    rows = [1920 + np.arange(128)]
    for s in range(4):
        rows.append(1536 + s + 4 * np.arange(128))
    for s in range(4):
        rows.append(s + 16 * np.arange(128))
    rows = np.stack(rows, 0)
    sh, sc, sf = f(inp['state_rglru_h']), f(inp['state_rglru_conv']), f(inp['state_ffn_conv'])
    maps = []
    posv = com.pop('pos')
    for c in range(8):
        m = dict(com)
        bq, hf = c // 2, c % 2
        if hf == 1:
            m['xT'] = np.ascontiguousarray(xp[bq].T)
            m['pos'] = posv
        else:
            xt_ = np.zeros((1024, TOK), np.float32)
            xt_[:, 2 * P:] = xp[bq, 0:2 * P].T
            m['xT'] = xt_
            pz = posv.copy()
            pz[:, 0:2 * P] = 0.0
            pz[:, 2 * P:TOK] = posv[:, 0:2 * P]
            m['pos'] = pz
        pc = par.copy()
        pc[:, _po['vh']] = float(hf)
        m['par'] = pc
        b0 = NB * c
        m['xsT'] = np.ascontiguousarray(xs[b0:b0 + NB].reshape(NS, 1024).T)
        kk = ck_[b0:b0 + NB][:, rows]
        kk = kk.reshape(NB, 9, 128, 4, 2, 64).transpose(0, 1, 4, 5, 3, 2)
        m['ck'] = np.ascontiguousarray(kk.reshape(NB, 9, 128, 512))
        m['cv'] = np.ascontiguousarray(cv_[b0:b0 + NB][:, rows].reshape(NB, 9, 128, 512))
        a = sh[:, b0:b0 + NB].reshape(2, NB, 8, 128)
        m['sh'] = np.ascontiguousarray(a.transpose(0, 3, 2, 1).reshape(2, 128, 128))
        a = sc[:, b0:b0 + NB].reshape(2, NB, 3, 8, 128)
        m['sc'] = np.ascontiguousarray(a.transpose(0, 4, 3, 1, 2).reshape(2, 128, 384))
        a = sf[:, b0:b0 + NB].reshape(4, NB, 2, 24, 128)
        m['sf'] = np.ascontiguousarray(a.transpose(0, 4, 3, 1, 2).reshape(4, 128, 768))
        maps.append(m)
    return maps


_NC_CACHE = {}


def kernel(**inputs):
    maps = _host_layout(inputs)
    if 'nc' not in _NC_CACHE:
        _NC_CACHE['nc'] = build_program()
    nc = _NC_CACHE['nc']
    res = run_bass_kernel_spmd(nc, maps, core_ids=list(range(8)))
    R = res.results
    y = np.stack([np.concatenate([R[2 * b]['yT'].T, R[2 * b + 1]['yT'].T], 0) for b in range(4)], 0)
    ys = np.concatenate([R[c]['ysT'].T.reshape(NB, 4, 1024) for c in range(8)], 0)
    p_h = np.stack([R[2 * b + 1]['oh'].reshape(128, 2, 8).transpose(1, 2, 0).reshape(2, 1024) for b in range(4)], 1)
    p_c = np.stack([R[2 * b + 1]['oc'].reshape(128, 2, 8, 3).transpose(1, 3, 2, 0).reshape(2, 3, 1024) for b in range(4)], 1)
    p_f = np.stack([R[2 * b + 1]['of'].reshape(128, 4, 24, 2).transpose(1, 3, 2, 0).reshape(4, 2, 3072) for b in range(4)], 1)

    def fm2tok(a, n):
        return a.reshape(4, 2, 64, n).transpose(3, 0, 1, 2).reshape(n, 8, 64)
    p_k = np.stack([fm2tok(R[2 * b + 1]['pkT'], 2048) for b in range(4)], 0)
    p_v = np.stack([fm2tok(R[2 * b + 1]['pvT'], 2048) for b in range(4)], 0)
    s_h = np.concatenate([R[c]['soh'].reshape(2, 128, 8, NB).transpose(0, 3, 2, 1).reshape(2, NB, 1024)
                          for c in range(8)], 1)
    s_c = np.concatenate([R[c]['soc'].reshape(2, 128, 8, NB, 3).transpose(0, 3, 4, 2, 1).reshape(2, NB, 3, 1024)
                          for c in range(8)], 1)
    s_f = np.concatenate([R[c]['sof'].reshape(4, 128, 24, NB, 2).transpose(0, 3, 4, 2, 1).reshape(4, NB, 2, 3072)
                          for c in range(8)], 1)
    s_k = np.concatenate([fm2tok(R[c]['skT'], NS).reshape(NB, 4, 8, 64) for c in range(8)], 0)
    s_v = np.concatenate([R[c]['svtok'].reshape(NB, 4, 8, 64) for c in range(8)], 0)
    outs = (y, ys, p_h, p_c, p_f, p_k, p_v, s_h, s_c, s_f, s_k, s_v)
    return tuple(np.ascontiguousarray(o, dtype=np.float32) for o in outs)
```
